# Optimizing a Trainium2 kernel written in Bass

```python
import math
import jax, jax.numpy as jnp
from jax import lax
import numpy as np

D_MODEL = 2048
BATCH = 2
SEQ = 4096
DEPTH = 4

MIX_WIDTH = D_MODEL // 2
N_BRANCHES = 3
EPS = 1e-6
CHUNK = 128
GM_WIDTH = MIX_WIDTH
GM_GROUP_CH = 128
GM_GROUPS = GM_WIDTH // GM_GROUP_CH
S5_WIDTH = MIX_WIDTH
S5_GROUP_CH = 16
S5_GROUPS = S5_WIDTH // S5_GROUP_CH
S5_STATE = 64
DT_MIN = 1e-3
DT_MAX = 1e-1
SB_WIDTH = MIX_WIDTH
SB_HEAD_DIM = 128
SB_HEADS = SB_WIDTH // SB_HEAD_DIM
SB_BLOCK = 128
PROJ_COLS = 2 * GM_WIDTH + S5_WIDTH + 3 * SB_WIDTH + N_BRANCHES * D_MODEL
D_FF = 4 * D_MODEL

kernel_name = "hybrid_gated_gmlp_s5_stickbreak"


def rmsnorm(x, g):
    xf = x.astype(jnp.float32)
    y = xf * lax.rsqrt(jnp.mean(xf * xf, axis=-1, keepdims=True) + EPS)
    return (y * g.astype(jnp.float32)).astype(x.dtype)


def gmlp_mixer(uv, norm_g, w_s, b_s):
    bsz, seq, _ = uv.shape
    z = jax.nn.gelu(uv)
    u, v = jnp.split(z, 2, axis=-1)
    v = rmsnorm(v, norm_g)
    v = v.reshape(bsz, seq // CHUNK, CHUNK, GM_GROUPS, GM_GROUP_CH)
    w = jnp.tril(w_s)
    mixed = jnp.einsum('gts,bcsgh->bctgh', w, v) + jnp.transpose(b_s)[None, None, :, :, None]
    return u * mixed.reshape(bsz, seq, GM_WIDTH)


def s5_mixer(xin, lam_re, lam_im, log_dt, b_re, b_im, c_re, c_im, d, w_glu, b_glu):
    f32 = jnp.float32
    bsz, seq, _ = xin.shape
    u = xin.astype(f32).reshape(bsz, seq, S5_GROUPS, S5_GROUP_CH)
    lam_re = lam_re.astype(f32)
    lam_im = lam_im.astype(f32)
    b_re = b_re.astype(f32)
    b_im = b_im.astype(f32)
    dt = jnp.exp(log_dt.astype(f32))[:, None]
    mag = jnp.exp(lam_re * dt)
    ab_re = mag * jnp.cos(lam_im * dt)
    ab_im = mag * jnp.sin(lam_im * dt)
    den = lam_re * lam_re + lam_im * lam_im
    n_re = ab_re - 1.0
    n_im = ab_im
    k_re = (n_re * lam_re + n_im * lam_im) / den
    k_im = (n_im * lam_re - n_re * lam_im) / den
    bb_re = k_re[..., None] * b_re - k_im[..., None] * b_im
    bb_im = k_re[..., None] * b_im + k_im[..., None] * b_re
    bu_re = jnp.einsum('blgh,gph->blgp', u, bb_re)
    bu_im = jnp.einsum('blgh,gph->blgp', u, bb_im)
    a_re = jnp.broadcast_to(ab_re, bu_re.shape)
    a_im = jnp.broadcast_to(ab_im, bu_im.shape)

    def combine(e_i, e_j):
        ar_i, ai_i, br_i, bi_i = e_i
        ar_j, ai_j, br_j, bi_j = e_j
        return (ar_j * ar_i - ai_j * ai_i,
                ar_j * ai_i + ai_j * ar_i,
                ar_j * br_i - ai_j * bi_i + br_j,
                ar_j * bi_i + ai_j * br_i + bi_j)

    _, _, s_re, s_im = lax.associative_scan(combine, (a_re, a_im, bu_re, bu_im), axis=1)
    y = (jnp.einsum('blgp,ghp->blgh', s_re, c_re.astype(f32))
         - jnp.einsum('blgp,ghp->blgh', s_im, c_im.astype(f32)))
    y = y.reshape(bsz, seq, S5_WIDTH) + d.astype(f32) * xin.astype(f32)
    g = jax.nn.gelu(y)
    out = g * jax.nn.sigmoid(g @ w_glu.astype(f32) + b_glu.astype(f32))
    return out.astype(xin.dtype)


def stick_breaking_attention(q, k, v):
    f32 = jnp.float32
    seq = q.shape[2]
    scale = SB_HEAD_DIM ** -0.5
    outs = []
    for blk in range(seq // SB_BLOCK):
        t0 = blk * SB_BLOCK
        t1 = t0 + SB_BLOCK
        qb = q[:, :, t0:t1].astype(f32)
        kb = k[:, :, :t1].astype(f32)
        vb = v[:, :, :t1].astype(f32)
        z = jnp.einsum('bhtd,bhsd->bhts', qb, kb) * scale
        mask = jnp.arange(t1)[None, :] < jnp.arange(t0, t1)[:, None]
        log_1m = jnp.where(mask, jax.nn.log_sigmoid(-z), 0.0)
        after = lax.cumsum(log_1m, axis=3, reverse=True) - log_1m
        w = jnp.where(mask, jnp.exp(jax.nn.log_sigmoid(z) + after), 0.0)
        outs.append(jnp.einsum('bhts,bhsd->bhtd', w, vb))
    return jnp.concatenate(outs, axis=2).astype(q.dtype)


def setup_inputs(seed: int = 0) -> dict:
    key = jax.random.key(seed)
    ks = jax.random.split(key, 32)

    def nrm(k, shape, scale):
        return jax.random.normal(k, shape, jnp.float32) * scale

    x = nrm(ks[0], (BATCH, SEQ, D_MODEL), 1.0)
    norm1_g = 1.0 + nrm(ks[1], (DEPTH, D_MODEL), 0.05)
    w_in = nrm(ks[2], (DEPTH, D_MODEL, PROJ_COLS), D_MODEL ** -0.5)
    b_gate = nrm(ks[3], (DEPTH, N_BRANCHES, D_MODEL), 0.1)
    gm_norm_g = 1.0 + nrm(ks[4], (DEPTH, GM_WIDTH), 0.05)
    gm_w_s = nrm(ks[5], (DEPTH, GM_GROUPS, CHUNK, CHUNK), CHUNK ** -0.5)
    gm_b_s = 1.0 + nrm(ks[6], (DEPTH, GM_GROUPS, CHUNK), 0.1)
    s5_lambda_re = -0.5 + nrm(ks[7], (DEPTH, S5_GROUPS, S5_STATE), 0.01)
    s5_lambda_im = (math.pi * jnp.arange(S5_STATE, dtype=jnp.float32))[None, None, :] \
        + nrm(ks[8], (DEPTH, S5_GROUPS, S5_STATE), 0.01)
    s5_log_dt = jax.random.uniform(ks[9], (DEPTH, S5_GROUPS), jnp.float32,
                                   math.log(DT_MIN), math.log(DT_MAX))
    s5_b_re = nrm(ks[10], (DEPTH, S5_GROUPS, S5_STATE, S5_GROUP_CH), (2 * S5_GROUP_CH) ** -0.5)
    s5_b_im = nrm(ks[11], (DEPTH, S5_GROUPS, S5_STATE, S5_GROUP_CH), (2 * S5_GROUP_CH) ** -0.5)
    s5_c_re = nrm(ks[12], (DEPTH, S5_GROUPS, S5_GROUP_CH, S5_STATE), (2 * S5_STATE) ** -0.5)
    s5_c_im = nrm(ks[13], (DEPTH, S5_GROUPS, S5_GROUP_CH, S5_STATE), (2 * S5_STATE) ** -0.5)
    s5_d = nrm(ks[14], (DEPTH, S5_WIDTH), 1.0)
    s5_w_glu = nrm(ks[15], (DEPTH, S5_WIDTH, S5_WIDTH), S5_WIDTH ** -0.5)
    s5_b_glu = nrm(ks[16], (DEPTH, S5_WIDTH), 0.02)
    w_branch = nrm(ks[17], (DEPTH, N_BRANCHES, MIX_WIDTH, D_MODEL), MIX_WIDTH ** -0.5)
    w_out = nrm(ks[18], (DEPTH, D_MODEL, D_MODEL), D_MODEL ** -0.5)
    norm2_g = 1.0 + nrm(ks[19], (DEPTH, D_MODEL), 0.05)
    w_mlp_in = nrm(ks[20], (DEPTH, D_MODEL, D_FF), D_MODEL ** -0.5)
    w_mlp_out = nrm(ks[21], (DEPTH, D_FF, D_MODEL), D_FF ** -0.5)
    final_g = 1.0 + nrm(ks[22], (D_MODEL,), 0.05)
    return {"x": x, "norm1_g": norm1_g, "w_in": w_in, "b_gate": b_gate,
            "gm_norm_g": gm_norm_g, "gm_w_s": gm_w_s, "gm_b_s": gm_b_s,
            "s5_lambda_re": s5_lambda_re, "s5_lambda_im": s5_lambda_im, "s5_log_dt": s5_log_dt,
            "s5_b_re": s5_b_re, "s5_b_im": s5_b_im, "s5_c_re": s5_c_re, "s5_c_im": s5_c_im,
            "s5_d": s5_d, "s5_w_glu": s5_w_glu, "s5_b_glu": s5_b_glu,
            "w_branch": w_branch, "w_out": w_out, "norm2_g": norm2_g,
            "w_mlp_in": w_mlp_in, "w_mlp_out": w_mlp_out, "final_g": final_g}


def reference(x, norm1_g, w_in, b_gate, gm_norm_g, gm_w_s, gm_b_s,
              s5_lambda_re, s5_lambda_im, s5_log_dt, s5_b_re, s5_b_im, s5_c_re, s5_c_im,
              s5_d, s5_w_glu, s5_b_glu, w_branch, w_out, norm2_g,
              w_mlp_in, w_mlp_out, final_g):
    bsz, seq, _ = x.shape
    o_a = 2 * GM_WIDTH
    o_b = o_a + S5_WIDTH
    o_c = o_b + 3 * SB_WIDTH
    for l in range(DEPTH):
        h = rmsnorm(x, norm1_g[l])
        proj = h @ w_in[l]
        uv = proj[..., :o_a]
        s5_in = proj[..., o_a:o_b]
        qkv = proj[..., o_b:o_c].reshape(bsz, seq, 3, SB_HEADS, SB_HEAD_DIM)
        qkv = jnp.transpose(qkv, (2, 0, 3, 1, 4))
        gates = jax.nn.sigmoid(proj[..., o_c:].reshape(bsz, seq, N_BRANCHES, D_MODEL) + b_gate[l])

        y_a = gmlp_mixer(uv, gm_norm_g[l], gm_w_s[l], gm_b_s[l])
        y_b = s5_mixer(s5_in, s5_lambda_re[l], s5_lambda_im[l], s5_log_dt[l],
                       s5_b_re[l], s5_b_im[l], s5_c_re[l], s5_c_im[l],
                       s5_d[l], s5_w_glu[l], s5_b_glu[l])
        y_c = stick_breaking_attention(qkv[0], qkv[1], qkv[2])
        y_c = jnp.transpose(y_c, (0, 2, 1, 3)).reshape(bsz, seq, SB_WIDTH)

        ys = jnp.stack([y_a, y_b, y_c], axis=2)
        br = jnp.einsum('blnw,nwd->blnd', ys, w_branch[l])
        merged = jnp.sum(gates * br, axis=2)
        x = x + merged @ w_out[l]
        h2 = rmsnorm(x, norm2_g[l])
        x = x + jnp.square(jax.nn.relu(h2 @ w_mlp_in[l])) @ w_mlp_out[l]
    return rmsnorm(x, final_g)
```

```python
import numpy as np
from contextlib import ExitStack
import ml_dtypes
import concourse.bass as bass
import concourse.mybir as mybir
from concourse.bass_utils import run_bass_kernel_spmd

F32 = mybir.dt.float32
BF16 = mybir.dt.bfloat16
AF = mybir.ActivationFunctionType
ALU = mybir.AluOpType
AX = mybir.AxisListType
NPBF = ml_dtypes.bfloat16

NCORES = 8
TOK = 1024
D = 2048
EPS = 1e-6
QSCALE = 128 ** -0.5


class Dep:
    __slots__ = ("w", "r")

    def __init__(self):
        self.w = None
        self.r = {}


class Prog:
    ENGS = ("pe", "act", "dve", "pool", "sp")

    def __init__(self, nc, stack, n_dma_sems=(("sp", 12), ("pool", 8), ("act", 4))):
        self.nc = nc
        self.stack = stack
        self.q = {e: [] for e in self.ENGS}
        self.sem = {e: stack.enter_context(nc.semaphore("s_" + e)) for e in ("pe", "act", "dve", "pool")}
        self.cnt = {e: 0 for e in self.sem}
        tot = sum(n for _, n in n_dma_sems)
        self.dsem = [stack.enter_context(nc.semaphore("d%d" % i)) for i in range(tot)]
        self.dcnt = [0] * tot
        self.dpool = {}
        b = 0
        for qn, n in n_dma_sems:
            self.dpool[qn] = [list(range(b, b + n)), 0]
            b += n
        self.seen = {e: {} for e in self.ENGS}
        self.same_sync = {"pe": False, "act": True, "dve": True, "pool": True, "sp": True}

    def _semobj(self, key):
        return self.sem[key[1]] if key[0] == "e" else self.dsem[key[1]]

    def _collect(self, eng, reads, writes, extra=None):
        need = dict(extra or {})

        def req(k, v):
            if v > need.get(k, 0):
                need[k] = v
        for t in reads:
            if t.w:
                req(*t.w)
        for t in writes:
            if t.w:
                req(*t.w)
            for k, v in t.r.items():
                req(k, v)
        for k, v in need.items():
            if k == ("e", eng) and not self.same_sync[eng]:
                continue
            if self.seen[eng].get(k, 0) < v:
                self.seen[eng][k] = v
                s = self._semobj(k)
                self.q[eng].append(lambda e, s=s, v=v: e.wait_ge(s, v))

    def op(self, eng, fn, reads=(), writes=()):
        self._collect(eng, reads, writes)
        self.cnt[eng] += 1
        c = self.cnt[eng]
        key = ("e", eng)
        s = self.sem[eng]
        self.q[eng].append(lambda e, fn=fn, s=s: fn(e).then_inc(s, 1))
        for t in reads:
            if t.r.get(key, 0) < c:
                t.r[key] = c
        for t in writes:
            t.w = (key, c)
            t.r = {}

    def dma(self, queue, out, in_, reads=(), writes=(), **kw):
        pl = self.dpool[queue]
        i = pl[0][pl[1] % len(pl[0])]
        pl[1] += 1
        extra = {}
        if self.dcnt[i] > 0:
            extra[("d", i)] = self.dcnt[i]
        self._collect(queue, reads, writes, extra)
        self.dcnt[i] += 16
        v = self.dcnt[i]
        key = ("d", i)
        s = self.dsem[i]
        self.q[queue].append(
            lambda e, s=s, out=out, in_=in_, kw=kw: e.dma_start(out=out, in_=in_, **kw).then_inc(s, 16))
        for t in reads:
            if t.r.get(key, 0) < v:
                t.r[key] = v
        for t in writes:
            t.w = (key, v)
            t.r = {}

    def finish(self):
        for i, v in enumerate(self.dcnt):
            if v > 0 and self.seen["sp"].get(("d", i), 0) < v:
                s = self.dsem[i]
                self.q["sp"].append(lambda e, s=s, v=v: e.wait_ge(s, v))
        for e_, c in self.cnt.items():
            if c > 0:
                s = self.sem[e_]
                self.q["sp"].append(lambda e, s=s, c=c: e.wait_ge(s, c))
        q = self.q
        with self.nc.Block() as block:
            @block.tensor
            def _(e):
                for f in q["pe"]:
                    f(e)

            @block.scalar
            def _(e):
                for f in q["act"]:
                    f(e)

            @block.vector
            def _(e):
                for f in q["dve"]:
                    f(e)

            @block.gpsimd
            def _(e):
                for f in q["pool"]:
                    f(e)

            @block.sync
            def _(e):
                for f in q["sp"]:
                    f(e)


class Ctx:
    def __init__(self, nc, st):
        self.nc, self.st = nc, st

    def sb(self, name, shape, dt):
        return self.st.enter_context(self.nc.sbuf_tensor(name, shape, dt))

    def ps(self, name, shape, dt=F32):
        return self.st.enter_context(self.nc.psum_tensor(name, shape, dt))

    def din(self, name, shape, dt=F32):
        return self.nc.dram_tensor(name, list(shape), dt, kind="ExternalInput").ap()

    def dout(self, name, shape, dt=F32):
        return self.nc.dram_tensor(name, list(shape), dt, kind="ExternalOutput").ap()


def emit_rmsnorm(P, C, x_sb, dx, g_sb, dg, hT, dh, ones_bf, dones, sq, dsq, ps, dps, rstd, drstd, epsb, deps, ntok):
    for kt in range(16):
        i = kt % 2
        P.op("act", lambda e, kt=kt, i=i: e.activation(out=sq[i][:], in_=x_sb[:, kt, :], func=AF.Square),
             reads=[dx], writes=[dsq[i]])
        for hf in range(ntok // 512):
            P.op("pe", lambda e, kt=kt, i=i, hf=hf: e.matmul(ps[:, hf * 512:(hf + 1) * 512], ones_bf[:],
                                                             sq[i][:, hf * 512:(hf + 1) * 512],
                                                             start=(kt == 0), stop=(kt == 15)),
                 reads=[dsq[i], dones], writes=[dps])
    P.op("act", lambda e: e.activation(out=rstd[:], in_=ps[:, :ntok], func=AF.Sqrt, bias=epsb[:], scale=1.0 / D),
         reads=[dps, deps], writes=[drstd])
    P.op("dve", lambda e: e.reciprocal(out=rstd[:], in_=rstd[:]), reads=[drstd], writes=[drstd])
    for kt in range(16):
        P.op("dve", lambda e, kt=kt: e.scalar_tensor_tensor(out=hT[:, kt, :], in0=x_sb[:, kt, :],
                                                           scalar=g_sb[:, kt:kt + 1], in1=rstd[:],
                                                           op0=ALU.mult, op1=ALU.mult),
             reads=[dx, dg, drstd], writes=[dh])


def build_A():
    nc = bass.Bass("TRN2", target_bir_lowering=False)
    with ExitStack() as st:
        st.enter_context(nc.allow_low_precision("bf16 matmul operands, fp32 accumulation"))
        C = Ctx(nc, st)
        xT = C.din("xT", [D, TOK])
        n1g = C.din("n1g", [128, 16])
        w_in = C.din("w_in", [D, 12288])
        bg = C.din("bg", [128, 48])
        gng = C.din("gng", [1, 1024])
        wsT = C.din("wsT", [8, 128, 128])
        bs = C.din("bs", [1, 1024])
        yaT = C.dout("yaT", [1024, TOK], BF16)
        s5T = C.dout("s5T", [1024, TOK], BF16)
        qT = C.dout("qT", [1024, TOK], BF16)
        kT = C.dout("kT", [1024, TOK], BF16)
        vtk = C.dout("vtk", [TOK, 1024], BF16)
        gT = C.dout("gT", [6144, TOK], BF16)

        P = Prog(nc, st)
        x_sb = C.sb("x_sb", [128, 16, TOK], F32)
        hT = C.sb("hT", [128, 16, TOK], BF16)
        g_sb = C.sb("g_sb", [128, 16], F32)
        bg_sb = C.sb("bg_sb", [128, 48], F32)
        gng_b = C.sb("gng_b", [128, 1024], F32)
        bs_b = C.sb("bs_b", [128, 1024], F32)
        ws_f = C.sb("ws_f", [128, 8, 128], F32)
        ws_b = C.sb("ws_b", [128, 8, 128], BF16)
        ones_bf = C.sb("ones_bf", [128, 128], BF16)
        epsb = C.sb("epsb", [128, 1], F32)
        sq = [C.sb("sq%d" % i, [128, TOK], BF16) for i in range(2)]
        rstd = C.sb("rstd", [128, TOK], F32)
        wt = [C.sb("wt%d" % i, [128, 16, 512], BF16) for i in range(2)]
        stg = [C.sb("stg%d" % i, [128, TOK], BF16) for i in range(2)]
        uT = C.sb("uT", [128, 8, TOK], BF16)
        vtok = C.sb("vtok", [128, 8, 1024], BF16)
        vn = C.sb("vn", [128, 1024], BF16)
        junk = sq[0]
        vss = C.sb("vss", [128, 8], F32)
        vrs = C.sb("vrs", [128, 8], F32)
        mixt = rstd
        ya_sb = C.sb("ya_sb", [128, 8, TOK], BF16)
        ps = [C.ps("ps%d" % i, [128, 1024]) for i in range(2)]
        psm = C.ps("psm", [128, 1024])

        dx, dh, dg, dbg, dgng, dbs, dwsf, dwsb, dones, deps = [Dep() for _ in range(10)]
        dsq = [Dep(), Dep()]
        drstd = Dep()
        dwt = [Dep(), Dep()]
        dstg = [Dep(), Dep()]
        dps = [Dep(), Dep()]
        dpsm, duT, dvtok, dvn, dvss, dvrs, dya = [Dep() for _ in range(7)]
        djunk = dsq[0]
        dmixt = drstd

        xv = xT.rearrange("(kt p) t -> p kt t", p=128)
        for i in range(4):
            P.dma("sp", x_sb[:, 4 * i:4 * i + 4, :], xv[:, 4 * i:4 * i + 4, :], writes=[dx])
        P.dma("sp", g_sb[:], n1g, writes=[dg])
        P.dma("sp", bg_sb[:], bg, writes=[dbg])
        P.dma("sp", gng_b[:], gng.partition_broadcast(128), writes=[dgng])
        P.dma("sp", bs_b[:], bs.partition_broadcast(128), writes=[dbs])
        P.dma("sp", ws_f[:], wsT.rearrange("g s t -> s g t"), writes=[dwsf])
        wv = w_in.rearrange("(kt p) c -> p kt c", p=128)

        def load_w(cb):
            P.dma("pool", wt[cb % 2][:], wv[:, :, cb * 512:(cb + 1) * 512], writes=[dwt[cb % 2]])
        load_w(0)
        P.op("dve", lambda e: e.memset(ones_bf[:], 1.0), writes=[dones])
        P.op("dve", lambda e: e.memset(epsb[:], EPS), writes=[deps])
        P.op("pool", lambda e: e.affine_select(out=ws_f[:], in_=ws_f[:], pattern=[[0, 8], [1, 128]],
                                               compare_op=ALU.is_ge, fill=0.0, base=0, channel_multiplier=-1),
             reads=[dwsf], writes=[dwsf])
        P.op("dve", lambda e: e.tensor_copy(out=ws_b[:], in_=ws_f[:]), reads=[dwsf], writes=[dwsb])

        emit_rmsnorm(P, C, x_sb, dx, g_sb, dg, hT, dh, ones_bf, dones, sq, dsq, ps[0], dps[0], rstd, drstd,
                     epsb, deps, TOK)

        pcount = [0]
        scount = [0]

        def form2(cb, epi):
            w = wt[cb % 2]
            for m in range(4):
                pi = pcount[0] % 2
                pcount[0] += 1
                for hf in range(2):
                    for kt in range(16):
                        P.op("pe", lambda e, w=w, m=m, hf=hf, kt=kt, pi=pi: e.matmul(
                            ps[pi][:, hf * 512:(hf + 1) * 512], w[:, kt, m * 128:(m + 1) * 128],
                            hT[:, kt, hf * 512:(hf + 1) * 512], start=(kt == 0), stop=(kt == 15)),
                            reads=[dwt[cb % 2], dh], writes=[dps[pi]])
                epi(cb * 4 + m, pi)

        def epi_store(func, dst, colbase, scale=1.0, bias_col=None):
            def epi(blk, pi):
                si = scount[0] % 2
                scount[0] += 1
                r0 = (blk - colbase) * 128
                if bias_col is None:
                    P.op("act", lambda e: e.activation(out=stg[si][:], in_=ps[pi][:], func=func, scale=scale),
                         reads=[dps[pi]], writes=[dstg[si]])
                else:
                    bc = bias_col(blk)
                    P.op("act", lambda e: e.activation(out=stg[si][:], in_=ps[pi][:], func=func,
                                                       bias=bg_sb[:, bc:bc + 1]),
                         reads=[dps[pi], dbg], writes=[dstg[si]])
                P.dma("sp", dst[r0:r0 + 128, :], stg[si][:], reads=[dstg[si]])
            return epi

        def epi_u(blk, pi):
            P.op("act", lambda e: e.activation(out=uT[:, blk, :], in_=ps[pi][:], func=AF.Gelu_apprx_tanh),
                 reads=[dps[pi]], writes=[duT])

        def form1(cb, epi):
            w = wt[cb % 2]
            for c in range(8):
                pi = pcount[0] % 2
                pcount[0] += 1
                for kt in range(16):
                    P.op("pe", lambda e, w=w, c=c, kt=kt, pi=pi: e.matmul(
                        ps[pi][:, 0:512], hT[:, kt, c * 128:(c + 1) * 128], w[:, kt, :],
                        start=(kt == 0), stop=(kt == 15)),
                        reads=[dwt[cb % 2], dh], writes=[dps[pi]])
                epi(cb, c, pi)

        def epi_vg(cb, c, pi):
            off = (cb - 2) * 512
            P.op("act", lambda e: e.activation(out=vtok[:, c, off:off + 512], in_=ps[pi][:, 0:512],
                                               func=AF.Gelu_apprx_tanh),
                 reads=[dps[pi]], writes=[dvtok])

        def epi_va(cb, c, pi):
            off = (cb - 10) * 512
            si = scount[0] % 2
            scount[0] += 1
            P.op("act", lambda e: e.activation(out=stg[si][:, 0:512], in_=ps[pi][:, 0:512], func=AF.Copy),
                 reads=[dps[pi]], writes=[dstg[si]])
            P.dma("sp", vtk[c * 128:(c + 1) * 128, off:off + 512], stg[si][:, 0:512], reads=[dstg[si]])

        def gmlp_finish():
            for c in range(8):
                P.op("act", lambda e, c=c: e.activation(out=junk[:], in_=vtok[:, c, :], func=AF.Square,
                                                        accum_out=vss[:, c:c + 1]),
                     reads=[dvtok], writes=[djunk, dvss])
            P.op("act", lambda e: e.activation(out=vrs[:], in_=vss[:], func=AF.Sqrt, bias=epsb[:], scale=1.0 / 1024),
                 reads=[dvss, deps], writes=[dvrs])
            P.op("dve", lambda e: e.reciprocal(out=vrs[:], in_=vrs[:]), reads=[dvrs], writes=[dvrs])
            for c in range(8):
                P.op("dve", lambda e, c=c: e.scalar_tensor_tensor(out=vn[:], in0=vtok[:, c, :],
                                                                  scalar=vrs[:, c:c + 1], in1=gng_b[:],
                                                                  op0=ALU.mult, op1=ALU.mult),
                     reads=[dvtok, dvrs, dgng], writes=[dvn])
                for g in range(8):
                    P.op("pe", lambda e, g=g: e.matmul(psm[:, g * 128:(g + 1) * 128], vn[:, g * 128:(g + 1) * 128],
                                                       ws_b[:, g, :], start=True, stop=True),
                         reads=[dvn, dwsb], writes=[dpsm])
                P.op("dve", lambda e: e.tensor_tensor(out=mixt[:], in0=psm[:], in1=bs_b[:], op=ALU.add),
                     reads=[dpsm, dbs], writes=[dmixt])
                P.op("pool", lambda e, c=c: e.tensor_tensor(
                    out=ya_sb[:, :, c * 128:(c + 1) * 128], in0=mixt[:].rearrange("p (g t) -> p g t", g=8),
                    in1=uT[:, :, c * 128:(c + 1) * 128], op=ALU.mult),
                    reads=[dmixt, duT], writes=[dya])
            P.dma("sp", yaT.rearrange("(g p) t -> p g t", p=128), ya_sb[:], reads=[dya])

        for cb in range(24):
            if cb + 1 < 24:
                load_w(cb + 1)
            if cb < 2:
                form2(cb, epi_u)
            elif cb < 4:
                form1(cb, epi_vg)
                if cb == 3:
                    gmlp_finish()
            elif cb < 6:
                form2(cb, epi_store(AF.Copy, s5T, 16))
            elif cb < 8:
                form2(cb, epi_store(AF.Copy, qT, 24, scale=QSCALE))
            elif cb < 10:
                form2(cb, epi_store(AF.Copy, kT, 32))
            elif cb < 12:
                form1(cb, epi_va)
            else:
                form2(cb, epi_store(AF.Sigmoid, gT, 48, bias_col=lambda blk: blk - 48))
        P.finish()
    return nc


def host_A(x_tok, l, inp):
    maps = []
    n1g = np.ascontiguousarray(inp["norm1_g"][l].reshape(16, 128).T)
    bgv = np.ascontiguousarray(inp["b_gate"][l].reshape(48, 128).T)
    gng = np.ascontiguousarray(inp["gm_norm_g"][l].reshape(1, 1024))
    wsT = np.ascontiguousarray(np.transpose(inp["gm_w_s"][l], (0, 2, 1)))
    bsv = np.ascontiguousarray(inp["gm_b_s"][l].reshape(1, 1024))
    w_in = np.ascontiguousarray(inp["w_in"][l])
    for c in range(NCORES):
        xT = np.ascontiguousarray(x_tok[c * TOK:(c + 1) * TOK, :].T)
        maps.append({"xT": xT, "n1g": n1g, "w_in": w_in, "bg": bgv, "gng": gng, "wsT": wsT, "bs": bsv})
    return maps


def build_C(last, debug=False):
    nc = bass.Bass("TRN2", target_bir_lowering=False)
    with ExitStack() as st:
        st.enter_context(nc.allow_low_precision("bf16 matmul operands, fp32 accumulation"))
        C = Ctx(nc, st)
        xT = C.din("xT", [D, TOK])
        yaT = C.din("yaT", [1024, TOK], BF16)
        ybpT = C.din("ybpT", [1024, TOK], BF16)
        ycT = C.din("ycT", [1024, TOK], BF16)
        gT = C.din("gT", [6144, TOK], BF16)
        w_glu = C.din("w_glu", [1024, 1024])
        b_glu = C.din("b_glu", [128, 8])
        w_br = C.din("w_br", [3, 1024, D])
        w_out = C.din("w_out", [D, D])
        n2g = C.din("n2g", [128, 16])
        w_m1 = C.din("w_m1", [D, 8192])
        w_m2 = C.din("w_m2", [8192, D])
        fing = C.din("fing", [128, 16])
        xo = C.dout("xo", [D, TOK])
        if debug:
            dbg_yb = C.dout("dbg_yb", [1024, TOK], BF16)
            dbg_mg = C.dout("dbg_mg", [D, TOK], BF16)
            dbg_x1 = C.dout("dbg_x1", [D, TOK])
            dbg_h2 = C.dout("dbg_h2", [D, TOK], BF16)

        P = Prog(nc, st)
        x_sb = C.sb("x_sb", [128, 16, TOK], F32)
        R2 = C.sb("R2", [128, 3, 8, TOK], BF16)
        R3 = C.sb("R3", [128, 16 * TOK], BF16)
        R4 = C.sb("R4", [128, 4, 8 * 512], BF16)
        gts = [C.sb("gts%d" % i, [128, TOK], BF16) for i in range(6)]
        acc = C.sb("acc", [128, TOK], F32)
        tmp = C.sb("tmp", [128, TOK], F32)
        stg = [C.sb("stg%d" % i, [128, TOK], BF16) for i in range(2)]
        rstd = C.sb("rstd", [128, TOK], F32)
        sq = stg
        g2_sb = C.sb("g2_sb", [128, 16], F32)
        gf_sb = C.sb("gf_sb", [128, 16], F32)
        bgl_sb = C.sb("bgl_sb", [128, 8], F32)
        ones_bf = C.sb("ones_bf", [128, 128], BF16)
        epsb = C.sb("epsb", [128, 1], F32)
        ps = [C.ps("ps%d" % i, [128, 1024]) for i in range(3)]

        dx, dR3, dg2, dgf, dbgl, dones, deps, dacc, dtmp, drstd = [Dep() for _ in range(10)]
        dR2 = [Dep(), Dep(), Dep()]
        dR4 = [Dep() for _ in range(4)]
        dgts = [Dep() for _ in range(6)]
        dstg = [Dep(), Dep()]
        dsq = dstg
        dps = [Dep() for _ in range(3)]

        gS = R3[:, 0:8 * TOK].rearrange("p (k t) -> p k t", k=8)
        mg = R3[:].rearrange("p (k t) -> p k t", k=16)
        aT = R2[:].rearrange("p a k t -> p (a k) t")
        w8 = [R4[:, i, :].rearrange("p (k c) -> p k c", k=8) for i in range(4)]
        w16 = [R4[:, 2 * i:2 * i + 2, :].rearrange("p a (k c) -> p (a k) c", k=8) for i in range(2)]

        xv = xT.rearrange("(kt p) t -> p kt t", p=128)
        P.dma("sp", R2[:, 1], ybpT.rearrange("(k p) t -> p k t", p=128), writes=[dR2[1]])
        P.dma("sp", bgl_sb[:], b_glu, writes=[dbgl])
        for i in range(4):
            P.dma("sp", x_sb[:, 4 * i:4 * i + 4, :], xv[:, 4 * i:4 * i + 4, :], writes=[dx])
        P.dma("sp", R2[:, 0], yaT.rearrange("(k p) t -> p k t", p=128), writes=[dR2[0]])
        P.dma("sp", R2[:, 2], ycT.rearrange("(k p) t -> p k t", p=128), writes=[dR2[2]])
        P.dma("sp", g2_sb[:], n2g, writes=[dg2])
        P.dma("sp", gf_sb[:], fing, writes=[dgf])
        P.op("dve", lambda e: e.memset(ones_bf[:], 1.0), writes=[dones])
        P.op("dve", lambda e: e.memset(epsb[:], EPS), writes=[deps])

        jobs = []
        glu_v = w_glu.rearrange("(kt p) c -> p kt c", p=128)
        for cb in range(2):
            jobs.append(("glu", cb, 8, glu_v[:, :, cb * 512:(cb + 1) * 512]))
        for cb in range(4):
            for n in range(3):
                jobs.append(("br", (cb, n), 8, w_br[n].rearrange("(kt p) c -> p kt c", p=128)[:, :, cb * 512:(cb + 1) * 512]))
        wo_v = w_out.rearrange("(kt p) c -> p kt c", p=128)
        for cb in range(4):
            jobs.append(("wo", cb, 16, wo_v[:, :, cb * 512:(cb + 1) * 512]))
        m1_v = w_m1.rearrange("(kt p) c -> p kt c", p=128)
        m2_v = w_m2.rearrange("(kt p) c -> p kt c", p=128)
        for fc in range(4):
            for fb in range(4):
                c0 = fc * 2048 + fb * 512
                jobs.append(("m1", (fc, fb), 16, m1_v[:, :, c0:c0 + 512]))
            for cb in range(4):
                jobs.append(("m2", (fc, cb), 16, m2_v[:, fc * 16:(fc + 1) * 16, cb * 512:(cb + 1) * 512]))
        slot8 = 0
        slots = []
        quarters = []
        for kind, key, nk, src in jobs:
            if nk == 8:
                s = slot8 % 4
                slot8 += 1
                slots.append((w8[s], [dR4[s]]))
                quarters.append([s])
            else:
                if slot8 % 2:
                    slot8 += 1
                s = (slot8 // 2) % 2
                slot8 += 2
                slots.append((w16[s], [dR4[2 * s], dR4[2 * s + 1]]))
                quarters.append([2 * s, 2 * s + 1])
        issued = [0]
        occupant = [None] * 4
        consumed = set()

        def prefetch(upto):
            while issued[0] < min(upto, len(jobs)):
                j = issued[0]
                if any(occupant[q] is not None and occupant[q] not in consumed for q in quarters[j]):
                    return
                for q in quarters[j]:
                    occupant[q] = j
                P.dma("pool", slots[j][0], jobs[j][3], writes=slots[j][1])
                issued[0] += 1

        def done(j):
            consumed.add(j)
            prefetch(j + 4)
        pc = [0]

        def mm_block(j, m, rhs, drhs, nkt):
            pi = pc[0] % 3
            pc[0] += 1
            prefetch(j + 1)
            assert issued[0] > j, "weight job %d not loadable (ring slot busy)" % j
            wv, wd = slots[j]
            for hf in range(2):
                for kt in range(nkt):
                    P.op("pe", lambda e, wv=wv, m=m, hf=hf, kt=kt, pi=pi: e.matmul(
                        ps[pi][:, hf * 512:(hf + 1) * 512], wv[:, kt, m * 128:(m + 1) * 128],
                        rhs[:, kt, hf * 512:(hf + 1) * 512], start=(kt == 0), stop=(kt == nkt - 1)),
                        reads=wd + drhs, writes=[dps[pi]])
            return pi

        jn = [0]
        prefetch(3)
        for k in range(8):
            P.op("act", lambda e, k=k: e.activation(out=gS[:, k, :], in_=R2[:, 1, k, :], func=AF.Gelu_apprx_tanh),
                 reads=[dR2[1]], writes=[dR3])
        sc = [0]
        for cb in range(2):
            j = jn[0]
            jn[0] += 1
            prefetch(j + 3)
            for m in range(4):
                blk = cb * 4 + m
                pi = mm_block(j, m, gS, [dR3], 8)
                si = sc[0] % 2
                sc[0] += 1
                P.op("act", lambda e, pi=pi, si=si, blk=blk: e.activation(
                    out=stg[si][:], in_=ps[pi][:], func=AF.Sigmoid, bias=bgl_sb[:, blk:blk + 1]),
                    reads=[dps[pi], dbgl], writes=[dstg[si]])
                P.op("dve", lambda e, si=si, blk=blk: e.tensor_tensor(
                    out=R2[:, 1, blk, :], in0=gS[:, blk, :], in1=stg[si][:], op=ALU.mult),
                    reads=[dR3, dstg[si]], writes=[dR2[1]])
            done(j)
        if debug:
            P.dma("sp", dbg_yb.rearrange("(k p) t -> p k t", p=128), R2[:, 1], reads=[dR2[1]])
        gc = [0]
        for cb in range(4):
            js = [jn[0], jn[0] + 1, jn[0] + 2]
            jn[0] += 3
            prefetch(js[2] + 2)
            for m in range(4):
                dt_ = cb * 4 + m
                for n in range(3):
                    gi = gc[0] % 6
                    gc[0] += 1
                    r0 = (n * 16 + dt_) * 128
                    P.dma("sp", gts[gi][:], gT[r0:r0 + 128, :], writes=[dgts[gi]])
                    pi = mm_block(js[n], m, R2[:, n], [dR2[n]], 8)
                    if n == 0:
                        P.op("dve", lambda e, pi=pi, gi=gi: e.tensor_tensor(out=acc[:], in0=ps[pi][:], in1=gts[gi][:],
                                                                            op=ALU.mult),
                             reads=[dps[pi], dgts[gi]], writes=[dacc])
                    else:
                        P.op("dve", lambda e, pi=pi, gi=gi: e.tensor_tensor(out=tmp[:], in0=ps[pi][:], in1=gts[gi][:],
                                                                            op=ALU.mult),
                             reads=[dps[pi], dgts[gi]], writes=[dtmp])
                        if n == 1:
                            P.op("dve", lambda e: e.tensor_tensor(out=acc[:], in0=acc[:], in1=tmp[:], op=ALU.add),
                                 reads=[dacc, dtmp], writes=[dacc])
                        else:
                            P.op("dve", lambda e, dt_=dt_: e.tensor_tensor(out=mg[:, dt_, :], in0=acc[:], in1=tmp[:],
                                                                          op=ALU.add),
                                 reads=[dacc, dtmp], writes=[dR3])
            for j_ in js:
                done(j_)
        if debug:
            P.dma("sp", dbg_mg.rearrange("(k p) t -> p k t", p=128), mg, reads=[dR3])
        for cb in range(4):
            j = jn[0]
            jn[0] += 1
            prefetch(j + 2)
            for m in range(4):
                dt_ = cb * 4 + m
                pi = mm_block(j, m, mg, [dR3], 16)
                P.op("dve", lambda e, pi=pi, dt_=dt_: e.tensor_tensor(out=x_sb[:, dt_, :], in0=x_sb[:, dt_, :],
                                                                      in1=ps[pi][:], op=ALU.add),
                     reads=[dps[pi], dx], writes=[dx])
            done(j)
        if debug:
            P.dma("sp", dbg_x1.rearrange("(k p) t -> p k t", p=128), x_sb[:], reads=[dx])
        emit_rmsnorm(P, C, x_sb, dx, g2_sb, dg2, mg, dR3, ones_bf, dones, sq, dsq, ps[0], dps[0], rstd, drstd,
                     epsb, deps, TOK)
        if debug:
            P.dma("sp", dbg_h2.rearrange("(k p) t -> p k t", p=128), mg, reads=[dR3])
        for fc in range(4):
            for fb in range(4):
                j = jn[0]
                jn[0] += 1
                prefetch(j + 2)
                for m in range(4):
                    ft = fb * 4 + m
                    pi = mm_block(j, m, mg, [dR3], 16)
                    P.op("act", lambda e, pi=pi: e.activation(out=tmp[:], in_=ps[pi][:], func=AF.Relu),
                         reads=[dps[pi]], writes=[dtmp])
                    P.op("dve", lambda e, ft=ft: e.tensor_tensor(out=aT[:, ft, :], in0=tmp[:], in1=tmp[:], op=ALU.mult),
                         reads=[dtmp], writes=dR2)
                done(j)
            for cb in range(4):
                j = jn[0]
                jn[0] += 1
                prefetch(j + 2)
                for m in range(4):
                    dt_ = cb * 4 + m
                    pi = mm_block(j, m, aT, dR2, 16)
                    P.op("dve", lambda e, pi=pi, dt_=dt_: e.tensor_tensor(out=x_sb[:, dt_, :], in0=x_sb[:, dt_, :],
                                                                          in1=ps[pi][:], op=ALU.add),
                         reads=[dps[pi], dx], writes=[dx])
                done(j)
        xov = xo.rearrange("(kt p) t -> p kt t", p=128)
        if not last:
            for i in range(4):
                P.dma("sp", xov[:, 4 * i:4 * i + 4, :], x_sb[:, 4 * i:4 * i + 4, :], reads=[dx])
        else:
            for kt in range(16):
                i = kt % 2
                P.op("act", lambda e, kt=kt, i=i: e.activation(out=sq[i][:], in_=x_sb[:, kt, :], func=AF.Square),
                     reads=[dx], writes=[dsq[i]])
                for hf in range(2):
                    P.op("pe", lambda e, kt=kt, i=i, hf=hf: e.matmul(ps[0][:, hf * 512:(hf + 1) * 512], ones_bf[:],
                                                                     sq[i][:, hf * 512:(hf + 1) * 512],
                                                                     start=(kt == 0), stop=(kt == 15)),
                         reads=[dsq[i], dones], writes=[dps[0]])
            P.op("act", lambda e: e.activation(out=rstd[:], in_=ps[0][:], func=AF.Sqrt, bias=epsb[:], scale=1.0 / D),
                 reads=[dps[0], deps], writes=[drstd])
            P.op("dve", lambda e: e.reciprocal(out=rstd[:], in_=rstd[:]), reads=[drstd], writes=[drstd])
            fo = [acc, tmp]
            dfo = [dacc, dtmp]
            for kt in range(16):
                i = kt % 2
                P.op("dve", lambda e, kt=kt, i=i: e.scalar_tensor_tensor(out=fo[i][:], in0=x_sb[:, kt, :],
                                                                       scalar=gf_sb[:, kt:kt + 1], in1=rstd[:],
                                                                       op0=ALU.mult, op1=ALU.mult),
                     reads=[dx, dgf, drstd], writes=[dfo[i]])
                P.dma("sp", xov[:, kt, :], fo[i][:], reads=[dfo[i]])
        P.finish()
    return nc


def host_C(x_tok, l, inp, yaT, ybpT, ycT, gT, last):
    maps = []
    r = lambda v, n: np.ascontiguousarray(v.reshape(n, 128).T)
    com = {"w_glu": np.ascontiguousarray(inp["s5_w_glu"][l]), "b_glu": r(inp["s5_b_glu"][l], 8),
           "w_br": np.ascontiguousarray(inp["w_branch"][l]), "w_out": np.ascontiguousarray(inp["w_out"][l]),
           "n2g": r(inp["norm2_g"][l], 16), "w_m1": np.ascontiguousarray(inp["w_mlp_in"][l]),
           "w_m2": np.ascontiguousarray(inp["w_mlp_out"][l]), "fing": r(inp["final_g"], 16)}
    for c in range(NCORES):
        m = dict(com)
        m["xT"] = np.ascontiguousarray(x_tok[c * TOK:(c + 1) * TOK, :].T)
        m["yaT"], m["ybpT"], m["ycT"], m["gT"] = yaT[c], ybpT[c], ycT[c], gT[c]
        maps.append(m)
    return maps


SEQ = 4096
NPAIR = 2


def build_B1():
    nc = bass.Bass("TRN2", target_bir_lowering=False)
    with ExitStack() as st:
        st.enter_context(nc.allow_low_precision("bf16 matmul operands, fp32 accumulation"))
        C = Ctx(nc, st)
        qT = C.din("qT", [NPAIR, 128, SEQ], BF16)
        kT = C.din("kT", [NPAIR, 128, SEQ], BF16)
        vv = C.din("v", [NPAIR, SEQ, 128], BF16)
        yc = C.dout("yc", [NPAIR, 128, SEQ], BF16)
        P = Prog(nc, st)
        q_sb = C.sb("q_sb", [128, NPAIR, SEQ], BF16)
        k_sb = C.sb("k_sb", [128, NPAIR, SEQ], BF16)
        v_sb = C.sb("v_sb", [128, NPAIR, 32, 128], BF16)
        o_sb = C.sb("o_sb", [128, NPAIR, SEQ], BF16)
        ones_f = C.sb("ones_f", [128, 128], F32)
        mstrict = C.sb("mstrict", [128, 128], BF16)
        negtri = C.sb("negtri", [128, 128], BF16)
        negones = C.sb("negones", [128, 128], BF16)
        spsum = C.sb("spsum", [128, 512], BF16)
        ebuf = [C.sb("ebuf%d" % i, [128, 512], F32) for i in range(2)]
        spb = [C.sb("spb%d" % i, [128, 512], BF16) for i in range(3)]
        wbuf = [C.sb("wbuf%d" % i, [128, 512], BF16) for i in range(3)]
        psA = [C.ps("psA%d" % i, [128, 512]) for i in range(2)]
        psB = [C.ps("psB%d" % i, [128, 512]) for i in range(2)]
        psO = [C.ps("psO%d" % i, [128, 512]) for i in range(2)]
        dq, dk, dv, do_, dconst, dspsum = [Dep() for _ in range(6)]
        debuf = [Dep(), Dep()]
        dspb = [Dep() for _ in range(3)]
        dwbuf = [Dep() for _ in range(3)]
        dpsA = [Dep(), Dep()]
        dpsB = [Dep(), Dep()]
        dpsO = [Dep(), Dep()]

        for p in range(NPAIR):
            P.dma("sp", q_sb[:, p, :], qT[p], writes=[dq])
            P.dma("sp", k_sb[:, p, :], kT[p], writes=[dk])
            P.dma("sp", v_sb[:, p], vv[p].rearrange("(b s) d -> s b d", s=128), writes=[dv])
        P.op("pool", lambda e: e.memset(ones_f[:], 1.0), writes=[dconst])
        P.op("pool", lambda e: e.affine_select(out=mstrict[:], in_=ones_f[:], pattern=[[1, 128]],
                                               compare_op=ALU.is_gt, fill=0.0, base=0, channel_multiplier=-1),
             reads=[dconst], writes=[dconst])
        P.op("pool", lambda e: e.memset(ones_f[:], -1.0), reads=[dconst], writes=[dconst])
        P.op("pool", lambda e: e.affine_select(out=negtri[:], in_=ones_f[:], pattern=[[-1, 128]],
                                               compare_op=ALU.is_ge, fill=0.0, base=0, channel_multiplier=1),
             reads=[dconst], writes=[dconst])
        P.op("pool", lambda e: e.tensor_copy(out=negones[:], in_=ones_f[:]), reads=[dconst], writes=[dconst])

        steps = []
        for p in range(NPAIR):
            for g in range(8):
                for sb in range(4 * g + 3, -1, -1):
                    steps.append((p, g, sb))
        n = len(steps)

        def info(i):
            p, g, sb = steps[i]
            tl = max(0, sb - 4 * g) * 128
            return p, g, sb, tl, (sb >= 4 * g)

        def stage1(i):
            p, g, sb, tl, diag = info(i)
            a, s3 = i % 2, i % 3
            P.op("pe", lambda e: e.matmul(psA[a][:, tl:512], k_sb[:, p, sb * 128:(sb + 1) * 128],
                                          q_sb[:, p, g * 512 + tl:(g + 1) * 512], start=True, stop=True),
                 reads=[dq, dk], writes=[dpsA[a]])
            P.op("act", lambda e: e.activation(out=ebuf[a][:, tl:512], in_=psA[a][:, tl:512], func=AF.Exp),
                 reads=[dpsA[a]], writes=[debuf[a]])
            P.op("act", lambda e: e.activation(out=spb[s3][:, tl:512], in_=ebuf[a][:, tl:512], func=AF.Ln, bias=1.0),
                 reads=[debuf[a]], writes=[dspb[s3]])
            if diag:
                P.op("pool", lambda e: e.tensor_tensor(out=spb[s3][:, tl:tl + 128], in0=spb[s3][:, tl:tl + 128],
                                                       in1=mstrict[:], op=ALU.mult),
                     reads=[dspb[s3], dconst], writes=[dspb[s3]])

        def stage2(i):
            p, g, sb, tl, diag = info(i)
            a, s3 = i % 2, i % 3
            if sb == 4 * g + 3:
                P.op("pool", lambda e: e.memset(spsum[:], 0.0), writes=[dspsum])
            P.op("pe", lambda e: e.matmul(psB[a][:, tl:512], k_sb[:, p, sb * 128:(sb + 1) * 128],
                                          q_sb[:, p, g * 512 + tl:(g + 1) * 512], start=True, stop=False),
                 reads=[dq, dk], writes=[dpsB[a]])
            P.op("pe", lambda e: e.matmul(psB[a][:, tl:512], negtri[:], spb[s3][:, tl:512], start=False, stop=False),
                 reads=[dspb[s3], dconst], writes=[dpsB[a]])
            P.op("pe", lambda e: e.matmul(psB[a][:, tl:512], negones[:], spsum[:, tl:512], start=False, stop=True),
                 reads=[dspsum, dconst], writes=[dpsB[a]])
            P.op("pool", lambda e: e.tensor_tensor(out=spsum[:, tl:512], in0=spsum[:, tl:512], in1=spb[s3][:, tl:512],
                                                   op=ALU.add),
                 reads=[dspsum, dspb[s3]], writes=[dspsum])
            P.op("act", lambda e: e.activation(out=wbuf[s3][:, tl:512], in_=psB[a][:, tl:512], func=AF.Exp),
                 reads=[dpsB[a]], writes=[dwbuf[s3]])
            if diag:
                P.op("pool", lambda e: e.tensor_tensor(out=wbuf[s3][:, tl:tl + 128], in0=wbuf[s3][:, tl:tl + 128],
                                                       in1=mstrict[:], op=ALU.mult),
                     reads=[dwbuf[s3], dconst], writes=[dwbuf[s3]])

        def stage3(i):
            p, g, sb, tl, diag = info(i)
            s3 = i % 3
            o = (p * 8 + g) % 2
            for tb in range(tl // 128, 4):
                P.op("pe", lambda e, tb=tb: e.matmul(psO[o][:, tb * 128:(tb + 1) * 128], v_sb[:, p, sb, :],
                                                     wbuf[s3][:, tb * 128:(tb + 1) * 128],
                                                     start=(sb == 4 * g + 3 and tb == 3), stop=(sb == 0),
                                                     skip_group_check=True),
                     reads=[dv, dwbuf[s3]], writes=[dpsO[o]])
            if sb == 0:
                P.op("dve", lambda e: e.tensor_copy(out=o_sb[:, p, g * 512:(g + 1) * 512], in_=psO[o][:]),
                     reads=[dpsO[o]], writes=[do_])

        for it in range(n + 2):
            if it < n:
                stage1(it)
            if 0 <= it - 1 < n:
                stage2(it - 1)
            if 0 <= it - 2 < n:
                stage3(it - 2)
        for p in range(NPAIR):
            P.dma("sp", yc[p], o_sb[:, p, :], reads=[do_])
        P.finish()
    return nc


I32 = mybir.dt.int32
PI = 3.14159265358979
TWO_PI = 2.0 * PI
NT = 2 * SEQ


def build_B2(debug=False):
    nc = bass.Bass("TRN2", target_bir_lowering=False)
    with ExitStack() as st:
        st.enter_context(nc.allow_low_precision("bf16 matmul operands, fp32 accumulation"))
        C = Ctx(nc, st)
        uT = C.din("uT", [128, NT], BF16)
        lam_re = C.din("lam_re", [128, 4])
        lam_im = C.din("lam_im", [128, 4])
        log_dt = C.din("log_dt", [128, 4])
        b_re = C.din("b_re", [128, 4, 16])
        b_im = C.din("b_im", [128, 4, 16])
        c_reT = C.din("c_reT", [128, 4, 16])
        c_imT = C.din("c_imT", [128, 4, 16])
        dvec = C.din("dvec", [128, 1])
        yp = C.dout("yp", [128, NT], BF16)
        P = Prog(nc, st)
        f4 = lambda n: C.sb(n, [128, 4], F32)
        u_sb = C.sb("u_sb", [128, NT], BF16)
        y_sb = C.sb("y_sb", [128, NT], BF16)
        lre, lim, ldt, dtt, rho, th, mag, cth, sth, abre, abim, nre, den, kre, kim, q0, q1, q2 = [
            f4("s4_%d" % i) for i in range(18)]
        m127, a7re, a7im = f4("m127"), f4("a7re"), f4("a7im")
        Adre = C.sb("Adre", [128, 4, 5], F32)
        Adim = C.sb("Adim", [128, 4, 5], F32)
        Adimn = C.sb("Adimn", [128, 4, 5], F32)
        abimn = f4("abimn")
        s127n = f4("s127n")
        bre = C.sb("bre", [128, 4, 16], F32)
        bim = C.sb("bim", [128, 4, 16], F32)
        cre = C.sb("cre", [128, 4, 16], F32)
        cim = C.sb("cim", [128, 4, 16], F32)
        bbre = C.sb("bbre", [128, 4, 16], F32)
        bbim = C.sb("bbim", [128, 4, 16], F32)
        bt0 = C.sb("bt0", [128, 16], F32)
        d_sb = C.sb("d_sb", [128, 1], F32)
        ident = C.sb("ident", [128, 128], F32)
        pad = C.sb("pad", [128, 128], F32)
        BBT = [C.sb("BBT%d" % i, [128, 4, 128], BF16) for i in range(2)]
        Cp = [C.sb("Cp%d" % i, [128, 4, 128], BF16) for i in range(2)]
        jidx = C.sb("jidx", [128, 512], F32)
        ang = C.sb("ang", [128, 512], F32)
        kf = C.sb("kf", [128, 512], F32)
        ki = C.sb("ki", [128, 512], I32)
        cs = [C.sb("cs%d" % i, [128, 4, 512], F32) for i in range(2)]
        m0 = C.sb("m0", [128, SEQ], F32)
        rfull = C.sb("rfull", [128, SEQ], F32)
        w = [C.sb("w%d" % i, [128, SEQ], F32) for i in range(2)]
        z = [C.sb("z%d" % i, [128, SEQ], F32) for i in range(2)]
        bu = [C.sb("bu%d" % i, [128, 512], F32) for i in range(2)]
        tt = [C.sb("tt%d" % i, [128, 512], F32) for i in range(4)]
        sbf = [C.sb("sbf%d" % i, [128, 512], BF16) for i in range(2)]
        cst = [C.sb("cst%d" % i, [128, 32], F32) for i in range(10)]
        psb = [C.ps("psb%d" % i, [128, 512]) for i in range(2)]
        psy = [C.ps("psy%d" % i, [128, 512]) for i in range(2)]
        pst = C.ps("pst", [128, 128])

        du, dy, dpre, dtab, dm0, drf = [Dep() for _ in range(6)]
        dw = [Dep(), Dep()]
        dz = [Dep(), Dep()]
        dbu = [Dep(), Dep()]
        dtq = [Dep() for _ in range(4)]
        dsbf = [Dep(), Dep()]
        dcst = Dep()
        dpsb = [Dep(), Dep()]
        dpsy = [Dep(), Dep()]
        dpst = Dep()

        def V(fn, r, wr, eng="dve"):
            P.op(eng, fn, reads=r, writes=wr)

        P.dma("sp", u_sb[:, 0:SEQ], uT[:, 0:SEQ], writes=[du])
        P.dma("sp", u_sb[:, SEQ:NT], uT[:, SEQ:NT], writes=[du])
        for t_, src in ((lre, lam_re), (lim, lam_im), (ldt, log_dt), (bre, b_re), (bim, b_im), (cre, c_reT),
                        (cim, c_imT), (d_sb, dvec)):
            P.dma("sp", t_[:], src, writes=[dpre])
        pre = [dpre]

        V(lambda e: e.memset(pad[:], 1.0), [], pre, "pool")
        V(lambda e: e.affine_select(out=ident[:], in_=pad[:], pattern=[[-1, 128]], compare_op=ALU.is_equal,
                                    fill=0.0, base=0, channel_multiplier=1), pre, pre, "pool")
        V(lambda e: e.iota(jidx[:], pattern=[[0, 4], [1, 128]], base=0, channel_multiplier=0,
                           allow_small_or_imprecise_dtypes=True), [], [dtab], "pool")
        V(lambda e: e.memset(m0[:], 1.0), [], [dm0], "pool")
        V(lambda e: e.memset(m0[:].rearrange("p (c j) -> p c j", j=128)[:, :, 0:1], 0.0), [dm0], [dm0], "pool")

        def reduce_angle(x, n):
            V(lambda e: e.tensor_scalar(out=kf[:, :n], in0=x, scalar1=1.0 / TWO_PI, scalar2=None, op0=ALU.mult),
              [dtab], [dtab])
            V(lambda e: e.tensor_copy(out=ki[:, :n], in_=kf[:, :n]), [dtab], [dtab])
            V(lambda e: e.tensor_copy(out=kf[:, :n], in_=ki[:, :n]), [dtab], [dtab])
            V(lambda e: e.scalar_tensor_tensor(out=x, in0=kf[:, :n], scalar=-TWO_PI, in1=x, op0=ALU.mult,
                                               op1=ALU.add), [dtab], [dtab])
            V(lambda e: e.tensor_scalar(out=kf[:, :n], in0=x, scalar1=PI, scalar2=-TWO_PI, op0=ALU.is_gt,
                                        op1=ALU.mult), [dtab], [dtab])
            V(lambda e: e.tensor_tensor(out=x, in0=x, in1=kf[:, :n], op=ALU.add), [dtab], [dtab])
            V(lambda e: e.tensor_scalar(out=kf[:, :n], in0=x, scalar1=-PI, scalar2=TWO_PI, op0=ALU.is_lt,
                                        op1=ALU.mult), [dtab], [dtab])
            V(lambda e: e.tensor_tensor(out=x, in0=x, in1=kf[:, :n], op=ALU.add), [dtab], [dtab])

        PT = [dpre, dtab]
        P.op("act", lambda e: e.activation(out=dtt[:], in_=ldt[:], func=AF.Exp), reads=PT, writes=PT)
        V(lambda e: e.tensor_tensor(out=rho[:], in0=lre[:], in1=dtt[:], op=ALU.mult), PT, PT)
        V(lambda e: e.tensor_tensor(out=th[:], in0=lim[:], in1=dtt[:], op=ALU.mult), PT, PT)
        P.op("act", lambda e: e.activation(out=mag[:], in_=rho[:], func=AF.Exp), reads=PT, writes=PT)
        P.op("act", lambda e: e.activation(out=m127[:], in_=rho[:], func=AF.Exp, scale=127.0), reads=PT, writes=PT)
        for gp in range(4):
            for k_, shift in ((1, 0.0), (0, PI / 2)):
                V(lambda e, gp=gp, shift=shift: e.tensor_scalar(out=ang[:], in0=jidx[:], scalar1=th[:, gp:gp + 1],
                                                                scalar2=shift, op0=ALU.mult, op1=ALU.add), PT, PT)
                reduce_angle(ang[:], 512)
                P.op("act", lambda e, gp=gp, k_=k_: e.activation(out=cs[k_][:, gp, :], in_=ang[:], func=AF.Sin),
                     reads=PT, writes=PT)
        V(lambda e: e.tensor_copy(out=cth[:], in_=cs[0][:, :, 1]), PT, PT)
        V(lambda e: e.tensor_copy(out=sth[:], in_=cs[1][:, :, 1]), PT, PT)
        V(lambda e: e.tensor_tensor(out=abre[:], in0=mag[:], in1=cth[:], op=ALU.mult), PT, PT)
        V(lambda e: e.tensor_tensor(out=abim[:], in0=mag[:], in1=sth[:], op=ALU.mult), PT, PT)
        V(lambda e: e.tensor_scalar(out=abimn[:], in0=abim[:], scalar1=-1.0, scalar2=None, op0=ALU.mult), PT, PT)
        V(lambda e: e.tensor_scalar(out=s127n[:], in0=cs[1][:, :, 127], scalar1=-1.0, scalar2=None, op0=ALU.mult), PT, PT)
        V(lambda e: e.tensor_scalar(out=nre[:], in0=abre[:], scalar1=-1.0, scalar2=None, op0=ALU.add), PT, PT)
        V(lambda e: e.tensor_tensor(out=q0[:], in0=lre[:], in1=lre[:], op=ALU.mult), PT, PT)
        V(lambda e: e.tensor_tensor(out=q1[:], in0=lim[:], in1=lim[:], op=ALU.mult), PT, PT)
        V(lambda e: e.tensor_tensor(out=den[:], in0=q0[:], in1=q1[:], op=ALU.add), PT, PT)
        V(lambda e: e.reciprocal(out=den[:], in_=den[:]), PT, PT)
        V(lambda e: e.tensor_tensor(out=q0[:], in0=nre[:], in1=lre[:], op=ALU.mult), PT, PT)
        V(lambda e: e.tensor_tensor(out=q1[:], in0=abim[:], in1=lim[:], op=ALU.mult), PT, PT)
        V(lambda e: e.tensor_tensor(out=q0[:], in0=q0[:], in1=q1[:], op=ALU.add), PT, PT)
        V(lambda e: e.tensor_tensor(out=kre[:], in0=q0[:], in1=den[:], op=ALU.mult), PT, PT)
        V(lambda e: e.tensor_tensor(out=q0[:], in0=abim[:], in1=lre[:], op=ALU.mult), PT, PT)
        V(lambda e: e.tensor_tensor(out=q1[:], in0=nre[:], in1=lim[:], op=ALU.mult), PT, PT)
        V(lambda e: e.tensor_tensor(out=q0[:], in0=q0[:], in1=q1[:], op=ALU.subtract), PT, PT)
        V(lambda e: e.tensor_tensor(out=kim[:], in0=q0[:], in1=den[:], op=ALU.mult), PT, PT)
        for gp in range(4):
            g1 = slice(gp, gp + 1)
            V(lambda e, gp=gp, g1=g1: e.tensor_scalar(out=bt0[:], in0=bim[:, gp, :], scalar1=kim[:, g1], scalar2=None,
                                                      op0=ALU.mult), PT, PT)
            V(lambda e, gp=gp, g1=g1: e.scalar_tensor_tensor(out=bbre[:, gp, :], in0=bre[:, gp, :], scalar=kre[:, g1],
                                                             in1=bt0[:], op0=ALU.mult, op1=ALU.subtract), PT, PT)
            V(lambda e, gp=gp, g1=g1: e.tensor_scalar(out=bt0[:], in0=bre[:, gp, :], scalar1=kim[:, g1], scalar2=None,
                                                      op0=ALU.mult), PT, PT)
            V(lambda e, gp=gp, g1=g1: e.scalar_tensor_tensor(out=bbim[:, gp, :], in0=bim[:, gp, :], scalar=kre[:, g1],
                                                             in1=bt0[:], op0=ALU.mult, op1=ALU.add), PT, PT)
        V(lambda e: e.tensor_tensor(out=a7re[:], in0=m127[:], in1=cs[0][:, :, 127], op=ALU.mult), PT, PT)
        V(lambda e: e.tensor_tensor(out=a7im[:], in0=m127[:], in1=cs[1][:, :, 127], op=ALU.mult), PT, PT)

        def cmul(ore, oim, are, aim, bre_, bim_):
            V(lambda e: e.tensor_tensor(out=q0[:], in0=are, in1=bre_, op=ALU.mult), PT, PT)
            V(lambda e: e.tensor_tensor(out=q1[:], in0=aim, in1=bim_, op=ALU.mult), PT, PT)
            V(lambda e: e.tensor_tensor(out=q2[:], in0=are, in1=bim_, op=ALU.mult), PT, PT)
            V(lambda e: e.tensor_tensor(out=ore, in0=q0[:], in1=q1[:], op=ALU.subtract), PT, PT)
            V(lambda e: e.tensor_tensor(out=q0[:], in0=aim, in1=bre_, op=ALU.mult), PT, PT)
            V(lambda e: e.tensor_tensor(out=oim, in0=q2[:], in1=q0[:], op=ALU.add), PT, PT)
        cmul(Adre[:, :, 0], Adim[:, :, 0], a7re[:], a7im[:], abre[:], abim[:])
        for k in range(1, 5):
            cmul(Adre[:, :, k], Adim[:, :, k], Adre[:, :, k - 1], Adim[:, :, k - 1], Adre[:, :, k - 1], Adim[:, :, k - 1])
        V(lambda e: e.tensor_scalar(out=Adimn[:], in0=Adim[:], scalar1=-1.0, scalar2=None, op0=ALU.mult), PT, PT)
        for gp in range(4):
            for ri, src in ((0, bbre), (1, bbim)):
                V(lambda e: e.memset(pad[:], 0.0), PT, PT)
                for j in range(2):
                    c0 = 16 * (2 * gp + j)
                    V(lambda e, j=j, c0=c0, src=src, gp=gp: e.tensor_copy(out=pad[64 * j:64 * j + 64, c0:c0 + 16],
                                                                        in_=src[64 * j:64 * j + 64, gp, :]), PT, PT)
                P.op("pe", lambda e: e.transpose(pst[:], pad[:], ident[:]), reads=PT, writes=[dpst])
                V(lambda e, ri=ri, gp=gp: e.tensor_copy(out=BBT[ri][:, gp, :], in_=pst[:]), [dpst] + PT, PT)
            for ri, src, sgn in ((0, cre, 1.0), (1, cim, -1.0)):
                V(lambda e, ri=ri, gp=gp: e.memset(Cp[ri][:, gp, :], 0.0), PT, PT)
                for j in range(2):
                    c0 = 16 * (2 * gp + j)
                    V(lambda e, j=j, c0=c0, src=src, gp=gp, ri=ri, sgn=sgn: e.tensor_scalar(
                        out=Cp[ri][64 * j:64 * j + 64, gp, c0:c0 + 16], in0=src[64 * j:64 * j + 64, gp, :],
                        scalar1=sgn, scalar2=None, op0=ALU.mult), PT, PT)

        if debug:
            for nm, t_, shp in (("th", th, [128, 4]), ("mag", mag, [128, 4]), ("kre", kre, [128, 4]), ("kim", kim, [128, 4]),
                                ("cos", cs[0], [128, 4, 512]), ("sin", cs[1], [128, 4, 512]), ("Adre", Adre, [128, 4, 5]),
                                ("Adim", Adim, [128, 4, 5]), ("bbre", bbre, [128, 4, 16]), ("jidx", jidx, [128, 512])):
                P.dma("sp", C.dout("dbg_" + nm, shp), t_[:], reads=PT)
            P.dma("sp", C.dout("dbg_BBT0", [128, 4, 128], BF16), BBT[0][:], reads=PT)
            P.dma("sp", C.dout("dbg_Cp1", [128, 4, 128], BF16), Cp[1][:], reads=PT)
        def mod_tile(gp, tok0, t8):
            sl = slice(t8 * 512, (t8 + 1) * 512)
            for ri in range(2):
                P.op("pe", lambda e, ri=ri: e.matmul(psb[ri][:], BBT[ri][:, gp, :],
                                                     u_sb[:, tok0 + sl.start:tok0 + sl.stop], start=True, stop=True),
                     reads=[du] + PT, writes=[dpsb[ri]])
                P.op("act", lambda e, ri=ri: e.activation(out=bu[ri][:], in_=psb[ri][:], func=AF.Copy),
                     reads=[dpsb[ri]], writes=[dbu[ri]])
            V(lambda e: e.tensor_tensor(out=tt[0][:], in0=bu[0][:], in1=cs[0][:, gp, :], op=ALU.mult),
              [dbu[0]] + PT, [dtq[0]])
            V(lambda e: e.tensor_tensor(out=tt[1][:], in0=bu[1][:], in1=cs[1][:, gp, :], op=ALU.mult),
              [dbu[1]] + PT, [dtq[1]])
            V(lambda e: e.tensor_tensor(out=w[0][:, sl], in0=tt[0][:], in1=tt[1][:], op=ALU.add),
              [dtq[0], dtq[1]], [dw[0]])
            V(lambda e: e.tensor_tensor(out=tt[2][:], in0=bu[1][:], in1=cs[0][:, gp, :], op=ALU.mult),
              [dbu[1]] + PT, [dtq[2]], "pool")
            V(lambda e: e.tensor_tensor(out=tt[3][:], in0=bu[0][:], in1=cs[1][:, gp, :], op=ALU.mult),
              [dbu[0]] + PT, [dtq[3]], "pool")
            V(lambda e: e.tensor_tensor(out=w[1][:, sl], in0=tt[2][:], in1=tt[3][:], op=ALU.subtract),
              [dtq[2], dtq[3]], [dw[1]], "pool")

        def scans(extra):
            for i in range(2):
                V(lambda e, i=i: e.tensor_tensor_scan(out=z[i][:], data0=rfull[:], data1=w[i][:], initial=0.0,
                                                      op0=ALU.mult, op1=ALU.add), [drf, dw[i]] + extra, [dz[i]])

        def hs_step(gp, k, cur):
            d_ = 1 << k
            nxt = 2 - cur
            ar = Adre[:, gp, k:k + 1]
            ai = Adim[:, gp, k:k + 1]
            ain = Adimn[:, gp, k:k + 1]
            sre, sim = cst[cur], cst[cur + 1]
            nre_, nim_ = cst[nxt], cst[nxt + 1]
            CS = [dcst]
            V(lambda e: e.tensor_copy(out=nre_[:, 0:d_], in_=sre[:, 0:d_]), CS, CS)
            V(lambda e: e.tensor_copy(out=nim_[:, 0:d_], in_=sim[:, 0:d_]), CS, CS)
            V(lambda e: e.scalar_tensor_tensor(out=cst[4][:, d_:32], in0=sre[:, 0:32 - d_], scalar=ar, in1=sre[:, d_:32],
                                               op0=ALU.mult, op1=ALU.add), CS + PT, CS)
            V(lambda e: e.scalar_tensor_tensor(out=nre_[:, d_:32], in0=sim[:, 0:32 - d_], scalar=ain,
                                               in1=cst[4][:, d_:32], op0=ALU.mult, op1=ALU.add), CS + PT, CS)
            V(lambda e: e.scalar_tensor_tensor(out=cst[4][:, d_:32], in0=sim[:, 0:32 - d_], scalar=ar, in1=sim[:, d_:32],
                                               op0=ALU.mult, op1=ALU.add), CS + PT, CS)
            V(lambda e: e.scalar_tensor_tensor(out=nim_[:, d_:32], in0=sre[:, 0:32 - d_], scalar=ai,
                                               in1=cst[4][:, d_:32], op0=ALU.mult, op1=ALU.add), CS + PT, CS)
            return nxt

        def carry(gp):
            g1 = slice(gp, gp + 1)
            CS = [dcst]
            zE = [z[i][:].rearrange("p (c j) -> p c j", j=128)[:, :, 127] for i in range(2)]
            c127 = cs[0][:, gp, 127:128]
            s127 = cs[1][:, gp, 127:128]
            V(lambda e: e.tensor_scalar(out=cst[2][:], in0=zE[0], scalar1=c127, scalar2=None, op0=ALU.mult),
              [dz[0]] + PT, CS)
            V(lambda e: e.scalar_tensor_tensor(out=cst[0][:], in0=zE[1], scalar=s127n[:, g1], in1=cst[2][:],
                                               op0=ALU.mult, op1=ALU.add), [dz[1]] + PT + CS, CS)
            V(lambda e: e.tensor_scalar(out=cst[2][:], in0=zE[0], scalar1=s127, scalar2=None, op0=ALU.mult),
              [dz[0]] + PT + CS, CS)
            V(lambda e: e.scalar_tensor_tensor(out=cst[1][:], in0=zE[1], scalar=c127, in1=cst[2][:],
                                               op0=ALU.mult, op1=ALU.add), [dz[1]] + PT + CS, CS)
            cur = 0
            for k in range(5):
                cur = hs_step(gp, k, cur)
            Sre, Sim = cst[cur], cst[cur + 1]
            w0 = [w[i][:].rearrange("p (c j) -> p c j", j=128)[:, 1:32, 0] for i in range(2)]
            V(lambda e: e.scalar_tensor_tensor(out=cst[5][:, 0:31], in0=Sre[:, 0:31], scalar=abre[:, g1],
                                               in1=w0[0], op0=ALU.mult, op1=ALU.add), CS + PT + [dw[0]], CS)
            V(lambda e: e.scalar_tensor_tensor(out=w0[0], in0=Sim[:, 0:31], scalar=abimn[:, g1],
                                               in1=cst[5][:, 0:31], op0=ALU.mult, op1=ALU.add), CS + PT, [dw[0]])
            V(lambda e: e.scalar_tensor_tensor(out=cst[6][:, 0:31], in0=Sim[:, 0:31], scalar=abre[:, g1],
                                               in1=w0[1], op0=ALU.mult, op1=ALU.add), CS + PT + [dw[1]], CS)
            V(lambda e: e.scalar_tensor_tensor(out=w0[1], in0=Sre[:, 0:31], scalar=abim[:, g1],
                                               in1=cst[6][:, 0:31], op0=ALU.mult, op1=ALU.add), CS + PT, [dw[1]])

        def out_tile(gp, tok0, t8):
            sl = slice(t8 * 512, (t8 + 1) * 512)
            V(lambda e: e.tensor_tensor(out=tt[0][:], in0=z[0][:, sl], in1=cs[0][:, gp, :], op=ALU.mult),
              [dz[0]] + PT, [dtq[0]])
            V(lambda e: e.tensor_tensor(out=tt[1][:], in0=z[1][:, sl], in1=cs[1][:, gp, :], op=ALU.mult),
              [dz[1]] + PT, [dtq[1]])
            V(lambda e: e.tensor_tensor(out=sbf[0][:], in0=tt[0][:], in1=tt[1][:], op=ALU.subtract),
              [dtq[0], dtq[1]], [dsbf[0]])
            V(lambda e: e.tensor_tensor(out=tt[2][:], in0=z[0][:, sl], in1=cs[1][:, gp, :], op=ALU.mult),
              [dz[0]] + PT, [dtq[2]], "pool")
            V(lambda e: e.tensor_tensor(out=tt[3][:], in0=z[1][:, sl], in1=cs[0][:, gp, :], op=ALU.mult),
              [dz[1]] + PT, [dtq[3]], "pool")
            V(lambda e: e.tensor_tensor(out=sbf[1][:], in0=tt[2][:], in1=tt[3][:], op=ALU.add),
              [dtq[2], dtq[3]], [dsbf[1]], "pool")
            yi = t8 % 2
            P.op("pe", lambda e: e.matmul(psy[yi][:], Cp[0][:, gp, :], sbf[0][:], start=True, stop=False),
                 reads=[dsbf[0]] + PT, writes=[dpsy[yi]])
            P.op("pe", lambda e: e.matmul(psy[yi][:], Cp[1][:, gp, :], sbf[1][:], start=False, stop=True),
                 reads=[dsbf[1]] + PT, writes=[dpsy[yi]])
            r0 = 32 * gp
            V(lambda e: e.scalar_tensor_tensor(
                out=y_sb[r0:r0 + 32, tok0 + sl.start:tok0 + sl.stop],
                in0=u_sb[r0:r0 + 32, tok0 + sl.start:tok0 + sl.stop], scalar=d_sb[r0:r0 + 32, 0:1],
                in1=psy[yi][r0:r0 + 32, :], op0=ALU.mult, op1=ALU.add),
              [du, dpsy[yi]] + PT, [dy])

        def set_rfull(gp):
            V(lambda e: e.tensor_scalar(out=rfull[:], in0=m0[:], scalar1=mag[:, gp:gp + 1], scalar2=None, op0=ALU.mult),
              [dm0] + PT, [drf], "pool")

        for gp in range(4):
            set_rfull(gp)
            for b in range(2):
                for t8 in range(8):
                    mod_tile(gp, b * SEQ, t8)
                scans([])
                carry(gp)
                scans([dcst])
                for t8 in range(8):
                    out_tile(gp, b * SEQ, t8)
        if debug:
            for nm, t_, shp, dd in (("w0", w[0], [128, SEQ], dw[0]), ("w1", w[1], [128, SEQ], dw[1]),
                                    ("z0", z[0], [128, SEQ], dz[0]), ("z1", z[1], [128, SEQ], dz[1]),
                                    ("rfull", rfull, [128, SEQ], drf), ("bu0", bu[0], [128, 512], dbu[0]),
                                    ("tt0", tt[0], [128, 512], dtq[0])):
                P.dma("sp", C.dout("dbg_" + nm, shp), t_[:], reads=[dd])
            for k in range(7):
                P.dma("sp", C.dout("dbg_cst%d" % k, [128, 32]), cst[k][:], reads=[dcst])
        P.dma("sp", yp[:, 0:SEQ], y_sb[:, 0:SEQ], reads=[dy])
        P.dma("sp", yp[:, SEQ:NT], y_sb[:, SEQ:NT], reads=[dy])
        P.finish()
    return nc


def host_B2_params(l, inp, core):
    gs = slice(8 * core, 8 * core + 8)

    def pg(a):
        return np.ascontiguousarray(a.reshape(4, 2, 64).transpose(1, 2, 0).reshape(128, 4))
    lam_re = pg(inp["s5_lambda_re"][l][gs])
    lam_im = pg(inp["s5_lambda_im"][l][gs])
    log_dt = pg(np.repeat(inp["s5_log_dt"][l][gs][:, None], 64, axis=1))

    def pb(a):
        return np.ascontiguousarray(a.reshape(4, 2, 64, 16).transpose(1, 2, 0, 3).reshape(128, 4, 16))
    b_re = pb(inp["s5_b_re"][l][gs])
    b_im = pb(inp["s5_b_im"][l][gs])
    c_reT = pb(np.transpose(inp["s5_c_re"][l][gs], (0, 2, 1)))
    c_imT = pb(np.transpose(inp["s5_c_im"][l][gs], (0, 2, 1)))
    dvec = np.ascontiguousarray(inp["s5_d"][l][128 * core:128 * core + 128].reshape(128, 1))
    return {"lam_re": lam_re, "lam_im": lam_im, "log_dt": log_dt, "b_re": b_re, "b_im": b_im,
            "c_reT": c_reT, "c_imT": c_imT, "dvec": dvec}


_NC_CACHE = {}


def _get(name, fn):
    if name not in _NC_CACHE:
        _NC_CACHE[name] = fn()
    return _NC_CACHE[name]


def _run(nc, maps):
    res = run_bass_kernel_spmd(nc, maps, core_ids=list(range(NCORES)))
    return res.results


def kernel(**inp):
    inp = {k: np.asarray(v) for k, v in inp.items()}
    x_tok = np.ascontiguousarray(inp["x"].reshape(2 * SEQ, D).astype(np.float32))
    depth = inp["w_in"].shape[0]
    for l in range(depth):
        last = (l == depth - 1)
        rA = _run(_get("A", build_A), host_A(x_tok, l, inp))
        yaT = [np.asarray(r["yaT"]) for r in rA]
        gT = [np.asarray(r["gT"]) for r in rA]
        s5T = [np.asarray(r["s5T"]) for r in rA]
        qT = [np.asarray(r["qT"]) for r in rA]
        kT = [np.asarray(r["kT"]) for r in rA]
        vtk = [np.asarray(r["vtk"]) for r in rA]
        mB1, mB2 = [], []
        for c in range(NCORES):
            qs, ks, vs = [], [], []
            for pl in range(NPAIR):
                b, h = divmod(c * NPAIR + pl, 8)
                hs = slice(128 * h, 128 * h + 128)
                qs.append(np.concatenate([qT[4 * b + i][hs, :] for i in range(4)], axis=1))
                ks.append(np.concatenate([kT[4 * b + i][hs, :] for i in range(4)], axis=1))
                vs.append(np.concatenate([vtk[4 * b + i][:, hs] for i in range(4)], axis=0))
            mB1.append({"qT": np.ascontiguousarray(np.stack(qs)), "kT": np.ascontiguousarray(np.stack(ks)),
                        "v": np.ascontiguousarray(np.stack(vs))})
            m2 = host_B2_params(l, inp, c)
            m2["uT"] = np.ascontiguousarray(np.concatenate([s5T[i][128 * c:128 * c + 128, :] for i in range(8)], axis=1))
            mB2.append(m2)
        rB1 = _run(_get("B1", build_B1), mB1)
        rB2 = _run(_get("B2", build_B2), mB2)
        yc = [np.asarray(r["yc"]) for r in rB1]
        yp = [np.asarray(r["yp"]) for r in rB2]
        ycT, ybpT = [], []
        for tc in range(NCORES):
            b, i = divmod(tc, 4)
            rows = []
            for h in range(8):
                c, pl = divmod(b * 8 + h, NPAIR)
                rows.append(yc[c][pl][:, i * TOK:(i + 1) * TOK])
            ycT.append(np.ascontiguousarray(np.concatenate(rows, axis=0)))
            ybpT.append(np.ascontiguousarray(np.concatenate([yp[c][:, tc * TOK:(tc + 1) * TOK] for c in range(8)], axis=0)))
        ncC = _get("C%d" % int(last), lambda: build_C(last))
        rC = _run(ncC, host_C(x_tok, l, inp, yaT, ybpT, ycT, gT, last))
        x_tok = np.ascontiguousarray(np.concatenate([np.asarray(r["xo"]).T for r in rC], axis=0).astype(np.float32))
    return x_tok.reshape(2, SEQ, D)
```

```python
import numpy as np
from contextlib import ExitStack
import ml_dtypes
import concourse.bass as bass
import concourse.mybir as mybir
from concourse.bass_utils import run_bass_kernel_spmd

F32 = mybir.dt.float32
BF16 = mybir.dt.bfloat16
AF = mybir.ActivationFunctionType
ALU = mybir.AluOpType
AX = mybir.AxisListType
NPBF = ml_dtypes.bfloat16

NCORES = 8
TOK = 1024
D = 2048
EPS = 1e-6
QSCALE = 128 ** -0.5


SAME_SYNC = {"pe": False, "act": True, "dve": True, "pool": True, "sp": True}


class Dep:
    __slots__ = ("w", "r")

    def __init__(self):
        self.w = None
        self.r = {}


class Prog:
    ENGS = ("pe", "act", "dve", "pool", "sp")

    def __init__(self, nc, stack, n_dma_sems=(("sp", 12), ("pool", 8), ("act", 4))):
        self.nc = nc
        self.stack = stack
        self.q = {e: [] for e in self.ENGS}
        self.sem = {e: stack.enter_context(nc.semaphore("s_" + e)) for e in ("pe", "act", "dve", "pool")}
        self.cnt = {e: 0 for e in self.sem}
        tot = sum(n for _, n in n_dma_sems)
        self.dsem = [stack.enter_context(nc.semaphore("d%d" % i)) for i in range(tot)]
        self.dcnt = [0] * tot
        self.dpool = {}
        b = 0
        for qn, n in n_dma_sems:
            self.dpool[qn] = [list(range(b, b + n)), 0]
            b += n
        self.seen = {e: {} for e in self.ENGS}
        self.same_sync = dict(SAME_SYNC)
        self._rec = None
        self._streams = {}

    def _semobj(self, key):
        return self.sem[key[1]] if key[0] == "e" else self.dsem[key[1]]

    def _collect(self, eng, reads, writes, extra=None):
        need = dict(extra or {})

        def req(k, v):
            if v > need.get(k, 0):
                need[k] = v
        for t in reads:
            if t.w:
                req(*t.w)
        for t in writes:
            if t.w:
                req(*t.w)
            for k, v in t.r.items():
                req(k, v)
        for k, v in need.items():
            if k == ("e", eng) and not self.same_sync[eng]:
                continue
            if self.seen[eng].get(k, 0) < v:
                self.seen[eng][k] = v
                s = self._semobj(k)
                self.q[eng].append(lambda e, s=s, v=v: e.wait_ge(s, v))

    def begin(self, name):
        self._rec = []
        self._streams[name] = self._rec

    def end(self):
        self._rec = None

    def replay(self, names):
        recs = [self._streams[n] for n in names]
        pos = [0] * len(recs)
        total = sum(len(r) for r in recs)
        for _ in range(total):
            best, bestv = None, None
            for i, r in enumerate(recs):
                if pos[i] < len(r):
                    v = (pos[i] + 0.5) / len(r)
                    if bestv is None or v < bestv:
                        best, bestv = i, v
            kind, args, kw = recs[best][pos[best]]
            pos[best] += 1
            getattr(self, kind)(*args, **kw)

    def op(self, eng, fn, reads=(), writes=()):
        if self._rec is not None:
            self._rec.append(("op", (eng, fn, tuple(reads), tuple(writes)), {}))
            return
        self._collect(eng, reads, writes)
        self.cnt[eng] += 1
        c = self.cnt[eng]
        key = ("e", eng)
        s = self.sem[eng]
        self.q[eng].append(lambda e, fn=fn, s=s: fn(e).then_inc(s, 1))
        for t in reads:
            if t.r.get(key, 0) < c:
                t.r[key] = c
        for t in writes:
            t.w = (key, c)
            t.r = {}

    def dma(self, queue, out, in_, reads=(), writes=(), in_fn=None, out_fn=None, **kw):
        if self._rec is not None:
            self._rec.append(("dma", (queue, out, in_, tuple(reads), tuple(writes), in_fn, out_fn), dict(kw)))
            return
        pl = self.dpool[queue]
        i = pl[0][pl[1] % len(pl[0])]
        pl[1] += 1
        extra = {}
        if self.dcnt[i] > 0:
            extra[("d", i)] = self.dcnt[i]
        self._collect(queue, reads, writes, extra)
        self.dcnt[i] += 16
        v = self.dcnt[i]
        key = ("d", i)
        s = self.dsem[i]
        self.q[queue].append(
            lambda e, s=s, out=out, in_=in_, kw=kw: e.dma_start(
                out=(out_fn() if out_fn else out), in_=(in_fn() if in_fn else in_), **kw).then_inc(s, 16))
        for t in reads:
            if t.r.get(key, 0) < v:
                t.r[key] = v
        for t in writes:
            t.w = (key, v)
            t.r = {}

    def raw(self, eng, fn, reads=(), writes=()):
        self._collect(eng, reads, writes)
        self.q[eng].append(lambda e, fn=fn: fn(e))

    def coll(self, queue, fn, reads=(), writes=()):
        pl = self.dpool[queue]
        i = pl[0][pl[1] % len(pl[0])]
        pl[1] += 1
        extra = {}
        if self.dcnt[i] > 0:
            extra[("d", i)] = self.dcnt[i]
        self._collect(queue, reads, writes, extra)
        self.dcnt[i] += 16
        v = self.dcnt[i]
        key = ("d", i)
        s = self.dsem[i]
        self.q[queue].append(lambda e, s=s, fn=fn: fn(e).then_inc(s, 16))
        for t in reads:
            if t.r.get(key, 0) < v:
                t.r[key] = v
        for t in writes:
            t.w = (key, v)
            t.r = {}

    def finish(self):
        for i, v in enumerate(self.dcnt):
            if v > 0 and self.seen["sp"].get(("d", i), 0) < v:
                s = self.dsem[i]
                self.q["sp"].append(lambda e, s=s, v=v: e.wait_ge(s, v))
        for e_, c in self.cnt.items():
            if c > 0:
                s = self.sem[e_]
                self.q["sp"].append(lambda e, s=s, c=c: e.wait_ge(s, c))
        q = self.q
        with self.nc.Block() as block:
            @block.tensor
            def _(e):
                for f in q["pe"]:
                    f(e)

            @block.scalar
            def _(e):
                for f in q["act"]:
                    f(e)

            @block.vector
            def _(e):
                for f in q["dve"]:
                    f(e)

            @block.gpsimd
            def _(e):
                for f in q["pool"]:
                    f(e)

            @block.sync
            def _(e):
                for f in q["sp"]:
                    f(e)


class Ctx:
    def __init__(self, nc, st):
        self.nc, self.st = nc, st

    def sb(self, name, shape, dt):
        return self.st.enter_context(self.nc.sbuf_tensor(name, shape, dt))

    def ps(self, name, shape, dt=F32):
        return self.st.enter_context(self.nc.psum_tensor(name, shape, dt))

    def din(self, name, shape, dt=F32):
        return self.nc.dram_tensor(name, list(shape), dt, kind="ExternalInput").ap()

    def dout(self, name, shape, dt=F32):
        return self.nc.dram_tensor(name, list(shape), dt, kind="ExternalOutput").ap()


def emit_rmsnorm(P, C, x_sb, dx, g_sb, dg, hT, dh, ones_bf, dones, sq, dsq, ps, dps, rstd, drstd, epsb, deps, ntok):
    for kt in range(16):
        i = kt % 2
        P.op("act", lambda e, kt=kt, i=i: e.activation(out=sq[i][:], in_=x_sb[:, kt, :], func=AF.Square),
             reads=[dx], writes=[dsq[i]])
        for hf in range(ntok // 512):
            P.op("pe", lambda e, kt=kt, i=i, hf=hf: e.matmul(ps[:, hf * 512:(hf + 1) * 512], ones_bf[:],
                                                             sq[i][:, hf * 512:(hf + 1) * 512],
                                                             start=(kt == 0), stop=(kt == 15)),
                 reads=[dsq[i], dones], writes=[dps])
    P.op("act", lambda e: e.activation(out=rstd[:], in_=ps[:, :ntok], func=AF.Sqrt, bias=epsb[:], scale=1.0 / D),
         reads=[dps, deps], writes=[drstd])
    P.op("dve", lambda e: e.reciprocal(out=rstd[:], in_=rstd[:]), reads=[drstd], writes=[drstd])
    for kt in range(16):
        P.op("dve", lambda e, kt=kt: e.scalar_tensor_tensor(out=hT[:, kt, :], in0=x_sb[:, kt, :],
                                                           scalar=g_sb[:, kt:kt + 1], in1=rstd[:],
                                                           op0=ALU.mult, op1=ALU.mult),
             reads=[dx, dg, drstd], writes=[dh])


def build_A():
    nc = bass.Bass("TRN2", target_bir_lowering=False)
    with ExitStack() as st:
        st.enter_context(nc.allow_low_precision("bf16 matmul operands, fp32 accumulation"))
        C = Ctx(nc, st)
        xT = C.din("xT", [D, TOK])
        n1g = C.din("n1g", [128, 16])
        w_in = C.din("w_in", [D, 12288])
        bg = C.din("bg", [128, 48])
        gng = C.din("gng", [1, 1024])
        wsT = C.din("wsT", [8, 128, 128])
        bs = C.din("bs", [1, 1024])
        yaT = C.dout("yaT", [1024, TOK], BF16)
        s5T = C.dout("s5T", [1024, TOK], BF16)
        qT = C.dout("qT", [1024, TOK], BF16)
        kT = C.dout("kT", [1024, TOK], BF16)
        vtk = C.dout("vtk", [TOK, 1024], BF16)
        gT = C.dout("gT", [6144, TOK], BF16)

        P = Prog(nc, st)
        x_sb = C.sb("x_sb", [128, 16, TOK], F32)
        hT = C.sb("hT", [128, 16, TOK], BF16)
        g_sb = C.sb("g_sb", [128, 16], F32)
        bg_sb = C.sb("bg_sb", [128, 48], F32)
        gng_b = C.sb("gng_b", [128, 1024], F32)
        bs_b = C.sb("bs_b", [128, 1024], F32)
        ws_f = C.sb("ws_f", [128, 8, 128], F32)
        ws_b = C.sb("ws_b", [128, 8, 128], BF16)
        ones_bf = C.sb("ones_bf", [128, 128], BF16)
        epsb = C.sb("epsb", [128, 1], F32)
        sq = [C.sb("sq%d" % i, [128, TOK], BF16) for i in range(2)]
        rstd = C.sb("rstd", [128, TOK], F32)
        wt = [C.sb("wt%d" % i, [128, 16, 512], BF16) for i in range(2)]
        stg = [C.sb("stg%d" % i, [128, TOK], BF16) for i in range(2)]
        uT = C.sb("uT", [128, 8, TOK], BF16)
        vtok = C.sb("vtok", [128, 8, 1024], BF16)
        vn = C.sb("vn", [128, 1024], BF16)
        junk = sq[0]
        vss = C.sb("vss", [128, 8], F32)
        vrs = C.sb("vrs", [128, 8], F32)
        mixt = rstd
        ya_sb = C.sb("ya_sb", [128, 8, TOK], BF16)
        ps = [C.ps("ps%d" % i, [128, 1024]) for i in range(2)]
        psm = C.ps("psm", [128, 1024])

        dx, dh, dg, dbg, dgng, dbs, dwsf, dwsb, dones, deps = [Dep() for _ in range(10)]
        dsq = [Dep(), Dep()]
        drstd = Dep()
        dwt = [Dep(), Dep()]
        dstg = [Dep(), Dep()]
        dps = [Dep(), Dep()]
        dpsm, duT, dvtok, dvn, dvss, dvrs, dya = [Dep() for _ in range(7)]
        djunk = dsq[0]
        dmixt = drstd

        xv = xT.rearrange("(kt p) t -> p kt t", p=128)
        for i in range(4):
            P.dma("sp", x_sb[:, 4 * i:4 * i + 4, :], xv[:, 4 * i:4 * i + 4, :], writes=[dx])
        P.dma("sp", g_sb[:], n1g, writes=[dg])
        P.dma("sp", bg_sb[:], bg, writes=[dbg])
        P.dma("sp", gng_b[:], gng.partition_broadcast(128), writes=[dgng])
        P.dma("sp", bs_b[:], bs.partition_broadcast(128), writes=[dbs])
        P.dma("sp", ws_f[:], wsT.rearrange("g s t -> s g t"), writes=[dwsf])
        wv = w_in.rearrange("(kt p) c -> p kt c", p=128)

        def load_w(cb):
            P.dma("pool", wt[cb % 2][:], wv[:, :, cb * 512:(cb + 1) * 512], writes=[dwt[cb % 2]])
        load_w(0)
        P.op("dve", lambda e: e.memset(ones_bf[:], 1.0), writes=[dones])
        P.op("dve", lambda e: e.memset(epsb[:], EPS), writes=[deps])
        P.op("pool", lambda e: e.affine_select(out=ws_f[:], in_=ws_f[:], pattern=[[0, 8], [1, 128]],
                                               compare_op=ALU.is_ge, fill=0.0, base=0, channel_multiplier=-1),
             reads=[dwsf], writes=[dwsf])
        P.op("dve", lambda e: e.tensor_copy(out=ws_b[:], in_=ws_f[:]), reads=[dwsf], writes=[dwsb])

        emit_rmsnorm(P, C, x_sb, dx, g_sb, dg, hT, dh, ones_bf, dones, sq, dsq, ps[0], dps[0], rstd, drstd,
                     epsb, deps, TOK)

        pcount = [0]
        scount = [0]

        def form2(cb, epi):
            w = wt[cb % 2]
            for m in range(4):
                pi = pcount[0] % 2
                pcount[0] += 1
                for hf in range(2):
                    for kt in range(16):
                        P.op("pe", lambda e, w=w, m=m, hf=hf, kt=kt, pi=pi: e.matmul(
                            ps[pi][:, hf * 512:(hf + 1) * 512], w[:, kt, m * 128:(m + 1) * 128],
                            hT[:, kt, hf * 512:(hf + 1) * 512], start=(kt == 0), stop=(kt == 15)),
                            reads=[dwt[cb % 2], dh], writes=[dps[pi]])
                epi(cb * 4 + m, pi)

        def epi_store(func, dst, colbase, scale=1.0, bias_col=None):
            def epi(blk, pi):
                si = scount[0] % 2
                scount[0] += 1
                r0 = (blk - colbase) * 128
                if bias_col is None:
                    P.op("act", lambda e: e.activation(out=stg[si][:], in_=ps[pi][:], func=func, scale=scale),
                         reads=[dps[pi]], writes=[dstg[si]])
                else:
                    bc = bias_col(blk)
                    P.op("act", lambda e: e.activation(out=stg[si][:], in_=ps[pi][:], func=func,
                                                       bias=bg_sb[:, bc:bc + 1]),
                         reads=[dps[pi], dbg], writes=[dstg[si]])
                P.dma("sp", dst[r0:r0 + 128, :], stg[si][:], reads=[dstg[si]])
            return epi

        def epi_u(blk, pi):
            P.op("act", lambda e: e.activation(out=uT[:, blk, :], in_=ps[pi][:], func=AF.Gelu_apprx_tanh),
                 reads=[dps[pi]], writes=[duT])

        def form1(cb, epi):
            w = wt[cb % 2]
            for c in range(8):
                pi = pcount[0] % 2
                pcount[0] += 1
                for kt in range(16):
                    P.op("pe", lambda e, w=w, c=c, kt=kt, pi=pi: e.matmul(
                        ps[pi][:, 0:512], hT[:, kt, c * 128:(c + 1) * 128], w[:, kt, :],
                        start=(kt == 0), stop=(kt == 15)),
                        reads=[dwt[cb % 2], dh], writes=[dps[pi]])
                epi(cb, c, pi)

        def epi_vg(cb, c, pi):
            off = (cb - 2) * 512
            P.op("act", lambda e: e.activation(out=vtok[:, c, off:off + 512], in_=ps[pi][:, 0:512],
                                               func=AF.Gelu_apprx_tanh),
                 reads=[dps[pi]], writes=[dvtok])

        def epi_va(cb, c, pi):
            off = (cb - 10) * 512
            si = scount[0] % 2
            scount[0] += 1
            P.op("act", lambda e: e.activation(out=stg[si][:, 0:512], in_=ps[pi][:, 0:512], func=AF.Copy),
                 reads=[dps[pi]], writes=[dstg[si]])
            P.dma("sp", vtk[c * 128:(c + 1) * 128, off:off + 512], stg[si][:, 0:512], reads=[dstg[si]])

        def gmlp_finish():
            for c in range(8):
                P.op("act", lambda e, c=c: e.activation(out=junk[:], in_=vtok[:, c, :], func=AF.Square,
                                                        accum_out=vss[:, c:c + 1]),
                     reads=[dvtok], writes=[djunk, dvss])
            P.op("act", lambda e: e.activation(out=vrs[:], in_=vss[:], func=AF.Sqrt, bias=epsb[:], scale=1.0 / 1024),
                 reads=[dvss, deps], writes=[dvrs])
            P.op("dve", lambda e: e.reciprocal(out=vrs[:], in_=vrs[:]), reads=[dvrs], writes=[dvrs])
            for c in range(8):
                P.op("dve", lambda e, c=c: e.scalar_tensor_tensor(out=vn[:], in0=vtok[:, c, :],
                                                                  scalar=vrs[:, c:c + 1], in1=gng_b[:],
                                                                  op0=ALU.mult, op1=ALU.mult),
                     reads=[dvtok, dvrs, dgng], writes=[dvn])
                for g in range(8):
                    P.op("pe", lambda e, g=g: e.matmul(psm[:, g * 128:(g + 1) * 128], vn[:, g * 128:(g + 1) * 128],
                                                       ws_b[:, g, :], start=True, stop=True),
                         reads=[dvn, dwsb], writes=[dpsm])
                P.op("dve", lambda e: e.tensor_tensor(out=mixt[:], in0=psm[:], in1=bs_b[:], op=ALU.add),
                     reads=[dpsm, dbs], writes=[dmixt])
                P.op("pool", lambda e, c=c: e.tensor_tensor(
                    out=ya_sb[:, :, c * 128:(c + 1) * 128], in0=mixt[:].rearrange("p (g t) -> p g t", g=8),
                    in1=uT[:, :, c * 128:(c + 1) * 128], op=ALU.mult),
                    reads=[dmixt, duT], writes=[dya])
            P.dma("sp", yaT.rearrange("(g p) t -> p g t", p=128), ya_sb[:], reads=[dya])

        for cb in range(24):
            if cb + 1 < 24:
                load_w(cb + 1)
            if cb < 2:
                form2(cb, epi_u)
            elif cb < 4:
                form1(cb, epi_vg)
                if cb == 3:
                    gmlp_finish()
            elif cb < 6:
                form2(cb, epi_store(AF.Copy, s5T, 16))
            elif cb < 8:
                form2(cb, epi_store(AF.Copy, qT, 24, scale=QSCALE))
            elif cb < 10:
                form2(cb, epi_store(AF.Copy, kT, 32))
            elif cb < 12:
                form1(cb, epi_va)
            else:
                form2(cb, epi_store(AF.Sigmoid, gT, 48, bias_col=lambda blk: blk - 48))
        P.finish()
    return nc


def host_A(x_tok, l, inp):
    maps = []
    n1g = np.ascontiguousarray(inp["norm1_g"][l].reshape(16, 128).T)
    bgv = np.ascontiguousarray(inp["b_gate"][l].reshape(48, 128).T)
    gng = np.ascontiguousarray(inp["gm_norm_g"][l].reshape(1, 1024))
    wsT = np.ascontiguousarray(np.transpose(inp["gm_w_s"][l], (0, 2, 1)))
    bsv = np.ascontiguousarray(inp["gm_b_s"][l].reshape(1, 1024))
    w_in = np.ascontiguousarray(inp["w_in"][l])
    for c in range(NCORES):
        xT = np.ascontiguousarray(x_tok[c * TOK:(c + 1) * TOK, :].T)
        maps.append({"xT": xT, "n1g": n1g, "w_in": w_in, "bg": bgv, "gng": gng, "wsT": wsT, "bs": bsv})
    return maps


def build_C(last, debug=False):
    nc = bass.Bass("TRN2", target_bir_lowering=False)
    with ExitStack() as st:
        st.enter_context(nc.allow_low_precision("bf16 matmul operands, fp32 accumulation"))
        C = Ctx(nc, st)
        xT = C.din("xT", [D, TOK])
        yaT = C.din("yaT", [1024, TOK], BF16)
        ybpT = C.din("ybpT", [1024, TOK], BF16)
        ycT = C.din("ycT", [1024, TOK], BF16)
        gT = C.din("gT", [6144, TOK], BF16)
        w_glu = C.din("w_glu", [1024, 1024])
        b_glu = C.din("b_glu", [128, 8])
        w_br = C.din("w_br", [3, 1024, D])
        w_out = C.din("w_out", [D, D])
        n2g = C.din("n2g", [128, 16])
        w_m1 = C.din("w_m1", [D, 8192])
        w_m2 = C.din("w_m2", [8192, D])
        fing = C.din("fing", [128, 16])
        xo = C.dout("xo", [D, TOK])
        if debug:
            dbg_yb = C.dout("dbg_yb", [1024, TOK], BF16)
            dbg_mg = C.dout("dbg_mg", [D, TOK], BF16)
            dbg_x1 = C.dout("dbg_x1", [D, TOK])
            dbg_h2 = C.dout("dbg_h2", [D, TOK], BF16)

        P = Prog(nc, st)
        x_sb = C.sb("x_sb", [128, 16, TOK], F32)
        R2 = C.sb("R2", [128, 3, 8, TOK], BF16)
        R3 = C.sb("R3", [128, 16 * TOK], BF16)
        R4 = C.sb("R4", [128, 4, 8 * 512], BF16)
        gts = [C.sb("gts%d" % i, [128, TOK], BF16) for i in range(6)]
        acc = C.sb("acc", [128, TOK], F32)
        tmp = C.sb("tmp", [128, TOK], F32)
        stg = [C.sb("stg%d" % i, [128, TOK], BF16) for i in range(2)]
        rstd = C.sb("rstd", [128, TOK], F32)
        sq = stg
        g2_sb = C.sb("g2_sb", [128, 16], F32)
        gf_sb = C.sb("gf_sb", [128, 16], F32)
        bgl_sb = C.sb("bgl_sb", [128, 8], F32)
        ones_bf = C.sb("ones_bf", [128, 128], BF16)
        epsb = C.sb("epsb", [128, 1], F32)
        ps = [C.ps("ps%d" % i, [128, 1024]) for i in range(3)]

        dx, dR3, dg2, dgf, dbgl, dones, deps, dacc, dtmp, drstd = [Dep() for _ in range(10)]
        dR2 = [Dep(), Dep(), Dep()]
        dR4 = [Dep() for _ in range(4)]
        dgts = [Dep() for _ in range(6)]
        dstg = [Dep(), Dep()]
        dsq = dstg
        dps = [Dep() for _ in range(3)]

        gS = R3[:, 0:8 * TOK].rearrange("p (k t) -> p k t", k=8)
        mg = R3[:].rearrange("p (k t) -> p k t", k=16)
        aT = R2[:].rearrange("p a k t -> p (a k) t")
        w8 = [R4[:, i, :].rearrange("p (k c) -> p k c", k=8) for i in range(4)]
        w16 = [R4[:, 2 * i:2 * i + 2, :].rearrange("p a (k c) -> p (a k) c", k=8) for i in range(2)]

        xv = xT.rearrange("(kt p) t -> p kt t", p=128)
        P.dma("sp", R2[:, 1], ybpT.rearrange("(k p) t -> p k t", p=128), writes=[dR2[1]])
        P.dma("sp", bgl_sb[:], b_glu, writes=[dbgl])
        for i in range(4):
            P.dma("sp", x_sb[:, 4 * i:4 * i + 4, :], xv[:, 4 * i:4 * i + 4, :], writes=[dx])
        P.dma("sp", R2[:, 0], yaT.rearrange("(k p) t -> p k t", p=128), writes=[dR2[0]])
        P.dma("sp", R2[:, 2], ycT.rearrange("(k p) t -> p k t", p=128), writes=[dR2[2]])
        P.dma("sp", g2_sb[:], n2g, writes=[dg2])
        P.dma("sp", gf_sb[:], fing, writes=[dgf])
        P.op("dve", lambda e: e.memset(ones_bf[:], 1.0), writes=[dones])
        P.op("dve", lambda e: e.memset(epsb[:], EPS), writes=[deps])

        jobs = []
        glu_v = w_glu.rearrange("(kt p) c -> p kt c", p=128)
        for cb in range(2):
            jobs.append(("glu", cb, 8, glu_v[:, :, cb * 512:(cb + 1) * 512]))
        for cb in range(4):
            for n in range(3):
                jobs.append(("br", (cb, n), 8, w_br[n].rearrange("(kt p) c -> p kt c", p=128)[:, :, cb * 512:(cb + 1) * 512]))
        wo_v = w_out.rearrange("(kt p) c -> p kt c", p=128)
        for cb in range(4):
            jobs.append(("wo", cb, 16, wo_v[:, :, cb * 512:(cb + 1) * 512]))
        m1_v = w_m1.rearrange("(kt p) c -> p kt c", p=128)
        m2_v = w_m2.rearrange("(kt p) c -> p kt c", p=128)
        for fc in range(4):
            for fb in range(4):
                c0 = fc * 2048 + fb * 512
                jobs.append(("m1", (fc, fb), 16, m1_v[:, :, c0:c0 + 512]))
            for cb in range(4):
                jobs.append(("m2", (fc, cb), 16, m2_v[:, fc * 16:(fc + 1) * 16, cb * 512:(cb + 1) * 512]))
        slot8 = 0
        slots = []
        quarters = []
        for kind, key, nk, src in jobs:
            if nk == 8:
                s = slot8 % 4
                slot8 += 1
                slots.append((w8[s], [dR4[s]]))
                quarters.append([s])
            else:
                if slot8 % 2:
                    slot8 += 1
                s = (slot8 // 2) % 2
                slot8 += 2
                slots.append((w16[s], [dR4[2 * s], dR4[2 * s + 1]]))
                quarters.append([2 * s, 2 * s + 1])
        issued = [0]
        occupant = [None] * 4
        consumed = set()

        def prefetch(upto):
            while issued[0] < min(upto, len(jobs)):
                j = issued[0]
                if any(occupant[q] is not None and occupant[q] not in consumed for q in quarters[j]):
                    return
                for q in quarters[j]:
                    occupant[q] = j
                P.dma("pool", slots[j][0], jobs[j][3], writes=slots[j][1])
                issued[0] += 1

        def done(j):
            consumed.add(j)
            prefetch(j + 4)
        pc = [0]

        def mm_block(j, m, rhs, drhs, nkt):
            pi = pc[0] % 3
            pc[0] += 1
            prefetch(j + 1)
            assert issued[0] > j, "weight job %d not loadable (ring slot busy)" % j
            wv, wd = slots[j]
            for hf in range(2):
                for kt in range(nkt):
                    P.op("pe", lambda e, wv=wv, m=m, hf=hf, kt=kt, pi=pi: e.matmul(
                        ps[pi][:, hf * 512:(hf + 1) * 512], wv[:, kt, m * 128:(m + 1) * 128],
                        rhs[:, kt, hf * 512:(hf + 1) * 512], start=(kt == 0), stop=(kt == nkt - 1)),
                        reads=wd + drhs, writes=[dps[pi]])
            return pi

        jn = [0]
        prefetch(3)
        for k in range(8):
            P.op("act", lambda e, k=k: e.activation(out=gS[:, k, :], in_=R2[:, 1, k, :], func=AF.Gelu_apprx_tanh),
                 reads=[dR2[1]], writes=[dR3])
        sc = [0]
        for cb in range(2):
            j = jn[0]
            jn[0] += 1
            prefetch(j + 3)
            for m in range(4):
                blk = cb * 4 + m
                pi = mm_block(j, m, gS, [dR3], 8)
                si = sc[0] % 2
                sc[0] += 1
                P.op("act", lambda e, pi=pi, si=si, blk=blk: e.activation(
                    out=stg[si][:], in_=ps[pi][:], func=AF.Sigmoid, bias=bgl_sb[:, blk:blk + 1]),
                    reads=[dps[pi], dbgl], writes=[dstg[si]])
                P.op("dve", lambda e, si=si, blk=blk: e.tensor_tensor(
                    out=R2[:, 1, blk, :], in0=gS[:, blk, :], in1=stg[si][:], op=ALU.mult),
                    reads=[dR3, dstg[si]], writes=[dR2[1]])
            done(j)
        if debug:
            P.dma("sp", dbg_yb.rearrange("(k p) t -> p k t", p=128), R2[:, 1], reads=[dR2[1]])
        gc = [0]
        for cb in range(4):
            js = [jn[0], jn[0] + 1, jn[0] + 2]
            jn[0] += 3
            prefetch(js[2] + 2)
            for m in range(4):
                dt_ = cb * 4 + m
                for n in range(3):
                    gi = gc[0] % 6
                    gc[0] += 1
                    r0 = (n * 16 + dt_) * 128
                    P.dma("sp", gts[gi][:], gT[r0:r0 + 128, :], writes=[dgts[gi]])
                    pi = mm_block(js[n], m, R2[:, n], [dR2[n]], 8)
                    if n == 0:
                        P.op("dve", lambda e, pi=pi, gi=gi: e.tensor_tensor(out=acc[:], in0=ps[pi][:], in1=gts[gi][:],
                                                                            op=ALU.mult),
                             reads=[dps[pi], dgts[gi]], writes=[dacc])
                    else:
                        P.op("dve", lambda e, pi=pi, gi=gi: e.tensor_tensor(out=tmp[:], in0=ps[pi][:], in1=gts[gi][:],
                                                                            op=ALU.mult),
                             reads=[dps[pi], dgts[gi]], writes=[dtmp])
                        if n == 1:
                            P.op("dve", lambda e: e.tensor_tensor(out=acc[:], in0=acc[:], in1=tmp[:], op=ALU.add),
                                 reads=[dacc, dtmp], writes=[dacc])
                        else:
                            P.op("dve", lambda e, dt_=dt_: e.tensor_tensor(out=mg[:, dt_, :], in0=acc[:], in1=tmp[:],
                                                                          op=ALU.add),
                                 reads=[dacc, dtmp], writes=[dR3])
            for j_ in js:
                done(j_)
        if debug:
            P.dma("sp", dbg_mg.rearrange("(k p) t -> p k t", p=128), mg, reads=[dR3])
        for cb in range(4):
            j = jn[0]
            jn[0] += 1
            prefetch(j + 2)
            for m in range(4):
                dt_ = cb * 4 + m
                pi = mm_block(j, m, mg, [dR3], 16)
                P.op("dve", lambda e, pi=pi, dt_=dt_: e.tensor_tensor(out=x_sb[:, dt_, :], in0=x_sb[:, dt_, :],
                                                                      in1=ps[pi][:], op=ALU.add),
                     reads=[dps[pi], dx], writes=[dx])
            done(j)
        if debug:
            P.dma("sp", dbg_x1.rearrange("(k p) t -> p k t", p=128), x_sb[:], reads=[dx])
        emit_rmsnorm(P, C, x_sb, dx, g2_sb, dg2, mg, dR3, ones_bf, dones, sq, dsq, ps[0], dps[0], rstd, drstd,
                     epsb, deps, TOK)
        if debug:
            P.dma("sp", dbg_h2.rearrange("(k p) t -> p k t", p=128), mg, reads=[dR3])
        for fc in range(4):
            for fb in range(4):
                j = jn[0]
                jn[0] += 1
                prefetch(j + 2)
                for m in range(4):
                    ft = fb * 4 + m
                    pi = mm_block(j, m, mg, [dR3], 16)
                    P.op("act", lambda e, pi=pi: e.activation(out=tmp[:], in_=ps[pi][:], func=AF.Relu),
                         reads=[dps[pi]], writes=[dtmp])
                    P.op("dve", lambda e, ft=ft: e.tensor_tensor(out=aT[:, ft, :], in0=tmp[:], in1=tmp[:], op=ALU.mult),
                         reads=[dtmp], writes=dR2)
                done(j)
            for cb in range(4):
                j = jn[0]
                jn[0] += 1
                prefetch(j + 2)
                for m in range(4):
                    dt_ = cb * 4 + m
                    pi = mm_block(j, m, aT, dR2, 16)
                    P.op("dve", lambda e, pi=pi, dt_=dt_: e.tensor_tensor(out=x_sb[:, dt_, :], in0=x_sb[:, dt_, :],
                                                                          in1=ps[pi][:], op=ALU.add),
                         reads=[dps[pi], dx], writes=[dx])
                done(j)
        xov = xo.rearrange("(kt p) t -> p kt t", p=128)
        if not last:
            for i in range(4):
                P.dma("sp", xov[:, 4 * i:4 * i + 4, :], x_sb[:, 4 * i:4 * i + 4, :], reads=[dx])
        else:
            for kt in range(16):
                i = kt % 2
                P.op("act", lambda e, kt=kt, i=i: e.activation(out=sq[i][:], in_=x_sb[:, kt, :], func=AF.Square),
                     reads=[dx], writes=[dsq[i]])
                for hf in range(2):
                    P.op("pe", lambda e, kt=kt, i=i, hf=hf: e.matmul(ps[0][:, hf * 512:(hf + 1) * 512], ones_bf[:],
                                                                     sq[i][:, hf * 512:(hf + 1) * 512],
                                                                     start=(kt == 0), stop=(kt == 15)),
                         reads=[dsq[i], dones], writes=[dps[0]])
            P.op("act", lambda e: e.activation(out=rstd[:], in_=ps[0][:], func=AF.Sqrt, bias=epsb[:], scale=1.0 / D),
                 reads=[dps[0], deps], writes=[drstd])
            P.op("dve", lambda e: e.reciprocal(out=rstd[:], in_=rstd[:]), reads=[drstd], writes=[drstd])
            fo = [acc, tmp]
            dfo = [dacc, dtmp]
            for kt in range(16):
                i = kt % 2
                P.op("dve", lambda e, kt=kt, i=i: e.scalar_tensor_tensor(out=fo[i][:], in0=x_sb[:, kt, :],
                                                                       scalar=gf_sb[:, kt:kt + 1], in1=rstd[:],
                                                                       op0=ALU.mult, op1=ALU.mult),
                     reads=[dx, dgf, drstd], writes=[dfo[i]])
                P.dma("sp", xov[:, kt, :], fo[i][:], reads=[dfo[i]])
        P.finish()
    return nc


def host_C(x_tok, l, inp, yaT, ybpT, ycT, gT, last):
    maps = []
    r = lambda v, n: np.ascontiguousarray(v.reshape(n, 128).T)
    com = {"w_glu": np.ascontiguousarray(inp["s5_w_glu"][l]), "b_glu": r(inp["s5_b_glu"][l], 8),
           "w_br": np.ascontiguousarray(inp["w_branch"][l]), "w_out": np.ascontiguousarray(inp["w_out"][l]),
           "n2g": r(inp["norm2_g"][l], 16), "w_m1": np.ascontiguousarray(inp["w_mlp_in"][l]),
           "w_m2": np.ascontiguousarray(inp["w_mlp_out"][l]), "fing": r(inp["final_g"], 16)}
    for c in range(NCORES):
        m = dict(com)
        m["xT"] = np.ascontiguousarray(x_tok[c * TOK:(c + 1) * TOK, :].T)
        m["yaT"], m["ybpT"], m["ycT"], m["gT"] = yaT[c], ybpT[c], ycT[c], gT[c]
        maps.append(m)
    return maps


SEQ = 4096
NPAIR = 2


def emit_B1(nc, C, P, merged=False):
    qT = C.din("qT", [NPAIR, 128, SEQ], BF16)
    kT = C.din("kT", [NPAIR, 128, SEQ], BF16)
    vv = C.din("v", [NPAIR, SEQ, 128], BF16)
    yc = C.dout("yc", [NPAIR, 128, SEQ], BF16)
    NB = 1 if merged else NPAIR
    q_sb = C.sb("q_sb", [128, NB, SEQ], BF16)
    k_sb = C.sb("k_sb", [128, NB, SEQ], BF16)
    v_sb = C.sb("v_sb", [128, NB, 32, 128], BF16)
    o_st = [C.sb("o_st%d" % i, [128, 512], BF16) for i in range(2)]
    ones_f = C.sb("ones_f", [128, 128], F32)
    mstrict = C.sb("mstrict", [128, 128], BF16)
    negtri = C.sb("negtri", [128, 128], BF16)
    negones = C.sb("negones", [128, 128], BF16)
    spsum = C.sb("spsum", [128, 512], BF16)
    ebuf = [C.sb("ebuf%d" % i, [128, 512], F32) for i in range(2)]
    spb = [C.sb("spb%d" % i, [128, 512], BF16) for i in range(3)]
    wbuf = [C.sb("wbuf%d" % i, [128, 512], BF16) for i in range(3)]
    psA = [C.ps("psA%d" % i, [128, 512]) for i in range(2)]
    psB = [C.ps("psB%d" % i, [128, 512]) for i in range(2)]
    psO = [C.ps("psO%d" % i, [128, 512]) for i in range(1 if merged else 2)]
    if merged:
        psO = [psO[0], psO[0]]
    dq, dk, dv, dconst, dspsum = [Dep() for _ in range(5)]
    do_ = [Dep(), Dep()]
    debuf = [Dep(), Dep()]
    dspb = [Dep() for _ in range(3)]
    dwbuf = [Dep() for _ in range(3)]
    dpsA = [Dep(), Dep()]
    dpsB = [Dep(), Dep()]
    dpsO = [Dep(), Dep()]
    if merged:
        dpsO = [dpsO[0], dpsO[0]]

    def load_pair(p):
        pb = p % NB
        P.dma("sp", q_sb[:, pb, :], qT[p], writes=[dq])
        P.dma("sp", k_sb[:, pb, :], kT[p], writes=[dk])
        P.dma("sp", v_sb[:, pb], vv[p].rearrange("(b s) d -> s b d", s=128), writes=[dv])
    for p in range(NB):
        load_pair(p)
    P.op("pool", lambda e: e.memset(ones_f[:], 1.0), writes=[dconst])
    P.op("pool", lambda e: e.affine_select(out=mstrict[:], in_=ones_f[:], pattern=[[1, 128]],
                                           compare_op=ALU.is_gt, fill=0.0, base=0, channel_multiplier=-1),
         reads=[dconst], writes=[dconst])
    P.op("pool", lambda e: e.memset(ones_f[:], -1.0), reads=[dconst], writes=[dconst])
    P.op("pool", lambda e: e.affine_select(out=negtri[:], in_=ones_f[:], pattern=[[-1, 128]],
                                           compare_op=ALU.is_ge, fill=0.0, base=0, channel_multiplier=1),
         reads=[dconst], writes=[dconst])
    P.op("pool", lambda e: e.tensor_copy(out=negones[:], in_=ones_f[:]), reads=[dconst], writes=[dconst])

    steps = []
    for p in range(NPAIR):
        for g in range(8):
            for sb in range(4 * g + 3, -1, -1):
                steps.append((p, g, sb))
    n = len(steps)

    def info(i):
        p, g, sb = steps[i]
        tl = max(0, sb - 4 * g) * 128
        return p, g, sb, tl, (sb >= 4 * g)

    def stage1(i):
        p, g, sb, tl, diag = info(i)
        a, s3 = i % 2, i % 3
        if merged and p > 0 and g == 0 and sb == 3:
            load_pair(p)
        P.op("pe", lambda e: e.matmul(psA[a][:, tl:512], k_sb[:, p % NB, sb * 128:(sb + 1) * 128],
                                      q_sb[:, p % NB, g * 512 + tl:(g + 1) * 512], start=True, stop=True),
             reads=[dq, dk], writes=[dpsA[a]])
        P.op("act", lambda e: e.activation(out=ebuf[a][:, tl:512], in_=psA[a][:, tl:512], func=AF.Exp),
             reads=[dpsA[a]], writes=[debuf[a]])
        P.op("act", lambda e: e.activation(out=spb[s3][:, tl:512], in_=ebuf[a][:, tl:512], func=AF.Ln, bias=1.0),
             reads=[debuf[a]], writes=[dspb[s3]])
        if diag:
            P.op("pool", lambda e: e.tensor_tensor(out=spb[s3][:, tl:tl + 128], in0=spb[s3][:, tl:tl + 128],
                                                   in1=mstrict[:], op=ALU.mult),
                 reads=[dspb[s3], dconst], writes=[dspb[s3]])

    def stage2(i):
        p, g, sb, tl, diag = info(i)
        a, s3 = i % 2, i % 3
        if sb == 4 * g + 3:
            P.op("pool", lambda e: e.memset(spsum[:], 0.0), writes=[dspsum])
        P.op("pe", lambda e: e.matmul(psB[a][:, tl:512], k_sb[:, p % NB, sb * 128:(sb + 1) * 128],
                                      q_sb[:, p % NB, g * 512 + tl:(g + 1) * 512], start=True, stop=False),
             reads=[dq, dk], writes=[dpsB[a]])
        P.op("pe", lambda e: e.matmul(psB[a][:, tl:512], negtri[:], spb[s3][:, tl:512], start=False, stop=False),
             reads=[dspb[s3], dconst], writes=[dpsB[a]])
        P.op("pe", lambda e: e.matmul(psB[a][:, tl:512], negones[:], spsum[:, tl:512], start=False, stop=True),
             reads=[dspsum, dconst], writes=[dpsB[a]])
        P.op("pool", lambda e: e.tensor_tensor(out=spsum[:, tl:512], in0=spsum[:, tl:512], in1=spb[s3][:, tl:512],
                                               op=ALU.add),
             reads=[dspsum, dspb[s3]], writes=[dspsum])
        P.op("act", lambda e: e.activation(out=wbuf[s3][:, tl:512], in_=psB[a][:, tl:512], func=AF.Exp),
             reads=[dpsB[a]], writes=[dwbuf[s3]])
        if diag:
            P.op("pool", lambda e: e.tensor_tensor(out=wbuf[s3][:, tl:tl + 128], in0=wbuf[s3][:, tl:tl + 128],
                                                   in1=mstrict[:], op=ALU.mult),
                 reads=[dwbuf[s3], dconst], writes=[dwbuf[s3]])

    def stage3(i):
        p, g, sb, tl, diag = info(i)
        s3 = i % 3
        o = (p * 8 + g) % 2
        for tb in range(tl // 128, 4):
            P.op("pe", lambda e, tb=tb: e.matmul(psO[o][:, tb * 128:(tb + 1) * 128], v_sb[:, p % NB, sb, :],
                                                 wbuf[s3][:, tb * 128:(tb + 1) * 128],
                                                 start=(sb == 4 * g + 3 and tb == 3), stop=(sb == 0),
                                                 skip_group_check=True),
                 reads=[dv, dwbuf[s3]], writes=[dpsO[o]])
        if sb == 0:
            P.op("dve", lambda e: e.tensor_copy(out=o_st[o][:], in_=psO[o][:]), reads=[dpsO[o]], writes=[do_[o]])
            P.dma("sp", yc[p][:, g * 512:(g + 1) * 512], o_st[o][:], reads=[do_[o]])

    for it in range(n + 2):
        if it < n:
            stage1(it)
        if 0 <= it - 1 < n:
            stage2(it - 1)
        if 0 <= it - 2 < n:
            stage3(it - 2)


def build_B1():
    nc = bass.Bass("TRN2", target_bir_lowering=False)
    with ExitStack() as st:
        st.enter_context(nc.allow_low_precision("bf16 matmul operands, fp32 accumulation"))
        C = Ctx(nc, st)
        P = Prog(nc, st)
        emit_B1(nc, C, P)
        P.finish()
    return nc


I32 = mybir.dt.int32
PI = 3.14159265358979
TWO_PI = 2.0 * PI
NT = 2 * SEQ


def emit_B2(nc, C, P, merged=False, debug=False):
    uT = C.din("uT", [128, NT], BF16)
    lam_re = C.din("lam_re", [128, 4])
    lam_im = C.din("lam_im", [128, 4])
    log_dt = C.din("log_dt", [128, 4])
    b_re = C.din("b_re", [128, 4, 16])
    b_im = C.din("b_im", [128, 4, 16])
    c_reT = C.din("c_reT", [128, 4, 16])
    c_imT = C.din("c_imT", [128, 4, 16])
    dvec = C.din("dvec", [128, 1])
    yp = C.dout("yp", [128, NT], BF16)
    f4 = lambda n: C.sb(n, [128, 4], F32)
    u_sb = C.sb("u_sb", [128, NT], BF16)
    y_sb = C.sb("y_sb", [128, NT], BF16)
    lre, lim, ldt, dtt, rho, th, mag, cth, sth, abre, abim, nre, den, kre, kim, q0, q1, q2 = [
        f4("s4_%d" % i) for i in range(18)]
    m127, a7re, a7im = f4("m127"), f4("a7re"), f4("a7im")
    Adre = C.sb("Adre", [128, 4, 5], F32)
    Adim = C.sb("Adim", [128, 4, 5], F32)
    Adimn = C.sb("Adimn", [128, 4, 5], F32)
    abimn = f4("abimn")
    s127n = f4("s127n")
    bre = C.sb("bre", [128, 4, 16], F32)
    bim = C.sb("bim", [128, 4, 16], F32)
    cre = C.sb("cre", [128, 4, 16], F32)
    cim = C.sb("cim", [128, 4, 16], F32)
    bbre = C.sb("bbre", [128, 4, 16], F32)
    bbim = C.sb("bbim", [128, 4, 16], F32)
    bt0 = C.sb("bt0", [128, 16], F32)
    d_sb = C.sb("d_sb", [128, 1], F32)
    ident = C.sb("ident", [128, 128], F32)
    pad = C.sb("pad", [128, 128], F32)
    BBT = [C.sb("BBT%d" % i, [128, 4, 128], BF16) for i in range(2)]
    Cp = [C.sb("Cp%d" % i, [128, 4, 128], BF16) for i in range(2)]
    jidx = C.sb("jidx", [128, 512], F32)
    ang = C.sb("ang", [128, 512], F32)
    kf = C.sb("kf", [128, 512], F32)
    ki = C.sb("ki", [128, 512], I32)
    cs = [C.sb("cs%d" % i, [128, 4, 512], F32) for i in range(2)]
    m0 = C.sb("m0", [128, SEQ], BF16)
    rfull = C.sb("rfull", [128, SEQ], F32)
    w = [C.sb("w%d" % i, [128, SEQ], F32) for i in range(2)]
    z = [C.sb("z%d" % i, [128, SEQ], F32) for i in range(2)]
    bu = [C.sb("bu%d" % i, [128, 512], F32) for i in range(2)]
    tt = [C.sb("tt%d" % i, [128, 512], F32) for i in range(4)]
    sbf = [C.sb("sbf%d" % i, [128, 512], BF16) for i in range(2)]
    cst = [C.sb("cst%d" % i, [128, 32], F32) for i in range(10)]
    psb = [C.ps("psb%d" % i, [128, 512]) for i in range(2)]
    if merged:
        psy0 = C.ps("psy0", [128, 512])
        psy = [psy0, psy0]
        pst = psy0
    else:
        psy = [C.ps("psy%d" % i, [128, 512]) for i in range(2)]
        pst = C.ps("pst", [128, 128])

    du, dy, dpre, dtab, dm0, drf = [Dep() for _ in range(6)]
    dw = [Dep(), Dep()]
    dz = [Dep(), Dep()]
    dbu = [Dep(), Dep()]
    dtq = [Dep() for _ in range(4)]
    dsbf = [Dep(), Dep()]
    dcst = Dep()
    dpsb = [Dep(), Dep()]
    dpsy = [Dep(), Dep()]
    dpst = Dep()
    if merged:
        dpsy = [dpsy[0], dpsy[0]]
        dpst = dpsy[0]

    def V(fn, r, wr, eng="dve"):
        P.op(eng, fn, reads=r, writes=wr)

    P.dma("sp", u_sb[:, 0:SEQ], uT[:, 0:SEQ], writes=[du])
    P.dma("sp", u_sb[:, SEQ:NT], uT[:, SEQ:NT], writes=[du])
    for t_, src in ((lre, lam_re), (lim, lam_im), (ldt, log_dt), (bre, b_re), (bim, b_im), (cre, c_reT),
                    (cim, c_imT), (d_sb, dvec)):
        P.dma("sp", t_[:], src, writes=[dpre])
    pre = [dpre]

    V(lambda e: e.memset(pad[:], 1.0), [], pre, "pool")
    V(lambda e: e.affine_select(out=ident[:], in_=pad[:], pattern=[[-1, 128]], compare_op=ALU.is_equal,
                                fill=0.0, base=0, channel_multiplier=1), pre, pre, "pool")
    V(lambda e: e.iota(jidx[:], pattern=[[0, 4], [1, 128]], base=0, channel_multiplier=0,
                       allow_small_or_imprecise_dtypes=True), [], [dtab], "pool")
    V(lambda e: e.memset(m0[:], 1.0), [], [dm0], "pool")
    V(lambda e: e.memset(m0[:].rearrange("p (c j) -> p c j", j=128)[:, :, 0:1], 0.0), [dm0], [dm0], "pool")

    def reduce_angle(x, n):
        V(lambda e: e.tensor_scalar(out=kf[:, :n], in0=x, scalar1=1.0 / TWO_PI, scalar2=None, op0=ALU.mult),
          [dtab], [dtab])
        V(lambda e: e.tensor_copy(out=ki[:, :n], in_=kf[:, :n]), [dtab], [dtab])
        V(lambda e: e.tensor_copy(out=kf[:, :n], in_=ki[:, :n]), [dtab], [dtab])
        V(lambda e: e.scalar_tensor_tensor(out=x, in0=kf[:, :n], scalar=-TWO_PI, in1=x, op0=ALU.mult,
                                           op1=ALU.add), [dtab], [dtab])
        V(lambda e: e.tensor_scalar(out=kf[:, :n], in0=x, scalar1=PI, scalar2=-TWO_PI, op0=ALU.is_gt,
                                    op1=ALU.mult), [dtab], [dtab])
        V(lambda e: e.tensor_tensor(out=x, in0=x, in1=kf[:, :n], op=ALU.add), [dtab], [dtab])
        V(lambda e: e.tensor_scalar(out=kf[:, :n], in0=x, scalar1=-PI, scalar2=TWO_PI, op0=ALU.is_lt,
                                    op1=ALU.mult), [dtab], [dtab])
        V(lambda e: e.tensor_tensor(out=x, in0=x, in1=kf[:, :n], op=ALU.add), [dtab], [dtab])

    PT = [dpre, dtab]
    P.op("act", lambda e: e.activation(out=dtt[:], in_=ldt[:], func=AF.Exp), reads=PT, writes=PT)
    V(lambda e: e.tensor_tensor(out=rho[:], in0=lre[:], in1=dtt[:], op=ALU.mult), PT, PT)
    V(lambda e: e.tensor_tensor(out=th[:], in0=lim[:], in1=dtt[:], op=ALU.mult), PT, PT)
    P.op("act", lambda e: e.activation(out=mag[:], in_=rho[:], func=AF.Exp), reads=PT, writes=PT)
    P.op("act", lambda e: e.activation(out=m127[:], in_=rho[:], func=AF.Exp, scale=127.0), reads=PT, writes=PT)
    for gp in range(4):
        for k_, shift in ((1, 0.0), (0, PI / 2)):
            V(lambda e, gp=gp, shift=shift: e.tensor_scalar(out=ang[:], in0=jidx[:], scalar1=th[:, gp:gp + 1],
                                                            scalar2=shift, op0=ALU.mult, op1=ALU.add), PT, PT)
            reduce_angle(ang[:], 512)
            P.op("act", lambda e, gp=gp, k_=k_: e.activation(out=cs[k_][:, gp, :], in_=ang[:], func=AF.Sin),
                 reads=PT, writes=PT)
    V(lambda e: e.tensor_copy(out=cth[:], in_=cs[0][:, :, 1]), PT, PT)
    V(lambda e: e.tensor_copy(out=sth[:], in_=cs[1][:, :, 1]), PT, PT)
    V(lambda e: e.tensor_tensor(out=abre[:], in0=mag[:], in1=cth[:], op=ALU.mult), PT, PT)
    V(lambda e: e.tensor_tensor(out=abim[:], in0=mag[:], in1=sth[:], op=ALU.mult), PT, PT)
    V(lambda e: e.tensor_scalar(out=abimn[:], in0=abim[:], scalar1=-1.0, scalar2=None, op0=ALU.mult), PT, PT)
    V(lambda e: e.tensor_scalar(out=s127n[:], in0=cs[1][:, :, 127], scalar1=-1.0, scalar2=None, op0=ALU.mult), PT, PT)
    V(lambda e: e.tensor_scalar(out=nre[:], in0=abre[:], scalar1=-1.0, scalar2=None, op0=ALU.add), PT, PT)
    V(lambda e: e.tensor_tensor(out=q0[:], in0=lre[:], in1=lre[:], op=ALU.mult), PT, PT)
    V(lambda e: e.tensor_tensor(out=q1[:], in0=lim[:], in1=lim[:], op=ALU.mult), PT, PT)
    V(lambda e: e.tensor_tensor(out=den[:], in0=q0[:], in1=q1[:], op=ALU.add), PT, PT)
    V(lambda e: e.reciprocal(out=den[:], in_=den[:]), PT, PT)
    V(lambda e: e.tensor_tensor(out=q0[:], in0=nre[:], in1=lre[:], op=ALU.mult), PT, PT)
    V(lambda e: e.tensor_tensor(out=q1[:], in0=abim[:], in1=lim[:], op=ALU.mult), PT, PT)
    V(lambda e: e.tensor_tensor(out=q0[:], in0=q0[:], in1=q1[:], op=ALU.add), PT, PT)
    V(lambda e: e.tensor_tensor(out=kre[:], in0=q0[:], in1=den[:], op=ALU.mult), PT, PT)
    V(lambda e: e.tensor_tensor(out=q0[:], in0=abim[:], in1=lre[:], op=ALU.mult), PT, PT)
    V(lambda e: e.tensor_tensor(out=q1[:], in0=nre[:], in1=lim[:], op=ALU.mult), PT, PT)
    V(lambda e: e.tensor_tensor(out=q0[:], in0=q0[:], in1=q1[:], op=ALU.subtract), PT, PT)
    V(lambda e: e.tensor_tensor(out=kim[:], in0=q0[:], in1=den[:], op=ALU.mult), PT, PT)
    for gp in range(4):
        g1 = slice(gp, gp + 1)
        V(lambda e, gp=gp, g1=g1: e.tensor_scalar(out=bt0[:], in0=bim[:, gp, :], scalar1=kim[:, g1], scalar2=None,
                                                  op0=ALU.mult), PT, PT)
        V(lambda e, gp=gp, g1=g1: e.scalar_tensor_tensor(out=bbre[:, gp, :], in0=bre[:, gp, :], scalar=kre[:, g1],
                                                         in1=bt0[:], op0=ALU.mult, op1=ALU.subtract), PT, PT)
        V(lambda e, gp=gp, g1=g1: e.tensor_scalar(out=bt0[:], in0=bre[:, gp, :], scalar1=kim[:, g1], scalar2=None,
                                                  op0=ALU.mult), PT, PT)
        V(lambda e, gp=gp, g1=g1: e.scalar_tensor_tensor(out=bbim[:, gp, :], in0=bim[:, gp, :], scalar=kre[:, g1],
                                                         in1=bt0[:], op0=ALU.mult, op1=ALU.add), PT, PT)
    V(lambda e: e.tensor_tensor(out=a7re[:], in0=m127[:], in1=cs[0][:, :, 127], op=ALU.mult), PT, PT)
    V(lambda e: e.tensor_tensor(out=a7im[:], in0=m127[:], in1=cs[1][:, :, 127], op=ALU.mult), PT, PT)

    def cmul(ore, oim, are, aim, bre_, bim_):
        V(lambda e: e.tensor_tensor(out=q0[:], in0=are, in1=bre_, op=ALU.mult), PT, PT)
        V(lambda e: e.tensor_tensor(out=q1[:], in0=aim, in1=bim_, op=ALU.mult), PT, PT)
        V(lambda e: e.tensor_tensor(out=q2[:], in0=are, in1=bim_, op=ALU.mult), PT, PT)
        V(lambda e: e.tensor_tensor(out=ore, in0=q0[:], in1=q1[:], op=ALU.subtract), PT, PT)
        V(lambda e: e.tensor_tensor(out=q0[:], in0=aim, in1=bre_, op=ALU.mult), PT, PT)
        V(lambda e: e.tensor_tensor(out=oim, in0=q2[:], in1=q0[:], op=ALU.add), PT, PT)
    cmul(Adre[:, :, 0], Adim[:, :, 0], a7re[:], a7im[:], abre[:], abim[:])
    for k in range(1, 5):
        cmul(Adre[:, :, k], Adim[:, :, k], Adre[:, :, k - 1], Adim[:, :, k - 1], Adre[:, :, k - 1], Adim[:, :, k - 1])
    V(lambda e: e.tensor_scalar(out=Adimn[:], in0=Adim[:], scalar1=-1.0, scalar2=None, op0=ALU.mult), PT, PT)
    for gp in range(4):
        for ri, src in ((0, bbre), (1, bbim)):
            V(lambda e: e.memset(pad[:], 0.0), PT, PT)
            for j in range(2):
                c0 = 16 * (2 * gp + j)
                V(lambda e, j=j, c0=c0, src=src, gp=gp: e.tensor_copy(out=pad[64 * j:64 * j + 64, c0:c0 + 16],
                                                                    in_=src[64 * j:64 * j + 64, gp, :]), PT, PT)
            P.op("pe", lambda e: e.transpose(pst[:, 0:128], pad[:], ident[:]), reads=PT, writes=[dpst])
            V(lambda e, ri=ri, gp=gp: e.tensor_copy(out=BBT[ri][:, gp, :], in_=pst[:, 0:128]), [dpst] + PT, PT)
        for ri, src, sgn in ((0, cre, 1.0), (1, cim, -1.0)):
            V(lambda e, ri=ri, gp=gp: e.memset(Cp[ri][:, gp, :], 0.0), PT, PT)
            for j in range(2):
                c0 = 16 * (2 * gp + j)
                V(lambda e, j=j, c0=c0, src=src, gp=gp, ri=ri, sgn=sgn: e.tensor_scalar(
                    out=Cp[ri][64 * j:64 * j + 64, gp, c0:c0 + 16], in0=src[64 * j:64 * j + 64, gp, :],
                    scalar1=sgn, scalar2=None, op0=ALU.mult), PT, PT)

    if debug:
        for nm, t_, shp in (("th", th, [128, 4]), ("mag", mag, [128, 4]), ("kre", kre, [128, 4]), ("kim", kim, [128, 4]),
                            ("cos", cs[0], [128, 4, 512]), ("sin", cs[1], [128, 4, 512]), ("Adre", Adre, [128, 4, 5]),
                            ("Adim", Adim, [128, 4, 5]), ("bbre", bbre, [128, 4, 16]), ("jidx", jidx, [128, 512])):
            P.dma("sp", C.dout("dbg_" + nm, shp), t_[:], reads=PT)
        P.dma("sp", C.dout("dbg_BBT0", [128, 4, 128], BF16), BBT[0][:], reads=PT)
        P.dma("sp", C.dout("dbg_Cp1", [128, 4, 128], BF16), Cp[1][:], reads=PT)
    def mod_tile(gp, tok0, t8):
        sl = slice(t8 * 512, (t8 + 1) * 512)
        for ri in range(2):
            P.op("pe", lambda e, ri=ri: e.matmul(psb[ri][:], BBT[ri][:, gp, :],
                                                 u_sb[:, tok0 + sl.start:tok0 + sl.stop], start=True, stop=True),
                 reads=[du] + PT, writes=[dpsb[ri]])
            P.op("act", lambda e, ri=ri: e.activation(out=bu[ri][:], in_=psb[ri][:], func=AF.Copy),
                 reads=[dpsb[ri]], writes=[dbu[ri]])
        V(lambda e: e.tensor_tensor(out=tt[0][:], in0=bu[0][:], in1=cs[0][:, gp, :], op=ALU.mult),
          [dbu[0]] + PT, [dtq[0]])
        V(lambda e: e.tensor_tensor(out=tt[1][:], in0=bu[1][:], in1=cs[1][:, gp, :], op=ALU.mult),
          [dbu[1]] + PT, [dtq[1]])
        V(lambda e: e.tensor_tensor(out=w[0][:, sl], in0=tt[0][:], in1=tt[1][:], op=ALU.add),
          [dtq[0], dtq[1]], [dw[0]])
        V(lambda e: e.tensor_tensor(out=tt[2][:], in0=bu[1][:], in1=cs[0][:, gp, :], op=ALU.mult),
          [dbu[1]] + PT, [dtq[2]], "pool")
        V(lambda e: e.tensor_tensor(out=tt[3][:], in0=bu[0][:], in1=cs[1][:, gp, :], op=ALU.mult),
          [dbu[0]] + PT, [dtq[3]], "pool")
        V(lambda e: e.tensor_tensor(out=w[1][:, sl], in0=tt[2][:], in1=tt[3][:], op=ALU.subtract),
          [dtq[2], dtq[3]], [dw[1]], "pool")

    def scans(extra):
        for i in range(2):
            V(lambda e, i=i: e.tensor_tensor_scan(out=z[i][:], data0=rfull[:], data1=w[i][:], initial=0.0,
                                                  op0=ALU.mult, op1=ALU.add), [drf, dw[i]] + extra, [dz[i]])

    def hs_step(gp, k, cur):
        d_ = 1 << k
        nxt = 2 - cur
        ar = Adre[:, gp, k:k + 1]
        ai = Adim[:, gp, k:k + 1]
        ain = Adimn[:, gp, k:k + 1]
        sre, sim = cst[cur], cst[cur + 1]
        nre_, nim_ = cst[nxt], cst[nxt + 1]
        CS = [dcst]
        V(lambda e: e.tensor_copy(out=nre_[:, 0:d_], in_=sre[:, 0:d_]), CS, CS)
        V(lambda e: e.tensor_copy(out=nim_[:, 0:d_], in_=sim[:, 0:d_]), CS, CS)
        V(lambda e: e.scalar_tensor_tensor(out=cst[4][:, d_:32], in0=sre[:, 0:32 - d_], scalar=ar, in1=sre[:, d_:32],
                                           op0=ALU.mult, op1=ALU.add), CS + PT, CS)
        V(lambda e: e.scalar_tensor_tensor(out=nre_[:, d_:32], in0=sim[:, 0:32 - d_], scalar=ain,
                                           in1=cst[4][:, d_:32], op0=ALU.mult, op1=ALU.add), CS + PT, CS)
        V(lambda e: e.scalar_tensor_tensor(out=cst[4][:, d_:32], in0=sim[:, 0:32 - d_], scalar=ar, in1=sim[:, d_:32],
                                           op0=ALU.mult, op1=ALU.add), CS + PT, CS)
        V(lambda e: e.scalar_tensor_tensor(out=nim_[:, d_:32], in0=sre[:, 0:32 - d_], scalar=ai,
                                           in1=cst[4][:, d_:32], op0=ALU.mult, op1=ALU.add), CS + PT, CS)
        return nxt

    def carry(gp):
        g1 = slice(gp, gp + 1)
        CS = [dcst]
        zE = [z[i][:].rearrange("p (c j) -> p c j", j=128)[:, :, 127] for i in range(2)]
        c127 = cs[0][:, gp, 127:128]
        s127 = cs[1][:, gp, 127:128]
        V(lambda e: e.tensor_scalar(out=cst[2][:], in0=zE[0], scalar1=c127, scalar2=None, op0=ALU.mult),
          [dz[0]] + PT, CS)
        V(lambda e: e.scalar_tensor_tensor(out=cst[0][:], in0=zE[1], scalar=s127n[:, g1], in1=cst[2][:],
                                           op0=ALU.mult, op1=ALU.add), [dz[1]] + PT + CS, CS)
        V(lambda e: e.tensor_scalar(out=cst[2][:], in0=zE[0], scalar1=s127, scalar2=None, op0=ALU.mult),
          [dz[0]] + PT + CS, CS)
        V(lambda e: e.scalar_tensor_tensor(out=cst[1][:], in0=zE[1], scalar=c127, in1=cst[2][:],
                                           op0=ALU.mult, op1=ALU.add), [dz[1]] + PT + CS, CS)
        cur = 0
        for k in range(5):
            cur = hs_step(gp, k, cur)
        Sre, Sim = cst[cur], cst[cur + 1]
        w0 = [w[i][:].rearrange("p (c j) -> p c j", j=128)[:, 1:32, 0] for i in range(2)]
        V(lambda e: e.scalar_tensor_tensor(out=cst[5][:, 0:31], in0=Sre[:, 0:31], scalar=abre[:, g1],
                                           in1=w0[0], op0=ALU.mult, op1=ALU.add), CS + PT + [dw[0]], CS)
        V(lambda e: e.scalar_tensor_tensor(out=w0[0], in0=Sim[:, 0:31], scalar=abimn[:, g1],
                                           in1=cst[5][:, 0:31], op0=ALU.mult, op1=ALU.add), CS + PT, [dw[0]])
        V(lambda e: e.scalar_tensor_tensor(out=cst[6][:, 0:31], in0=Sim[:, 0:31], scalar=abre[:, g1],
                                           in1=w0[1], op0=ALU.mult, op1=ALU.add), CS + PT + [dw[1]], CS)
        V(lambda e: e.scalar_tensor_tensor(out=w0[1], in0=Sre[:, 0:31], scalar=abim[:, g1],
                                           in1=cst[6][:, 0:31], op0=ALU.mult, op1=ALU.add), CS + PT, [dw[1]])

    def out_tile(gp, tok0, t8):
        sl = slice(t8 * 512, (t8 + 1) * 512)
        V(lambda e: e.tensor_tensor(out=tt[0][:], in0=z[0][:, sl], in1=cs[0][:, gp, :], op=ALU.mult),
          [dz[0]] + PT, [dtq[0]])
        V(lambda e: e.tensor_tensor(out=tt[1][:], in0=z[1][:, sl], in1=cs[1][:, gp, :], op=ALU.mult),
          [dz[1]] + PT, [dtq[1]])
        V(lambda e: e.tensor_tensor(out=sbf[0][:], in0=tt[0][:], in1=tt[1][:], op=ALU.subtract),
          [dtq[0], dtq[1]], [dsbf[0]])
        V(lambda e: e.tensor_tensor(out=tt[2][:], in0=z[0][:, sl], in1=cs[1][:, gp, :], op=ALU.mult),
          [dz[0]] + PT, [dtq[2]], "pool")
        V(lambda e: e.tensor_tensor(out=tt[3][:], in0=z[1][:, sl], in1=cs[0][:, gp, :], op=ALU.mult),
          [dz[1]] + PT, [dtq[3]], "pool")
        V(lambda e: e.tensor_tensor(out=sbf[1][:], in0=tt[2][:], in1=tt[3][:], op=ALU.add),
          [dtq[2], dtq[3]], [dsbf[1]], "pool")
        yi = t8 % 2
        P.op("pe", lambda e: e.matmul(psy[yi][:], Cp[0][:, gp, :], sbf[0][:], start=True, stop=False),
             reads=[dsbf[0]] + PT, writes=[dpsy[yi]])
        P.op("pe", lambda e: e.matmul(psy[yi][:], Cp[1][:, gp, :], sbf[1][:], start=False, stop=True),
             reads=[dsbf[1]] + PT, writes=[dpsy[yi]])
        r0 = 32 * gp
        V(lambda e: e.scalar_tensor_tensor(
            out=y_sb[r0:r0 + 32, tok0 + sl.start:tok0 + sl.stop],
            in0=u_sb[r0:r0 + 32, tok0 + sl.start:tok0 + sl.stop], scalar=d_sb[r0:r0 + 32, 0:1],
            in1=psy[yi][r0:r0 + 32, :], op0=ALU.mult, op1=ALU.add),
          [du, dpsy[yi]] + PT, [dy])

    def set_rfull(gp):
        P.op("act", lambda e: e.activation(out=rfull[:], in_=m0[:], func=AF.Copy, scale=mag[:, gp:gp + 1]),
             reads=[dm0] + PT, writes=[drf])

    for gp in range(4):
        set_rfull(gp)
        for b in range(2):
            for t8 in range(8):
                mod_tile(gp, b * SEQ, t8)
            scans([])
            carry(gp)
            scans([dcst])
            for t8 in range(8):
                out_tile(gp, b * SEQ, t8)
    if debug:
        for nm, t_, shp, dd in (("w0", w[0], [128, SEQ], dw[0]), ("w1", w[1], [128, SEQ], dw[1]),
                                ("z0", z[0], [128, SEQ], dz[0]), ("z1", z[1], [128, SEQ], dz[1]),
                                ("rfull", rfull, [128, SEQ], drf), ("bu0", bu[0], [128, 512], dbu[0]),
                                ("tt0", tt[0], [128, 512], dtq[0])):
            P.dma("sp", C.dout("dbg_" + nm, shp), t_[:], reads=[dd])
        for k in range(7):
            P.dma("sp", C.dout("dbg_cst%d" % k, [128, 32]), cst[k][:], reads=[dcst])
    P.dma("sp", yp[:, 0:SEQ], y_sb[:, 0:SEQ], reads=[dy])
    P.dma("sp", yp[:, SEQ:NT], y_sb[:, SEQ:NT], reads=[dy])


def build_B2(debug=False):
    nc = bass.Bass("TRN2", target_bir_lowering=False)
    with ExitStack() as st:
        st.enter_context(nc.allow_low_precision("bf16 matmul operands, fp32 accumulation"))
        C = Ctx(nc, st)
        P = Prog(nc, st)
        emit_B2(nc, C, P, debug=debug)
        P.finish()
    return nc


def host_B2_params(l, inp, core):
    gs = slice(8 * core, 8 * core + 8)

    def pg(a):
        return np.ascontiguousarray(a.reshape(4, 2, 64).transpose(1, 2, 0).reshape(128, 4))
    lam_re = pg(inp["s5_lambda_re"][l][gs])
    lam_im = pg(inp["s5_lambda_im"][l][gs])
    log_dt = pg(np.repeat(inp["s5_log_dt"][l][gs][:, None], 64, axis=1))

    def pb(a):
        return np.ascontiguousarray(a.reshape(4, 2, 64, 16).transpose(1, 2, 0, 3).reshape(128, 4, 16))
    b_re = pb(inp["s5_b_re"][l][gs])
    b_im = pb(inp["s5_b_im"][l][gs])
    c_reT = pb(np.transpose(inp["s5_c_re"][l][gs], (0, 2, 1)))
    c_imT = pb(np.transpose(inp["s5_c_im"][l][gs], (0, 2, 1)))
    dvec = np.ascontiguousarray(inp["s5_d"][l][128 * core:128 * core + 128].reshape(128, 1))
    return {"lam_re": lam_re, "lam_im": lam_im, "log_dt": log_dt, "b_re": b_re, "b_im": b_im,
            "c_reT": c_reT, "c_imT": c_imT, "dvec": dvec}


_NC_CACHE = {}


def _get(name, fn):
    if name not in _NC_CACHE:
        _NC_CACHE[name] = fn()
    return _NC_CACHE[name]


def _run(nc, maps):
    res = run_bass_kernel_spmd(nc, maps, core_ids=list(range(NCORES)))
    return res.results


def kernel(**inp):
    inp = {k: np.asarray(v) for k, v in inp.items()}
    x_tok = np.ascontiguousarray(inp["x"].reshape(2 * SEQ, D).astype(np.float32))
    depth = inp["w_in"].shape[0]
    for l in range(depth):
        last = (l == depth - 1)
        rA = _run(_get("A", build_A), host_A(x_tok, l, inp))
        yaT = [np.asarray(r["yaT"]) for r in rA]
        gT = [np.asarray(r["gT"]) for r in rA]
        s5T = [np.asarray(r["s5T"]) for r in rA]
        qT = [np.asarray(r["qT"]) for r in rA]
        kT = [np.asarray(r["kT"]) for r in rA]
        vtk = [np.asarray(r["vtk"]) for r in rA]
        mB1, mB2 = [], []
        for c in range(NCORES):
            qs, ks, vs = [], [], []
            for pl in range(NPAIR):
                b, h = divmod(c * NPAIR + pl, 8)
                hs = slice(128 * h, 128 * h + 128)
                qs.append(np.concatenate([qT[4 * b + i][hs, :] for i in range(4)], axis=1))
                ks.append(np.concatenate([kT[4 * b + i][hs, :] for i in range(4)], axis=1))
                vs.append(np.concatenate([vtk[4 * b + i][:, hs] for i in range(4)], axis=0))
            mB1.append({"qT": np.ascontiguousarray(np.stack(qs)), "kT": np.ascontiguousarray(np.stack(ks)),
                        "v": np.ascontiguousarray(np.stack(vs))})
            m2 = host_B2_params(l, inp, c)
            m2["uT"] = np.ascontiguousarray(np.concatenate([s5T[i][128 * c:128 * c + 128, :] for i in range(8)], axis=1))
            mB2.append(m2)
        mB = [dict(mB1[c], **mB2[c]) for c in range(NCORES)]
        rB = _run(_get("B", build_B), mB)
        yc = [np.asarray(r["yc"]) for r in rB]
        yp = [np.asarray(r["yp"]) for r in rB]
        ycT, ybpT = [], []
        for tc in range(NCORES):
            b, i = divmod(tc, 4)
            rows = []
            for h in range(8):
                c, pl = divmod(b * 8 + h, NPAIR)
                rows.append(yc[c][pl][:, i * TOK:(i + 1) * TOK])
            ycT.append(np.ascontiguousarray(np.concatenate(rows, axis=0)))
            ybpT.append(np.ascontiguousarray(np.concatenate([yp[c][:, tc * TOK:(tc + 1) * TOK] for c in range(8)], axis=0)))
        ncC = _get("C%d" % int(last), lambda: build_C(last))
        rC = _run(ncC, host_C(x_tok, l, inp, yaT, ybpT, ycT, gT, last))
        x_tok = np.ascontiguousarray(np.concatenate([np.asarray(r["xo"]).T for r in rC], axis=0).astype(np.float32))
    return x_tok.reshape(2, SEQ, D)


def build_B():
    nc = bass.Bass("TRN2", target_bir_lowering=False)
    with ExitStack() as st:
        st.enter_context(nc.allow_low_precision("bf16 matmul operands, fp32 accumulation"))
        C = Ctx(nc, st)
        P = Prog(nc, st)
        P.begin("b2")
        emit_B2(nc, C, P, merged=True)
        P.end()
        P.begin("b1")
        emit_B1(nc, C, P, merged=True)
        P.end()
        P.replay(["b2", "b1"])
        P.finish()
    return nc
```

```python
import numpy as np
from contextlib import ExitStack
import ml_dtypes
import concourse.bass as bass
import concourse.mybir as mybir
from concourse.bass_utils import run_bass_kernel_spmd

F32 = mybir.dt.float32
BF16 = mybir.dt.bfloat16
AF = mybir.ActivationFunctionType
ALU = mybir.AluOpType
AX = mybir.AxisListType
NPBF = ml_dtypes.bfloat16

NCORES = 8
TOK = 1024
D = 2048
EPS = 1e-6
QSCALE = 128 ** -0.5


SAME_SYNC = {"pe": False, "act": True, "dve": True, "pool": True, "sp": True}


class Dep:
    __slots__ = ("w", "r")

    def __init__(self):
        self.w = None
        self.r = {}


class Prog:
    ENGS = ("pe", "act", "dve", "pool", "sp")

    def __init__(self, nc, stack, n_dma_sems=(("sp", 12), ("pool", 8), ("act", 4))):
        self.nc = nc
        self.stack = stack
        self.q = {e: [] for e in self.ENGS}
        self.sem = {e: stack.enter_context(nc.semaphore("s_" + e)) for e in ("pe", "act", "dve", "pool")}
        self.cnt = {e: 0 for e in self.sem}
        tot = sum(n for _, n in n_dma_sems)
        self.dsem = [stack.enter_context(nc.semaphore("d%d" % i)) for i in range(tot)]
        self.dcnt = [0] * tot
        self.dpool = {}
        b = 0
        for qn, n in n_dma_sems:
            self.dpool[qn] = [list(range(b, b + n)), 0]
            b += n
        self.seen = {e: {} for e in self.ENGS}
        self.same_sync = dict(SAME_SYNC)
        self._rec = None
        self._streams = {}

    def _semobj(self, key):
        return self.sem[key[1]] if key[0] == "e" else self.dsem[key[1]]

    def _collect(self, eng, reads, writes, extra=None):
        need = dict(extra or {})

        def req(k, v):
            if v > need.get(k, 0):
                need[k] = v
        for t in reads:
            if t.w:
                req(*t.w)
        for t in writes:
            if t.w:
                req(*t.w)
            for k, v in t.r.items():
                req(k, v)
        for k, v in need.items():
            if k == ("e", eng) and not self.same_sync[eng]:
                continue
            if self.seen[eng].get(k, 0) < v:
                self.seen[eng][k] = v
                s = self._semobj(k)
                self.q[eng].append(lambda e, s=s, v=v: e.wait_ge(s, v))

    def begin(self, name):
        self._rec = []
        self._streams[name] = self._rec

    def end(self):
        self._rec = None

    def replay(self, names):
        recs = [self._streams[n] for n in names]
        pos = [0] * len(recs)
        total = sum(len(r) for r in recs)
        for _ in range(total):
            best, bestv = None, None
            for i, r in enumerate(recs):
                if pos[i] < len(r):
                    v = (pos[i] + 0.5) / len(r)
                    if bestv is None or v < bestv:
                        best, bestv = i, v
            kind, args, kw = recs[best][pos[best]]
            pos[best] += 1
            getattr(self, kind)(*args, **kw)

    def op(self, eng, fn, reads=(), writes=()):
        if self._rec is not None:
            self._rec.append(("op", (eng, fn, tuple(reads), tuple(writes)), {}))
            return
        self._collect(eng, reads, writes)
        self.cnt[eng] += 1
        c = self.cnt[eng]
        key = ("e", eng)
        s = self.sem[eng]
        self.q[eng].append(lambda e, fn=fn, s=s: fn(e).then_inc(s, 1))
        for t in reads:
            if t.r.get(key, 0) < c:
                t.r[key] = c
        for t in writes:
            t.w = (key, c)
            t.r = {}

    def dma(self, queue, out, in_, reads=(), writes=(), in_fn=None, out_fn=None, **kw):
        if self._rec is not None:
            self._rec.append(("dma", (queue, out, in_, tuple(reads), tuple(writes), in_fn, out_fn), dict(kw)))
            return
        pl = self.dpool[queue]
        i = pl[0][pl[1] % len(pl[0])]
        pl[1] += 1
        extra = {}
        if self.dcnt[i] > 0:
            extra[("d", i)] = self.dcnt[i]
        self._collect(queue, reads, writes, extra)
        self.dcnt[i] += 16
        v = self.dcnt[i]
        key = ("d", i)
        s = self.dsem[i]
        self.q[queue].append(
            lambda e, s=s, out=out, in_=in_, kw=kw: e.dma_start(
                out=(out_fn() if out_fn else out), in_=(in_fn() if in_fn else in_), **kw).then_inc(s, 16))
        for t in reads:
            if t.r.get(key, 0) < v:
                t.r[key] = v
        for t in writes:
            t.w = (key, v)
            t.r = {}

    def raw(self, eng, fn, reads=(), writes=()):
        self._collect(eng, reads, writes)
        self.q[eng].append(lambda e, fn=fn: fn(e))

    def coll(self, queue, fn, reads=(), writes=()):
        pl = self.dpool[queue]
        i = pl[0][pl[1] % len(pl[0])]
        pl[1] += 1
        extra = {}
        if self.dcnt[i] > 0:
            extra[("d", i)] = self.dcnt[i]
        self._collect(queue, reads, writes, extra)
        self.dcnt[i] += 16
        v = self.dcnt[i]
        key = ("d", i)
        s = self.dsem[i]
        self.q[queue].append(lambda e, s=s, fn=fn: fn(e).then_inc(s, 16))
        for t in reads:
            if t.r.get(key, 0) < v:
                t.r[key] = v
        for t in writes:
            t.w = (key, v)
            t.r = {}

    def finish(self):
        for i, v in enumerate(self.dcnt):
            if v > 0 and self.seen["sp"].get(("d", i), 0) < v:
                s = self.dsem[i]
                self.q["sp"].append(lambda e, s=s, v=v: e.wait_ge(s, v))
        for e_, c in self.cnt.items():
            if c > 0:
                s = self.sem[e_]
                self.q["sp"].append(lambda e, s=s, c=c: e.wait_ge(s, c))
        q = self.q
        with self.nc.Block() as block:
            @block.tensor
            def _(e):
                for f in q["pe"]:
                    f(e)

            @block.scalar
            def _(e):
                for f in q["act"]:
                    f(e)

            @block.vector
            def _(e):
                for f in q["dve"]:
                    f(e)

            @block.gpsimd
            def _(e):
                for f in q["pool"]:
                    f(e)

            @block.sync
            def _(e):
                for f in q["sp"]:
                    f(e)


class Ctx:
    def __init__(self, nc, st):
        self.nc, self.st = nc, st

    def sb(self, name, shape, dt):
        return self.st.enter_context(self.nc.sbuf_tensor(name, shape, dt))

    def ps(self, name, shape, dt=F32):
        return self.st.enter_context(self.nc.psum_tensor(name, shape, dt))

    def din(self, name, shape, dt=F32):
        return self.nc.dram_tensor(name, list(shape), dt, kind="ExternalInput").ap()

    def dout(self, name, shape, dt=F32):
        return self.nc.dram_tensor(name, list(shape), dt, kind="ExternalOutput").ap()


def emit_rmsnorm(P, C, x_sb, dx, g_sb, dg, hT, dh, ones_bf, dones, sq, dsq, ps, dps, rstd, drstd, epsb, deps, ntok):
    for kt in range(16):
        i = kt % 2
        P.op("act", lambda e, kt=kt, i=i: e.activation(out=sq[i][:], in_=x_sb[:, kt, :], func=AF.Square),
             reads=[dx], writes=[dsq[i]])
        for hf in range(ntok // 512):
            P.op("pe", lambda e, kt=kt, i=i, hf=hf: e.matmul(ps[:, hf * 512:(hf + 1) * 512], ones_bf[:],
                                                             sq[i][:, hf * 512:(hf + 1) * 512],
                                                             start=(kt == 0), stop=(kt == 15)),
                 reads=[dsq[i], dones], writes=[dps])
    P.op("act", lambda e: e.activation(out=rstd[:], in_=ps[:, :ntok], func=AF.Sqrt, bias=epsb[:], scale=1.0 / D),
         reads=[dps, deps], writes=[drstd])
    P.op("dve", lambda e: e.reciprocal(out=rstd[:], in_=rstd[:]), reads=[drstd], writes=[drstd])
    for kt in range(16):
        P.op("dve", lambda e, kt=kt: e.scalar_tensor_tensor(out=hT[:, kt, :], in0=x_sb[:, kt, :],
                                                           scalar=g_sb[:, kt:kt + 1], in1=rstd[:],
                                                           op0=ALU.mult, op1=ALU.mult),
             reads=[dx, dg, drstd], writes=[dh])


def build_A():
    nc = bass.Bass("TRN2", target_bir_lowering=False)
    with ExitStack() as st:
        st.enter_context(nc.allow_low_precision("bf16 matmul operands, fp32 accumulation"))
        C = Ctx(nc, st)
        xT = C.din("xT", [D, TOK])
        n1g = C.din("n1g", [128, 16])
        w_in = C.din("w_in", [D, 12288])
        bg = C.din("bg", [128, 48])
        gng = C.din("gng", [1, 1024])
        wsT = C.din("wsT", [8, 128, 128])
        bs = C.din("bs", [1, 1024])
        yaT = C.dout("yaT", [1024, TOK], BF16)
        s5T = C.dout("s5T", [1024, TOK], BF16)
        qT = C.dout("qT", [1024, TOK], BF16)
        kT = C.dout("kT", [1024, TOK], BF16)
        vtk = C.dout("vtk", [TOK, 1024], BF16)
        gT = C.dout("gT", [6144, TOK], BF16)

        P = Prog(nc, st)
        x_sb = C.sb("x_sb", [128, 16, TOK], F32)
        hT = C.sb("hT", [128, 16, TOK], BF16)
        g_sb = C.sb("g_sb", [128, 16], F32)
        bg_sb = C.sb("bg_sb", [128, 48], F32)
        gng_b = C.sb("gng_b", [128, 1024], F32)
        bs_b = C.sb("bs_b", [128, 1024], F32)
        ws_f = C.sb("ws_f", [128, 8, 128], F32)
        ws_b = C.sb("ws_b", [128, 8, 128], BF16)
        ones_bf = C.sb("ones_bf", [128, 128], BF16)
        epsb = C.sb("epsb", [128, 1], F32)
        sq = [C.sb("sq%d" % i, [128, TOK], BF16) for i in range(2)]
        rstd = C.sb("rstd", [128, TOK], F32)
        wt = [C.sb("wt%d" % i, [128, 16, 512], BF16) for i in range(2)]
        stg = [C.sb("stg%d" % i, [128, TOK], BF16) for i in range(2)]
        uT = C.sb("uT", [128, 8, TOK], BF16)
        vtok = C.sb("vtok", [128, 8, 1024], BF16)
        vn = C.sb("vn", [128, 1024], BF16)
        junk = sq[0]
        vss = C.sb("vss", [128, 8], F32)
        vrs = C.sb("vrs", [128, 8], F32)
        mixt = rstd
        ya_sb = C.sb("ya_sb", [128, 8, TOK], BF16)
        ps = [C.ps("ps%d" % i, [128, 1024]) for i in range(2)]
        psm = C.ps("psm", [128, 1024])

        dx, dh, dg, dbg, dgng, dbs, dwsf, dwsb, dones, deps = [Dep() for _ in range(10)]
        dsq = [Dep(), Dep()]
        drstd = Dep()
        dwt = [Dep(), Dep()]
        dstg = [Dep(), Dep()]
        dps = [Dep(), Dep()]
        dpsm, duT, dvtok, dvn, dvss, dvrs, dya = [Dep() for _ in range(7)]
        djunk = dsq[0]
        dmixt = drstd

        xv = xT.rearrange("(kt p) t -> p kt t", p=128)
        for i in range(4):
            P.dma("sp", x_sb[:, 4 * i:4 * i + 4, :], xv[:, 4 * i:4 * i + 4, :], writes=[dx])
        P.dma("sp", g_sb[:], n1g, writes=[dg])
        P.dma("sp", bg_sb[:], bg, writes=[dbg])
        P.dma("sp", gng_b[:], gng.partition_broadcast(128), writes=[dgng])
        P.dma("sp", bs_b[:], bs.partition_broadcast(128), writes=[dbs])
        P.dma("sp", ws_f[:], wsT.rearrange("g s t -> s g t"), writes=[dwsf])
        wv = w_in.rearrange("(kt p) c -> p kt c", p=128)

        def load_w(cb):
            P.dma("pool", wt[cb % 2][:], wv[:, :, cb * 512:(cb + 1) * 512], writes=[dwt[cb % 2]])
        load_w(0)
        P.op("dve", lambda e: e.memset(ones_bf[:], 1.0), writes=[dones])
        P.op("dve", lambda e: e.memset(epsb[:], EPS), writes=[deps])
        P.op("pool", lambda e: e.affine_select(out=ws_f[:], in_=ws_f[:], pattern=[[0, 8], [1, 128]],
                                               compare_op=ALU.is_ge, fill=0.0, base=0, channel_multiplier=-1),
             reads=[dwsf], writes=[dwsf])
        P.op("dve", lambda e: e.tensor_copy(out=ws_b[:], in_=ws_f[:]), reads=[dwsf], writes=[dwsb])

        emit_rmsnorm(P, C, x_sb, dx, g_sb, dg, hT, dh, ones_bf, dones, sq, dsq, ps[0], dps[0], rstd, drstd,
                     epsb, deps, TOK)

        pcount = [0]
        scount = [0]

        def form2(cb, epi):
            w = wt[cb % 2]
            for m in range(4):
                pi = pcount[0] % 2
                pcount[0] += 1
                for hf in range(2):
                    for kt in range(16):
                        P.op("pe", lambda e, w=w, m=m, hf=hf, kt=kt, pi=pi: e.matmul(
                            ps[pi][:, hf * 512:(hf + 1) * 512], w[:, kt, m * 128:(m + 1) * 128],
                            hT[:, kt, hf * 512:(hf + 1) * 512], start=(kt == 0), stop=(kt == 15)),
                            reads=[dwt[cb % 2], dh], writes=[dps[pi]])
                epi(cb * 4 + m, pi)

        def epi_store(func, dst, colbase, scale=1.0, bias_col=None):
            def epi(blk, pi):
                si = scount[0] % 2
                scount[0] += 1
                r0 = (blk - colbase) * 128
                if bias_col is None:
                    P.op("act", lambda e: e.activation(out=stg[si][:], in_=ps[pi][:], func=func, scale=scale),
                         reads=[dps[pi]], writes=[dstg[si]])
                else:
                    bc = bias_col(blk)
                    P.op("act", lambda e: e.activation(out=stg[si][:], in_=ps[pi][:], func=func,
                                                       bias=bg_sb[:, bc:bc + 1]),
                         reads=[dps[pi], dbg], writes=[dstg[si]])
                P.dma("sp", dst[r0:r0 + 128, :], stg[si][:], reads=[dstg[si]])
            return epi

        def epi_u(blk, pi):
            P.op("act", lambda e: e.activation(out=uT[:, blk, :], in_=ps[pi][:], func=AF.Gelu_apprx_tanh),
                 reads=[dps[pi]], writes=[duT])

        def form1(cb, epi):
            w = wt[cb % 2]
            for c in range(8):
                pi = pcount[0] % 2
                pcount[0] += 1
                for kt in range(16):
                    P.op("pe", lambda e, w=w, c=c, kt=kt, pi=pi: e.matmul(
                        ps[pi][:, 0:512], hT[:, kt, c * 128:(c + 1) * 128], w[:, kt, :],
                        start=(kt == 0), stop=(kt == 15)),
                        reads=[dwt[cb % 2], dh], writes=[dps[pi]])
                epi(cb, c, pi)

        def epi_vg(cb, c, pi):
            off = (cb - 2) * 512
            P.op("act", lambda e: e.activation(out=vtok[:, c, off:off + 512], in_=ps[pi][:, 0:512],
                                               func=AF.Gelu_apprx_tanh),
                 reads=[dps[pi]], writes=[dvtok])

        def epi_va(cb, c, pi):
            off = (cb - 10) * 512
            si = scount[0] % 2
            scount[0] += 1
            P.op("act", lambda e: e.activation(out=stg[si][:, 0:512], in_=ps[pi][:, 0:512], func=AF.Copy),
                 reads=[dps[pi]], writes=[dstg[si]])
            P.dma("sp", vtk[c * 128:(c + 1) * 128, off:off + 512], stg[si][:, 0:512], reads=[dstg[si]])

        def gmlp_finish():
            for c in range(8):
                P.op("act", lambda e, c=c: e.activation(out=junk[:], in_=vtok[:, c, :], func=AF.Square,
                                                        accum_out=vss[:, c:c + 1]),
                     reads=[dvtok], writes=[djunk, dvss])
            P.op("act", lambda e: e.activation(out=vrs[:], in_=vss[:], func=AF.Sqrt, bias=epsb[:], scale=1.0 / 1024),
                 reads=[dvss, deps], writes=[dvrs])
            P.op("dve", lambda e: e.reciprocal(out=vrs[:], in_=vrs[:]), reads=[dvrs], writes=[dvrs])
            for c in range(8):
                P.op("dve", lambda e, c=c: e.scalar_tensor_tensor(out=vn[:], in0=vtok[:, c, :],
                                                                  scalar=vrs[:, c:c + 1], in1=gng_b[:],
                                                                  op0=ALU.mult, op1=ALU.mult),
                     reads=[dvtok, dvrs, dgng], writes=[dvn])
                for g in range(8):
                    P.op("pe", lambda e, g=g: e.matmul(psm[:, g * 128:(g + 1) * 128], vn[:, g * 128:(g + 1) * 128],
                                                       ws_b[:, g, :], start=True, stop=True),
                         reads=[dvn, dwsb], writes=[dpsm])
                P.op("dve", lambda e: e.tensor_tensor(out=mixt[:], in0=psm[:], in1=bs_b[:], op=ALU.add),
                     reads=[dpsm, dbs], writes=[dmixt])
                P.op("pool", lambda e, c=c: e.tensor_tensor(
                    out=ya_sb[:, :, c * 128:(c + 1) * 128], in0=mixt[:].rearrange("p (g t) -> p g t", g=8),
                    in1=uT[:, :, c * 128:(c + 1) * 128], op=ALU.mult),
                    reads=[dmixt, duT], writes=[dya])
            P.dma("sp", yaT.rearrange("(g p) t -> p g t", p=128), ya_sb[:], reads=[dya])

        for cb in range(24):
            if cb + 1 < 24:
                load_w(cb + 1)
            if cb < 2:
                form2(cb, epi_u)
            elif cb < 4:
                form1(cb, epi_vg)
                if cb == 3:
                    gmlp_finish()
            elif cb < 6:
                form2(cb, epi_store(AF.Copy, s5T, 16))
            elif cb < 8:
                form2(cb, epi_store(AF.Copy, qT, 24, scale=QSCALE))
            elif cb < 10:
                form2(cb, epi_store(AF.Copy, kT, 32))
            elif cb < 12:
                form1(cb, epi_va)
            else:
                form2(cb, epi_store(AF.Sigmoid, gT, 48, bias_col=lambda blk: blk - 48))
        P.finish()
    return nc


def host_A(x_tok, l, inp):
    maps = []
    n1g = np.ascontiguousarray(inp["norm1_g"][l].reshape(16, 128).T)
    bgv = np.ascontiguousarray(inp["b_gate"][l].reshape(48, 128).T)
    gng = np.ascontiguousarray(inp["gm_norm_g"][l].reshape(1, 1024))
    wsT = np.ascontiguousarray(np.transpose(inp["gm_w_s"][l], (0, 2, 1)))
    bsv = np.ascontiguousarray(inp["gm_b_s"][l].reshape(1, 1024))
    w_in = np.ascontiguousarray(inp["w_in"][l])
    for c in range(NCORES):
        xT = np.ascontiguousarray(x_tok[c * TOK:(c + 1) * TOK, :].T)
        maps.append({"xT": xT, "n1g": n1g, "w_in": w_in, "bg": bgv, "gng": gng, "wsT": wsT, "bs": bsv})
    return maps


def build_C(last, debug=False):
    nc = bass.Bass("TRN2", target_bir_lowering=False)
    with ExitStack() as st:
        st.enter_context(nc.allow_low_precision("bf16 matmul operands, fp32 accumulation"))
        C = Ctx(nc, st)
        xT = C.din("xT", [D, TOK])
        yaT = C.din("yaT", [1024, TOK], BF16)
        ybpT = C.din("ybpT", [1024, TOK], BF16)
        ycT = C.din("ycT", [1024, TOK], BF16)
        gT = C.din("gT", [6144, TOK], BF16)
        w_glu = C.din("w_glu", [1024, 1024])
        b_glu = C.din("b_glu", [128, 8])
        w_br = C.din("w_br", [3, 1024, D])
        w_out = C.din("w_out", [D, D])
        n2g = C.din("n2g", [128, 16])
        w_m1 = C.din("w_m1", [D, 8192])
        w_m2 = C.din("w_m2", [8192, D])
        fing = C.din("fing", [128, 16])
        xo = C.dout("xo", [D, TOK])
        if debug:
            dbg_yb = C.dout("dbg_yb", [1024, TOK], BF16)
            dbg_mg = C.dout("dbg_mg", [D, TOK], BF16)
            dbg_x1 = C.dout("dbg_x1", [D, TOK])
            dbg_h2 = C.dout("dbg_h2", [D, TOK], BF16)

        P = Prog(nc, st)
        x_sb = C.sb("x_sb", [128, 16, TOK], F32)
        R2 = C.sb("R2", [128, 3, 8, TOK], BF16)
        R3 = C.sb("R3", [128, 16 * TOK], BF16)
        R4 = C.sb("R4", [128, 4, 8 * 512], BF16)
        gts = [C.sb("gts%d" % i, [128, TOK], BF16) for i in range(6)]
        acc = C.sb("acc", [128, TOK], F32)
        tmp = C.sb("tmp", [128, TOK], F32)
        stg = [C.sb("stg%d" % i, [128, TOK], BF16) for i in range(2)]
        rstd = C.sb("rstd", [128, TOK], F32)
        sq = stg
        g2_sb = C.sb("g2_sb", [128, 16], F32)
        gf_sb = C.sb("gf_sb", [128, 16], F32)
        bgl_sb = C.sb("bgl_sb", [128, 8], F32)
        ones_bf = C.sb("ones_bf", [128, 128], BF16)
        epsb = C.sb("epsb", [128, 1], F32)
        ps = [C.ps("ps%d" % i, [128, 1024]) for i in range(3)]

        dx, dR3, dg2, dgf, dbgl, dones, deps, dacc, dtmp, drstd = [Dep() for _ in range(10)]
        dR2 = [Dep(), Dep(), Dep()]
        dR4 = [Dep() for _ in range(4)]
        dgts = [Dep() for _ in range(6)]
        dstg = [Dep(), Dep()]
        dsq = dstg
        dps = [Dep() for _ in range(3)]

        gS = R3[:, 0:8 * TOK].rearrange("p (k t) -> p k t", k=8)
        mg = R3[:].rearrange("p (k t) -> p k t", k=16)
        aT = R2[:].rearrange("p a k t -> p (a k) t")
        w8 = [R4[:, i, :].rearrange("p (k c) -> p k c", k=8) for i in range(4)]
        w16 = [R4[:, 2 * i:2 * i + 2, :].rearrange("p a (k c) -> p (a k) c", k=8) for i in range(2)]

        xv = xT.rearrange("(kt p) t -> p kt t", p=128)
        P.dma("sp", R2[:, 1], ybpT.rearrange("(k p) t -> p k t", p=128), writes=[dR2[1]])
        P.dma("sp", bgl_sb[:], b_glu, writes=[dbgl])
        for i in range(4):
            P.dma("sp", x_sb[:, 4 * i:4 * i + 4, :], xv[:, 4 * i:4 * i + 4, :], writes=[dx])
        P.dma("sp", R2[:, 0], yaT.rearrange("(k p) t -> p k t", p=128), writes=[dR2[0]])
        P.dma("sp", R2[:, 2], ycT.rearrange("(k p) t -> p k t", p=128), writes=[dR2[2]])
        P.dma("sp", g2_sb[:], n2g, writes=[dg2])
        P.dma("sp", gf_sb[:], fing, writes=[dgf])
        P.op("dve", lambda e: e.memset(ones_bf[:], 1.0), writes=[dones])
        P.op("dve", lambda e: e.memset(epsb[:], EPS), writes=[deps])

        jobs = []
        glu_v = w_glu.rearrange("(kt p) c -> p kt c", p=128)
        for cb in range(2):
            jobs.append(("glu", cb, 8, glu_v[:, :, cb * 512:(cb + 1) * 512]))
        for cb in range(4):
            for n in range(3):
                jobs.append(("br", (cb, n), 8, w_br[n].rearrange("(kt p) c -> p kt c", p=128)[:, :, cb * 512:(cb + 1) * 512]))
        wo_v = w_out.rearrange("(kt p) c -> p kt c", p=128)
        for cb in range(4):
            jobs.append(("wo", cb, 16, wo_v[:, :, cb * 512:(cb + 1) * 512]))
        m1_v = w_m1.rearrange("(kt p) c -> p kt c", p=128)
        m2_v = w_m2.rearrange("(kt p) c -> p kt c", p=128)
        for fc in range(4):
            for fb in range(4):
                c0 = fc * 2048 + fb * 512
                jobs.append(("m1", (fc, fb), 16, m1_v[:, :, c0:c0 + 512]))
            for cb in range(4):
                jobs.append(("m2", (fc, cb), 16, m2_v[:, fc * 16:(fc + 1) * 16, cb * 512:(cb + 1) * 512]))
        slot8 = 0
        slots = []
        quarters = []
        for kind, key, nk, src in jobs:
            if nk == 8:
                s = slot8 % 4
                slot8 += 1
                slots.append((w8[s], [dR4[s]]))
                quarters.append([s])
            else:
                if slot8 % 2:
                    slot8 += 1
                s = (slot8 // 2) % 2
                slot8 += 2
                slots.append((w16[s], [dR4[2 * s], dR4[2 * s + 1]]))
                quarters.append([2 * s, 2 * s + 1])
        issued = [0]
        occupant = [None] * 4
        consumed = set()

        def prefetch(upto):
            while issued[0] < min(upto, len(jobs)):
                j = issued[0]
                if any(occupant[q] is not None and occupant[q] not in consumed for q in quarters[j]):
                    return
                for q in quarters[j]:
                    occupant[q] = j
                P.dma("pool", slots[j][0], jobs[j][3], writes=slots[j][1])
                issued[0] += 1

        def done(j):
            consumed.add(j)
            prefetch(j + 4)
        pc = [0]

        def mm_block(j, m, rhs, drhs, nkt):
            pi = pc[0] % 3
            pc[0] += 1
            prefetch(j + 1)
            assert issued[0] > j, "weight job %d not loadable (ring slot busy)" % j
            wv, wd = slots[j]
            for hf in range(2):
                for kt in range(nkt):
                    P.op("pe", lambda e, wv=wv, m=m, hf=hf, kt=kt, pi=pi: e.matmul(
                        ps[pi][:, hf * 512:(hf + 1) * 512], wv[:, kt, m * 128:(m + 1) * 128],
                        rhs[:, kt, hf * 512:(hf + 1) * 512], start=(kt == 0), stop=(kt == nkt - 1)),
                        reads=wd + drhs, writes=[dps[pi]])
            return pi

        jn = [0]
        prefetch(3)
        for k in range(8):
            P.op("act", lambda e, k=k: e.activation(out=gS[:, k, :], in_=R2[:, 1, k, :], func=AF.Gelu_apprx_tanh),
                 reads=[dR2[1]], writes=[dR3])
        sc = [0]
        for cb in range(2):
            j = jn[0]
            jn[0] += 1
            prefetch(j + 3)
            for m in range(4):
                blk = cb * 4 + m
                pi = mm_block(j, m, gS, [dR3], 8)
                si = sc[0] % 2
                sc[0] += 1
                P.op("act", lambda e, pi=pi, si=si, blk=blk: e.activation(
                    out=stg[si][:], in_=ps[pi][:], func=AF.Sigmoid, bias=bgl_sb[:, blk:blk + 1]),
                    reads=[dps[pi], dbgl], writes=[dstg[si]])
                P.op("dve", lambda e, si=si, blk=blk: e.tensor_tensor(
                    out=R2[:, 1, blk, :], in0=gS[:, blk, :], in1=stg[si][:], op=ALU.mult),
                    reads=[dR3, dstg[si]], writes=[dR2[1]])
            done(j)
        if debug:
            P.dma("sp", dbg_yb.rearrange("(k p) t -> p k t", p=128), R2[:, 1], reads=[dR2[1]])
        gc = [0]
        for cb in range(4):
            js = [jn[0], jn[0] + 1, jn[0] + 2]
            jn[0] += 3
            prefetch(js[2] + 2)
            for m in range(4):
                dt_ = cb * 4 + m
                for n in range(3):
                    gi = gc[0] % 6
                    gc[0] += 1
                    r0 = (n * 16 + dt_) * 128
                    P.dma("sp", gts[gi][:], gT[r0:r0 + 128, :], writes=[dgts[gi]])
                    pi = mm_block(js[n], m, R2[:, n], [dR2[n]], 8)
                    if n == 0:
                        P.op("dve", lambda e, pi=pi, gi=gi: e.tensor_tensor(out=acc[:], in0=ps[pi][:], in1=gts[gi][:],
                                                                            op=ALU.mult),
                             reads=[dps[pi], dgts[gi]], writes=[dacc])
                    else:
                        P.op("dve", lambda e, pi=pi, gi=gi: e.tensor_tensor(out=tmp[:], in0=ps[pi][:], in1=gts[gi][:],
                                                                            op=ALU.mult),
                             reads=[dps[pi], dgts[gi]], writes=[dtmp])
                        if n == 1:
                            P.op("dve", lambda e: e.tensor_tensor(out=acc[:], in0=acc[:], in1=tmp[:], op=ALU.add),
                                 reads=[dacc, dtmp], writes=[dacc])
                        else:
                            P.op("dve", lambda e, dt_=dt_: e.tensor_tensor(out=mg[:, dt_, :], in0=acc[:], in1=tmp[:],
                                                                          op=ALU.add),
                                 reads=[dacc, dtmp], writes=[dR3])
            for j_ in js:
                done(j_)
        if debug:
            P.dma("sp", dbg_mg.rearrange("(k p) t -> p k t", p=128), mg, reads=[dR3])
        for cb in range(4):
            j = jn[0]
            jn[0] += 1
            prefetch(j + 2)
            for m in range(4):
                dt_ = cb * 4 + m
                pi = mm_block(j, m, mg, [dR3], 16)
                P.op("dve", lambda e, pi=pi, dt_=dt_: e.tensor_tensor(out=x_sb[:, dt_, :], in0=x_sb[:, dt_, :],
                                                                      in1=ps[pi][:], op=ALU.add),
                     reads=[dps[pi], dx], writes=[dx])
            done(j)
        if debug:
            P.dma("sp", dbg_x1.rearrange("(k p) t -> p k t", p=128), x_sb[:], reads=[dx])
        emit_rmsnorm(P, C, x_sb, dx, g2_sb, dg2, mg, dR3, ones_bf, dones, sq, dsq, ps[0], dps[0], rstd, drstd,
                     epsb, deps, TOK)
        if debug:
            P.dma("sp", dbg_h2.rearrange("(k p) t -> p k t", p=128), mg, reads=[dR3])
        for fc in range(4):
            for fb in range(4):
                j = jn[0]
                jn[0] += 1
                prefetch(j + 2)
                for m in range(4):
                    ft = fb * 4 + m
                    pi = mm_block(j, m, mg, [dR3], 16)
                    P.op("act", lambda e, pi=pi: e.activation(out=tmp[:], in_=ps[pi][:], func=AF.Relu),
                         reads=[dps[pi]], writes=[dtmp])
                    P.op("dve", lambda e, ft=ft: e.tensor_tensor(out=aT[:, ft, :], in0=tmp[:], in1=tmp[:], op=ALU.mult),
                         reads=[dtmp], writes=dR2)
                done(j)
            for cb in range(4):
                j = jn[0]
                jn[0] += 1
                prefetch(j + 2)
                for m in range(4):
                    dt_ = cb * 4 + m
                    pi = mm_block(j, m, aT, dR2, 16)
                    P.op("dve", lambda e, pi=pi, dt_=dt_: e.tensor_tensor(out=x_sb[:, dt_, :], in0=x_sb[:, dt_, :],
                                                                          in1=ps[pi][:], op=ALU.add),
                         reads=[dps[pi], dx], writes=[dx])
                done(j)
        xov = xo.rearrange("(kt p) t -> p kt t", p=128)
        if not last:
            for i in range(4):
                P.dma("sp", xov[:, 4 * i:4 * i + 4, :], x_sb[:, 4 * i:4 * i + 4, :], reads=[dx])
        else:
            for kt in range(16):
                i = kt % 2
                P.op("act", lambda e, kt=kt, i=i: e.activation(out=sq[i][:], in_=x_sb[:, kt, :], func=AF.Square),
                     reads=[dx], writes=[dsq[i]])
                for hf in range(2):
                    P.op("pe", lambda e, kt=kt, i=i, hf=hf: e.matmul(ps[0][:, hf * 512:(hf + 1) * 512], ones_bf[:],
                                                                     sq[i][:, hf * 512:(hf + 1) * 512],
                                                                     start=(kt == 0), stop=(kt == 15)),
                         reads=[dsq[i], dones], writes=[dps[0]])
            P.op("act", lambda e: e.activation(out=rstd[:], in_=ps[0][:], func=AF.Sqrt, bias=epsb[:], scale=1.0 / D),
                 reads=[dps[0], deps], writes=[drstd])
            P.op("dve", lambda e: e.reciprocal(out=rstd[:], in_=rstd[:]), reads=[drstd], writes=[drstd])
            fo = [acc, tmp]
            dfo = [dacc, dtmp]
            for kt in range(16):
                i = kt % 2
                P.op("dve", lambda e, kt=kt, i=i: e.scalar_tensor_tensor(out=fo[i][:], in0=x_sb[:, kt, :],
                                                                       scalar=gf_sb[:, kt:kt + 1], in1=rstd[:],
                                                                       op0=ALU.mult, op1=ALU.mult),
                     reads=[dx, dgf, drstd], writes=[dfo[i]])
                P.dma("sp", xov[:, kt, :], fo[i][:], reads=[dfo[i]])
        P.finish()
    return nc


def host_C(x_tok, l, inp, yaT, ybpT, ycT, gT, last):
    maps = []
    r = lambda v, n: np.ascontiguousarray(v.reshape(n, 128).T)
    com = {"w_glu": np.ascontiguousarray(inp["s5_w_glu"][l]), "b_glu": r(inp["s5_b_glu"][l], 8),
           "w_br": np.ascontiguousarray(inp["w_branch"][l]), "w_out": np.ascontiguousarray(inp["w_out"][l]),
           "n2g": r(inp["norm2_g"][l], 16), "w_m1": np.ascontiguousarray(inp["w_mlp_in"][l]),
           "w_m2": np.ascontiguousarray(inp["w_mlp_out"][l]), "fing": r(inp["final_g"], 16)}
    for c in range(NCORES):
        m = dict(com)
        m["xT"] = np.ascontiguousarray(x_tok[c * TOK:(c + 1) * TOK, :].T)
        m["yaT"], m["ybpT"], m["ycT"], m["gT"] = yaT[c], ybpT[c], ycT[c], gT[c]
        maps.append(m)
    return maps


SEQ = 4096
NPAIR = 2


def emit_B1(nc, C, P, merged=False):
    qT = C.din("qT", [NPAIR, 128, SEQ], BF16)
    kT = C.din("kT", [NPAIR, 128, SEQ], BF16)
    vv = C.din("v", [NPAIR, SEQ, 128], BF16)
    yc = C.dout("yc", [NPAIR, 128, SEQ], BF16)
    NB = 1 if merged else NPAIR
    q_sb = C.sb("q_sb", [128, NB, SEQ], BF16)
    k_sb = C.sb("k_sb", [128, NB, SEQ], BF16)
    v_sb = C.sb("v_sb", [128, NB, 32, 128], BF16)
    o_st = [C.sb("o_st%d" % i, [128, 512], BF16) for i in range(2)]
    ones_f = C.sb("ones_f", [128, 128], F32)
    mstrict = C.sb("mstrict", [128, 128], BF16)
    negtri = C.sb("negtri", [128, 128], BF16)
    negones = C.sb("negones", [128, 128], BF16)
    spsum = C.sb("spsum", [128, 512], BF16)
    ebuf = [C.sb("ebuf%d" % i, [128, 512], F32) for i in range(2)]
    spb = [C.sb("spb%d" % i, [128, 512], BF16) for i in range(3)]
    wbuf = [C.sb("wbuf%d" % i, [128, 512], BF16) for i in range(3)]
    psA = [C.ps("psA%d" % i, [128, 512]) for i in range(2)]
    psB = [C.ps("psB%d" % i, [128, 512]) for i in range(2)]
    psO = [C.ps("psO%d" % i, [128, 512]) for i in range(1 if merged else 2)]
    if merged:
        psO = [psO[0], psO[0]]
    dq, dk, dv, dconst, dspsum = [Dep() for _ in range(5)]
    do_ = [Dep(), Dep()]
    debuf = [Dep(), Dep()]
    dspb = [Dep() for _ in range(3)]
    dwbuf = [Dep() for _ in range(3)]
    dpsA = [Dep(), Dep()]
    dpsB = [Dep(), Dep()]
    dpsO = [Dep(), Dep()]
    if merged:
        dpsO = [dpsO[0], dpsO[0]]

    def load_pair(p):
        pb = p % NB
        P.dma("sp", q_sb[:, pb, :], qT[p], writes=[dq])
        P.dma("sp", k_sb[:, pb, :], kT[p], writes=[dk])
        P.dma("sp", v_sb[:, pb], vv[p].rearrange("(b s) d -> s b d", s=128), writes=[dv])
    for p in range(NB):
        load_pair(p)
    P.op("pool", lambda e: e.memset(ones_f[:], 1.0), writes=[dconst])
    P.op("pool", lambda e: e.affine_select(out=mstrict[:], in_=ones_f[:], pattern=[[1, 128]],
                                           compare_op=ALU.is_gt, fill=0.0, base=0, channel_multiplier=-1),
         reads=[dconst], writes=[dconst])
    P.op("pool", lambda e: e.memset(ones_f[:], -1.0), reads=[dconst], writes=[dconst])
    P.op("pool", lambda e: e.affine_select(out=negtri[:], in_=ones_f[:], pattern=[[-1, 128]],
                                           compare_op=ALU.is_ge, fill=0.0, base=0, channel_multiplier=1),
         reads=[dconst], writes=[dconst])
    P.op("pool", lambda e: e.tensor_copy(out=negones[:], in_=ones_f[:]), reads=[dconst], writes=[dconst])

    steps = []
    for p in range(NPAIR):
        for g in range(8):
            for sb in range(4 * g + 3, -1, -1):
                steps.append((p, g, sb))
    n = len(steps)

    def info(i):
        p, g, sb = steps[i]
        tl = max(0, sb - 4 * g) * 128
        return p, g, sb, tl, (sb >= 4 * g)

    def stage1(i):
        p, g, sb, tl, diag = info(i)
        a, s3 = i % 2, i % 3
        if merged and p > 0 and g == 0 and sb == 3:
            load_pair(p)
        P.op("pe", lambda e: e.matmul(psA[a][:, tl:512], k_sb[:, p % NB, sb * 128:(sb + 1) * 128],
                                      q_sb[:, p % NB, g * 512 + tl:(g + 1) * 512], start=True, stop=True),
             reads=[dq, dk], writes=[dpsA[a]])
        P.op("act", lambda e: e.activation(out=ebuf[a][:, tl:512], in_=psA[a][:, tl:512], func=AF.Exp),
             reads=[dpsA[a]], writes=[debuf[a]])
        P.op("act", lambda e: e.activation(out=spb[s3][:, tl:512], in_=ebuf[a][:, tl:512], func=AF.Ln, bias=1.0),
             reads=[debuf[a]], writes=[dspb[s3]])
        if diag:
            P.op("pool", lambda e: e.tensor_tensor(out=spb[s3][:, tl:tl + 128], in0=spb[s3][:, tl:tl + 128],
                                                   in1=mstrict[:], op=ALU.mult),
                 reads=[dspb[s3], dconst], writes=[dspb[s3]])

    def stage2(i):
        p, g, sb, tl, diag = info(i)
        a, s3 = i % 2, i % 3
        if sb == 4 * g + 3:
            P.op("pool", lambda e: e.memset(spsum[:], 0.0), writes=[dspsum])
        P.op("pe", lambda e: e.matmul(psB[a][:, tl:512], k_sb[:, p % NB, sb * 128:(sb + 1) * 128],
                                      q_sb[:, p % NB, g * 512 + tl:(g + 1) * 512], start=True, stop=False),
             reads=[dq, dk], writes=[dpsB[a]])
        P.op("pe", lambda e: e.matmul(psB[a][:, tl:512], negtri[:], spb[s3][:, tl:512], start=False, stop=False),
             reads=[dspb[s3], dconst], writes=[dpsB[a]])
        P.op("pe", lambda e: e.matmul(psB[a][:, tl:512], negones[:], spsum[:, tl:512], start=False, stop=True),
             reads=[dspsum, dconst], writes=[dpsB[a]])
        P.op("pool", lambda e: e.tensor_tensor(out=spsum[:, tl:512], in0=spsum[:, tl:512], in1=spb[s3][:, tl:512],
                                               op=ALU.add),
             reads=[dspsum, dspb[s3]], writes=[dspsum])
        P.op("act", lambda e: e.activation(out=wbuf[s3][:, tl:512], in_=psB[a][:, tl:512], func=AF.Exp),
             reads=[dpsB[a]], writes=[dwbuf[s3]])
        if diag:
            P.op("pool", lambda e: e.tensor_tensor(out=wbuf[s3][:, tl:tl + 128], in0=wbuf[s3][:, tl:tl + 128],
                                                   in1=mstrict[:], op=ALU.mult),
                 reads=[dwbuf[s3], dconst], writes=[dwbuf[s3]])

    def stage3(i):
        p, g, sb, tl, diag = info(i)
        s3 = i % 3
        o = (p * 8 + g) % 2
        for tb in range(tl // 128, 4):
            P.op("pe", lambda e, tb=tb: e.matmul(psO[o][:, tb * 128:(tb + 1) * 128], v_sb[:, p % NB, sb, :],
                                                 wbuf[s3][:, tb * 128:(tb + 1) * 128],
                                                 start=(sb == 4 * g + 3 and tb == 3), stop=(sb == 0),
                                                 skip_group_check=True),
                 reads=[dv, dwbuf[s3]], writes=[dpsO[o]])
        if sb == 0:
            P.op("dve", lambda e: e.tensor_copy(out=o_st[o][:], in_=psO[o][:]), reads=[dpsO[o]], writes=[do_[o]])
            P.dma("sp", yc[p][:, g * 512:(g + 1) * 512], o_st[o][:], reads=[do_[o]])

    for it in range(n + 2):
        if it < n:
            stage1(it)
        if 0 <= it - 1 < n:
            stage2(it - 1)
        if 0 <= it - 2 < n:
            stage3(it - 2)


def build_B1():
    nc = bass.Bass("TRN2", target_bir_lowering=False)
    with ExitStack() as st:
        st.enter_context(nc.allow_low_precision("bf16 matmul operands, fp32 accumulation"))
        C = Ctx(nc, st)
        P = Prog(nc, st)
        emit_B1(nc, C, P)
        P.finish()
    return nc


I32 = mybir.dt.int32
PI = 3.14159265358979
TWO_PI = 2.0 * PI
NT = 2 * SEQ


def emit_B2(nc, C, P, merged=False, debug=False):
    uT = C.din("uT", [128, NT], BF16)
    lam_re = C.din("lam_re", [128, 4])
    lam_im = C.din("lam_im", [128, 4])
    log_dt = C.din("log_dt", [128, 4])
    b_re = C.din("b_re", [128, 4, 16])
    b_im = C.din("b_im", [128, 4, 16])
    c_reT = C.din("c_reT", [128, 4, 16])
    c_imT = C.din("c_imT", [128, 4, 16])
    dvec = C.din("dvec", [128, 1])
    yp = C.dout("yp", [128, NT], BF16)
    f4 = lambda n: C.sb(n, [128, 4], F32)
    u_sb = C.sb("u_sb", [128, NT], BF16)
    y_sb = C.sb("y_sb", [128, NT], BF16)
    lre, lim, ldt, dtt, rho, th, mag, cth, sth, abre, abim, nre, den, kre, kim, q0, q1, q2 = [
        f4("s4_%d" % i) for i in range(18)]
    m127, a7re, a7im = f4("m127"), f4("a7re"), f4("a7im")
    Adre = C.sb("Adre", [128, 4, 5], F32)
    Adim = C.sb("Adim", [128, 4, 5], F32)
    Adimn = C.sb("Adimn", [128, 4, 5], F32)
    abimn = f4("abimn")
    s127n = f4("s127n")
    bre = C.sb("bre", [128, 4, 16], F32)
    bim = C.sb("bim", [128, 4, 16], F32)
    cre = C.sb("cre", [128, 4, 16], F32)
    cim = C.sb("cim", [128, 4, 16], F32)
    bbre = C.sb("bbre", [128, 4, 16], F32)
    bbim = C.sb("bbim", [128, 4, 16], F32)
    bt0 = C.sb("bt0", [128, 16], F32)
    d_sb = C.sb("d_sb", [128, 1], F32)
    ident = C.sb("ident", [128, 128], F32)
    pad = C.sb("pad", [128, 128], F32)
    BBT = [C.sb("BBT%d" % i, [128, 4, 128], BF16) for i in range(2)]
    Cp = [C.sb("Cp%d" % i, [128, 4, 128], BF16) for i in range(2)]
    jidx = C.sb("jidx", [128, 512], F32)
    ang = C.sb("ang", [128, 512], F32)
    kf = C.sb("kf", [128, 512], F32)
    ki = C.sb("ki", [128, 512], I32)
    cs = [C.sb("cs%d" % i, [128, 4, 512], F32) for i in range(2)]
    m0 = C.sb("m0", [128, SEQ], BF16)
    rfull = C.sb("rfull", [128, SEQ], F32)
    w = [C.sb("w%d" % i, [128, SEQ], F32) for i in range(2)]
    z = [C.sb("z%d" % i, [128, SEQ], F32) for i in range(2)]
    bu = [C.sb("bu%d" % i, [128, 512], F32) for i in range(2)]
    tt = [C.sb("tt%d" % i, [128, 512], F32) for i in range(4)]
    sbf = [C.sb("sbf%d" % i, [128, 512], BF16) for i in range(2)]
    cst = [C.sb("cst%d" % i, [128, 32], F32) for i in range(10)]
    psb = [C.ps("psb%d" % i, [128, 512]) for i in range(2)]
    if merged:
        psy0 = C.ps("psy0", [128, 512])
        psy = [psy0, psy0]
        pst = psy0
    else:
        psy = [C.ps("psy%d" % i, [128, 512]) for i in range(2)]
        pst = C.ps("pst", [128, 128])

    du, dy, dpre, dtab, dm0, drf = [Dep() for _ in range(6)]
    dw = [Dep(), Dep()]
    dz = [Dep(), Dep()]
    dbu = [Dep(), Dep()]
    dtq = [Dep() for _ in range(4)]
    dsbf = [Dep(), Dep()]
    dcst = Dep()
    dpsb = [Dep(), Dep()]
    dpsy = [Dep(), Dep()]
    dpst = Dep()
    if merged:
        dpsy = [dpsy[0], dpsy[0]]
        dpst = dpsy[0]

    def V(fn, r, wr, eng="dve"):
        P.op(eng, fn, reads=r, writes=wr)

    P.dma("sp", u_sb[:, 0:SEQ], uT[:, 0:SEQ], writes=[du])
    P.dma("sp", u_sb[:, SEQ:NT], uT[:, SEQ:NT], writes=[du])
    for t_, src in ((lre, lam_re), (lim, lam_im), (ldt, log_dt), (bre, b_re), (bim, b_im), (cre, c_reT),
                    (cim, c_imT), (d_sb, dvec)):
        P.dma("sp", t_[:], src, writes=[dpre])
    pre = [dpre]

    V(lambda e: e.memset(pad[:], 1.0), [], pre, "pool")
    V(lambda e: e.affine_select(out=ident[:], in_=pad[:], pattern=[[-1, 128]], compare_op=ALU.is_equal,
                                fill=0.0, base=0, channel_multiplier=1), pre, pre, "pool")
    V(lambda e: e.iota(jidx[:], pattern=[[0, 4], [1, 128]], base=0, channel_multiplier=0,
                       allow_small_or_imprecise_dtypes=True), [], [dtab], "pool")
    V(lambda e: e.memset(m0[:], 1.0), [], [dm0], "pool")
    V(lambda e: e.memset(m0[:].rearrange("p (c j) -> p c j", j=128)[:, :, 0:1], 0.0), [dm0], [dm0], "pool")

    def reduce_angle(x, n):
        V(lambda e: e.tensor_scalar(out=kf[:, :n], in0=x, scalar1=1.0 / TWO_PI, scalar2=None, op0=ALU.mult),
          [dtab], [dtab])
        V(lambda e: e.tensor_copy(out=ki[:, :n], in_=kf[:, :n]), [dtab], [dtab])
        V(lambda e: e.tensor_copy(out=kf[:, :n], in_=ki[:, :n]), [dtab], [dtab])
        V(lambda e: e.scalar_tensor_tensor(out=x, in0=kf[:, :n], scalar=-TWO_PI, in1=x, op0=ALU.mult,
                                           op1=ALU.add), [dtab], [dtab])
        V(lambda e: e.tensor_scalar(out=kf[:, :n], in0=x, scalar1=PI, scalar2=-TWO_PI, op0=ALU.is_gt,
                                    op1=ALU.mult), [dtab], [dtab])
        V(lambda e: e.tensor_tensor(out=x, in0=x, in1=kf[:, :n], op=ALU.add), [dtab], [dtab])
        V(lambda e: e.tensor_scalar(out=kf[:, :n], in0=x, scalar1=-PI, scalar2=TWO_PI, op0=ALU.is_lt,
                                    op1=ALU.mult), [dtab], [dtab])
        V(lambda e: e.tensor_tensor(out=x, in0=x, in1=kf[:, :n], op=ALU.add), [dtab], [dtab])

    PT = [dpre, dtab]
    P.op("act", lambda e: e.activation(out=dtt[:], in_=ldt[:], func=AF.Exp), reads=PT, writes=PT)
    V(lambda e: e.tensor_tensor(out=rho[:], in0=lre[:], in1=dtt[:], op=ALU.mult), PT, PT)
    V(lambda e: e.tensor_tensor(out=th[:], in0=lim[:], in1=dtt[:], op=ALU.mult), PT, PT)
    P.op("act", lambda e: e.activation(out=mag[:], in_=rho[:], func=AF.Exp), reads=PT, writes=PT)
    P.op("act", lambda e: e.activation(out=m127[:], in_=rho[:], func=AF.Exp, scale=127.0), reads=PT, writes=PT)
    for gp in range(4):
        for k_, shift in ((1, 0.0), (0, PI / 2)):
            V(lambda e, gp=gp, shift=shift: e.tensor_scalar(out=ang[:], in0=jidx[:], scalar1=th[:, gp:gp + 1],
                                                            scalar2=shift, op0=ALU.mult, op1=ALU.add), PT, PT)
            reduce_angle(ang[:], 512)
            P.op("act", lambda e, gp=gp, k_=k_: e.activation(out=cs[k_][:, gp, :], in_=ang[:], func=AF.Sin),
                 reads=PT, writes=PT)
    V(lambda e: e.tensor_copy(out=cth[:], in_=cs[0][:, :, 1]), PT, PT)
    V(lambda e: e.tensor_copy(out=sth[:], in_=cs[1][:, :, 1]), PT, PT)
    V(lambda e: e.tensor_tensor(out=abre[:], in0=mag[:], in1=cth[:], op=ALU.mult), PT, PT)
    V(lambda e: e.tensor_tensor(out=abim[:], in0=mag[:], in1=sth[:], op=ALU.mult), PT, PT)
    V(lambda e: e.tensor_scalar(out=abimn[:], in0=abim[:], scalar1=-1.0, scalar2=None, op0=ALU.mult), PT, PT)
    V(lambda e: e.tensor_scalar(out=s127n[:], in0=cs[1][:, :, 127], scalar1=-1.0, scalar2=None, op0=ALU.mult), PT, PT)
    V(lambda e: e.tensor_scalar(out=nre[:], in0=abre[:], scalar1=-1.0, scalar2=None, op0=ALU.add), PT, PT)
    V(lambda e: e.tensor_tensor(out=q0[:], in0=lre[:], in1=lre[:], op=ALU.mult), PT, PT)
    V(lambda e: e.tensor_tensor(out=q1[:], in0=lim[:], in1=lim[:], op=ALU.mult), PT, PT)
    V(lambda e: e.tensor_tensor(out=den[:], in0=q0[:], in1=q1[:], op=ALU.add), PT, PT)
    V(lambda e: e.reciprocal(out=den[:], in_=den[:]), PT, PT)
    V(lambda e: e.tensor_tensor(out=q0[:], in0=nre[:], in1=lre[:], op=ALU.mult), PT, PT)
    V(lambda e: e.tensor_tensor(out=q1[:], in0=abim[:], in1=lim[:], op=ALU.mult), PT, PT)
    V(lambda e: e.tensor_tensor(out=q0[:], in0=q0[:], in1=q1[:], op=ALU.add), PT, PT)
    V(lambda e: e.tensor_tensor(out=kre[:], in0=q0[:], in1=den[:], op=ALU.mult), PT, PT)
    V(lambda e: e.tensor_tensor(out=q0[:], in0=abim[:], in1=lre[:], op=ALU.mult), PT, PT)
    V(lambda e: e.tensor_tensor(out=q1[:], in0=nre[:], in1=lim[:], op=ALU.mult), PT, PT)
    V(lambda e: e.tensor_tensor(out=q0[:], in0=q0[:], in1=q1[:], op=ALU.subtract), PT, PT)
    V(lambda e: e.tensor_tensor(out=kim[:], in0=q0[:], in1=den[:], op=ALU.mult), PT, PT)
    for gp in range(4):
        g1 = slice(gp, gp + 1)
        V(lambda e, gp=gp, g1=g1: e.tensor_scalar(out=bt0[:], in0=bim[:, gp, :], scalar1=kim[:, g1], scalar2=None,
                                                  op0=ALU.mult), PT, PT)
        V(lambda e, gp=gp, g1=g1: e.scalar_tensor_tensor(out=bbre[:, gp, :], in0=bre[:, gp, :], scalar=kre[:, g1],
                                                         in1=bt0[:], op0=ALU.mult, op1=ALU.subtract), PT, PT)
        V(lambda e, gp=gp, g1=g1: e.tensor_scalar(out=bt0[:], in0=bre[:, gp, :], scalar1=kim[:, g1], scalar2=None,
                                                  op0=ALU.mult), PT, PT)
        V(lambda e, gp=gp, g1=g1: e.scalar_tensor_tensor(out=bbim[:, gp, :], in0=bim[:, gp, :], scalar=kre[:, g1],
                                                         in1=bt0[:], op0=ALU.mult, op1=ALU.add), PT, PT)
    V(lambda e: e.tensor_tensor(out=a7re[:], in0=m127[:], in1=cs[0][:, :, 127], op=ALU.mult), PT, PT)
    V(lambda e: e.tensor_tensor(out=a7im[:], in0=m127[:], in1=cs[1][:, :, 127], op=ALU.mult), PT, PT)

    def cmul(ore, oim, are, aim, bre_, bim_):
        V(lambda e: e.tensor_tensor(out=q0[:], in0=are, in1=bre_, op=ALU.mult), PT, PT)
        V(lambda e: e.tensor_tensor(out=q1[:], in0=aim, in1=bim_, op=ALU.mult), PT, PT)
        V(lambda e: e.tensor_tensor(out=q2[:], in0=are, in1=bim_, op=ALU.mult), PT, PT)
        V(lambda e: e.tensor_tensor(out=ore, in0=q0[:], in1=q1[:], op=ALU.subtract), PT, PT)
        V(lambda e: e.tensor_tensor(out=q0[:], in0=aim, in1=bre_, op=ALU.mult), PT, PT)
        V(lambda e: e.tensor_tensor(out=oim, in0=q2[:], in1=q0[:], op=ALU.add), PT, PT)
    cmul(Adre[:, :, 0], Adim[:, :, 0], a7re[:], a7im[:], abre[:], abim[:])
    for k in range(1, 5):
        cmul(Adre[:, :, k], Adim[:, :, k], Adre[:, :, k - 1], Adim[:, :, k - 1], Adre[:, :, k - 1], Adim[:, :, k - 1])
    V(lambda e: e.tensor_scalar(out=Adimn[:], in0=Adim[:], scalar1=-1.0, scalar2=None, op0=ALU.mult), PT, PT)
    for gp in range(4):
        for ri, src in ((0, bbre), (1, bbim)):
            V(lambda e: e.memset(pad[:], 0.0), PT, PT)
            for j in range(2):
                c0 = 16 * (2 * gp + j)
                V(lambda e, j=j, c0=c0, src=src, gp=gp: e.tensor_copy(out=pad[64 * j:64 * j + 64, c0:c0 + 16],
                                                                    in_=src[64 * j:64 * j + 64, gp, :]), PT, PT)
            P.op("pe", lambda e: e.transpose(pst[:, 0:128], pad[:], ident[:]), reads=PT, writes=[dpst])
            V(lambda e, ri=ri, gp=gp: e.tensor_copy(out=BBT[ri][:, gp, :], in_=pst[:, 0:128]), [dpst] + PT, PT)
        for ri, src, sgn in ((0, cre, 1.0), (1, cim, -1.0)):
            V(lambda e, ri=ri, gp=gp: e.memset(Cp[ri][:, gp, :], 0.0), PT, PT)
            for j in range(2):
                c0 = 16 * (2 * gp + j)
                V(lambda e, j=j, c0=c0, src=src, gp=gp, ri=ri, sgn=sgn: e.tensor_scalar(
                    out=Cp[ri][64 * j:64 * j + 64, gp, c0:c0 + 16], in0=src[64 * j:64 * j + 64, gp, :],
                    scalar1=sgn, scalar2=None, op0=ALU.mult), PT, PT)

    if debug:
        for nm, t_, shp in (("th", th, [128, 4]), ("mag", mag, [128, 4]), ("kre", kre, [128, 4]), ("kim", kim, [128, 4]),
                            ("cos", cs[0], [128, 4, 512]), ("sin", cs[1], [128, 4, 512]), ("Adre", Adre, [128, 4, 5]),
                            ("Adim", Adim, [128, 4, 5]), ("bbre", bbre, [128, 4, 16]), ("jidx", jidx, [128, 512])):
            P.dma("sp", C.dout("dbg_" + nm, shp), t_[:], reads=PT)
        P.dma("sp", C.dout("dbg_BBT0", [128, 4, 128], BF16), BBT[0][:], reads=PT)
        P.dma("sp", C.dout("dbg_Cp1", [128, 4, 128], BF16), Cp[1][:], reads=PT)
    def mod_tile(gp, tok0, t8):
        sl = slice(t8 * 512, (t8 + 1) * 512)
        for ri in range(2):
            P.op("pe", lambda e, ri=ri: e.matmul(psb[ri][:], BBT[ri][:, gp, :],
                                                 u_sb[:, tok0 + sl.start:tok0 + sl.stop], start=True, stop=True),
                 reads=[du] + PT, writes=[dpsb[ri]])
            P.op("act", lambda e, ri=ri: e.activation(out=bu[ri][:], in_=psb[ri][:], func=AF.Copy),
                 reads=[dpsb[ri]], writes=[dbu[ri]])
        V(lambda e: e.tensor_tensor(out=tt[0][:], in0=bu[0][:], in1=cs[0][:, gp, :], op=ALU.mult),
          [dbu[0]] + PT, [dtq[0]])
        V(lambda e: e.tensor_tensor(out=tt[1][:], in0=bu[1][:], in1=cs[1][:, gp, :], op=ALU.mult),
          [dbu[1]] + PT, [dtq[1]])
        V(lambda e: e.tensor_tensor(out=w[0][:, sl], in0=tt[0][:], in1=tt[1][:], op=ALU.add),
          [dtq[0], dtq[1]], [dw[0]])
        V(lambda e: e.tensor_tensor(out=tt[2][:], in0=bu[1][:], in1=cs[0][:, gp, :], op=ALU.mult),
          [dbu[1]] + PT, [dtq[2]], "pool")
        V(lambda e: e.tensor_tensor(out=tt[3][:], in0=bu[0][:], in1=cs[1][:, gp, :], op=ALU.mult),
          [dbu[0]] + PT, [dtq[3]], "pool")
        V(lambda e: e.tensor_tensor(out=w[1][:, sl], in0=tt[2][:], in1=tt[3][:], op=ALU.subtract),
          [dtq[2], dtq[3]], [dw[1]], "pool")

    def scans(extra):
        for i in range(2):
            V(lambda e, i=i: e.tensor_tensor_scan(out=z[i][:], data0=rfull[:], data1=w[i][:], initial=0.0,
                                                  op0=ALU.mult, op1=ALU.add), [drf, dw[i]] + extra, [dz[i]])

    def hs_step(gp, k, cur):
        d_ = 1 << k
        nxt = 2 - cur
        ar = Adre[:, gp, k:k + 1]
        ai = Adim[:, gp, k:k + 1]
        ain = Adimn[:, gp, k:k + 1]
        sre, sim = cst[cur], cst[cur + 1]
        nre_, nim_ = cst[nxt], cst[nxt + 1]
        CS = [dcst]
        V(lambda e: e.tensor_copy(out=nre_[:, 0:d_], in_=sre[:, 0:d_]), CS, CS)
        V(lambda e: e.tensor_copy(out=nim_[:, 0:d_], in_=sim[:, 0:d_]), CS, CS)
        V(lambda e: e.scalar_tensor_tensor(out=cst[4][:, d_:32], in0=sre[:, 0:32 - d_], scalar=ar, in1=sre[:, d_:32],
                                           op0=ALU.mult, op1=ALU.add), CS + PT, CS)
        V(lambda e: e.scalar_tensor_tensor(out=nre_[:, d_:32], in0=sim[:, 0:32 - d_], scalar=ain,
                                           in1=cst[4][:, d_:32], op0=ALU.mult, op1=ALU.add), CS + PT, CS)
        V(lambda e: e.scalar_tensor_tensor(out=cst[4][:, d_:32], in0=sim[:, 0:32 - d_], scalar=ar, in1=sim[:, d_:32],
                                           op0=ALU.mult, op1=ALU.add), CS + PT, CS)
        V(lambda e: e.scalar_tensor_tensor(out=nim_[:, d_:32], in0=sre[:, 0:32 - d_], scalar=ai,
                                           in1=cst[4][:, d_:32], op0=ALU.mult, op1=ALU.add), CS + PT, CS)
        return nxt

    def carry(gp):
        g1 = slice(gp, gp + 1)
        CS = [dcst]
        zE = [z[i][:].rearrange("p (c j) -> p c j", j=128)[:, :, 127] for i in range(2)]
        c127 = cs[0][:, gp, 127:128]
        s127 = cs[1][:, gp, 127:128]
        V(lambda e: e.tensor_scalar(out=cst[2][:], in0=zE[0], scalar1=c127, scalar2=None, op0=ALU.mult),
          [dz[0]] + PT, CS)
        V(lambda e: e.scalar_tensor_tensor(out=cst[0][:], in0=zE[1], scalar=s127n[:, g1], in1=cst[2][:],
                                           op0=ALU.mult, op1=ALU.add), [dz[1]] + PT + CS, CS)
        V(lambda e: e.tensor_scalar(out=cst[2][:], in0=zE[0], scalar1=s127, scalar2=None, op0=ALU.mult),
          [dz[0]] + PT + CS, CS)
        V(lambda e: e.scalar_tensor_tensor(out=cst[1][:], in0=zE[1], scalar=c127, in1=cst[2][:],
                                           op0=ALU.mult, op1=ALU.add), [dz[1]] + PT + CS, CS)
        cur = 0
        for k in range(5):
            cur = hs_step(gp, k, cur)
        Sre, Sim = cst[cur], cst[cur + 1]
        w0 = [w[i][:].rearrange("p (c j) -> p c j", j=128)[:, 1:32, 0] for i in range(2)]
        V(lambda e: e.scalar_tensor_tensor(out=cst[5][:, 0:31], in0=Sre[:, 0:31], scalar=abre[:, g1],
                                           in1=w0[0], op0=ALU.mult, op1=ALU.add), CS + PT + [dw[0]], CS)
        V(lambda e: e.scalar_tensor_tensor(out=w0[0], in0=Sim[:, 0:31], scalar=abimn[:, g1],
                                           in1=cst[5][:, 0:31], op0=ALU.mult, op1=ALU.add), CS + PT, [dw[0]])
        V(lambda e: e.scalar_tensor_tensor(out=cst[6][:, 0:31], in0=Sim[:, 0:31], scalar=abre[:, g1],
                                           in1=w0[1], op0=ALU.mult, op1=ALU.add), CS + PT + [dw[1]], CS)
        V(lambda e: e.scalar_tensor_tensor(out=w0[1], in0=Sre[:, 0:31], scalar=abim[:, g1],
                                           in1=cst[6][:, 0:31], op0=ALU.mult, op1=ALU.add), CS + PT, [dw[1]])

    def out_tile(gp, tok0, t8):
        sl = slice(t8 * 512, (t8 + 1) * 512)
        V(lambda e: e.tensor_tensor(out=tt[0][:], in0=z[0][:, sl], in1=cs[0][:, gp, :], op=ALU.mult),
          [dz[0]] + PT, [dtq[0]])
        V(lambda e: e.tensor_tensor(out=tt[1][:], in0=z[1][:, sl], in1=cs[1][:, gp, :], op=ALU.mult),
          [dz[1]] + PT, [dtq[1]])
        V(lambda e: e.tensor_tensor(out=sbf[0][:], in0=tt[0][:], in1=tt[1][:], op=ALU.subtract),
          [dtq[0], dtq[1]], [dsbf[0]])
        V(lambda e: e.tensor_tensor(out=tt[2][:], in0=z[0][:, sl], in1=cs[1][:, gp, :], op=ALU.mult),
          [dz[0]] + PT, [dtq[2]], "pool")
        V(lambda e: e.tensor_tensor(out=tt[3][:], in0=z[1][:, sl], in1=cs[0][:, gp, :], op=ALU.mult),
          [dz[1]] + PT, [dtq[3]], "pool")
        V(lambda e: e.tensor_tensor(out=sbf[1][:], in0=tt[2][:], in1=tt[3][:], op=ALU.add),
          [dtq[2], dtq[3]], [dsbf[1]], "pool")
        yi = t8 % 2
        P.op("pe", lambda e: e.matmul(psy[yi][:], Cp[0][:, gp, :], sbf[0][:], start=True, stop=False),
             reads=[dsbf[0]] + PT, writes=[dpsy[yi]])
        P.op("pe", lambda e: e.matmul(psy[yi][:], Cp[1][:, gp, :], sbf[1][:], start=False, stop=True),
             reads=[dsbf[1]] + PT, writes=[dpsy[yi]])
        r0 = 32 * gp
        V(lambda e: e.scalar_tensor_tensor(
            out=y_sb[r0:r0 + 32, tok0 + sl.start:tok0 + sl.stop],
            in0=u_sb[r0:r0 + 32, tok0 + sl.start:tok0 + sl.stop], scalar=d_sb[r0:r0 + 32, 0:1],
            in1=psy[yi][r0:r0 + 32, :], op0=ALU.mult, op1=ALU.add),
          [du, dpsy[yi]] + PT, [dy])

    def set_rfull(gp):
        P.op("act", lambda e: e.activation(out=rfull[:], in_=m0[:], func=AF.Copy, scale=mag[:, gp:gp + 1]),
             reads=[dm0] + PT, writes=[drf])

    for gp in range(4):
        set_rfull(gp)
        for b in range(2):
            for t8 in range(8):
                mod_tile(gp, b * SEQ, t8)
            scans([])
            carry(gp)
            scans([dcst])
            for t8 in range(8):
                out_tile(gp, b * SEQ, t8)
    if debug:
        for nm, t_, shp, dd in (("w0", w[0], [128, SEQ], dw[0]), ("w1", w[1], [128, SEQ], dw[1]),
                                ("z0", z[0], [128, SEQ], dz[0]), ("z1", z[1], [128, SEQ], dz[1]),
                                ("rfull", rfull, [128, SEQ], drf), ("bu0", bu[0], [128, 512], dbu[0]),
                                ("tt0", tt[0], [128, 512], dtq[0])):
            P.dma("sp", C.dout("dbg_" + nm, shp), t_[:], reads=[dd])
        for k in range(7):
            P.dma("sp", C.dout("dbg_cst%d" % k, [128, 32]), cst[k][:], reads=[dcst])
    P.dma("sp", yp[:, 0:SEQ], y_sb[:, 0:SEQ], reads=[dy])
    P.dma("sp", yp[:, SEQ:NT], y_sb[:, SEQ:NT], reads=[dy])


def build_B2(debug=False):
    nc = bass.Bass("TRN2", target_bir_lowering=False)
    with ExitStack() as st:
        st.enter_context(nc.allow_low_precision("bf16 matmul operands, fp32 accumulation"))
        C = Ctx(nc, st)
        P = Prog(nc, st)
        emit_B2(nc, C, P, debug=debug)
        P.finish()
    return nc


def host_B2_params(l, inp, core):
    gs = slice(8 * core, 8 * core + 8)

    def pg(a):
        return np.ascontiguousarray(a.reshape(4, 2, 64).transpose(1, 2, 0).reshape(128, 4))
    lam_re = pg(inp["s5_lambda_re"][l][gs])
    lam_im = pg(inp["s5_lambda_im"][l][gs])
    log_dt = pg(np.repeat(inp["s5_log_dt"][l][gs][:, None], 64, axis=1))

    def pb(a):
        return np.ascontiguousarray(a.reshape(4, 2, 64, 16).transpose(1, 2, 0, 3).reshape(128, 4, 16))
    b_re = pb(inp["s5_b_re"][l][gs])
    b_im = pb(inp["s5_b_im"][l][gs])
    c_reT = pb(np.transpose(inp["s5_c_re"][l][gs], (0, 2, 1)))
    c_imT = pb(np.transpose(inp["s5_c_im"][l][gs], (0, 2, 1)))
    dvec = np.ascontiguousarray(inp["s5_d"][l][128 * core:128 * core + 128].reshape(128, 1))
    return {"lam_re": lam_re, "lam_im": lam_im, "log_dt": log_dt, "b_re": b_re, "b_im": b_im,
            "c_reT": c_reT, "c_imT": c_imT, "dvec": dvec}


_NC_CACHE = {}


def _get(name, fn):
    if name not in _NC_CACHE:
        _NC_CACHE[name] = fn()
    return _NC_CACHE[name]


def _run(nc, maps):
    res = run_bass_kernel_spmd(nc, maps, core_ids=list(range(NCORES)))
    return res.results


def kernel(**inp):
    inp = {k: np.asarray(v) for k, v in inp.items()}
    x_tok = np.ascontiguousarray(inp["x"].reshape(2 * SEQ, D).astype(np.float32))
    depth = inp["w_in"].shape[0]
    for l in range(depth):
        last = (l == depth - 1)
        rA = _run(_get("A", build_A), host_A(x_tok, l, inp))
        yaT = [np.asarray(r["yaT"]) for r in rA]
        gT = [np.asarray(r["gT"]) for r in rA]
        s5T = [np.asarray(r["s5T"]) for r in rA]
        qT = [np.asarray(r["qT"]) for r in rA]
        kT = [np.asarray(r["kT"]) for r in rA]
        vtk = [np.asarray(r["vtk"]) for r in rA]
        mB1, mB2 = [], []
        for c in range(NCORES):
            qs, ks, vs = [], [], []
            for pl in range(NPAIR):
                b, h = divmod(c * NPAIR + pl, 8)
                hs = slice(128 * h, 128 * h + 128)
                qs.append(np.concatenate([qT[4 * b + i][hs, :] for i in range(4)], axis=1))
                ks.append(np.concatenate([kT[4 * b + i][hs, :] for i in range(4)], axis=1))
                vs.append(np.concatenate([vtk[4 * b + i][:, hs] for i in range(4)], axis=0))
            mB1.append({"qT": np.ascontiguousarray(np.stack(qs)), "kT": np.ascontiguousarray(np.stack(ks)),
                        "v": np.ascontiguousarray(np.stack(vs))})
            m2 = host_B2_params(l, inp, c)
            m2["uT"] = np.ascontiguousarray(np.concatenate([s5T[i][128 * c:128 * c + 128, :] for i in range(8)], axis=1))
            mB2.append(m2)
        mB = [dict(mB1[c], **mB2[c]) for c in range(NCORES)]
        rB = _run(_get("B", build_B), mB)
        yc = [np.asarray(r["yc"]) for r in rB]
        yp = [np.asarray(r["yp"]) for r in rB]
        ycT, ybpT = [], []
        for tc in range(NCORES):
            b, i = divmod(tc, 4)
            rows = []
            for h in range(8):
                c, pl = divmod(b * 8 + h, NPAIR)
                rows.append(yc[c][pl][:, i * TOK:(i + 1) * TOK])
            ycT.append(np.ascontiguousarray(np.concatenate(rows, axis=0)))
            ybpT.append(np.ascontiguousarray(np.concatenate([yp[c][:, tc * TOK:(tc + 1) * TOK] for c in range(8)], axis=0)))
        ncC = _get("C%d" % int(last), lambda: build_C(last))
        rC = _run(ncC, host_C(x_tok, l, inp, yaT, ybpT, ycT, gT, last))
        x_tok = np.ascontiguousarray(np.concatenate([np.asarray(r["xo"]).T for r in rC], axis=0).astype(np.float32))
    return x_tok.reshape(2, SEQ, D)


def build_B():
    nc = bass.Bass("TRN2", target_bir_lowering=False)
    with ExitStack() as st:
        st.enter_context(nc.allow_low_precision("bf16 matmul operands, fp32 accumulation"))
        C = Ctx(nc, st)
        P = Prog(nc, st)
        P.begin("b2")
        emit_B2F(nc, C, P, merged=True)
        P.end()
        P.begin("b1")
        emit_B1(nc, C, P, merged=True)
        P.end()
        P.replay(["b2", "b1"])
        P.finish()
    return nc


TS = 8
NM = SEQ // TS
J2 = 64
NC2 = NM // J2


def emit_B2F(nc, C, P, merged=False):
    uT = C.din("uT", [128, NT], BF16)
    lam_re = C.din("lam_re", [128, 4])
    lam_im = C.din("lam_im", [128, 4])
    log_dt = C.din("log_dt", [128, 4])
    b_re = C.din("b_re", [128, 4, 16])
    b_im = C.din("b_im", [128, 4, 16])
    c_reT = C.din("c_reT", [128, 4, 16])
    c_imT = C.din("c_imT", [128, 4, 16])
    dvec = C.din("dvec", [128, 1])
    yp = C.dout("yp", [128, NT], BF16)
    f4 = lambda n: C.sb(n, [128, 4], F32)
    u_sb = C.sb("u_sb", [128, NT], BF16)
    y_sb = C.sb("y_sb", [128, NT], BF16)
    lre, lim, ldt, dtt, rho, th, th8, nre, den, kre, kim, q0, q1, q2 = [f4("s4_%d" % i) for i in range(14)]
    kidx = C.sb("kidx", [128, 16], F32)
    csk = [C.sb("csk%d" % i, [128, 4, 16], F32) for i in range(2)]
    magk = C.sb("magk", [128, 4, 16], F32)
    PW = [C.sb("PW%d" % i, [128, 4, 16], F32) for i in range(2)]
    PWn = [C.sb("PWn%d" % i, [128, 4, 16], F32) for i in range(2)]
    bre = C.sb("bre", [128, 4, 16], F32)
    bim = C.sb("bim", [128, 4, 16], F32)
    cre = C.sb("cre", [128, 4, 16], F32)
    cim = C.sb("cim", [128, 4, 16], F32)
    bbre = C.sb("bbre", [128, 4, 16], F32)
    bbim = C.sb("bbim", [128, 4, 16], F32)
    bt0 = C.sb("bt0", [128, 16], F32)
    d_sb = C.sb("d_sb", [128, 1], F32)
    ident = C.sb("ident", [128, 128], F32)
    R = C.sb("R", [128, 8192], F32)
    pads = [R[:, 4096 * i:4096 * (i + 1)].rearrange("p (t g c) -> p t g c", t=TS, g=4) for i in range(2)]
    Cpf = [C.sb("Cpf%d" % i, [128, 4, 128], F32) for i in range(2)]
    WE = [C.sb("WE%d" % i, [128, TS, 4, 128], BF16) for i in range(2)]
    WC = [C.sb("WC%d" % i, [128, TS, 4, 128], BF16) for i in range(2)]
    Kmat = C.sb("Kmat", [128, TS, 128], BF16)
    jidx = C.sb("jidx", [128, 512], F32)
    ang = R[:, 0:512]
    kf = R[:, 512:1024]
    ki = R[:, 1024:1536].bitcast(I32)
    cs2 = [C.sb("cs2_%d" % i, [128, 4, NM], F32) for i in range(2)]
    m02 = C.sb("m02", [128, NM], BF16)
    rf2 = C.sb("rf2", [128, 4, NM], F32)
    Ad = [C.sb("Ad%d" % i, [128, 4, 4], F32) for i in range(3)]
    sq_re, sq_im = f4("sq_re"), f4("sq_im")
    a8imn, s63n = f4("a8imn"), f4("s63n")
    bu = [R[:, NM * i:NM * (i + 1)] for i in range(2)]
    tt = [R[:, NM * (2 + i):NM * (3 + i)] for i in range(4)]
    w2 = [R[:, NM * (6 + i):NM * (7 + i)] for i in range(2)]
    z2 = [R[:, NM * (8 + i):NM * (9 + i)] for i in range(2)]
    cst = [C.sb("cst%d" % i, [128, NC2], F32) for i in range(7)]
    Sprev = C.sb("Sprev", [128, 4, 2, NM], BF16)
    psb = [C.ps("psb%d" % i, [128, 512]) for i in range(2)]
    psy = C.ps("psy", [128, 512])
    pst = psy

    du, dy, dpre, dtab, dS = [Dep() for _ in range(5)]
    dbu = [Dep(), Dep()]
    dtq = [Dep() for _ in range(4)]
    dw = [Dep(), Dep()]
    dz = [Dep(), Dep()]
    dcst = Dep()
    dpsb = [Dep(), Dep()]
    dpsy = Dep()
    PT = [dpre, dtab]

    def V(fn, r, wr, eng="dve"):
        P.op(eng, fn, reads=r, writes=wr)

    def A(fn, r, wr):
        P.op("act", fn, reads=r, writes=wr)

    P.dma("sp", u_sb[:, 0:SEQ], uT[:, 0:SEQ], writes=[du])
    P.dma("sp", u_sb[:, SEQ:NT], uT[:, SEQ:NT], writes=[du])
    for t_, src in ((lre, lam_re), (lim, lam_im), (ldt, log_dt), (bre, b_re), (bim, b_im), (cre, c_reT),
                    (cim, c_imT), (d_sb, dvec)):
        P.dma("sp", t_[:], src, writes=[dpre])
    V(lambda e: e.memset(ang[:, 0:128], 1.0), [], PT, "pool")
    V(lambda e: e.affine_select(out=ident[:], in_=ang[:, 0:128], pattern=[[-1, 128]], compare_op=ALU.is_equal,
                                fill=0.0, base=0, channel_multiplier=1), PT, PT, "pool")
    V(lambda e: e.iota(jidx[:], pattern=[[0, NC2], [1, J2]], base=0, channel_multiplier=0,
                       allow_small_or_imprecise_dtypes=True), [], PT, "pool")
    V(lambda e: e.iota(kidx[:], pattern=[[1, 16]], base=0, channel_multiplier=0,
                       allow_small_or_imprecise_dtypes=True), [], PT, "pool")
    V(lambda e: e.memset(m02[:], 1.0), [], PT, "pool")
    V(lambda e: e.memset(m02[:].rearrange("p (c j) -> p c j", j=J2)[:, :, 0:1], 0.0), PT, PT, "pool")
    for ri in range(2):
        V(lambda e, ri=ri: e.memset(Cpf[ri][:], 0.0), [], PT, "pool")
        V(lambda e, ri=ri: e.memset(WC[ri][:], 0.0), [], PT, "pool")
    V(lambda e: e.memset(Sprev[:, :, :, 0:1], 0.0), [], [dS], "pool")

    def reduce_angle(x, n):
        V(lambda e: e.tensor_scalar(out=kf[:, :n], in0=x, scalar1=1.0 / TWO_PI, scalar2=None, op0=ALU.mult), PT, PT)
        V(lambda e: e.tensor_copy(out=ki[:, :n], in_=kf[:, :n]), PT, PT)
        V(lambda e: e.tensor_copy(out=kf[:, :n], in_=ki[:, :n]), PT, PT)
        V(lambda e: e.scalar_tensor_tensor(out=x, in0=kf[:, :n], scalar=-TWO_PI, in1=x, op0=ALU.mult, op1=ALU.add),
          PT, PT)
        V(lambda e: e.tensor_scalar(out=kf[:, :n], in0=x, scalar1=PI, scalar2=-TWO_PI, op0=ALU.is_gt, op1=ALU.mult),
          PT, PT)
        V(lambda e: e.tensor_tensor(out=x, in0=x, in1=kf[:, :n], op=ALU.add), PT, PT)
        V(lambda e: e.tensor_scalar(out=kf[:, :n], in0=x, scalar1=-PI, scalar2=TWO_PI, op0=ALU.is_lt, op1=ALU.mult),
          PT, PT)
        V(lambda e: e.tensor_tensor(out=x, in0=x, in1=kf[:, :n], op=ALU.add), PT, PT)

    def sincos_table(dst, gp, idx_ap, n, thv):
        for k_, shift in ((1, 0.0), (0, PI / 2)):
            V(lambda e, shift=shift: e.tensor_scalar(out=ang[:, :n], in0=idx_ap, scalar1=thv[:, gp:gp + 1],
                                                     scalar2=shift, op0=ALU.mult, op1=ALU.add), PT, PT)
            reduce_angle(ang[:, :n], n)
            A(lambda e, k_=k_: e.activation(out=dst[k_][:, gp, :n], in_=ang[:, :n], func=AF.Sin), PT, PT)

    A(lambda e: e.activation(out=dtt[:], in_=ldt[:], func=AF.Exp), PT, PT)
    V(lambda e: e.tensor_tensor(out=rho[:], in0=lre[:], in1=dtt[:], op=ALU.mult), PT, PT)
    V(lambda e: e.tensor_tensor(out=th[:], in0=lim[:], in1=dtt[:], op=ALU.mult), PT, PT)
    V(lambda e: e.tensor_scalar(out=th8[:], in0=th[:], scalar1=float(TS), scalar2=None, op0=ALU.mult), PT, PT)
    for gp in range(4):
        sincos_table(csk, gp, kidx[:], 16, th)
        (lambda gp: A(lambda e: e.activation(out=magk[:, gp, :], in_=kidx[:], func=AF.Exp, scale=rho[:, gp:gp + 1]),
                      PT, PT))(gp)
    for ri in range(2):
        V(lambda e, ri=ri: e.tensor_tensor(out=PW[ri][:], in0=magk[:], in1=csk[ri][:], op=ALU.mult), PT, PT)
        V(lambda e, ri=ri: e.tensor_scalar(out=PWn[ri][:], in0=PW[ri][:], scalar1=-1.0, scalar2=None, op0=ALU.mult),
          PT, PT)
    abre, abim = PW[0][:, :, 1], PW[1][:, :, 1]
    V(lambda e: e.tensor_scalar(out=nre[:], in0=abre, scalar1=-1.0, scalar2=None, op0=ALU.add), PT, PT)
    V(lambda e: e.tensor_tensor(out=q0[:], in0=lre[:], in1=lre[:], op=ALU.mult), PT, PT)
    V(lambda e: e.tensor_tensor(out=q1[:], in0=lim[:], in1=lim[:], op=ALU.mult), PT, PT)
    V(lambda e: e.tensor_tensor(out=den[:], in0=q0[:], in1=q1[:], op=ALU.add), PT, PT)
    V(lambda e: e.reciprocal(out=den[:], in_=den[:]), PT, PT)
    V(lambda e: e.tensor_tensor(out=q0[:], in0=nre[:], in1=lre[:], op=ALU.mult), PT, PT)
    V(lambda e: e.tensor_tensor(out=q1[:], in0=abim, in1=lim[:], op=ALU.mult), PT, PT)
    V(lambda e: e.tensor_tensor(out=q0[:], in0=q0[:], in1=q1[:], op=ALU.add), PT, PT)
    V(lambda e: e.tensor_tensor(out=kre[:], in0=q0[:], in1=den[:], op=ALU.mult), PT, PT)
    V(lambda e: e.tensor_tensor(out=q0[:], in0=abim, in1=lre[:], op=ALU.mult), PT, PT)
    V(lambda e: e.tensor_tensor(out=q1[:], in0=nre[:], in1=lim[:], op=ALU.mult), PT, PT)
    V(lambda e: e.tensor_tensor(out=q0[:], in0=q0[:], in1=q1[:], op=ALU.subtract), PT, PT)
    V(lambda e: e.tensor_tensor(out=kim[:], in0=q0[:], in1=den[:], op=ALU.mult), PT, PT)

    def bbar(gp):
        g1 = slice(gp, gp + 1)
        V(lambda e: e.tensor_scalar(out=bt0[:], in0=bim[:, gp, :], scalar1=kim[:, g1], scalar2=None, op0=ALU.mult), PT, PT)
        V(lambda e: e.scalar_tensor_tensor(out=bbre[:, gp, :], in0=bre[:, gp, :], scalar=kre[:, g1], in1=bt0[:],
                                           op0=ALU.mult, op1=ALU.subtract), PT, PT)
        V(lambda e: e.tensor_scalar(out=bt0[:], in0=bre[:, gp, :], scalar1=kim[:, g1], scalar2=None, op0=ALU.mult), PT, PT)
        V(lambda e: e.scalar_tensor_tensor(out=bbim[:, gp, :], in0=bim[:, gp, :], scalar=kre[:, g1], in1=bt0[:],
                                           op0=ALU.mult, op1=ALU.add), PT, PT)
    for gp in range(4):
        bbar(gp)

    for gp in range(4):
        sincos_table(cs2, gp, jidx[:], NM, th8)
        (lambda gp: A(lambda e: e.activation(out=rf2[:, gp, :], in_=m02[:], func=AF.Copy, scale=magk[:, gp, 8:9]),
                      PT, PT))(gp)
    V(lambda e: e.tensor_scalar(out=a8imn[:], in0=PW[1][:, :, 8], scalar1=-1.0, scalar2=None, op0=ALU.mult), PT, PT)
    V(lambda e: e.tensor_scalar(out=s63n[:], in0=cs2[1][:, :, J2 - 1], scalar1=-1.0, scalar2=None, op0=ALU.mult), PT, PT)

    for ri in range(2):
        V(lambda e, ri=ri: e.memset(pads[ri], 0.0), PT, PT, "pool")

    def cscale(out_re, out_imn, xr, xi, r, gp, k, neg_im):
        pr, pi_ = PW[0][r, gp, k:k + 1], PW[1][r, gp, k:k + 1]
        prn, pin = PWn[0][r, gp, k:k + 1], PWn[1][r, gp, k:k + 1]
        V(lambda e: e.tensor_scalar(out=bt0[r, :], in0=xi, scalar1=pi_, scalar2=None, op0=ALU.mult), PT, PT)
        V(lambda e: e.scalar_tensor_tensor(out=out_re, in0=xr, scalar=pr, in1=bt0[r, :], op0=ALU.mult,
                                           op1=ALU.subtract), PT, PT)
        if neg_im:
            V(lambda e: e.tensor_scalar(out=bt0[r, :], in0=xr, scalar1=pin, scalar2=None, op0=ALU.mult), PT, PT)
            V(lambda e: e.scalar_tensor_tensor(out=out_imn, in0=xi, scalar=prn, in1=bt0[r, :], op0=ALU.mult,
                                               op1=ALU.add), PT, PT)
        else:
            V(lambda e: e.tensor_scalar(out=bt0[r, :], in0=xr, scalar1=pi_, scalar2=None, op0=ALU.mult), PT, PT)
            V(lambda e: e.scalar_tensor_tensor(out=out_imn, in0=xi, scalar=pr, in1=bt0[r, :], op0=ALU.mult,
                                               op1=ALU.add), PT, PT)

    for gp in range(4):
        for j in range(2):
            r = slice(64 * j, 64 * j + 64)
            c0 = 16 * (2 * gp + j)
            cc = slice(c0, c0 + 16)
            for tau in range(TS):
                cscale(pads[0][r, tau, gp, cc], pads[1][r, tau, gp, cc], bbre[r, gp, :], bbim[r, gp, :], r, gp, tau, False)
                cscale(WC[0][r, tau, gp, cc], WC[1][r, tau, gp, cc], cre[r, gp, :], cim[r, gp, :], r, gp, tau + 1, True)
            (lambda r=r, gp=gp, cc=cc: (
                V(lambda e: e.tensor_copy(out=Cpf[0][r, gp, cc], in_=cre[r, gp, :]), PT, PT),
                V(lambda e: e.tensor_scalar(out=Cpf[1][r, gp, cc], in0=cim[r, gp, :], scalar1=-1.0, scalar2=None,
                                            op0=ALU.mult), PT, PT)))()

    def build_we(ri, tau):
        for gp in range(4):
            P.op("pe", lambda e, gp=gp: e.transpose(pst[:, gp * 128:(gp + 1) * 128], pads[ri][:, tau, gp, :], ident[:]),
                 reads=PT, writes=[dpsy])
        V(lambda e: e.tensor_copy(out=WE[ri][:, TS - 1 - tau, :, :].rearrange("p g q -> p (g q)"), in_=pst[:]),
          [dpsy], PT)
    for ri in range(2):
        for tau in range(TS):
            build_we(ri, tau)

    def build_k(t0):
        for tau in range(t0, t0 + 4):
            n_ = 0
            for gp in range(4):
                for ri in range(2):
                    P.op("pe", lambda e, tau=tau, gp=gp, ri=ri, n_=n_: e.matmul(
                        pst[:, (tau - t0) * 128:(tau - t0 + 1) * 128], pads[ri][:, tau, gp, :], Cpf[ri][:, gp, :],
                        start=(n_ == 0), stop=(n_ == 7), skip_group_check=True), reads=PT, writes=[dpsy])
                    n_ += 1
        V(lambda e: e.tensor_copy(out=Kmat[:, t0:t0 + 4, :].rearrange("p t c -> p (t c)"), in_=pst[:]), [dpsy], PT)
    build_k(0)
    build_k(4)

    def cmul(ore, oim, are, aim, bre_, bim_):
        V(lambda e: e.tensor_tensor(out=q0[:], in0=are, in1=bre_, op=ALU.mult), PT, PT)
        V(lambda e: e.tensor_tensor(out=q1[:], in0=aim, in1=bim_, op=ALU.mult), PT, PT)
        V(lambda e: e.tensor_tensor(out=q2[:], in0=are, in1=bim_, op=ALU.mult), PT, PT)
        V(lambda e: e.tensor_tensor(out=den[:], in0=aim, in1=bre_, op=ALU.mult), PT, PT)
        V(lambda e: e.tensor_tensor(out=ore, in0=q0[:], in1=q1[:], op=ALU.subtract), PT, PT)
        V(lambda e: e.tensor_tensor(out=oim, in0=q2[:], in1=den[:], op=ALU.add), PT, PT)
    V(lambda e: e.tensor_copy(out=sq_re[:], in_=PW[0][:, :, 8]), PT, PT)
    V(lambda e: e.tensor_copy(out=sq_im[:], in_=PW[1][:, :, 8]), PT, PT)
    for _ in range(6):
        cmul(sq_re[:], sq_im[:], sq_re[:], sq_im[:], sq_re[:], sq_im[:])
    for k in range(3):
        (lambda k: (V(lambda e: e.tensor_copy(out=Ad[0][:, :, k], in_=sq_re[:]), PT, PT),
                    V(lambda e: e.tensor_copy(out=Ad[1][:, :, k], in_=sq_im[:]), PT, PT),
                    V(lambda e: e.tensor_scalar(out=Ad[2][:, :, k], in0=sq_im[:], scalar1=-1.0, scalar2=None,
                                                op0=ALU.mult), PT, PT)))(k)
        if k < 2:
            cmul(sq_re[:], sq_im[:], sq_re[:], sq_im[:], sq_re[:], sq_im[:])

    V(lambda e: e.memset(bt0[:], 0.0), PT, PT)

    def uview(b, s):
        return u_sb[:, b * SEQ:(b + 1) * SEQ].rearrange("p (m s) -> p m s", s=TS)[:, :, s]

    def e_pass(gp, b):
        for ri in range(2):
            for s in range(TS):
                P.op("pe", lambda e, ri=ri, s=s: e.matmul(psb[ri][:], WE[ri][:, s, gp, :], uview(b, s),
                                                          start=(s == 0), stop=(s == TS - 1)),
                     reads=[du] + PT, writes=[dpsb[ri]])
            A(lambda e, ri=ri: e.activation(out=bu[ri][:], in_=psb[ri][:], func=AF.Copy), [dpsb[ri]], [dbu[ri]])
        c_, s_ = cs2[0][:, gp, :], cs2[1][:, gp, :]
        V(lambda e: e.tensor_tensor(out=tt[0][:], in0=bu[0][:], in1=c_, op=ALU.mult), [dbu[0]] + PT, [dtq[0]])
        V(lambda e: e.tensor_tensor(out=tt[1][:], in0=bu[1][:], in1=s_, op=ALU.mult), [dbu[1]] + PT, [dtq[1]])
        V(lambda e: e.tensor_tensor(out=w2[0][:], in0=tt[0][:], in1=tt[1][:], op=ALU.add), [dtq[0], dtq[1]], [dw[0]])
        V(lambda e: e.tensor_tensor(out=tt[2][:], in0=bu[1][:], in1=c_, op=ALU.mult), [dbu[1]] + PT, [dtq[2]], "pool")
        V(lambda e: e.tensor_tensor(out=tt[3][:], in0=bu[0][:], in1=s_, op=ALU.mult), [dbu[0]] + PT, [dtq[3]], "pool")
        V(lambda e: e.tensor_tensor(out=w2[1][:], in0=tt[2][:], in1=tt[3][:], op=ALU.subtract), [dtq[2], dtq[3]],
          [dw[1]], "pool")

    def scans(gp, extra):
        for i in range(2):
            V(lambda e, i=i: e.tensor_tensor_scan(out=z2[i][:], data0=rf2[:, gp, :], data1=w2[i][:], initial=0.0,
                                                  op0=ALU.mult, op1=ALU.add), [dw[i]] + PT + extra, [dz[i]])

    def hs_step(gp, k, cur):
        d_ = 1 << k
        nxt = 2 - cur
        ar, ai, ain = Ad[0][:, gp, k:k + 1], Ad[1][:, gp, k:k + 1], Ad[2][:, gp, k:k + 1]
        sre, sim = cst[cur], cst[cur + 1]
        nre_, nim_ = cst[nxt], cst[nxt + 1]
        CS = [dcst]
        n = NC2
        V(lambda e: e.tensor_copy(out=nre_[:, 0:d_], in_=sre[:, 0:d_]), CS, CS)
        V(lambda e: e.tensor_copy(out=nim_[:, 0:d_], in_=sim[:, 0:d_]), CS, CS)
        V(lambda e: e.scalar_tensor_tensor(out=cst[4][:, d_:n], in0=sre[:, 0:n - d_], scalar=ar, in1=sre[:, d_:n],
                                           op0=ALU.mult, op1=ALU.add), CS + PT, CS)
        V(lambda e: e.scalar_tensor_tensor(out=nre_[:, d_:n], in0=sim[:, 0:n - d_], scalar=ain, in1=cst[4][:, d_:n],
                                           op0=ALU.mult, op1=ALU.add), CS + PT, CS)
        V(lambda e: e.scalar_tensor_tensor(out=cst[4][:, d_:n], in0=sim[:, 0:n - d_], scalar=ar, in1=sim[:, d_:n],
                                           op0=ALU.mult, op1=ALU.add), CS + PT, CS)
        V(lambda e: e.scalar_tensor_tensor(out=nim_[:, d_:n], in0=sre[:, 0:n - d_], scalar=ai, in1=cst[4][:, d_:n],
                                           op0=ALU.mult, op1=ALU.add), CS + PT, CS)
        return nxt

    def carry(gp):
        g1 = slice(gp, gp + 1)
        CS = [dcst]
        zE = [z2[i][:].rearrange("p (c j) -> p c j", j=J2)[:, :, J2 - 1] for i in range(2)]
        c63 = cs2[0][:, gp, J2 - 1:J2]
        s63 = cs2[1][:, gp, J2 - 1:J2]
        V(lambda e: e.tensor_scalar(out=cst[2][:], in0=zE[0], scalar1=c63, scalar2=None, op0=ALU.mult), [dz[0]] + PT, CS)
        V(lambda e: e.scalar_tensor_tensor(out=cst[0][:], in0=zE[1], scalar=s63n[:, g1], in1=cst[2][:], op0=ALU.mult,
                                           op1=ALU.add), [dz[1]] + PT + CS, CS)
        V(lambda e: e.tensor_scalar(out=cst[2][:], in0=zE[0], scalar1=s63, scalar2=None, op0=ALU.mult),
          [dz[0]] + PT + CS, CS)
        V(lambda e: e.scalar_tensor_tensor(out=cst[1][:], in0=zE[1], scalar=c63, in1=cst[2][:], op0=ALU.mult,
                                           op1=ALU.add), [dz[1]] + PT + CS, CS)
        cur = 0
        for k in range(3):
            cur = hs_step(gp, k, cur)
        Sre, Sim = cst[cur], cst[cur + 1]
        n1 = NC2 - 1
        w0 = [w2[i][:].rearrange("p (c j) -> p c j", j=J2)[:, 1:NC2, 0] for i in range(2)]
        a8r, a8i = PW[0][:, gp, 8:9], PW[1][:, gp, 8:9]
        V(lambda e: e.scalar_tensor_tensor(out=cst[5][:, 0:n1], in0=Sre[:, 0:n1], scalar=a8r, in1=w0[0], op0=ALU.mult,
                                           op1=ALU.add), CS + PT + [dw[0]], CS)
        V(lambda e: e.scalar_tensor_tensor(out=w0[0], in0=Sim[:, 0:n1], scalar=a8imn[:, g1], in1=cst[5][:, 0:n1],
                                           op0=ALU.mult, op1=ALU.add), CS + PT, [dw[0]])
        V(lambda e: e.scalar_tensor_tensor(out=cst[6][:, 0:n1], in0=Sim[:, 0:n1], scalar=a8r, in1=w0[1], op0=ALU.mult,
                                           op1=ALU.add), CS + PT + [dw[1]], CS)
        V(lambda e: e.scalar_tensor_tensor(out=w0[1], in0=Sre[:, 0:n1], scalar=a8i, in1=cst[6][:, 0:n1],
                                           op0=ALU.mult, op1=ALU.add), CS + PT, [dw[1]])

    def demod(gp):
        n1 = NM - 1
        c_, s_ = cs2[0][:, gp, 0:n1], cs2[1][:, gp, 0:n1]
        V(lambda e: e.tensor_tensor(out=tt[0][:, 0:n1], in0=z2[0][:, 0:n1], in1=c_, op=ALU.mult), [dz[0]] + PT, [dtq[0]])
        V(lambda e: e.tensor_tensor(out=tt[1][:, 0:n1], in0=z2[1][:, 0:n1], in1=s_, op=ALU.mult), [dz[1]] + PT, [dtq[1]])
        V(lambda e: e.tensor_tensor(out=Sprev[:, gp, 0, 1:NM], in0=tt[0][:, 0:n1], in1=tt[1][:, 0:n1], op=ALU.subtract),
          [dtq[0], dtq[1]], [dS])
        V(lambda e: e.tensor_tensor(out=tt[2][:, 0:n1], in0=z2[0][:, 0:n1], in1=s_, op=ALU.mult), [dz[0]] + PT, [dtq[2]],
          "pool")
        V(lambda e: e.tensor_tensor(out=tt[3][:, 0:n1], in0=z2[1][:, 0:n1], in1=c_, op=ALU.mult), [dz[1]] + PT, [dtq[3]],
          "pool")
        V(lambda e: e.tensor_tensor(out=Sprev[:, gp, 1, 1:NM], in0=tt[2][:, 0:n1], in1=tt[3][:, 0:n1], op=ALU.add),
          [dtq[2], dtq[3]], [dS], "pool")

    def y_out(b, s):
        mm = []
        for sp_ in range(s + 1):
            mm.append((Kmat[:, s - sp_, :], uview(b, sp_), [du] + PT))
        for gp in range(4):
            for ri in range(2):
                mm.append((WC[ri][:, s, gp, :], Sprev[:, gp, ri, :], [dS] + PT))
        for n_, (l_, r_, dd) in enumerate(mm):
            P.op("pe", lambda e, l_=l_, r_=r_, n_=n_: e.matmul(psy[:], l_, r_, start=(n_ == 0), stop=(n_ == len(mm) - 1)),
                 reads=dd, writes=[dpsy])
        yv = y_sb[:, b * SEQ:(b + 1) * SEQ].rearrange("p (m s) -> p m s", s=TS)[:, :, s]
        V(lambda e: e.scalar_tensor_tensor(out=yv, in0=uview(b, s), scalar=d_sb[:, 0:1], in1=psy[:], op0=ALU.mult,
                                           op1=ALU.add), [du, dpsy] + PT, [dy])

    for b in range(2):
        for gp in range(4):
            e_pass(gp, b)
            scans(gp, [])
            carry(gp)
            scans(gp, [dcst])
            demod(gp)
        for s in range(TS):
            y_out(b, s)
    P.dma("sp", yp[:, 0:SEQ], y_sb[:, 0:SEQ], reads=[dy])
    P.dma("sp", yp[:, SEQ:NT], y_sb[:, SEQ:NT], reads=[dy])


def build_B2F():
    nc = bass.Bass("TRN2", target_bir_lowering=False)
    with ExitStack() as st:
        st.enter_context(nc.allow_low_precision("bf16 matmul operands, fp32 accumulation"))
        C = Ctx(nc, st)
        P = Prog(nc, st)
        emit_B2F(nc, C, P)
        P.finish()
    return nc
```

```python
import numpy as np
from contextlib import ExitStack
import ml_dtypes
import concourse.bass as bass
import concourse.mybir as mybir
from concourse.bass_utils import run_bass_kernel_spmd

F32 = mybir.dt.float32
BF16 = mybir.dt.bfloat16
AF = mybir.ActivationFunctionType
ALU = mybir.AluOpType
AX = mybir.AxisListType
NPBF = ml_dtypes.bfloat16

NCORES = 8
TOK = 1024
D = 2048
EPS = 1e-6
QSCALE = 128 ** -0.5


SAME_SYNC = {"pe": False, "act": True, "dve": True, "pool": True, "sp": True}


class Dep:
    __slots__ = ("w", "r")

    def __init__(self):
        self.w = None
        self.r = {}


class Prog:
    ENGS = ("pe", "act", "dve", "pool", "sp")

    def __init__(self, nc, stack, n_dma_sems=(("sp", 12), ("pool", 8), ("act", 4))):
        self.nc = nc
        self.stack = stack
        self.q = {e: [] for e in self.ENGS}
        self.sem = {e: stack.enter_context(nc.semaphore("s_" + e)) for e in ("pe", "act", "dve", "pool")}
        self.cnt = {e: 0 for e in self.sem}
        tot = sum(n for _, n in n_dma_sems)
        self.dsem = [stack.enter_context(nc.semaphore("d%d" % i)) for i in range(tot)]
        self.dcnt = [0] * tot
        self.dpool = {}
        b = 0
        for qn, n in n_dma_sems:
            self.dpool[qn] = [list(range(b, b + n)), 0]
            b += n
        self.seen = {e: {} for e in self.ENGS}
        self.same_sync = dict(SAME_SYNC)
        self._rec = None
        self._streams = {}

    def _semobj(self, key):
        return self.sem[key[1]] if key[0] == "e" else self.dsem[key[1]]

    def _collect(self, eng, reads, writes, extra=None):
        need = dict(extra or {})

        def req(k, v):
            if v > need.get(k, 0):
                need[k] = v
        for t in reads:
            if t.w:
                req(*t.w)
        for t in writes:
            if t.w:
                req(*t.w)
            for k, v in t.r.items():
                req(k, v)
        for k, v in need.items():
            if k == ("e", eng) and not self.same_sync[eng]:
                continue
            if self.seen[eng].get(k, 0) < v:
                self.seen[eng][k] = v
                s = self._semobj(k)
                self.q[eng].append(lambda e, s=s, v=v: e.wait_ge(s, v))

    def begin(self, name):
        self._rec = []
        self._streams[name] = self._rec

    def end(self):
        self._rec = None

    def replay(self, names):
        recs = [self._streams[n] for n in names]
        pos = [0] * len(recs)
        total = sum(len(r) for r in recs)
        for _ in range(total):
            best, bestv = None, None
            for i, r in enumerate(recs):
                if pos[i] < len(r):
                    v = (pos[i] + 0.5) / len(r)
                    if bestv is None or v < bestv:
                        best, bestv = i, v
            kind, args, kw = recs[best][pos[best]]
            pos[best] += 1
            getattr(self, kind)(*args, **kw)

    def op(self, eng, fn, reads=(), writes=()):
        if self._rec is not None:
            self._rec.append(("op", (eng, fn, tuple(reads), tuple(writes)), {}))
            return
        self._collect(eng, reads, writes)
        self.cnt[eng] += 1
        c = self.cnt[eng]
        key = ("e", eng)
        s = self.sem[eng]
        self.q[eng].append(lambda e, fn=fn, s=s: fn(e).then_inc(s, 1))
        for t in reads:
            if t.r.get(key, 0) < c:
                t.r[key] = c
        for t in writes:
            t.w = (key, c)
            t.r = {}

    def dma(self, queue, out, in_, reads=(), writes=(), in_fn=None, out_fn=None, **kw):
        if self._rec is not None:
            self._rec.append(("dma", (queue, out, in_, tuple(reads), tuple(writes), in_fn, out_fn), dict(kw)))
            return
        pl = self.dpool[queue]
        i = pl[0][pl[1] % len(pl[0])]
        pl[1] += 1
        extra = {}
        if self.dcnt[i] > 0:
            extra[("d", i)] = self.dcnt[i]
        self._collect(queue, reads, writes, extra)
        self.dcnt[i] += 16
        v = self.dcnt[i]
        key = ("d", i)
        s = self.dsem[i]
        self.q[queue].append(
            lambda e, s=s, out=out, in_=in_, kw=kw: e.dma_start(
                out=(out_fn() if out_fn else out), in_=(in_fn() if in_fn else in_), **kw).then_inc(s, 16))
        for t in reads:
            if t.r.get(key, 0) < v:
                t.r[key] = v
        for t in writes:
            t.w = (key, v)
            t.r = {}

    def raw(self, eng, fn, reads=(), writes=()):
        self._collect(eng, reads, writes)
        self.q[eng].append(lambda e, fn=fn: fn(e))

    def coll(self, queue, fn, reads=(), writes=()):
        pl = self.dpool[queue]
        i = pl[0][pl[1] % len(pl[0])]
        pl[1] += 1
        extra = {}
        if self.dcnt[i] > 0:
            extra[("d", i)] = self.dcnt[i]
        self._collect(queue, reads, writes, extra)
        self.dcnt[i] += 16
        v = self.dcnt[i]
        key = ("d", i)
        s = self.dsem[i]
        self.q[queue].append(lambda e, s=s, fn=fn: fn(e).then_inc(s, 16))
        for t in reads:
            if t.r.get(key, 0) < v:
                t.r[key] = v
        for t in writes:
            t.w = (key, v)
            t.r = {}

    def finish(self):
        for i, v in enumerate(self.dcnt):
            if v > 0 and self.seen["sp"].get(("d", i), 0) < v:
                s = self.dsem[i]
                self.q["sp"].append(lambda e, s=s, v=v: e.wait_ge(s, v))
        for e_, c in self.cnt.items():
            if c > 0:
                s = self.sem[e_]
                self.q["sp"].append(lambda e, s=s, c=c: e.wait_ge(s, c))
        q = self.q
        with self.nc.Block() as block:
            @block.tensor
            def _(e):
                for f in q["pe"]:
                    f(e)

            @block.scalar
            def _(e):
                for f in q["act"]:
                    f(e)

            @block.vector
            def _(e):
                for f in q["dve"]:
                    f(e)

            @block.gpsimd
            def _(e):
                for f in q["pool"]:
                    f(e)

            @block.sync
            def _(e):
                for f in q["sp"]:
                    f(e)


class Ctx:
    def __init__(self, nc, st):
        self.nc, self.st = nc, st

    def sb(self, name, shape, dt):
        return self.st.enter_context(self.nc.sbuf_tensor(name, shape, dt))

    def ps(self, name, shape, dt=F32):
        return self.st.enter_context(self.nc.psum_tensor(name, shape, dt))

    def din(self, name, shape, dt=F32):
        return self.nc.dram_tensor(name, list(shape), dt, kind="ExternalInput").ap()

    def dout(self, name, shape, dt=F32):
        return self.nc.dram_tensor(name, list(shape), dt, kind="ExternalOutput").ap()


def emit_rmsnorm(P, C, x_sb, dx, g_sb, dg, hT, dh, ones_bf, dones, sq, dsq, ps, dps, rstd, drstd, epsb, deps, ntok):
    for kt in range(16):
        i = kt % 2
        P.op("act", lambda e, kt=kt, i=i: e.activation(out=sq[i][:], in_=x_sb[:, kt, :], func=AF.Square),
             reads=[dx], writes=[dsq[i]])
        for hf in range(ntok // 512):
            P.op("pe", lambda e, kt=kt, i=i, hf=hf: e.matmul(ps[:, hf * 512:(hf + 1) * 512], ones_bf[:],
                                                             sq[i][:, hf * 512:(hf + 1) * 512],
                                                             start=(kt == 0), stop=(kt == 15)),
                 reads=[dsq[i], dones], writes=[dps])
    P.op("act", lambda e: e.activation(out=rstd[:], in_=ps[:, :ntok], func=AF.Sqrt, bias=epsb[:], scale=1.0 / D),
         reads=[dps, deps], writes=[drstd])
    P.op("dve", lambda e: e.reciprocal(out=rstd[:], in_=rstd[:]), reads=[drstd], writes=[drstd])
    for kt in range(16):
        P.op("dve", lambda e, kt=kt: e.scalar_tensor_tensor(out=hT[:, kt, :], in0=x_sb[:, kt, :],
                                                           scalar=g_sb[:, kt:kt + 1], in1=rstd[:],
                                                           op0=ALU.mult, op1=ALU.mult),
             reads=[dx, dg, drstd], writes=[dh])


def build_A():
    nc = bass.Bass("TRN2", target_bir_lowering=False)
    with ExitStack() as st:
        st.enter_context(nc.allow_low_precision("bf16 matmul operands, fp32 accumulation"))
        C = Ctx(nc, st)
        xT = C.din("xT", [D, TOK])
        n1g = C.din("n1g", [128, 16])
        w_in = C.din("w_in", [D, 12288])
        bg = C.din("bg", [128, 48])
        gng = C.din("gng", [1, 1024])
        wsT = C.din("wsT", [8, 128, 128])
        bs = C.din("bs", [1, 1024])
        yaT = C.dout("yaT", [1024, TOK], BF16)
        s5T = C.dout("s5T", [1024, TOK], BF16)
        qT = C.dout("qT", [1024, TOK], BF16)
        kT = C.dout("kT", [1024, TOK], BF16)
        vtk = C.dout("vtk", [TOK, 1024], BF16)
        gT = C.dout("gT", [6144, TOK], BF16)

        P = Prog(nc, st)
        x_sb = C.sb("x_sb", [128, 16, TOK], F32)
        hT = C.sb("hT", [128, 16, TOK], BF16)
        g_sb = C.sb("g_sb", [128, 16], F32)
        bg_sb = C.sb("bg_sb", [128, 48], F32)
        gng_b = C.sb("gng_b", [128, 1024], F32)
        bs_b = C.sb("bs_b", [128, 1024], F32)
        ws_f = C.sb("ws_f", [128, 8, 128], F32)
        ws_b = C.sb("ws_b", [128, 8, 128], BF16)
        ones_bf = C.sb("ones_bf", [128, 128], BF16)
        epsb = C.sb("epsb", [128, 1], F32)
        sq = [C.sb("sq%d" % i, [128, TOK], BF16) for i in range(2)]
        rstd = C.sb("rstd", [128, TOK], F32)
        wt = [C.sb("wt%d" % i, [128, 16, 512], BF16) for i in range(2)]
        stg = [C.sb("stg%d" % i, [128, TOK], BF16) for i in range(2)]
        uT = C.sb("uT", [128, 8, TOK], BF16)
        vtok = C.sb("vtok", [128, 8, 1024], BF16)
        vn = C.sb("vn", [128, 1024], BF16)
        junk = sq[0]
        vss = C.sb("vss", [128, 8], F32)
        vrs = C.sb("vrs", [128, 8], F32)
        mixt = rstd
        ya_sb = C.sb("ya_sb", [128, 8, TOK], BF16)
        ps = [C.ps("ps%d" % i, [128, 1024]) for i in range(2)]
        psm = C.ps("psm", [128, 1024])

        dx, dh, dg, dbg, dgng, dbs, dwsf, dwsb, dones, deps = [Dep() for _ in range(10)]
        dsq = [Dep(), Dep()]
        drstd = Dep()
        dwt = [Dep(), Dep()]
        dstg = [Dep(), Dep()]
        dps = [Dep(), Dep()]
        dpsm, duT, dvtok, dvn, dvss, dvrs, dya = [Dep() for _ in range(7)]
        djunk = dsq[0]
        dmixt = drstd

        xv = xT.rearrange("(kt p) t -> p kt t", p=128)
        for i in range(4):
            P.dma("sp", x_sb[:, 4 * i:4 * i + 4, :], xv[:, 4 * i:4 * i + 4, :], writes=[dx])
        P.dma("sp", g_sb[:], n1g, writes=[dg])
        P.dma("sp", bg_sb[:], bg, writes=[dbg])
        P.dma("sp", gng_b[:], gng.partition_broadcast(128), writes=[dgng])
        P.dma("sp", bs_b[:], bs.partition_broadcast(128), writes=[dbs])
        P.dma("sp", ws_f[:], wsT.rearrange("g s t -> s g t"), writes=[dwsf])
        wv = w_in.rearrange("(kt p) c -> p kt c", p=128)

        def load_w(cb):
            P.dma("pool", wt[cb % 2][:], wv[:, :, cb * 512:(cb + 1) * 512], writes=[dwt[cb % 2]])
        load_w(0)
        P.op("dve", lambda e: e.memset(ones_bf[:], 1.0), writes=[dones])
        P.op("dve", lambda e: e.memset(epsb[:], EPS), writes=[deps])
        P.op("pool", lambda e: e.affine_select(out=ws_f[:], in_=ws_f[:], pattern=[[0, 8], [1, 128]],
                                               compare_op=ALU.is_ge, fill=0.0, base=0, channel_multiplier=-1),
             reads=[dwsf], writes=[dwsf])
        P.op("dve", lambda e: e.tensor_copy(out=ws_b[:], in_=ws_f[:]), reads=[dwsf], writes=[dwsb])

        emit_rmsnorm(P, C, x_sb, dx, g_sb, dg, hT, dh, ones_bf, dones, sq, dsq, ps[0], dps[0], rstd, drstd,
                     epsb, deps, TOK)

        pcount = [0]
        scount = [0]

        def form2(cb, epi):
            w = wt[cb % 2]
            for m in range(4):
                pi = pcount[0] % 2
                pcount[0] += 1
                for hf in range(2):
                    for kt in range(16):
                        P.op("pe", lambda e, w=w, m=m, hf=hf, kt=kt, pi=pi: e.matmul(
                            ps[pi][:, hf * 512:(hf + 1) * 512], w[:, kt, m * 128:(m + 1) * 128],
                            hT[:, kt, hf * 512:(hf + 1) * 512], start=(kt == 0), stop=(kt == 15)),
                            reads=[dwt[cb % 2], dh], writes=[dps[pi]])
                epi(cb * 4 + m, pi)

        def epi_store(func, dst, colbase, scale=1.0, bias_col=None):
            def epi(blk, pi):
                si = scount[0] % 2
                scount[0] += 1
                r0 = (blk - colbase) * 128
                if bias_col is None:
                    P.op("act", lambda e: e.activation(out=stg[si][:], in_=ps[pi][:], func=func, scale=scale),
                         reads=[dps[pi]], writes=[dstg[si]])
                else:
                    bc = bias_col(blk)
                    P.op("act", lambda e: e.activation(out=stg[si][:], in_=ps[pi][:], func=func,
                                                       bias=bg_sb[:, bc:bc + 1]),
                         reads=[dps[pi], dbg], writes=[dstg[si]])
                P.dma("sp", dst[r0:r0 + 128, :], stg[si][:], reads=[dstg[si]])
            return epi

        def epi_u(blk, pi):
            P.op("act", lambda e: e.activation(out=uT[:, blk, :], in_=ps[pi][:], func=AF.Gelu_apprx_tanh),
                 reads=[dps[pi]], writes=[duT])

        def form1(cb, epi):
            w = wt[cb % 2]
            for c in range(8):
                pi = pcount[0] % 2
                pcount[0] += 1
                for kt in range(16):
                    P.op("pe", lambda e, w=w, c=c, kt=kt, pi=pi: e.matmul(
                        ps[pi][:, 0:512], hT[:, kt, c * 128:(c + 1) * 128], w[:, kt, :],
                        start=(kt == 0), stop=(kt == 15)),
                        reads=[dwt[cb % 2], dh], writes=[dps[pi]])
                epi(cb, c, pi)

        def epi_vg(cb, c, pi):
            off = (cb - 2) * 512
            P.op("act", lambda e: e.activation(out=vtok[:, c, off:off + 512], in_=ps[pi][:, 0:512],
                                               func=AF.Gelu_apprx_tanh),
                 reads=[dps[pi]], writes=[dvtok])

        def epi_va(cb, c, pi):
            off = (cb - 10) * 512
            si = scount[0] % 2
            scount[0] += 1
            P.op("act", lambda e: e.activation(out=stg[si][:, 0:512], in_=ps[pi][:, 0:512], func=AF.Copy),
                 reads=[dps[pi]], writes=[dstg[si]])
            P.dma("sp", vtk[c * 128:(c + 1) * 128, off:off + 512], stg[si][:, 0:512], reads=[dstg[si]])

        def gmlp_finish():
            for c in range(8):
                P.op("act", lambda e, c=c: e.activation(out=junk[:], in_=vtok[:, c, :], func=AF.Square,
                                                        accum_out=vss[:, c:c + 1]),
                     reads=[dvtok], writes=[djunk, dvss])
            P.op("act", lambda e: e.activation(out=vrs[:], in_=vss[:], func=AF.Sqrt, bias=epsb[:], scale=1.0 / 1024),
                 reads=[dvss, deps], writes=[dvrs])
            P.op("dve", lambda e: e.reciprocal(out=vrs[:], in_=vrs[:]), reads=[dvrs], writes=[dvrs])
            for c in range(8):
                P.op("dve", lambda e, c=c: e.scalar_tensor_tensor(out=vn[:], in0=vtok[:, c, :],
                                                                  scalar=vrs[:, c:c + 1], in1=gng_b[:],
                                                                  op0=ALU.mult, op1=ALU.mult),
                     reads=[dvtok, dvrs, dgng], writes=[dvn])
                for g in range(8):
                    P.op("pe", lambda e, g=g: e.matmul(psm[:, g * 128:(g + 1) * 128], vn[:, g * 128:(g + 1) * 128],
                                                       ws_b[:, g, :], start=True, stop=True),
                         reads=[dvn, dwsb], writes=[dpsm])
                P.op("dve", lambda e: e.tensor_tensor(out=mixt[:], in0=psm[:], in1=bs_b[:], op=ALU.add),
                     reads=[dpsm, dbs], writes=[dmixt])
                P.op("pool", lambda e, c=c: e.tensor_tensor(
                    out=ya_sb[:, :, c * 128:(c + 1) * 128], in0=mixt[:].rearrange("p (g t) -> p g t", g=8),
                    in1=uT[:, :, c * 128:(c + 1) * 128], op=ALU.mult),
                    reads=[dmixt, duT], writes=[dya])
            P.dma("sp", yaT.rearrange("(g p) t -> p g t", p=128), ya_sb[:], reads=[dya])

        for cb in range(24):
            if cb + 1 < 24:
                load_w(cb + 1)
            if cb < 2:
                form2(cb, epi_u)
            elif cb < 4:
                form1(cb, epi_vg)
                if cb == 3:
                    gmlp_finish()
            elif cb < 6:
                form2(cb, epi_store(AF.Copy, s5T, 16))
            elif cb < 8:
                form2(cb, epi_store(AF.Copy, qT, 24, scale=QSCALE))
            elif cb < 10:
                form2(cb, epi_store(AF.Copy, kT, 32))
            elif cb < 12:
                form1(cb, epi_va)
            else:
                form2(cb, epi_store(AF.Sigmoid, gT, 48, bias_col=lambda blk: blk - 48))
        P.finish()
    return nc


def host_A(x_tok, l, inp):
    maps = []
    n1g = np.ascontiguousarray(inp["norm1_g"][l].reshape(16, 128).T)
    bgv = np.ascontiguousarray(inp["b_gate"][l].reshape(48, 128).T)
    gng = np.ascontiguousarray(inp["gm_norm_g"][l].reshape(1, 1024))
    wsT = np.ascontiguousarray(np.transpose(inp["gm_w_s"][l], (0, 2, 1)))
    bsv = np.ascontiguousarray(inp["gm_b_s"][l].reshape(1, 1024))
    w_in = np.ascontiguousarray(inp["w_in"][l])
    for c in range(NCORES):
        xT = np.ascontiguousarray(x_tok[c * TOK:(c + 1) * TOK, :].T)
        maps.append({"xT": xT, "n1g": n1g, "w_in": w_in, "bg": bgv, "gng": gng, "wsT": wsT, "bs": bsv})
    return maps


def build_C(last, debug=False):
    nc = bass.Bass("TRN2", target_bir_lowering=False)
    with ExitStack() as st:
        st.enter_context(nc.allow_low_precision("bf16 matmul operands, fp32 accumulation"))
        C = Ctx(nc, st)
        xT = C.din("xT", [D, TOK])
        yaT = C.din("yaT", [1024, TOK], BF16)
        ybpT = C.din("ybpT", [1024, TOK], BF16)
        ycT = C.din("ycT", [1024, TOK], BF16)
        gT = C.din("gT", [6144, TOK], BF16)
        w_glu = C.din("w_glu", [1024, 1024])
        b_glu = C.din("b_glu", [128, 8])
        w_br = C.din("w_br", [3, 1024, D])
        w_out = C.din("w_out", [D, D])
        n2g = C.din("n2g", [128, 16])
        w_m1 = C.din("w_m1", [D, 8192])
        w_m2 = C.din("w_m2", [8192, D])
        fing = C.din("fing", [128, 16])
        xo = C.dout("xo", [D, TOK])
        if debug:
            dbg_yb = C.dout("dbg_yb", [1024, TOK], BF16)
            dbg_mg = C.dout("dbg_mg", [D, TOK], BF16)
            dbg_x1 = C.dout("dbg_x1", [D, TOK])
            dbg_h2 = C.dout("dbg_h2", [D, TOK], BF16)

        P = Prog(nc, st)
        x_sb = C.sb("x_sb", [128, 16, TOK], F32)
        R2 = C.sb("R2", [128, 3, 8, TOK], BF16)
        R3 = C.sb("R3", [128, 16 * TOK], BF16)
        R4 = C.sb("R4", [128, 4, 8 * 512], BF16)
        gts = [C.sb("gts%d" % i, [128, TOK], BF16) for i in range(6)]
        acc = C.sb("acc", [128, TOK], F32)
        tmp = C.sb("tmp", [128, TOK], F32)
        stg = [C.sb("stg%d" % i, [128, TOK], BF16) for i in range(2)]
        rstd = C.sb("rstd", [128, TOK], F32)
        sq = stg
        g2_sb = C.sb("g2_sb", [128, 16], F32)
        gf_sb = C.sb("gf_sb", [128, 16], F32)
        bgl_sb = C.sb("bgl_sb", [128, 8], F32)
        ones_bf = C.sb("ones_bf", [128, 128], BF16)
        epsb = C.sb("epsb", [128, 1], F32)
        ps = [C.ps("ps%d" % i, [128, 1024]) for i in range(3)]

        dx, dR3, dg2, dgf, dbgl, dones, deps, dacc, dtmp, drstd = [Dep() for _ in range(10)]
        dR2 = [Dep(), Dep(), Dep()]
        dR4 = [Dep() for _ in range(4)]
        dgts = [Dep() for _ in range(6)]
        dstg = [Dep(), Dep()]
        dsq = dstg
        dps = [Dep() for _ in range(3)]

        gS = R3[:, 0:8 * TOK].rearrange("p (k t) -> p k t", k=8)
        mg = R3[:].rearrange("p (k t) -> p k t", k=16)
        aT = R2[:].rearrange("p a k t -> p (a k) t")
        w8 = [R4[:, i, :].rearrange("p (k c) -> p k c", k=8) for i in range(4)]
        w16 = [R4[:, 2 * i:2 * i + 2, :].rearrange("p a (k c) -> p (a k) c", k=8) for i in range(2)]

        xv = xT.rearrange("(kt p) t -> p kt t", p=128)
        P.dma("sp", R2[:, 1], ybpT.rearrange("(k p) t -> p k t", p=128), writes=[dR2[1]])
        P.dma("sp", bgl_sb[:], b_glu, writes=[dbgl])
        for i in range(4):
            P.dma("sp", x_sb[:, 4 * i:4 * i + 4, :], xv[:, 4 * i:4 * i + 4, :], writes=[dx])
        P.dma("sp", R2[:, 0], yaT.rearrange("(k p) t -> p k t", p=128), writes=[dR2[0]])
        P.dma("sp", R2[:, 2], ycT.rearrange("(k p) t -> p k t", p=128), writes=[dR2[2]])
        P.dma("sp", g2_sb[:], n2g, writes=[dg2])
        P.dma("sp", gf_sb[:], fing, writes=[dgf])
        P.op("dve", lambda e: e.memset(ones_bf[:], 1.0), writes=[dones])
        P.op("dve", lambda e: e.memset(epsb[:], EPS), writes=[deps])

        jobs = []
        glu_v = w_glu.rearrange("(kt p) c -> p kt c", p=128)
        for cb in range(2):
            jobs.append(("glu", cb, 8, glu_v[:, :, cb * 512:(cb + 1) * 512]))
        for cb in range(4):
            for n in range(3):
                jobs.append(("br", (cb, n), 8, w_br[n].rearrange("(kt p) c -> p kt c", p=128)[:, :, cb * 512:(cb + 1) * 512]))
        wo_v = w_out.rearrange("(kt p) c -> p kt c", p=128)
        for cb in range(4):
            jobs.append(("wo", cb, 16, wo_v[:, :, cb * 512:(cb + 1) * 512]))
        m1_v = w_m1.rearrange("(kt p) c -> p kt c", p=128)
        m2_v = w_m2.rearrange("(kt p) c -> p kt c", p=128)
        for fc in range(4):
            for fb in range(4):
                c0 = fc * 2048 + fb * 512
                jobs.append(("m1", (fc, fb), 16, m1_v[:, :, c0:c0 + 512]))
            for cb in range(4):
                jobs.append(("m2", (fc, cb), 16, m2_v[:, fc * 16:(fc + 1) * 16, cb * 512:(cb + 1) * 512]))
        slot8 = 0
        slots = []
        quarters = []
        for kind, key, nk, src in jobs:
            if nk == 8:
                s = slot8 % 4
                slot8 += 1
                slots.append((w8[s], [dR4[s]]))
                quarters.append([s])
            else:
                if slot8 % 2:
                    slot8 += 1
                s = (slot8 // 2) % 2
                slot8 += 2
                slots.append((w16[s], [dR4[2 * s], dR4[2 * s + 1]]))
                quarters.append([2 * s, 2 * s + 1])
        issued = [0]
        occupant = [None] * 4
        consumed = set()

        def prefetch(upto):
            while issued[0] < min(upto, len(jobs)):
                j = issued[0]
                if any(occupant[q] is not None and occupant[q] not in consumed for q in quarters[j]):
                    return
                for q in quarters[j]:
                    occupant[q] = j
                P.dma("pool", slots[j][0], jobs[j][3], writes=slots[j][1])
                issued[0] += 1

        def done(j):
            consumed.add(j)
            prefetch(j + 4)
        pc = [0]

        def mm_block(j, m, rhs, drhs, nkt):
            pi = pc[0] % 3
            pc[0] += 1
            prefetch(j + 1)
            assert issued[0] > j, "weight job %d not loadable (ring slot busy)" % j
            wv, wd = slots[j]
            for hf in range(2):
                for kt in range(nkt):
                    P.op("pe", lambda e, wv=wv, m=m, hf=hf, kt=kt, pi=pi: e.matmul(
                        ps[pi][:, hf * 512:(hf + 1) * 512], wv[:, kt, m * 128:(m + 1) * 128],
                        rhs[:, kt, hf * 512:(hf + 1) * 512], start=(kt == 0), stop=(kt == nkt - 1)),
                        reads=wd + drhs, writes=[dps[pi]])
            return pi

        jn = [0]
        prefetch(3)
        for k in range(8):
            P.op("act", lambda e, k=k: e.activation(out=gS[:, k, :], in_=R2[:, 1, k, :], func=AF.Gelu_apprx_tanh),
                 reads=[dR2[1]], writes=[dR3])
        sc = [0]
        for cb in range(2):
            j = jn[0]
            jn[0] += 1
            prefetch(j + 3)
            for m in range(4):
                blk = cb * 4 + m
                pi = mm_block(j, m, gS, [dR3], 8)
                si = sc[0] % 2
                sc[0] += 1
                P.op("act", lambda e, pi=pi, si=si, blk=blk: e.activation(
                    out=stg[si][:], in_=ps[pi][:], func=AF.Sigmoid, bias=bgl_sb[:, blk:blk + 1]),
                    reads=[dps[pi], dbgl], writes=[dstg[si]])
                P.op("dve", lambda e, si=si, blk=blk: e.tensor_tensor(
                    out=R2[:, 1, blk, :], in0=gS[:, blk, :], in1=stg[si][:], op=ALU.mult),
                    reads=[dR3, dstg[si]], writes=[dR2[1]])
            done(j)
        if debug:
            P.dma("sp", dbg_yb.rearrange("(k p) t -> p k t", p=128), R2[:, 1], reads=[dR2[1]])
        gc = [0]
        for cb in range(4):
            js = [jn[0], jn[0] + 1, jn[0] + 2]
            jn[0] += 3
            prefetch(js[2] + 2)
            for m in range(4):
                dt_ = cb * 4 + m
                for n in range(3):
                    gi = gc[0] % 6
                    gc[0] += 1
                    r0 = (n * 16 + dt_) * 128
                    P.dma("sp", gts[gi][:], gT[r0:r0 + 128, :], writes=[dgts[gi]])
                    pi = mm_block(js[n], m, R2[:, n], [dR2[n]], 8)
                    if n == 0:
                        P.op("dve", lambda e, pi=pi, gi=gi: e.tensor_tensor(out=acc[:], in0=ps[pi][:], in1=gts[gi][:],
                                                                            op=ALU.mult),
                             reads=[dps[pi], dgts[gi]], writes=[dacc])
                    else:
                        P.op("dve", lambda e, pi=pi, gi=gi: e.tensor_tensor(out=tmp[:], in0=ps[pi][:], in1=gts[gi][:],
                                                                            op=ALU.mult),
                             reads=[dps[pi], dgts[gi]], writes=[dtmp])
                        if n == 1:
                            P.op("dve", lambda e: e.tensor_tensor(out=acc[:], in0=acc[:], in1=tmp[:], op=ALU.add),
                                 reads=[dacc, dtmp], writes=[dacc])
                        else:
                            P.op("dve", lambda e, dt_=dt_: e.tensor_tensor(out=mg[:, dt_, :], in0=acc[:], in1=tmp[:],
                                                                          op=ALU.add),
                                 reads=[dacc, dtmp], writes=[dR3])
            for j_ in js:
                done(j_)
        if debug:
            P.dma("sp", dbg_mg.rearrange("(k p) t -> p k t", p=128), mg, reads=[dR3])
        for cb in range(4):
            j = jn[0]
            jn[0] += 1
            prefetch(j + 2)
            for m in range(4):
                dt_ = cb * 4 + m
                pi = mm_block(j, m, mg, [dR3], 16)
                P.op("dve", lambda e, pi=pi, dt_=dt_: e.tensor_tensor(out=x_sb[:, dt_, :], in0=x_sb[:, dt_, :],
                                                                      in1=ps[pi][:], op=ALU.add),
                     reads=[dps[pi], dx], writes=[dx])
            done(j)
        if debug:
            P.dma("sp", dbg_x1.rearrange("(k p) t -> p k t", p=128), x_sb[:], reads=[dx])
        emit_rmsnorm(P, C, x_sb, dx, g2_sb, dg2, mg, dR3, ones_bf, dones, sq, dsq, ps[0], dps[0], rstd, drstd,
                     epsb, deps, TOK)
        if debug:
            P.dma("sp", dbg_h2.rearrange("(k p) t -> p k t", p=128), mg, reads=[dR3])
        for fc in range(4):
            for fb in range(4):
                j = jn[0]
                jn[0] += 1
                prefetch(j + 2)
                for m in range(4):
                    ft = fb * 4 + m
                    pi = mm_block(j, m, mg, [dR3], 16)
                    P.op("act", lambda e, pi=pi: e.activation(out=tmp[:], in_=ps[pi][:], func=AF.Relu),
                         reads=[dps[pi]], writes=[dtmp])
                    P.op("dve", lambda e, ft=ft: e.tensor_tensor(out=aT[:, ft, :], in0=tmp[:], in1=tmp[:], op=ALU.mult),
                         reads=[dtmp], writes=dR2)
                done(j)
            for cb in range(4):
                j = jn[0]
                jn[0] += 1
                prefetch(j + 2)
                for m in range(4):
                    dt_ = cb * 4 + m
                    pi = mm_block(j, m, aT, dR2, 16)
                    P.op("dve", lambda e, pi=pi, dt_=dt_: e.tensor_tensor(out=x_sb[:, dt_, :], in0=x_sb[:, dt_, :],
                                                                          in1=ps[pi][:], op=ALU.add),
                         reads=[dps[pi], dx], writes=[dx])
                done(j)
        xov = xo.rearrange("(kt p) t -> p kt t", p=128)
        if not last:
            for i in range(4):
                P.dma("sp", xov[:, 4 * i:4 * i + 4, :], x_sb[:, 4 * i:4 * i + 4, :], reads=[dx])
        else:
            for kt in range(16):
                i = kt % 2
                P.op("act", lambda e, kt=kt, i=i: e.activation(out=sq[i][:], in_=x_sb[:, kt, :], func=AF.Square),
                     reads=[dx], writes=[dsq[i]])
                for hf in range(2):
                    P.op("pe", lambda e, kt=kt, i=i, hf=hf: e.matmul(ps[0][:, hf * 512:(hf + 1) * 512], ones_bf[:],
                                                                     sq[i][:, hf * 512:(hf + 1) * 512],
                                                                     start=(kt == 0), stop=(kt == 15)),
                         reads=[dsq[i], dones], writes=[dps[0]])
            P.op("act", lambda e: e.activation(out=rstd[:], in_=ps[0][:], func=AF.Sqrt, bias=epsb[:], scale=1.0 / D),
                 reads=[dps[0], deps], writes=[drstd])
            P.op("dve", lambda e: e.reciprocal(out=rstd[:], in_=rstd[:]), reads=[drstd], writes=[drstd])
            fo = [acc, tmp]
            dfo = [dacc, dtmp]
            for kt in range(16):
                i = kt % 2
                P.op("dve", lambda e, kt=kt, i=i: e.scalar_tensor_tensor(out=fo[i][:], in0=x_sb[:, kt, :],
                                                                       scalar=gf_sb[:, kt:kt + 1], in1=rstd[:],
                                                                       op0=ALU.mult, op1=ALU.mult),
                     reads=[dx, dgf, drstd], writes=[dfo[i]])
                P.dma("sp", xov[:, kt, :], fo[i][:], reads=[dfo[i]])
        P.finish()
    return nc


def host_C(x_tok, l, inp, yaT, ybpT, ycT, gT, last):
    maps = []
    r = lambda v, n: np.ascontiguousarray(v.reshape(n, 128).T)
    com = {"w_glu": np.ascontiguousarray(inp["s5_w_glu"][l]), "b_glu": r(inp["s5_b_glu"][l], 8),
           "w_br": np.ascontiguousarray(inp["w_branch"][l]), "w_out": np.ascontiguousarray(inp["w_out"][l]),
           "n2g": r(inp["norm2_g"][l], 16), "w_m1": np.ascontiguousarray(inp["w_mlp_in"][l]),
           "w_m2": np.ascontiguousarray(inp["w_mlp_out"][l]), "fing": r(inp["final_g"], 16)}
    for c in range(NCORES):
        m = dict(com)
        m["xT"] = np.ascontiguousarray(x_tok[c * TOK:(c + 1) * TOK, :].T)
        m["yaT"], m["ybpT"], m["ycT"], m["gT"] = yaT[c], ybpT[c], ycT[c], gT[c]
        maps.append(m)
    return maps


SEQ = 4096
NPAIR = 2


def emit_B1(nc, C, P, merged=False):
    qT = C.din("qT", [NPAIR, 128, SEQ], BF16)
    kT = C.din("kT", [NPAIR, 128, SEQ], BF16)
    vv = C.din("v", [NPAIR, SEQ, 128], BF16)
    yc = C.dout("yc", [NPAIR, 128, SEQ], BF16)
    NB = 1 if merged else NPAIR
    q_sb = C.sb("q_sb", [128, NB, SEQ], BF16)
    k_sb = C.sb("k_sb", [128, NB, SEQ], BF16)
    v_sb = C.sb("v_sb", [128, NB, 32, 128], BF16)
    o_st = [C.sb("o_st%d" % i, [128, 512], BF16) for i in range(2)]
    ones_f = C.sb("ones_f", [128, 128], F32)
    mstrict = C.sb("mstrict", [128, 128], BF16)
    negtri = C.sb("negtri", [128, 128], BF16)
    negones = C.sb("negones", [128, 128], BF16)
    spsum = C.sb("spsum", [128, 512], BF16)
    ebuf = [C.sb("ebuf%d" % i, [128, 512], F32) for i in range(2)]
    spb = [C.sb("spb%d" % i, [128, 512], BF16) for i in range(3)]
    wbuf = [C.sb("wbuf%d" % i, [128, 512], BF16) for i in range(3)]
    psA = [C.ps("psA%d" % i, [128, 512]) for i in range(2)]
    psB = [C.ps("psB%d" % i, [128, 512]) for i in range(2)]
    psO = [C.ps("psO%d" % i, [128, 512]) for i in range(1 if merged else 2)]
    if merged:
        psO = [psO[0], psO[0]]
    dq, dk, dv, dconst, dspsum = [Dep() for _ in range(5)]
    do_ = [Dep(), Dep()]
    debuf = [Dep(), Dep()]
    dspb = [Dep() for _ in range(3)]
    dwbuf = [Dep() for _ in range(3)]
    dpsA = [Dep(), Dep()]
    dpsB = [Dep(), Dep()]
    dpsO = [Dep(), Dep()]
    if merged:
        dpsO = [dpsO[0], dpsO[0]]

    def load_pair(p):
        pb = p % NB
        P.dma("sp", q_sb[:, pb, :], qT[p], writes=[dq])
        P.dma("sp", k_sb[:, pb, :], kT[p], writes=[dk])
        P.dma("sp", v_sb[:, pb], vv[p].rearrange("(b s) d -> s b d", s=128), writes=[dv])
    for p in range(NB):
        load_pair(p)
    P.op("pool", lambda e: e.memset(ones_f[:], 1.0), writes=[dconst])
    P.op("pool", lambda e: e.affine_select(out=mstrict[:], in_=ones_f[:], pattern=[[1, 128]],
                                           compare_op=ALU.is_gt, fill=0.0, base=0, channel_multiplier=-1),
         reads=[dconst], writes=[dconst])
    P.op("pool", lambda e: e.memset(ones_f[:], -1.0), reads=[dconst], writes=[dconst])
    P.op("pool", lambda e: e.affine_select(out=negtri[:], in_=ones_f[:], pattern=[[-1, 128]],
                                           compare_op=ALU.is_ge, fill=0.0, base=0, channel_multiplier=1),
         reads=[dconst], writes=[dconst])
    P.op("pool", lambda e: e.tensor_copy(out=negones[:], in_=ones_f[:]), reads=[dconst], writes=[dconst])

    steps = []
    for p in range(NPAIR):
        for g in range(8):
            for sb in range(4 * g + 3, -1, -1):
                steps.append((p, g, sb))
    n = len(steps)

    def info(i):
        p, g, sb = steps[i]
        tl = max(0, sb - 4 * g) * 128
        return p, g, sb, tl, (sb >= 4 * g)

    def stage1(i):
        p, g, sb, tl, diag = info(i)
        a, s3 = i % 2, i % 3
        if merged and p > 0 and g == 0 and sb == 3:
            load_pair(p)
        P.op("pe", lambda e: e.matmul(psA[a][:, tl:512], k_sb[:, p % NB, sb * 128:(sb + 1) * 128],
                                      q_sb[:, p % NB, g * 512 + tl:(g + 1) * 512], start=True, stop=True),
             reads=[dq, dk], writes=[dpsA[a]])
        P.op("act", lambda e: e.activation(out=ebuf[a][:, tl:512], in_=psA[a][:, tl:512], func=AF.Exp),
             reads=[dpsA[a]], writes=[debuf[a]])
        P.op("act", lambda e: e.activation(out=spb[s3][:, tl:512], in_=ebuf[a][:, tl:512], func=AF.Ln, bias=1.0),
             reads=[debuf[a]], writes=[dspb[s3]])
        if diag:
            P.op("pool", lambda e: e.tensor_tensor(out=spb[s3][:, tl:tl + 128], in0=spb[s3][:, tl:tl + 128],
                                                   in1=mstrict[:], op=ALU.mult),
                 reads=[dspb[s3], dconst], writes=[dspb[s3]])

    def stage2(i):
        p, g, sb, tl, diag = info(i)
        a, s3 = i % 2, i % 3
        if sb == 4 * g + 3:
            P.op("pool", lambda e: e.memset(spsum[:], 0.0), writes=[dspsum])
        P.op("pe", lambda e: e.matmul(psB[a][:, tl:512], k_sb[:, p % NB, sb * 128:(sb + 1) * 128],
                                      q_sb[:, p % NB, g * 512 + tl:(g + 1) * 512], start=True, stop=False),
             reads=[dq, dk], writes=[dpsB[a]])
        P.op("pe", lambda e: e.matmul(psB[a][:, tl:512], negtri[:], spb[s3][:, tl:512], start=False, stop=False),
             reads=[dspb[s3], dconst], writes=[dpsB[a]])
        P.op("pe", lambda e: e.matmul(psB[a][:, tl:512], negones[:], spsum[:, tl:512], start=False, stop=True),
             reads=[dspsum, dconst], writes=[dpsB[a]])
        P.op("pool", lambda e: e.tensor_tensor(out=spsum[:, tl:512], in0=spsum[:, tl:512], in1=spb[s3][:, tl:512],
                                               op=ALU.add),
             reads=[dspsum, dspb[s3]], writes=[dspsum])
        P.op("act", lambda e: e.activation(out=wbuf[s3][:, tl:512], in_=psB[a][:, tl:512], func=AF.Exp),
             reads=[dpsB[a]], writes=[dwbuf[s3]])
        if diag:
            P.op("pool", lambda e: e.tensor_tensor(out=wbuf[s3][:, tl:tl + 128], in0=wbuf[s3][:, tl:tl + 128],
                                                   in1=mstrict[:], op=ALU.mult),
                 reads=[dwbuf[s3], dconst], writes=[dwbuf[s3]])

    def stage3(i):
        p, g, sb, tl, diag = info(i)
        s3 = i % 3
        o = (p * 8 + g) % 2
        for tb in range(tl // 128, 4):
            P.op("pe", lambda e, tb=tb: e.matmul(psO[o][:, tb * 128:(tb + 1) * 128], v_sb[:, p % NB, sb, :],
                                                 wbuf[s3][:, tb * 128:(tb + 1) * 128],
                                                 start=(sb == 4 * g + 3 and tb == 3), stop=(sb == 0),
                                                 skip_group_check=True),
                 reads=[dv, dwbuf[s3]], writes=[dpsO[o]])
        if sb == 0:
            P.op("dve", lambda e: e.tensor_copy(out=o_st[o][:], in_=psO[o][:]), reads=[dpsO[o]], writes=[do_[o]])
            P.dma("sp", yc[p][:, g * 512:(g + 1) * 512], o_st[o][:], reads=[do_[o]])

    for it in range(n + 2):
        if it < n:
            stage1(it)
        if 0 <= it - 1 < n:
            stage2(it - 1)
        if 0 <= it - 2 < n:
            stage3(it - 2)


def build_B1():
    nc = bass.Bass("TRN2", target_bir_lowering=False)
    with ExitStack() as st:
        st.enter_context(nc.allow_low_precision("bf16 matmul operands, fp32 accumulation"))
        C = Ctx(nc, st)
        P = Prog(nc, st)
        emit_B1(nc, C, P)
        P.finish()
    return nc


I32 = mybir.dt.int32
PI = 3.14159265358979
TWO_PI = 2.0 * PI
NT = 2 * SEQ


def emit_B2(nc, C, P, merged=False, debug=False):
    uT = C.din("uT", [128, NT], BF16)
    lam_re = C.din("lam_re", [128, 4])
    lam_im = C.din("lam_im", [128, 4])
    log_dt = C.din("log_dt", [128, 4])
    b_re = C.din("b_re", [128, 4, 16])
    b_im = C.din("b_im", [128, 4, 16])
    c_reT = C.din("c_reT", [128, 4, 16])
    c_imT = C.din("c_imT", [128, 4, 16])
    dvec = C.din("dvec", [128, 1])
    yp = C.dout("yp", [128, NT], BF16)
    f4 = lambda n: C.sb(n, [128, 4], F32)
    u_sb = C.sb("u_sb", [128, NT], BF16)
    y_sb = C.sb("y_sb", [128, NT], BF16)
    lre, lim, ldt, dtt, rho, th, mag, cth, sth, abre, abim, nre, den, kre, kim, q0, q1, q2 = [
        f4("s4_%d" % i) for i in range(18)]
    m127, a7re, a7im = f4("m127"), f4("a7re"), f4("a7im")
    Adre = C.sb("Adre", [128, 4, 5], F32)
    Adim = C.sb("Adim", [128, 4, 5], F32)
    Adimn = C.sb("Adimn", [128, 4, 5], F32)
    abimn = f4("abimn")
    s127n = f4("s127n")
    bre = C.sb("bre", [128, 4, 16], F32)
    bim = C.sb("bim", [128, 4, 16], F32)
    cre = C.sb("cre", [128, 4, 16], F32)
    cim = C.sb("cim", [128, 4, 16], F32)
    bbre = C.sb("bbre", [128, 4, 16], F32)
    bbim = C.sb("bbim", [128, 4, 16], F32)
    bt0 = C.sb("bt0", [128, 16], F32)
    d_sb = C.sb("d_sb", [128, 1], F32)
    ident = C.sb("ident", [128, 128], F32)
    pad = C.sb("pad", [128, 128], F32)
    BBT = [C.sb("BBT%d" % i, [128, 4, 128], BF16) for i in range(2)]
    Cp = [C.sb("Cp%d" % i, [128, 4, 128], BF16) for i in range(2)]
    jidx = C.sb("jidx", [128, 512], F32)
    ang = C.sb("ang", [128, 512], F32)
    kf = C.sb("kf", [128, 512], F32)
    ki = C.sb("ki", [128, 512], I32)
    cs = [C.sb("cs%d" % i, [128, 4, 512], F32) for i in range(2)]
    m0 = C.sb("m0", [128, SEQ], BF16)
    rfull = C.sb("rfull", [128, SEQ], F32)
    w = [C.sb("w%d" % i, [128, SEQ], F32) for i in range(2)]
    z = [C.sb("z%d" % i, [128, SEQ], F32) for i in range(2)]
    bu = [C.sb("bu%d" % i, [128, 512], F32) for i in range(2)]
    tt = [C.sb("tt%d" % i, [128, 512], F32) for i in range(4)]
    sbf = [C.sb("sbf%d" % i, [128, 512], BF16) for i in range(2)]
    cst = [C.sb("cst%d" % i, [128, 32], F32) for i in range(10)]
    psb = [C.ps("psb%d" % i, [128, 512]) for i in range(2)]
    if merged:
        psy0 = C.ps("psy0", [128, 512])
        psy = [psy0, psy0]
        pst = psy0
    else:
        psy = [C.ps("psy%d" % i, [128, 512]) for i in range(2)]
        pst = C.ps("pst", [128, 128])

    du, dy, dpre, dtab, dm0, drf = [Dep() for _ in range(6)]
    dw = [Dep(), Dep()]
    dz = [Dep(), Dep()]
    dbu = [Dep(), Dep()]
    dtq = [Dep() for _ in range(4)]
    dsbf = [Dep(), Dep()]
    dcst = Dep()
    dpsb = [Dep(), Dep()]
    dpsy = [Dep(), Dep()]
    dpst = Dep()
    if merged:
        dpsy = [dpsy[0], dpsy[0]]
        dpst = dpsy[0]

    def V(fn, r, wr, eng="dve"):
        P.op(eng, fn, reads=r, writes=wr)

    P.dma("sp", u_sb[:, 0:SEQ], uT[:, 0:SEQ], writes=[du])
    P.dma("sp", u_sb[:, SEQ:NT], uT[:, SEQ:NT], writes=[du])
    for t_, src in ((lre, lam_re), (lim, lam_im), (ldt, log_dt), (bre, b_re), (bim, b_im), (cre, c_reT),
                    (cim, c_imT), (d_sb, dvec)):
        P.dma("sp", t_[:], src, writes=[dpre])
    pre = [dpre]

    V(lambda e: e.memset(pad[:], 1.0), [], pre, "pool")
    V(lambda e: e.affine_select(out=ident[:], in_=pad[:], pattern=[[-1, 128]], compare_op=ALU.is_equal,
                                fill=0.0, base=0, channel_multiplier=1), pre, pre, "pool")
    V(lambda e: e.iota(jidx[:], pattern=[[0, 4], [1, 128]], base=0, channel_multiplier=0,
                       allow_small_or_imprecise_dtypes=True), [], [dtab], "pool")
    V(lambda e: e.memset(m0[:], 1.0), [], [dm0], "pool")
    V(lambda e: e.memset(m0[:].rearrange("p (c j) -> p c j", j=128)[:, :, 0:1], 0.0), [dm0], [dm0], "pool")

    def reduce_angle(x, n):
        V(lambda e: e.tensor_scalar(out=kf[:, :n], in0=x, scalar1=1.0 / TWO_PI, scalar2=None, op0=ALU.mult),
          [dtab], [dtab])
        V(lambda e: e.tensor_copy(out=ki[:, :n], in_=kf[:, :n]), [dtab], [dtab])
        V(lambda e: e.tensor_copy(out=kf[:, :n], in_=ki[:, :n]), [dtab], [dtab])
        V(lambda e: e.scalar_tensor_tensor(out=x, in0=kf[:, :n], scalar=-TWO_PI, in1=x, op0=ALU.mult,
                                           op1=ALU.add), [dtab], [dtab])
        V(lambda e: e.tensor_scalar(out=kf[:, :n], in0=x, scalar1=PI, scalar2=-TWO_PI, op0=ALU.is_gt,
                                    op1=ALU.mult), [dtab], [dtab])
        V(lambda e: e.tensor_tensor(out=x, in0=x, in1=kf[:, :n], op=ALU.add), [dtab], [dtab])
        V(lambda e: e.tensor_scalar(out=kf[:, :n], in0=x, scalar1=-PI, scalar2=TWO_PI, op0=ALU.is_lt,
                                    op1=ALU.mult), [dtab], [dtab])
        V(lambda e: e.tensor_tensor(out=x, in0=x, in1=kf[:, :n], op=ALU.add), [dtab], [dtab])

    PT = [dpre, dtab]
    P.op("act", lambda e: e.activation(out=dtt[:], in_=ldt[:], func=AF.Exp), reads=PT, writes=PT)
    V(lambda e: e.tensor_tensor(out=rho[:], in0=lre[:], in1=dtt[:], op=ALU.mult), PT, PT)
    V(lambda e: e.tensor_tensor(out=th[:], in0=lim[:], in1=dtt[:], op=ALU.mult), PT, PT)
    P.op("act", lambda e: e.activation(out=mag[:], in_=rho[:], func=AF.Exp), reads=PT, writes=PT)
    P.op("act", lambda e: e.activation(out=m127[:], in_=rho[:], func=AF.Exp, scale=127.0), reads=PT, writes=PT)
    for gp in range(4):
        for k_, shift in ((1, 0.0), (0, PI / 2)):
            V(lambda e, gp=gp, shift=shift: e.tensor_scalar(out=ang[:], in0=jidx[:], scalar1=th[:, gp:gp + 1],
                                                            scalar2=shift, op0=ALU.mult, op1=ALU.add), PT, PT)
            reduce_angle(ang[:], 512)
            P.op("act", lambda e, gp=gp, k_=k_: e.activation(out=cs[k_][:, gp, :], in_=ang[:], func=AF.Sin),
                 reads=PT, writes=PT)
    V(lambda e: e.tensor_copy(out=cth[:], in_=cs[0][:, :, 1]), PT, PT)
    V(lambda e: e.tensor_copy(out=sth[:], in_=cs[1][:, :, 1]), PT, PT)
    V(lambda e: e.tensor_tensor(out=abre[:], in0=mag[:], in1=cth[:], op=ALU.mult), PT, PT)
    V(lambda e: e.tensor_tensor(out=abim[:], in0=mag[:], in1=sth[:], op=ALU.mult), PT, PT)
    V(lambda e: e.tensor_scalar(out=abimn[:], in0=abim[:], scalar1=-1.0, scalar2=None, op0=ALU.mult), PT, PT)
    V(lambda e: e.tensor_scalar(out=s127n[:], in0=cs[1][:, :, 127], scalar1=-1.0, scalar2=None, op0=ALU.mult), PT, PT)
    V(lambda e: e.tensor_scalar(out=nre[:], in0=abre[:], scalar1=-1.0, scalar2=None, op0=ALU.add), PT, PT)
    V(lambda e: e.tensor_tensor(out=q0[:], in0=lre[:], in1=lre[:], op=ALU.mult), PT, PT)
    V(lambda e: e.tensor_tensor(out=q1[:], in0=lim[:], in1=lim[:], op=ALU.mult), PT, PT)
    V(lambda e: e.tensor_tensor(out=den[:], in0=q0[:], in1=q1[:], op=ALU.add), PT, PT)
    V(lambda e: e.reciprocal(out=den[:], in_=den[:]), PT, PT)
    V(lambda e: e.tensor_tensor(out=q0[:], in0=nre[:], in1=lre[:], op=ALU.mult), PT, PT)
    V(lambda e: e.tensor_tensor(out=q1[:], in0=abim[:], in1=lim[:], op=ALU.mult), PT, PT)
    V(lambda e: e.tensor_tensor(out=q0[:], in0=q0[:], in1=q1[:], op=ALU.add), PT, PT)
    V(lambda e: e.tensor_tensor(out=kre[:], in0=q0[:], in1=den[:], op=ALU.mult), PT, PT)
    V(lambda e: e.tensor_tensor(out=q0[:], in0=abim[:], in1=lre[:], op=ALU.mult), PT, PT)
    V(lambda e: e.tensor_tensor(out=q1[:], in0=nre[:], in1=lim[:], op=ALU.mult), PT, PT)
    V(lambda e: e.tensor_tensor(out=q0[:], in0=q0[:], in1=q1[:], op=ALU.subtract), PT, PT)
    V(lambda e: e.tensor_tensor(out=kim[:], in0=q0[:], in1=den[:], op=ALU.mult), PT, PT)
    for gp in range(4):
        g1 = slice(gp, gp + 1)
        V(lambda e, gp=gp, g1=g1: e.tensor_scalar(out=bt0[:], in0=bim[:, gp, :], scalar1=kim[:, g1], scalar2=None,
                                                  op0=ALU.mult), PT, PT)
        V(lambda e, gp=gp, g1=g1: e.scalar_tensor_tensor(out=bbre[:, gp, :], in0=bre[:, gp, :], scalar=kre[:, g1],
                                                         in1=bt0[:], op0=ALU.mult, op1=ALU.subtract), PT, PT)
        V(lambda e, gp=gp, g1=g1: e.tensor_scalar(out=bt0[:], in0=bre[:, gp, :], scalar1=kim[:, g1], scalar2=None,
                                                  op0=ALU.mult), PT, PT)
        V(lambda e, gp=gp, g1=g1: e.scalar_tensor_tensor(out=bbim[:, gp, :], in0=bim[:, gp, :], scalar=kre[:, g1],
                                                         in1=bt0[:], op0=ALU.mult, op1=ALU.add), PT, PT)
    V(lambda e: e.tensor_tensor(out=a7re[:], in0=m127[:], in1=cs[0][:, :, 127], op=ALU.mult), PT, PT)
    V(lambda e: e.tensor_tensor(out=a7im[:], in0=m127[:], in1=cs[1][:, :, 127], op=ALU.mult), PT, PT)

    def cmul(ore, oim, are, aim, bre_, bim_):
        V(lambda e: e.tensor_tensor(out=q0[:], in0=are, in1=bre_, op=ALU.mult), PT, PT)
        V(lambda e: e.tensor_tensor(out=q1[:], in0=aim, in1=bim_, op=ALU.mult), PT, PT)
        V(lambda e: e.tensor_tensor(out=q2[:], in0=are, in1=bim_, op=ALU.mult), PT, PT)
        V(lambda e: e.tensor_tensor(out=ore, in0=q0[:], in1=q1[:], op=ALU.subtract), PT, PT)
        V(lambda e: e.tensor_tensor(out=q0[:], in0=aim, in1=bre_, op=ALU.mult), PT, PT)
        V(lambda e: e.tensor_tensor(out=oim, in0=q2[:], in1=q0[:], op=ALU.add), PT, PT)
    cmul(Adre[:, :, 0], Adim[:, :, 0], a7re[:], a7im[:], abre[:], abim[:])
    for k in range(1, 5):
        cmul(Adre[:, :, k], Adim[:, :, k], Adre[:, :, k - 1], Adim[:, :, k - 1], Adre[:, :, k - 1], Adim[:, :, k - 1])
    V(lambda e: e.tensor_scalar(out=Adimn[:], in0=Adim[:], scalar1=-1.0, scalar2=None, op0=ALU.mult), PT, PT)
    for gp in range(4):
        for ri, src in ((0, bbre), (1, bbim)):
            V(lambda e: e.memset(pad[:], 0.0), PT, PT)
            for j in range(2):
                c0 = 16 * (2 * gp + j)
                V(lambda e, j=j, c0=c0, src=src, gp=gp: e.tensor_copy(out=pad[64 * j:64 * j + 64, c0:c0 + 16],
                                                                    in_=src[64 * j:64 * j + 64, gp, :]), PT, PT)
            P.op("pe", lambda e: e.transpose(pst[:, 0:128], pad[:], ident[:]), reads=PT, writes=[dpst])
            V(lambda e, ri=ri, gp=gp: e.tensor_copy(out=BBT[ri][:, gp, :], in_=pst[:, 0:128]), [dpst] + PT, PT)
        for ri, src, sgn in ((0, cre, 1.0), (1, cim, -1.0)):
            V(lambda e, ri=ri, gp=gp: e.memset(Cp[ri][:, gp, :], 0.0), PT, PT)
            for j in range(2):
                c0 = 16 * (2 * gp + j)
                V(lambda e, j=j, c0=c0, src=src, gp=gp, ri=ri, sgn=sgn: e.tensor_scalar(
                    out=Cp[ri][64 * j:64 * j + 64, gp, c0:c0 + 16], in0=src[64 * j:64 * j + 64, gp, :],
                    scalar1=sgn, scalar2=None, op0=ALU.mult), PT, PT)

    if debug:
        for nm, t_, shp in (("th", th, [128, 4]), ("mag", mag, [128, 4]), ("kre", kre, [128, 4]), ("kim", kim, [128, 4]),
                            ("cos", cs[0], [128, 4, 512]), ("sin", cs[1], [128, 4, 512]), ("Adre", Adre, [128, 4, 5]),
                            ("Adim", Adim, [128, 4, 5]), ("bbre", bbre, [128, 4, 16]), ("jidx", jidx, [128, 512])):
            P.dma("sp", C.dout("dbg_" + nm, shp), t_[:], reads=PT)
        P.dma("sp", C.dout("dbg_BBT0", [128, 4, 128], BF16), BBT[0][:], reads=PT)
        P.dma("sp", C.dout("dbg_Cp1", [128, 4, 128], BF16), Cp[1][:], reads=PT)
    def mod_tile(gp, tok0, t8):
        sl = slice(t8 * 512, (t8 + 1) * 512)
        for ri in range(2):
            P.op("pe", lambda e, ri=ri: e.matmul(psb[ri][:], BBT[ri][:, gp, :],
                                                 u_sb[:, tok0 + sl.start:tok0 + sl.stop], start=True, stop=True),
                 reads=[du] + PT, writes=[dpsb[ri]])
            P.op("act", lambda e, ri=ri: e.activation(out=bu[ri][:], in_=psb[ri][:], func=AF.Copy),
                 reads=[dpsb[ri]], writes=[dbu[ri]])
        V(lambda e: e.tensor_tensor(out=tt[0][:], in0=bu[0][:], in1=cs[0][:, gp, :], op=ALU.mult),
          [dbu[0]] + PT, [dtq[0]])
        V(lambda e: e.tensor_tensor(out=tt[1][:], in0=bu[1][:], in1=cs[1][:, gp, :], op=ALU.mult),
          [dbu[1]] + PT, [dtq[1]])
        V(lambda e: e.tensor_tensor(out=w[0][:, sl], in0=tt[0][:], in1=tt[1][:], op=ALU.add),
          [dtq[0], dtq[1]], [dw[0]])
        V(lambda e: e.tensor_tensor(out=tt[2][:], in0=bu[1][:], in1=cs[0][:, gp, :], op=ALU.mult),
          [dbu[1]] + PT, [dtq[2]], "pool")
        V(lambda e: e.tensor_tensor(out=tt[3][:], in0=bu[0][:], in1=cs[1][:, gp, :], op=ALU.mult),
          [dbu[0]] + PT, [dtq[3]], "pool")
        V(lambda e: e.tensor_tensor(out=w[1][:, sl], in0=tt[2][:], in1=tt[3][:], op=ALU.subtract),
          [dtq[2], dtq[3]], [dw[1]], "pool")

    def scans(extra):
        for i in range(2):
            V(lambda e, i=i: e.tensor_tensor_scan(out=z[i][:], data0=rfull[:], data1=w[i][:], initial=0.0,
                                                  op0=ALU.mult, op1=ALU.add), [drf, dw[i]] + extra, [dz[i]])

    def hs_step(gp, k, cur):
        d_ = 1 << k
        nxt = 2 - cur
        ar = Adre[:, gp, k:k + 1]
        ai = Adim[:, gp, k:k + 1]
        ain = Adimn[:, gp, k:k + 1]
        sre, sim = cst[cur], cst[cur + 1]
        nre_, nim_ = cst[nxt], cst[nxt + 1]
        CS = [dcst]
        V(lambda e: e.tensor_copy(out=nre_[:, 0:d_], in_=sre[:, 0:d_]), CS, CS)
        V(lambda e: e.tensor_copy(out=nim_[:, 0:d_], in_=sim[:, 0:d_]), CS, CS)
        V(lambda e: e.scalar_tensor_tensor(out=cst[4][:, d_:32], in0=sre[:, 0:32 - d_], scalar=ar, in1=sre[:, d_:32],
                                           op0=ALU.mult, op1=ALU.add), CS + PT, CS)
        V(lambda e: e.scalar_tensor_tensor(out=nre_[:, d_:32], in0=sim[:, 0:32 - d_], scalar=ain,
                                           in1=cst[4][:, d_:32], op0=ALU.mult, op1=ALU.add), CS + PT, CS)
        V(lambda e: e.scalar_tensor_tensor(out=cst[4][:, d_:32], in0=sim[:, 0:32 - d_], scalar=ar, in1=sim[:, d_:32],
                                           op0=ALU.mult, op1=ALU.add), CS + PT, CS)
        V(lambda e: e.scalar_tensor_tensor(out=nim_[:, d_:32], in0=sre[:, 0:32 - d_], scalar=ai,
                                           in1=cst[4][:, d_:32], op0=ALU.mult, op1=ALU.add), CS + PT, CS)
        return nxt

    def carry(gp):
        g1 = slice(gp, gp + 1)
        CS = [dcst]
        zE = [z[i][:].rearrange("p (c j) -> p c j", j=128)[:, :, 127] for i in range(2)]
        c127 = cs[0][:, gp, 127:128]
        s127 = cs[1][:, gp, 127:128]
        V(lambda e: e.tensor_scalar(out=cst[2][:], in0=zE[0], scalar1=c127, scalar2=None, op0=ALU.mult),
          [dz[0]] + PT, CS)
        V(lambda e: e.scalar_tensor_tensor(out=cst[0][:], in0=zE[1], scalar=s127n[:, g1], in1=cst[2][:],
                                           op0=ALU.mult, op1=ALU.add), [dz[1]] + PT + CS, CS)
        V(lambda e: e.tensor_scalar(out=cst[2][:], in0=zE[0], scalar1=s127, scalar2=None, op0=ALU.mult),
          [dz[0]] + PT + CS, CS)
        V(lambda e: e.scalar_tensor_tensor(out=cst[1][:], in0=zE[1], scalar=c127, in1=cst[2][:],
                                           op0=ALU.mult, op1=ALU.add), [dz[1]] + PT + CS, CS)
        cur = 0
        for k in range(5):
            cur = hs_step(gp, k, cur)
        Sre, Sim = cst[cur], cst[cur + 1]
        w0 = [w[i][:].rearrange("p (c j) -> p c j", j=128)[:, 1:32, 0] for i in range(2)]
        V(lambda e: e.scalar_tensor_tensor(out=cst[5][:, 0:31], in0=Sre[:, 0:31], scalar=abre[:, g1],
                                           in1=w0[0], op0=ALU.mult, op1=ALU.add), CS + PT + [dw[0]], CS)
        V(lambda e: e.scalar_tensor_tensor(out=w0[0], in0=Sim[:, 0:31], scalar=abimn[:, g1],
                                           in1=cst[5][:, 0:31], op0=ALU.mult, op1=ALU.add), CS + PT, [dw[0]])
        V(lambda e: e.scalar_tensor_tensor(out=cst[6][:, 0:31], in0=Sim[:, 0:31], scalar=abre[:, g1],
                                           in1=w0[1], op0=ALU.mult, op1=ALU.add), CS + PT + [dw[1]], CS)
        V(lambda e: e.scalar_tensor_tensor(out=w0[1], in0=Sre[:, 0:31], scalar=abim[:, g1],
                                           in1=cst[6][:, 0:31], op0=ALU.mult, op1=ALU.add), CS + PT, [dw[1]])

    def out_tile(gp, tok0, t8):
        sl = slice(t8 * 512, (t8 + 1) * 512)
        V(lambda e: e.tensor_tensor(out=tt[0][:], in0=z[0][:, sl], in1=cs[0][:, gp, :], op=ALU.mult),
          [dz[0]] + PT, [dtq[0]])
        V(lambda e: e.tensor_tensor(out=tt[1][:], in0=z[1][:, sl], in1=cs[1][:, gp, :], op=ALU.mult),
          [dz[1]] + PT, [dtq[1]])
        V(lambda e: e.tensor_tensor(out=sbf[0][:], in0=tt[0][:], in1=tt[1][:], op=ALU.subtract),
          [dtq[0], dtq[1]], [dsbf[0]])
        V(lambda e: e.tensor_tensor(out=tt[2][:], in0=z[0][:, sl], in1=cs[1][:, gp, :], op=ALU.mult),
          [dz[0]] + PT, [dtq[2]], "pool")
        V(lambda e: e.tensor_tensor(out=tt[3][:], in0=z[1][:, sl], in1=cs[0][:, gp, :], op=ALU.mult),
          [dz[1]] + PT, [dtq[3]], "pool")
        V(lambda e: e.tensor_tensor(out=sbf[1][:], in0=tt[2][:], in1=tt[3][:], op=ALU.add),
          [dtq[2], dtq[3]], [dsbf[1]], "pool")
        yi = t8 % 2
        P.op("pe", lambda e: e.matmul(psy[yi][:], Cp[0][:, gp, :], sbf[0][:], start=True, stop=False),
             reads=[dsbf[0]] + PT, writes=[dpsy[yi]])
        P.op("pe", lambda e: e.matmul(psy[yi][:], Cp[1][:, gp, :], sbf[1][:], start=False, stop=True),
             reads=[dsbf[1]] + PT, writes=[dpsy[yi]])
        r0 = 32 * gp
        V(lambda e: e.scalar_tensor_tensor(
            out=y_sb[r0:r0 + 32, tok0 + sl.start:tok0 + sl.stop],
            in0=u_sb[r0:r0 + 32, tok0 + sl.start:tok0 + sl.stop], scalar=d_sb[r0:r0 + 32, 0:1],
            in1=psy[yi][r0:r0 + 32, :], op0=ALU.mult, op1=ALU.add),
          [du, dpsy[yi]] + PT, [dy])

    def set_rfull(gp):
        P.op("act", lambda e: e.activation(out=rfull[:], in_=m0[:], func=AF.Copy, scale=mag[:, gp:gp + 1]),
             reads=[dm0] + PT, writes=[drf])

    for gp in range(4):
        set_rfull(gp)
        for b in range(2):
            for t8 in range(8):
                mod_tile(gp, b * SEQ, t8)
            scans([])
            carry(gp)
            scans([dcst])
            for t8 in range(8):
                out_tile(gp, b * SEQ, t8)
    if debug:
        for nm, t_, shp, dd in (("w0", w[0], [128, SEQ], dw[0]), ("w1", w[1], [128, SEQ], dw[1]),
                                ("z0", z[0], [128, SEQ], dz[0]), ("z1", z[1], [128, SEQ], dz[1]),
                                ("rfull", rfull, [128, SEQ], drf), ("bu0", bu[0], [128, 512], dbu[0]),
                                ("tt0", tt[0], [128, 512], dtq[0])):
            P.dma("sp", C.dout("dbg_" + nm, shp), t_[:], reads=[dd])
        for k in range(7):
            P.dma("sp", C.dout("dbg_cst%d" % k, [128, 32]), cst[k][:], reads=[dcst])
    P.dma("sp", yp[:, 0:SEQ], y_sb[:, 0:SEQ], reads=[dy])
    P.dma("sp", yp[:, SEQ:NT], y_sb[:, SEQ:NT], reads=[dy])


def build_B2(debug=False):
    nc = bass.Bass("TRN2", target_bir_lowering=False)
    with ExitStack() as st:
        st.enter_context(nc.allow_low_precision("bf16 matmul operands, fp32 accumulation"))
        C = Ctx(nc, st)
        P = Prog(nc, st)
        emit_B2(nc, C, P, debug=debug)
        P.finish()
    return nc


def host_B2_params(l, inp, core):
    gs = slice(8 * core, 8 * core + 8)

    def pg(a):
        return np.ascontiguousarray(a.reshape(4, 2, 64).transpose(1, 2, 0).reshape(128, 4))
    lam_re = pg(inp["s5_lambda_re"][l][gs])
    lam_im = pg(inp["s5_lambda_im"][l][gs])
    log_dt = pg(np.repeat(inp["s5_log_dt"][l][gs][:, None], 64, axis=1))

    def pb(a):
        return np.ascontiguousarray(a.reshape(4, 2, 64, 16).transpose(1, 2, 0, 3).reshape(128, 4, 16))
    b_re = pb(inp["s5_b_re"][l][gs])
    b_im = pb(inp["s5_b_im"][l][gs])
    c_reT = pb(np.transpose(inp["s5_c_re"][l][gs], (0, 2, 1)))
    c_imT = pb(np.transpose(inp["s5_c_im"][l][gs], (0, 2, 1)))
    dvec = np.ascontiguousarray(inp["s5_d"][l][128 * core:128 * core + 128].reshape(128, 1))
    return {"lam_re": lam_re, "lam_im": lam_im, "log_dt": log_dt, "b_re": b_re, "b_im": b_im,
            "c_reT": c_reT, "c_imT": c_imT, "dvec": dvec}


_NC_CACHE = {}


def _get(name, fn):
    if name not in _NC_CACHE:
        _NC_CACHE[name] = fn()
    return _NC_CACHE[name]


def _run(nc, maps):
    res = run_bass_kernel_spmd(nc, maps, core_ids=list(range(NCORES)))
    return res.results


def kernel(**inp):
    inp = {k: np.asarray(v) for k, v in inp.items()}
    x_tok = np.ascontiguousarray(inp["x"].reshape(2 * SEQ, D).astype(np.float32))
    depth = inp["w_in"].shape[0]
    for l in range(depth):
        last = (l == depth - 1)
        rA = _run(_get("A", build_A), host_A(x_tok, l, inp))
        yaT = [np.asarray(r["yaT"]) for r in rA]
        gT = [np.asarray(r["gT"]) for r in rA]
        s5T = [np.asarray(r["s5T"]) for r in rA]
        qT = [np.asarray(r["qT"]) for r in rA]
        kT = [np.asarray(r["kT"]) for r in rA]
        vtk = [np.asarray(r["vtk"]) for r in rA]
        mB1, mB2 = [], []
        for c in range(NCORES):
            qs, ks, vs = [], [], []
            for pl in range(NPAIR):
                b, h = divmod(c * NPAIR + pl, 8)
                hs = slice(128 * h, 128 * h + 128)
                qs.append(np.concatenate([qT[4 * b + i][hs, :] for i in range(4)], axis=1))
                ks.append(np.concatenate([kT[4 * b + i][hs, :] for i in range(4)], axis=1))
                vs.append(np.concatenate([vtk[4 * b + i][:, hs] for i in range(4)], axis=0))
            mB1.append({"qT": np.ascontiguousarray(np.stack(qs)), "kT": np.ascontiguousarray(np.stack(ks)),
                        "v": np.ascontiguousarray(np.stack(vs))})
            m2 = host_B2_params(l, inp, c)
            m2["uT"] = np.ascontiguousarray(np.concatenate([s5T[i][128 * c:128 * c + 128, :] for i in range(8)], axis=1))
            mB2.append(m2)
        mB = [dict(mB1[c], **mB2[c]) for c in range(NCORES)]
        rB = _run(_get("B", build_B), mB)
        yc = [np.asarray(r["yc"]) for r in rB]
        yp = [np.asarray(r["yp"]) for r in rB]
        ycT, ybpT = [], []
        for tc in range(NCORES):
            b, i = divmod(tc, 4)
            rows = []
            for h in range(8):
                c, pl = divmod(b * 8 + h, NPAIR)
                rows.append(yc[c][pl][:, i * TOK:(i + 1) * TOK])
            ycT.append(np.ascontiguousarray(np.concatenate(rows, axis=0)))
            ybpT.append(np.ascontiguousarray(np.concatenate([yp[c][:, tc * TOK:(tc + 1) * TOK] for c in range(8)], axis=0)))
        ncC = _get("C%d" % int(last), lambda: build_C(last))
        rC = _run(ncC, host_C(x_tok, l, inp, yaT, ybpT, ycT, gT, last))
        x_tok = np.ascontiguousarray(np.concatenate([np.asarray(r["xo"]).T for r in rC], axis=0).astype(np.float32))
    return x_tok.reshape(2, SEQ, D)


def build_B():
    nc = bass.Bass("TRN2", target_bir_lowering=False)
    with ExitStack() as st:
        st.enter_context(nc.allow_low_precision("bf16 matmul operands, fp32 accumulation"))
        C = Ctx(nc, st)
        P = Prog(nc, st)
        P.begin("b2")
        emit_B2F(nc, C, P, merged=True)
        P.end()
        P.begin("b1")
        emit_B1(nc, C, P, merged=True)
        P.end()
        P.replay(["b2", "b1"])
        P.finish()
    return nc


TS = 8
NM = SEQ // TS
J2 = 64
NC2 = NM // J2


def emit_B2F(nc, C, P, merged=False):
    uT = C.din("uT", [128, NT], BF16)
    lam_re = C.din("lam_re", [128, 4])
    lam_im = C.din("lam_im", [128, 4])
    log_dt = C.din("log_dt", [128, 4])
    b_re = C.din("b_re", [128, 4, 16])
    b_im = C.din("b_im", [128, 4, 16])
    c_reT = C.din("c_reT", [128, 4, 16])
    c_imT = C.din("c_imT", [128, 4, 16])
    dvec = C.din("dvec", [128, 1])
    yp = C.dout("yp", [128, NT], BF16)
    f4 = lambda n: C.sb(n, [128, 4], F32)
    u_sb = C.sb("u_sb", [128, NT], BF16)
    y_sb = u_sb
    lre, lim, ldt, dtt, rho, th, th8, nre, den, kre, kim, q0, q1, q2 = [f4("s4_%d" % i) for i in range(14)]
    kidx = C.sb("kidx", [128, 16], F32)
    csk = [C.sb("csk%d" % i, [128, 4, 16], F32) for i in range(2)]
    magk = C.sb("magk", [128, 4, 16], F32)
    PW = [C.sb("PW%d" % i, [128, 4, 16], F32) for i in range(2)]
    PWn = [C.sb("PWn%d" % i, [128, 4, 16], F32) for i in range(2)]
    bre = C.sb("bre", [128, 4, 16], F32)
    bim = C.sb("bim", [128, 4, 16], F32)
    cre = C.sb("cre", [128, 4, 16], F32)
    cim = C.sb("cim", [128, 4, 16], F32)
    bbre = C.sb("bbre", [128, 4, 16], F32)
    bbim = C.sb("bbim", [128, 4, 16], F32)
    bt0 = C.sb("bt0", [128, 16], F32)
    d_sb = C.sb("d_sb", [128, 1], F32)
    ident = C.sb("ident", [128, 128], F32)
    R = C.sb("R", [128, 8192], F32)
    pads = [R[:, 4096 * i:4096 * (i + 1)].rearrange("p (t g c) -> p t g c", t=TS, g=4) for i in range(2)]
    Cpf = [C.sb("Cpf%d" % i, [128, 4, 128], F32) for i in range(2)]
    WE = [C.sb("WE%d" % i, [128, TS, 4, 128], BF16) for i in range(2)]
    WC = [C.sb("WC%d" % i, [128, TS, 4, 128], BF16) for i in range(2)]
    Kmat = C.sb("Kmat", [128, TS, 128], BF16)
    jidx = C.sb("jidx", [128, 512], F32)
    ang = R[:, 0:2048]
    kf = R[:, 2048:4096]
    ki = R[:, 4096:6144].bitcast(I32)
    AB = [C.sb("AB%d" % i, [128, TS, 4, 16], F32) for i in range(6)]
    cs2 = [C.sb("cs2_%d" % i, [128, 4, NM], F32) for i in range(2)]
    m02 = C.sb("m02", [128, NM], BF16)
    rf2 = C.sb("rf2", [128, 4, NM], F32)
    Ad = [C.sb("Ad%d" % i, [128, 4, 4], F32) for i in range(3)]
    sq_re, sq_im = f4("sq_re"), f4("sq_im")
    a8imn, s63n = f4("a8imn"), f4("s63n")
    bu = [R[:, NM * i:NM * (i + 1)] for i in range(2)]
    tt = [R[:, NM * (2 + i):NM * (3 + i)] for i in range(4)]
    w2 = [R[:, NM * (6 + i):NM * (7 + i)] for i in range(2)]
    z2 = [R[:, NM * (8 + i):NM * (9 + i)] for i in range(2)]
    cst = [C.sb("cst%d" % i, [128, NC2], F32) for i in range(7)]
    Sprev = C.sb("Sprev", [128, 4, 2, NM], BF16)
    psb = [C.ps("psb%d" % i, [128, 512]) for i in range(2)]
    psy = C.ps("psy", [128, 512])
    pst = psy

    du, dpre, dtab, dS = [Dep() for _ in range(4)]
    dy = du
    dbu = [Dep(), Dep()]
    dtq = [Dep() for _ in range(4)]
    dw = [Dep(), Dep()]
    dz = [Dep(), Dep()]
    dcst = Dep()
    dpsb = [Dep(), Dep()]
    dpsy = Dep()
    PT = [dpre, dtab]

    def V(fn, r, wr, eng="dve"):
        P.op(eng, fn, reads=r, writes=wr)

    def A(fn, r, wr):
        P.op("act", fn, reads=r, writes=wr)

    P.dma("sp", u_sb[:, 0:SEQ], uT[:, 0:SEQ], writes=[du])
    P.dma("sp", u_sb[:, SEQ:NT], uT[:, SEQ:NT], writes=[du])
    for t_, src in ((lre, lam_re), (lim, lam_im), (ldt, log_dt), (bre, b_re), (bim, b_im), (cre, c_reT),
                    (cim, c_imT), (d_sb, dvec)):
        P.dma("sp", t_[:], src, writes=[dpre])
    V(lambda e: e.memset(ang[:, 0:128], 1.0), [], PT, "pool")
    V(lambda e: e.affine_select(out=ident[:], in_=ang[:, 0:128], pattern=[[-1, 128]], compare_op=ALU.is_equal,
                                fill=0.0, base=0, channel_multiplier=1), PT, PT, "pool")
    V(lambda e: e.iota(jidx[:], pattern=[[0, NC2], [1, J2]], base=0, channel_multiplier=0,
                       allow_small_or_imprecise_dtypes=True), [], PT, "pool")
    V(lambda e: e.iota(kidx[:], pattern=[[1, 16]], base=0, channel_multiplier=0,
                       allow_small_or_imprecise_dtypes=True), [], PT, "pool")
    V(lambda e: e.memset(m02[:], 1.0), [], PT, "pool")
    V(lambda e: e.memset(m02[:].rearrange("p (c j) -> p c j", j=J2)[:, :, 0:1], 0.0), PT, PT, "pool")
    for ri in range(2):
        V(lambda e, ri=ri: e.memset(Cpf[ri][:], 0.0), [], PT, "pool")
        V(lambda e, ri=ri: e.memset(WC[ri][:], 0.0), [], PT, "pool")
    V(lambda e: e.memset(Sprev[:, :, :, 0:1], 0.0), [], [dS], "pool")

    def reduce_angle(x, n):
        V(lambda e: e.tensor_scalar(out=kf[:, :n], in0=x, scalar1=1.0 / TWO_PI, scalar2=None, op0=ALU.mult), PT, PT)
        V(lambda e: e.tensor_copy(out=ki[:, :n], in_=kf[:, :n]), PT, PT)
        V(lambda e: e.tensor_copy(out=kf[:, :n], in_=ki[:, :n]), PT, PT)
        V(lambda e: e.scalar_tensor_tensor(out=x, in0=kf[:, :n], scalar=-TWO_PI, in1=x, op0=ALU.mult, op1=ALU.add),
          PT, PT)
        V(lambda e: e.tensor_scalar(out=kf[:, :n], in0=x, scalar1=PI, scalar2=-TWO_PI, op0=ALU.is_gt, op1=ALU.mult),
          PT, PT)
        V(lambda e: e.tensor_tensor(out=x, in0=x, in1=kf[:, :n], op=ALU.add), PT, PT)
        V(lambda e: e.tensor_scalar(out=kf[:, :n], in0=x, scalar1=-PI, scalar2=TWO_PI, op0=ALU.is_lt, op1=ALU.mult),
          PT, PT)
        V(lambda e: e.tensor_tensor(out=x, in0=x, in1=kf[:, :n], op=ALU.add), PT, PT)

    def sincos_table(dst, idx_ap, n, thv):
        N4 = 4 * n
        a3 = ang[:, :N4].rearrange("p (g n) -> p g n", g=4)
        idx_b = idx_ap.unsqueeze(1).to_broadcast([128, 4, n])
        th_b = thv[:].unsqueeze(2).to_broadcast([128, 4, n])
        for k_, shift in ((1, 0.0), (0, PI / 2)):
            V(lambda e: e.tensor_tensor(out=a3, in0=idx_b, in1=th_b, op=ALU.mult), PT, PT)
            if shift:
                V(lambda e, shift=shift: e.tensor_scalar(out=ang[:, :N4], in0=ang[:, :N4], scalar1=shift, scalar2=None,
                                                         op0=ALU.add), PT, PT)
            reduce_angle(ang[:, :N4], N4)
            A(lambda e, k_=k_: e.activation(out=dst[k_][:, :, :n], in_=a3, func=AF.Sin), PT, PT)

    A(lambda e: e.activation(out=dtt[:], in_=ldt[:], func=AF.Exp), PT, PT)
    V(lambda e: e.tensor_tensor(out=rho[:], in0=lre[:], in1=dtt[:], op=ALU.mult), PT, PT)
    V(lambda e: e.tensor_tensor(out=th[:], in0=lim[:], in1=dtt[:], op=ALU.mult), PT, PT)
    V(lambda e: e.tensor_scalar(out=th8[:], in0=th[:], scalar1=float(TS), scalar2=None, op0=ALU.mult), PT, PT)
    sincos_table(csk, kidx[:], 16, th)
    for gp in range(4):
        (lambda gp: A(lambda e: e.activation(out=magk[:, gp, :], in_=kidx[:], func=AF.Exp, scale=rho[:, gp:gp + 1]),
                      PT, PT))(gp)
    for ri in range(2):
        V(lambda e, ri=ri: e.tensor_tensor(out=PW[ri][:], in0=magk[:], in1=csk[ri][:], op=ALU.mult), PT, PT)
        V(lambda e, ri=ri: e.tensor_scalar(out=PWn[ri][:], in0=PW[ri][:], scalar1=-1.0, scalar2=None, op0=ALU.mult),
          PT, PT)
    abre, abim = PW[0][:, :, 1], PW[1][:, :, 1]
    V(lambda e: e.tensor_scalar(out=nre[:], in0=abre, scalar1=-1.0, scalar2=None, op0=ALU.add), PT, PT)
    V(lambda e: e.tensor_tensor(out=q0[:], in0=lre[:], in1=lre[:], op=ALU.mult), PT, PT)
    V(lambda e: e.tensor_tensor(out=q1[:], in0=lim[:], in1=lim[:], op=ALU.mult), PT, PT)
    V(lambda e: e.tensor_tensor(out=den[:], in0=q0[:], in1=q1[:], op=ALU.add), PT, PT)
    V(lambda e: e.reciprocal(out=den[:], in_=den[:]), PT, PT)
    V(lambda e: e.tensor_tensor(out=q0[:], in0=nre[:], in1=lre[:], op=ALU.mult), PT, PT)
    V(lambda e: e.tensor_tensor(out=q1[:], in0=abim, in1=lim[:], op=ALU.mult), PT, PT)
    V(lambda e: e.tensor_tensor(out=q0[:], in0=q0[:], in1=q1[:], op=ALU.add), PT, PT)
    V(lambda e: e.tensor_tensor(out=kre[:], in0=q0[:], in1=den[:], op=ALU.mult), PT, PT)
    V(lambda e: e.tensor_tensor(out=q0[:], in0=abim, in1=lre[:], op=ALU.mult), PT, PT)
    V(lambda e: e.tensor_tensor(out=q1[:], in0=nre[:], in1=lim[:], op=ALU.mult), PT, PT)
    V(lambda e: e.tensor_tensor(out=q0[:], in0=q0[:], in1=q1[:], op=ALU.subtract), PT, PT)
    V(lambda e: e.tensor_tensor(out=kim[:], in0=q0[:], in1=den[:], op=ALU.mult), PT, PT)

    def bbar(gp):
        g1 = slice(gp, gp + 1)
        V(lambda e: e.tensor_scalar(out=bt0[:], in0=bim[:, gp, :], scalar1=kim[:, g1], scalar2=None, op0=ALU.mult), PT, PT)
        V(lambda e: e.scalar_tensor_tensor(out=bbre[:, gp, :], in0=bre[:, gp, :], scalar=kre[:, g1], in1=bt0[:],
                                           op0=ALU.mult, op1=ALU.subtract), PT, PT)
        V(lambda e: e.tensor_scalar(out=bt0[:], in0=bre[:, gp, :], scalar1=kim[:, g1], scalar2=None, op0=ALU.mult), PT, PT)
        V(lambda e: e.scalar_tensor_tensor(out=bbim[:, gp, :], in0=bim[:, gp, :], scalar=kre[:, g1], in1=bt0[:],
                                           op0=ALU.mult, op1=ALU.add), PT, PT)
    for gp in range(4):
        bbar(gp)

    sincos_table(cs2, jidx[:], NM, th8)
    for gp in range(4):
        (lambda gp: A(lambda e: e.activation(out=rf2[:, gp, :], in_=m02[:], func=AF.Copy, scale=magk[:, gp, 8:9]),
                      PT, PT))(gp)
    V(lambda e: e.tensor_scalar(out=a8imn[:], in0=PW[1][:, :, 8], scalar1=-1.0, scalar2=None, op0=ALU.mult), PT, PT)
    V(lambda e: e.tensor_scalar(out=s63n[:], in0=cs2[1][:, :, J2 - 1], scalar1=-1.0, scalar2=None, op0=ALU.mult), PT, PT)

    for ri in range(2):
        V(lambda e, ri=ri: e.memset(pads[ri], 0.0), PT, PT, "pool")

    def cpow_scale(o_re, o_im, xr, xi, k0_, neg_im):
        shp = [128, TS, 4, 16]
        pw = [PW[i][:, :, k0_:k0_ + TS].rearrange("p g t -> p t g").unsqueeze(3).to_broadcast(shp) for i in range(2)]
        pn = [PWn[i][:, :, k0_:k0_ + TS].rearrange("p g t -> p t g").unsqueeze(3).to_broadcast(shp) for i in range(2)]
        xr_b = xr.unsqueeze(1).to_broadcast(shp)
        xi_b = xi.unsqueeze(1).to_broadcast(shp)
        t0_, t1_ = AB[4][:], AB[5][:]
        V(lambda e: e.tensor_tensor(out=t0_, in0=xr_b, in1=pw[0], op=ALU.mult), PT, PT)
        V(lambda e: e.tensor_tensor(out=t1_, in0=xi_b, in1=pw[1], op=ALU.mult), PT, PT)
        V(lambda e: e.tensor_tensor(out=o_re, in0=t0_, in1=t1_, op=ALU.subtract), PT, PT)
        q_ = pn if neg_im else pw
        V(lambda e: e.tensor_tensor(out=t0_, in0=xr_b, in1=q_[1], op=ALU.mult), PT, PT)
        V(lambda e: e.tensor_tensor(out=t1_, in0=xi_b, in1=q_[0], op=ALU.mult), PT, PT)
        V(lambda e: e.tensor_tensor(out=o_im, in0=t0_, in1=t1_, op=ALU.add), PT, PT)

    cpow_scale(AB[0][:], AB[1][:], bbre[:], bbim[:], 0, False)
    cpow_scale(AB[2][:], AB[3][:], cre[:], cim[:], 1, True)

    def scatter(gp, j):
        r = slice(64 * j, 64 * j + 64)
        cc = slice(16 * (2 * gp + j), 16 * (2 * gp + j) + 16)
        eng = "dve" if j == 0 else "pool"
        for ri in range(2):
            V(lambda e, ri=ri: e.tensor_copy(out=pads[ri][r, :, gp, cc], in_=AB[ri][r, :, gp, :]), PT, PT, eng)
            V(lambda e, ri=ri: e.tensor_copy(out=WC[ri][r, :, gp, cc], in_=AB[2 + ri][r, :, gp, :]), PT, PT, eng)
        V(lambda e: e.tensor_copy(out=Cpf[0][r, gp, cc], in_=cre[r, gp, :]), PT, PT, eng)
        V(lambda e: e.tensor_scalar(out=Cpf[1][r, gp, cc], in0=cim[r, gp, :], scalar1=-1.0, scalar2=None, op0=ALU.mult),
          PT, PT, eng)
    for gp in range(4):
        for j in range(2):
            scatter(gp, j)

    def build_we(ri, tau):
        for gp in range(4):
            P.op("pe", lambda e, gp=gp: e.transpose(pst[:, gp * 128:(gp + 1) * 128], pads[ri][:, tau, gp, :], ident[:]),
                 reads=PT, writes=[dpsy])
        V(lambda e: e.tensor_copy(out=WE[ri][:, TS - 1 - tau, :, :].rearrange("p g q -> p (g q)"), in_=pst[:]),
          [dpsy], PT)
    for ri in range(2):
        for tau in range(TS):
            build_we(ri, tau)

    def build_k(t0):
        for tau in range(t0, t0 + 4):
            n_ = 0
            for gp in range(4):
                for ri in range(2):
                    P.op("pe", lambda e, tau=tau, gp=gp, ri=ri, n_=n_: e.matmul(
                        pst[:, (tau - t0) * 128:(tau - t0 + 1) * 128], pads[ri][:, tau, gp, :], Cpf[ri][:, gp, :],
                        start=(n_ == 0), stop=(n_ == 7), skip_group_check=True), reads=PT, writes=[dpsy])
                    n_ += 1
        V(lambda e: e.tensor_copy(out=Kmat[:, t0:t0 + 4, :].rearrange("p t c -> p (t c)"), in_=pst[:]), [dpsy], PT)
    build_k(0)
    build_k(4)

    def cmul(ore, oim, are, aim, bre_, bim_):
        V(lambda e: e.tensor_tensor(out=q0[:], in0=are, in1=bre_, op=ALU.mult), PT, PT)
        V(lambda e: e.tensor_tensor(out=q1[:], in0=aim, in1=bim_, op=ALU.mult), PT, PT)
        V(lambda e: e.tensor_tensor(out=q2[:], in0=are, in1=bim_, op=ALU.mult), PT, PT)
        V(lambda e: e.tensor_tensor(out=den[:], in0=aim, in1=bre_, op=ALU.mult), PT, PT)
        V(lambda e: e.tensor_tensor(out=ore, in0=q0[:], in1=q1[:], op=ALU.subtract), PT, PT)
        V(lambda e: e.tensor_tensor(out=oim, in0=q2[:], in1=den[:], op=ALU.add), PT, PT)
    V(lambda e: e.tensor_copy(out=sq_re[:], in_=PW[0][:, :, 8]), PT, PT)
    V(lambda e: e.tensor_copy(out=sq_im[:], in_=PW[1][:, :, 8]), PT, PT)
    for _ in range(6):
        cmul(sq_re[:], sq_im[:], sq_re[:], sq_im[:], sq_re[:], sq_im[:])
    for k in range(3):
        (lambda k: (V(lambda e: e.tensor_copy(out=Ad[0][:, :, k], in_=sq_re[:]), PT, PT),
                    V(lambda e: e.tensor_copy(out=Ad[1][:, :, k], in_=sq_im[:]), PT, PT),
                    V(lambda e: e.tensor_scalar(out=Ad[2][:, :, k], in0=sq_im[:], scalar1=-1.0, scalar2=None,
                                                op0=ALU.mult), PT, PT)))(k)
        if k < 2:
            cmul(sq_re[:], sq_im[:], sq_re[:], sq_im[:], sq_re[:], sq_im[:])

    V(lambda e: e.memset(bt0[:], 0.0), PT, PT)

    ud = C.sb("ud", [128, 2, TS, NM], BF16)
    dud = Dep()
    for b_ in range(2):
        (lambda b_: P.op("act", lambda e: e.activation(
            out=ud[:, b_], in_=u_sb[:, b_ * SEQ:(b_ + 1) * SEQ].rearrange("p (m s) -> p s m", s=TS), func=AF.Copy),
            reads=[du], writes=[dud]))(b_)

    def uview(b, s):
        return ud[:, b, s, :]

    def e_pass(gp, b):
        for ri in range(2):
            for s in range(TS):
                P.op("pe", lambda e, ri=ri, s=s: e.matmul(psb[ri][:], WE[ri][:, s, gp, :], uview(b, s),
                                                          start=(s == 0), stop=(s == TS - 1)),
                     reads=[dud] + PT, writes=[dpsb[ri]])
            A(lambda e, ri=ri: e.activation(out=bu[ri][:], in_=psb[ri][:], func=AF.Copy), [dpsb[ri]], [dbu[ri]])
        c_, s_ = cs2[0][:, gp, :], cs2[1][:, gp, :]
        V(lambda e: e.tensor_tensor(out=tt[0][:], in0=bu[0][:], in1=c_, op=ALU.mult), [dbu[0]] + PT, [dtq[0]])
        V(lambda e: e.tensor_tensor(out=tt[1][:], in0=bu[1][:], in1=s_, op=ALU.mult), [dbu[1]] + PT, [dtq[1]])
        V(lambda e: e.tensor_tensor(out=w2[0][:], in0=tt[0][:], in1=tt[1][:], op=ALU.add), [dtq[0], dtq[1]], [dw[0]])
        V(lambda e: e.tensor_tensor(out=tt[2][:], in0=bu[1][:], in1=c_, op=ALU.mult), [dbu[1]] + PT, [dtq[2]], "pool")
        V(lambda e: e.tensor_tensor(out=tt[3][:], in0=bu[0][:], in1=s_, op=ALU.mult), [dbu[0]] + PT, [dtq[3]], "pool")
        V(lambda e: e.tensor_tensor(out=w2[1][:], in0=tt[2][:], in1=tt[3][:], op=ALU.subtract), [dtq[2], dtq[3]],
          [dw[1]], "pool")

    def scans(gp, extra):
        for i in range(2):
            V(lambda e, i=i: e.tensor_tensor_scan(out=z2[i][:], data0=rf2[:, gp, :], data1=w2[i][:], initial=0.0,
                                                  op0=ALU.mult, op1=ALU.add), [dw[i]] + PT + extra, [dz[i]])

    def hs_step(gp, k, cur):
        d_ = 1 << k
        nxt = 2 - cur
        ar, ai, ain = Ad[0][:, gp, k:k + 1], Ad[1][:, gp, k:k + 1], Ad[2][:, gp, k:k + 1]
        sre, sim = cst[cur], cst[cur + 1]
        nre_, nim_ = cst[nxt], cst[nxt + 1]
        CS = [dcst]
        n = NC2
        V(lambda e: e.tensor_copy(out=nre_[:, 0:d_], in_=sre[:, 0:d_]), CS, CS)
        V(lambda e: e.tensor_copy(out=nim_[:, 0:d_], in_=sim[:, 0:d_]), CS, CS)
        V(lambda e: e.scalar_tensor_tensor(out=cst[4][:, d_:n], in0=sre[:, 0:n - d_], scalar=ar, in1=sre[:, d_:n],
                                           op0=ALU.mult, op1=ALU.add), CS + PT, CS)
        V(lambda e: e.scalar_tensor_tensor(out=nre_[:, d_:n], in0=sim[:, 0:n - d_], scalar=ain, in1=cst[4][:, d_:n],
                                           op0=ALU.mult, op1=ALU.add), CS + PT, CS)
        V(lambda e: e.scalar_tensor_tensor(out=cst[4][:, d_:n], in0=sim[:, 0:n - d_], scalar=ar, in1=sim[:, d_:n],
                                           op0=ALU.mult, op1=ALU.add), CS + PT, CS)
        V(lambda e: e.scalar_tensor_tensor(out=nim_[:, d_:n], in0=sre[:, 0:n - d_], scalar=ai, in1=cst[4][:, d_:n],
                                           op0=ALU.mult, op1=ALU.add), CS + PT, CS)
        return nxt

    def carry(gp):
        g1 = slice(gp, gp + 1)
        CS = [dcst]
        zE = [z2[i][:].rearrange("p (c j) -> p c j", j=J2)[:, :, J2 - 1] for i in range(2)]
        c63 = cs2[0][:, gp, J2 - 1:J2]
        s63 = cs2[1][:, gp, J2 - 1:J2]
        V(lambda e: e.tensor_scalar(out=cst[2][:], in0=zE[0], scalar1=c63, scalar2=None, op0=ALU.mult), [dz[0]] + PT, CS)
        V(lambda e: e.scalar_tensor_tensor(out=cst[0][:], in0=zE[1], scalar=s63n[:, g1], in1=cst[2][:], op0=ALU.mult,
                                           op1=ALU.add), [dz[1]] + PT + CS, CS)
        V(lambda e: e.tensor_scalar(out=cst[2][:], in0=zE[0], scalar1=s63, scalar2=None, op0=ALU.mult),
          [dz[0]] + PT + CS, CS)
        V(lambda e: e.scalar_tensor_tensor(out=cst[1][:], in0=zE[1], scalar=c63, in1=cst[2][:], op0=ALU.mult,
                                           op1=ALU.add), [dz[1]] + PT + CS, CS)
        cur = 0
        for k in range(3):
            cur = hs_step(gp, k, cur)
        Sre, Sim = cst[cur], cst[cur + 1]
        n1 = NC2 - 1
        w0 = [w2[i][:].rearrange("p (c j) -> p c j", j=J2)[:, 1:NC2, 0] for i in range(2)]
        a8r, a8i = PW[0][:, gp, 8:9], PW[1][:, gp, 8:9]
        V(lambda e: e.scalar_tensor_tensor(out=cst[5][:, 0:n1], in0=Sre[:, 0:n1], scalar=a8r, in1=w0[0], op0=ALU.mult,
                                           op1=ALU.add), CS + PT + [dw[0]], CS)
        V(lambda e: e.scalar_tensor_tensor(out=w0[0], in0=Sim[:, 0:n1], scalar=a8imn[:, g1], in1=cst[5][:, 0:n1],
                                           op0=ALU.mult, op1=ALU.add), CS + PT, [dw[0]])
        V(lambda e: e.scalar_tensor_tensor(out=cst[6][:, 0:n1], in0=Sim[:, 0:n1], scalar=a8r, in1=w0[1], op0=ALU.mult,
                                           op1=ALU.add), CS + PT + [dw[1]], CS)
        V(lambda e: e.scalar_tensor_tensor(out=w0[1], in0=Sre[:, 0:n1], scalar=a8i, in1=cst[6][:, 0:n1],
                                           op0=ALU.mult, op1=ALU.add), CS + PT, [dw[1]])

    def demod(gp):
        n1 = NM - 1
        c_, s_ = cs2[0][:, gp, 0:n1], cs2[1][:, gp, 0:n1]
        V(lambda e: e.tensor_tensor(out=tt[0][:, 0:n1], in0=z2[0][:, 0:n1], in1=c_, op=ALU.mult), [dz[0]] + PT, [dtq[0]])
        V(lambda e: e.tensor_tensor(out=tt[1][:, 0:n1], in0=z2[1][:, 0:n1], in1=s_, op=ALU.mult), [dz[1]] + PT, [dtq[1]])
        V(lambda e: e.tensor_tensor(out=Sprev[:, gp, 0, 1:NM], in0=tt[0][:, 0:n1], in1=tt[1][:, 0:n1], op=ALU.subtract),
          [dtq[0], dtq[1]], [dS])
        V(lambda e: e.tensor_tensor(out=tt[2][:, 0:n1], in0=z2[0][:, 0:n1], in1=s_, op=ALU.mult), [dz[0]] + PT, [dtq[2]],
          "pool")
        V(lambda e: e.tensor_tensor(out=tt[3][:, 0:n1], in0=z2[1][:, 0:n1], in1=c_, op=ALU.mult), [dz[1]] + PT, [dtq[3]],
          "pool")
        V(lambda e: e.tensor_tensor(out=Sprev[:, gp, 1, 1:NM], in0=tt[2][:, 0:n1], in1=tt[3][:, 0:n1], op=ALU.add),
          [dtq[2], dtq[3]], [dS], "pool")

    def y_out(b, s):
        mm = []
        for sp_ in range(s + 1):
            mm.append((Kmat[:, s - sp_, :], uview(b, sp_), [dud] + PT))
        for gp in range(4):
            for ri in range(2):
                mm.append((WC[ri][:, s, gp, :], Sprev[:, gp, ri, :], [dS] + PT))
        for n_, (l_, r_, dd) in enumerate(mm):
            P.op("pe", lambda e, l_=l_, r_=r_, n_=n_: e.matmul(psy[:], l_, r_, start=(n_ == 0), stop=(n_ == len(mm) - 1)),
                 reads=dd, writes=[dpsy])
        yv = y_sb[:, b * SEQ:(b + 1) * SEQ].rearrange("p (m s) -> p m s", s=TS)[:, :, s]
        V(lambda e: e.scalar_tensor_tensor(out=yv, in0=uview(b, s), scalar=d_sb[:, 0:1], in1=psy[:], op0=ALU.mult,
                                           op1=ALU.add), [dud, dpsy] + PT, [dy])

    for b in range(2):
        for gp in range(4):
            e_pass(gp, b)
            scans(gp, [])
            carry(gp)
            scans(gp, [dcst])
            demod(gp)
        for s in range(TS):
            y_out(b, s)
    P.dma("sp", yp[:, 0:SEQ], y_sb[:, 0:SEQ], reads=[dy])
    P.dma("sp", yp[:, SEQ:NT], y_sb[:, SEQ:NT], reads=[dy])


def build_B2F():
    nc = bass.Bass("TRN2", target_bir_lowering=False)
    with ExitStack() as st:
        st.enter_context(nc.allow_low_precision("bf16 matmul operands, fp32 accumulation"))
        C = Ctx(nc, st)
        P = Prog(nc, st)
        emit_B2F(nc, C, P)
        P.finish()
    return nc
```

```python
import numpy as np
from contextlib import ExitStack
import ml_dtypes
import concourse.bass as bass
import concourse.mybir as mybir
from concourse.bass_utils import run_bass_kernel_spmd

F32 = mybir.dt.float32
BF16 = mybir.dt.bfloat16
AF = mybir.ActivationFunctionType
ALU = mybir.AluOpType
AX = mybir.AxisListType
NPBF = ml_dtypes.bfloat16

NCORES = 8
TOK = 1024
D = 2048
EPS = 1e-6
QSCALE = 128 ** -0.5


SAME_SYNC = {"pe": False, "act": True, "dve": True, "pool": True, "sp": True}


class Dep:
    __slots__ = ("w", "r")

    def __init__(self):
        self.w = None
        self.r = {}


class Prog:
    ENGS = ("pe", "act", "dve", "pool", "sp")

    def __init__(self, nc, stack, n_dma_sems=(("sp", 12), ("pool", 8), ("act", 4))):
        self.nc = nc
        self.stack = stack
        self.q = {e: [] for e in self.ENGS}
        self.sem = {e: stack.enter_context(nc.semaphore("s_" + e)) for e in ("pe", "act", "dve", "pool")}
        self.cnt = {e: 0 for e in self.sem}
        tot = sum(n for _, n in n_dma_sems)
        self.dsem = [stack.enter_context(nc.semaphore("d%d" % i)) for i in range(tot)]
        self.dcnt = [0] * tot
        self.dpool = {}
        b = 0
        for qn, n in n_dma_sems:
            self.dpool[qn] = [list(range(b, b + n)), 0]
            b += n
        self.seen = {e: {} for e in self.ENGS}
        self.same_sync = dict(SAME_SYNC)
        self._rec = None
        self._streams = {}

    def _semobj(self, key):
        return self.sem[key[1]] if key[0] == "e" else self.dsem[key[1]]

    def _collect(self, eng, reads, writes, extra=None):
        need = dict(extra or {})

        def req(k, v):
            if v > need.get(k, 0):
                need[k] = v
        for t in reads:
            if t.w:
                req(*t.w)
        for t in writes:
            if t.w:
                req(*t.w)
            for k, v in t.r.items():
                req(k, v)
        for k, v in need.items():
            if k == ("e", eng) and not self.same_sync[eng]:
                continue
            if self.seen[eng].get(k, 0) < v:
                self.seen[eng][k] = v
                s = self._semobj(k)
                self.q[eng].append(lambda e, s=s, v=v: e.wait_ge(s, v))

    def begin(self, name):
        self._rec = []
        self._streams[name] = self._rec

    def end(self):
        self._rec = None

    def replay(self, names, speed=None):
        recs = [self._streams[n] for n in names]
        pos = [0] * len(recs)
        total = sum(len(r) for r in recs)
        for _ in range(total):
            best, bestv = None, None
            for i, r in enumerate(recs):
                if pos[i] < len(r):
                    v = (pos[i] + 0.5) / (len(r) * (speed[i] if speed else 1.0))
                    if bestv is None or v < bestv:
                        best, bestv = i, v
            kind, args, kw = recs[best][pos[best]]
            pos[best] += 1
            getattr(self, kind)(*args, **kw)

    def op(self, eng, fn, reads=(), writes=()):
        if self._rec is not None:
            self._rec.append(("op", (eng, fn, tuple(reads), tuple(writes)), {}))
            return
        self._collect(eng, reads, writes)
        self.cnt[eng] += 1
        c = self.cnt[eng]
        key = ("e", eng)
        s = self.sem[eng]
        self.q[eng].append(lambda e, fn=fn, s=s: fn(e).then_inc(s, 1))
        for t in reads:
            if t.r.get(key, 0) < c:
                t.r[key] = c
        for t in writes:
            t.w = (key, c)
            t.r = {}

    def dma(self, queue, out, in_, reads=(), writes=(), in_fn=None, out_fn=None, **kw):
        if self._rec is not None:
            self._rec.append(("dma", (queue, out, in_, tuple(reads), tuple(writes), in_fn, out_fn), dict(kw)))
            return
        pl = self.dpool[queue]
        i = pl[0][pl[1] % len(pl[0])]
        pl[1] += 1
        extra = {}
        if self.dcnt[i] > 0:
            extra[("d", i)] = self.dcnt[i]
        self._collect(queue, reads, writes, extra)
        self.dcnt[i] += 16
        v = self.dcnt[i]
        key = ("d", i)
        s = self.dsem[i]
        self.q[queue].append(
            lambda e, s=s, out=out, in_=in_, kw=kw: e.dma_start(
                out=(out_fn() if out_fn else out), in_=(in_fn() if in_fn else in_), **kw).then_inc(s, 16))
        for t in reads:
            if t.r.get(key, 0) < v:
                t.r[key] = v
        for t in writes:
            t.w = (key, v)
            t.r = {}

    def raw(self, eng, fn, reads=(), writes=()):
        self._collect(eng, reads, writes)
        self.q[eng].append(lambda e, fn=fn: fn(e))

    def coll(self, queue, fn, reads=(), writes=()):
        pl = self.dpool[queue]
        i = pl[0][pl[1] % len(pl[0])]
        pl[1] += 1
        extra = {}
        if self.dcnt[i] > 0:
            extra[("d", i)] = self.dcnt[i]
        self._collect(queue, reads, writes, extra)
        self.dcnt[i] += 16
        v = self.dcnt[i]
        key = ("d", i)
        s = self.dsem[i]
        self.q[queue].append(lambda e, s=s, fn=fn: fn(e).then_inc(s, 16))
        for t in reads:
            if t.r.get(key, 0) < v:
                t.r[key] = v
        for t in writes:
            t.w = (key, v)
            t.r = {}

    def finish(self):
        for i, v in enumerate(self.dcnt):
            if v > 0 and self.seen["sp"].get(("d", i), 0) < v:
                s = self.dsem[i]
                self.q["sp"].append(lambda e, s=s, v=v: e.wait_ge(s, v))
        for e_, c in self.cnt.items():
            if c > 0:
                s = self.sem[e_]
                self.q["sp"].append(lambda e, s=s, c=c: e.wait_ge(s, c))
        q = self.q
        with self.nc.Block() as block:
            @block.tensor
            def _(e):
                for f in q["pe"]:
                    f(e)

            @block.scalar
            def _(e):
                for f in q["act"]:
                    f(e)

            @block.vector
            def _(e):
                for f in q["dve"]:
                    f(e)

            @block.gpsimd
            def _(e):
                for f in q["pool"]:
                    f(e)

            @block.sync
            def _(e):
                for f in q["sp"]:
                    f(e)


class Ctx:
    def __init__(self, nc, st):
        self.nc, self.st = nc, st

    def sb(self, name, shape, dt):
        return self.st.enter_context(self.nc.sbuf_tensor(name, shape, dt))

    def ps(self, name, shape, dt=F32):
        return self.st.enter_context(self.nc.psum_tensor(name, shape, dt))

    def din(self, name, shape, dt=F32):
        return self.nc.dram_tensor(name, list(shape), dt, kind="ExternalInput").ap()

    def dout(self, name, shape, dt=F32):
        return self.nc.dram_tensor(name, list(shape), dt, kind="ExternalOutput").ap()


def emit_rmsnorm(P, C, x_sb, dx, g_sb, dg, hT, dh, ones_bf, dones, sq, dsq, ps, dps, rstd, drstd, epsb, deps, ntok):
    dxl = dx if isinstance(dx, list) else [dx] * 4
    dhl = dh if isinstance(dh, list) else [dh] * 16
    for kt in range(16):
        i = kt % 2
        P.op("act", lambda e, kt=kt, i=i: e.activation(out=sq[i][:], in_=x_sb[:, kt, :], func=AF.Square),
             reads=[dxl[kt // 4]], writes=[dsq[i]])
        for hf in range(ntok // 512):
            P.op("pe", lambda e, kt=kt, i=i, hf=hf: e.matmul(ps[:, hf * 512:(hf + 1) * 512], ones_bf[:],
                                                             sq[i][:, hf * 512:(hf + 1) * 512],
                                                             start=(kt == 0), stop=(kt == 15)),
                 reads=[dsq[i], dones], writes=[dps])
    P.op("act", lambda e: e.activation(out=rstd[:], in_=ps[:, :ntok], func=AF.Sqrt, bias=epsb[:], scale=1.0 / D),
         reads=[dps, deps], writes=[drstd])
    P.op("dve", lambda e: e.reciprocal(out=rstd[:], in_=rstd[:]), reads=[drstd], writes=[drstd])
    for kt in range(16):
        P.op("dve", lambda e, kt=kt: e.scalar_tensor_tensor(out=hT[:, kt, :], in0=x_sb[:, kt, :],
                                                           scalar=g_sb[:, kt:kt + 1], in1=rstd[:],
                                                           op0=ALU.mult, op1=ALU.mult),
             reads=[dxl[kt // 4], dg, drstd], writes=[dhl[kt]])


def build_A():
    nc = bass.Bass("TRN2", target_bir_lowering=False)
    with ExitStack() as st:
        st.enter_context(nc.allow_low_precision("bf16 matmul operands, fp32 accumulation"))
        C = Ctx(nc, st)
        xT = C.din("xT", [D, TOK])
        n1g = C.din("n1g", [128, 16])
        w_in = C.din("w_in", [D, 12288])
        bg = C.din("bg", [128, 48])
        gng = C.din("gng", [1, 1024])
        wsT = C.din("wsT", [8, 128, 128])
        bs = C.din("bs", [1, 1024])
        yaT = C.dout("yaT", [1024, TOK], BF16)
        s5T = C.dout("s5T", [1024, TOK], BF16)
        qT = C.dout("qT", [1024, TOK], BF16)
        kT = C.dout("kT", [1024, TOK], BF16)
        vtk = C.dout("vtk", [TOK, 1024], BF16)
        gT = C.dout("gT", [6144, TOK], BF16)

        P = Prog(nc, st)
        x_sb = C.sb("x_sb", [128, 16, TOK], F32)
        hT = C.sb("hT", [128, 16, TOK], BF16)
        g_sb = C.sb("g_sb", [128, 16], F32)
        bg_sb = C.sb("bg_sb", [128, 48], F32)
        gng_b = C.sb("gng_b", [128, 1024], F32)
        bs_b = C.sb("bs_b", [128, 1024], F32)
        ws_f = C.sb("ws_f", [128, 8, 128], F32)
        ws_b = C.sb("ws_b", [128, 8, 128], BF16)
        ones_bf = C.sb("ones_bf", [128, 128], BF16)
        epsb = C.sb("epsb", [128, 1], F32)
        sq = [C.sb("sq%d" % i, [128, TOK], BF16) for i in range(2)]
        rstd = C.sb("rstd", [128, TOK], F32)
        wt = [C.sb("wt%d" % i, [128, 16, 512], BF16) for i in range(2)]
        stg = [C.sb("stg%d" % i, [128, TOK], BF16) for i in range(2)]
        uT = C.sb("uT", [128, 8, TOK], BF16)
        vtok = C.sb("vtok", [128, 8, 1024], BF16)
        vn = C.sb("vn", [128, 1024], BF16)
        junk = sq[0]
        vss = C.sb("vss", [128, 8], F32)
        vrs = C.sb("vrs", [128, 8], F32)
        mixt = rstd
        ya_sb = C.sb("ya_sb", [128, 8, TOK], BF16)
        ps = [C.ps("ps%d" % i, [128, 1024]) for i in range(2)]
        psm = C.ps("psm", [128, 1024])

        dx, dh, dg, dbg, dgng, dbs, dwsf, dwsb, dones, deps = [Dep() for _ in range(10)]
        dsq = [Dep(), Dep()]
        drstd = Dep()
        dwt = [Dep(), Dep()]
        dstg = [Dep(), Dep()]
        dps = [Dep(), Dep()]
        dpsm, duT, dvtok, dvn, dvss, dvrs, dya = [Dep() for _ in range(7)]
        djunk = dsq[0]
        dmixt = drstd

        xv = xT.rearrange("(kt p) t -> p kt t", p=128)
        dx = [Dep() for _ in range(4)]
        dh = [Dep() for _ in range(16)]
        for i in range(4):
            P.dma("sp" if i % 2 == 0 else "act", x_sb[:, 4 * i:4 * i + 4, :], xv[:, 4 * i:4 * i + 4, :], writes=[dx[i]])
        P.dma("sp", g_sb[:], n1g, writes=[dg])
        P.dma("sp", bg_sb[:], bg, writes=[dbg])
        P.dma("sp", gng_b[:], gng.partition_broadcast(128), writes=[dgng])
        P.dma("sp", bs_b[:], bs.partition_broadcast(128), writes=[dbs])
        P.dma("sp", ws_f[:], wsT.rearrange("g s t -> s g t"), writes=[dwsf])
        wv = w_in.rearrange("(kt p) c -> p kt c", p=128)

        def load_w(cb):
            P.dma("pool", wt[cb % 2][:], wv[:, :, cb * 512:(cb + 1) * 512], writes=[dwt[cb % 2]])
        load_w(0)
        P.op("dve", lambda e: e.memset(ones_bf[:], 1.0), writes=[dones])
        P.op("dve", lambda e: e.memset(epsb[:], EPS), writes=[deps])
        P.op("pool", lambda e: e.affine_select(out=ws_f[:], in_=ws_f[:], pattern=[[0, 8], [1, 128]],
                                               compare_op=ALU.is_ge, fill=0.0, base=0, channel_multiplier=-1),
             reads=[dwsf], writes=[dwsf])
        P.op("dve", lambda e: e.tensor_copy(out=ws_b[:], in_=ws_f[:]), reads=[dwsf], writes=[dwsb])

        emit_rmsnorm(P, C, x_sb, dx, g_sb, dg, hT, dh, ones_bf, dones, sq, dsq, ps[0], dps[0], rstd, drstd,
                     epsb, deps, TOK)

        pcount = [0]
        scount = [0]

        def form2(cb, epi, hook=None):
            w = wt[cb % 2]
            for m in range(4):
                pi = pcount[0] % 2
                pcount[0] += 1
                for hf in range(2):
                    for kt in range(16):
                        P.op("pe", lambda e, w=w, m=m, hf=hf, kt=kt, pi=pi: e.matmul(
                            ps[pi][:, hf * 512:(hf + 1) * 512], w[:, kt, m * 128:(m + 1) * 128],
                            hT[:, kt, hf * 512:(hf + 1) * 512], start=(kt == 0), stop=(kt == 15)),
                            reads=[dwt[cb % 2], dh[kt]], writes=[dps[pi]])
                if hook is not None:
                    hook(m)
                epi(cb * 4 + m, pi)

        def epi_store(func, dst, colbase, scale=1.0, bias_col=None):
            def epi(blk, pi):
                si = scount[0] % 2
                scount[0] += 1
                r0 = (blk - colbase) * 128
                if bias_col is None:
                    P.op("act", lambda e: e.activation(out=stg[si][:], in_=ps[pi][:], func=func, scale=scale),
                         reads=[dps[pi]], writes=[dstg[si]])
                else:
                    bc = bias_col(blk)
                    P.op("act", lambda e: e.activation(out=stg[si][:], in_=ps[pi][:], func=func,
                                                       bias=bg_sb[:, bc:bc + 1]),
                         reads=[dps[pi], dbg], writes=[dstg[si]])
                P.dma("sp", dst[r0:r0 + 128, :], stg[si][:], reads=[dstg[si]])
            return epi

        def epi_u(blk, pi):
            P.op("act", lambda e: e.activation(out=uT[:, blk, :], in_=ps[pi][:], func=AF.Gelu_apprx_tanh),
                 reads=[dps[pi]], writes=[duT])

        def form1(cb, epi):
            w = wt[cb % 2]
            for c in range(8):
                pi = pcount[0] % 2
                pcount[0] += 1
                for kt in range(16):
                    P.op("pe", lambda e, w=w, c=c, kt=kt, pi=pi: e.matmul(
                        ps[pi][:, 0:512], hT[:, kt, c * 128:(c + 1) * 128], w[:, kt, :],
                        start=(kt == 0), stop=(kt == 15)),
                        reads=[dwt[cb % 2], dh[kt]], writes=[dps[pi]])
                epi(cb, c, pi)

        def epi_vg(cb, c, pi):
            off = (cb - 2) * 512
            P.op("act", lambda e: e.activation(out=vtok[:, c, off:off + 512], in_=ps[pi][:, 0:512],
                                               func=AF.Gelu_apprx_tanh),
                 reads=[dps[pi]], writes=[dvtok])

        def epi_va(cb, c, pi):
            off = (cb - 10) * 512
            si = scount[0] % 2
            scount[0] += 1
            P.op("act", lambda e: e.activation(out=stg[si][:, 0:512], in_=ps[pi][:, 0:512], func=AF.Copy),
                 reads=[dps[pi]], writes=[dstg[si]])
            P.dma("sp", vtk[c * 128:(c + 1) * 128, off:off + 512], stg[si][:, 0:512], reads=[dstg[si]])

        def gmlp_stats():
            for c in range(8):
                P.op("act", lambda e, c=c: e.activation(out=junk[:], in_=vtok[:, c, :], func=AF.Square,
                                                        accum_out=vss[:, c:c + 1]),
                     reads=[dvtok], writes=[djunk, dvss])
            P.op("act", lambda e: e.activation(out=vrs[:], in_=vss[:], func=AF.Sqrt, bias=epsb[:], scale=1.0 / 1024),
                 reads=[dvss, deps], writes=[dvrs])
            P.op("dve", lambda e: e.reciprocal(out=vrs[:], in_=vrs[:]), reads=[dvrs], writes=[dvrs])

        def gmlp_chunk(c):
            P.op("dve", lambda e: e.scalar_tensor_tensor(out=vn[:], in0=vtok[:, c, :], scalar=vrs[:, c:c + 1],
                                                         in1=gng_b[:], op0=ALU.mult, op1=ALU.mult),
                 reads=[dvtok, dvrs, dgng], writes=[dvn])
            for g in range(8):
                P.op("pe", lambda e, g=g: e.matmul(psm[:, g * 128:(g + 1) * 128], vn[:, g * 128:(g + 1) * 128],
                                                   ws_b[:, g, :], start=True, stop=True),
                     reads=[dvn, dwsb], writes=[dpsm])
            P.op("dve", lambda e: e.tensor_tensor(out=mixt[:], in0=psm[:], in1=bs_b[:], op=ALU.add),
                 reads=[dpsm, dbs], writes=[dmixt])
            P.op("pool", lambda e: e.tensor_tensor(
                out=ya_sb[:, :, c * 128:(c + 1) * 128], in0=mixt[:].rearrange("p (g t) -> p g t", g=8),
                in1=uT[:, :, c * 128:(c + 1) * 128], op=ALU.mult),
                reads=[dmixt, duT], writes=[dya])
            if c == 7:
                P.dma("sp", yaT.rearrange("(g p) t -> p g t", p=128), ya_sb[:], reads=[dya])

        for cb in range(24):
            if cb + 1 < 24:
                load_w(cb + 1)
            if cb < 2:
                form2(cb, epi_u)
            elif cb < 4:
                form1(cb, epi_vg)
                if cb == 3:
                    gmlp_stats()
            elif cb < 6:
                form2(cb, epi_store(AF.Copy, s5T, 16), hook=lambda m, cb=cb: gmlp_chunk((cb - 4) * 4 + m))
            elif cb < 8:
                form2(cb, epi_store(AF.Copy, qT, 24, scale=QSCALE))
            elif cb < 10:
                form2(cb, epi_store(AF.Copy, kT, 32))
            elif cb < 12:
                form1(cb, epi_va)
            else:
                form2(cb, epi_store(AF.Sigmoid, gT, 48, bias_col=lambda blk: blk - 48))
        P.finish()
    return nc


def host_A(x_tok, l, inp):
    maps = []
    n1g = np.ascontiguousarray(inp["norm1_g"][l].reshape(16, 128).T)
    bgv = np.ascontiguousarray(inp["b_gate"][l].reshape(48, 128).T)
    gng = np.ascontiguousarray(inp["gm_norm_g"][l].reshape(1, 1024))
    wsT = np.ascontiguousarray(np.transpose(inp["gm_w_s"][l], (0, 2, 1)))
    bsv = np.ascontiguousarray(inp["gm_b_s"][l].reshape(1, 1024))
    w_in = np.ascontiguousarray(inp["w_in"][l])
    for c in range(NCORES):
        xT = np.ascontiguousarray(x_tok[c * TOK:(c + 1) * TOK, :].T)
        maps.append({"xT": xT, "n1g": n1g, "w_in": w_in, "bg": bgv, "gng": gng, "wsT": wsT, "bs": bsv})
    return maps


def build_C(last, debug=False):
    nc = bass.Bass("TRN2", target_bir_lowering=False)
    with ExitStack() as st:
        st.enter_context(nc.allow_low_precision("bf16 matmul operands, fp32 accumulation"))
        C = Ctx(nc, st)
        xT = C.din("xT", [D, TOK])
        yaT = C.din("yaT", [1024, TOK], BF16)
        ybpT = C.din("ybpT", [1024, TOK], BF16)
        ycT = C.din("ycT", [1024, TOK], BF16)
        gT = C.din("gT", [6144, TOK], BF16)
        w_glu = C.din("w_glu", [1024, 1024])
        b_glu = C.din("b_glu", [128, 8])
        w_br = C.din("w_br", [3, 1024, D])
        w_out = C.din("w_out", [D, D])
        n2g = C.din("n2g", [128, 16])
        w_m1 = C.din("w_m1", [D, 8192])
        w_m2 = C.din("w_m2", [8192, D])
        fing = C.din("fing", [128, 16])
        xo = C.dout("xo", [D, TOK])
        if debug:
            dbg_yb = C.dout("dbg_yb", [1024, TOK], BF16)
            dbg_mg = C.dout("dbg_mg", [D, TOK], BF16)
            dbg_x1 = C.dout("dbg_x1", [D, TOK])
            dbg_h2 = C.dout("dbg_h2", [D, TOK], BF16)

        P = Prog(nc, st)
        x_sb = C.sb("x_sb", [128, 16, TOK], F32)
        R2 = C.sb("R2", [128, 3, 8, TOK], BF16)
        R3 = C.sb("R3", [128, 16 * TOK], BF16)
        R4 = C.sb("R4", [128, 4, 8 * 512], BF16)
        gts = [C.sb("gts%d" % i, [128, TOK], BF16) for i in range(6)]
        acc = C.sb("acc", [128, TOK], F32)
        tmp = C.sb("tmp", [128, TOK], F32)
        stg = [C.sb("stg%d" % i, [128, TOK], BF16) for i in range(2)]
        rstd = C.sb("rstd", [128, TOK], F32)
        sq = stg
        g2_sb = C.sb("g2_sb", [128, 16], F32)
        gf_sb = C.sb("gf_sb", [128, 16], F32)
        bgl_sb = C.sb("bgl_sb", [128, 8], F32)
        ones_bf = C.sb("ones_bf", [128, 128], BF16)
        epsb = C.sb("epsb", [128, 1], F32)
        ps = [C.ps("ps%d" % i, [128, 1024]) for i in range(3)]

        dx, dR3, dg2, dgf, dbgl, dones, deps, dacc, dtmp, drstd = [Dep() for _ in range(10)]
        dR2 = [Dep(), Dep(), Dep()]
        dR4 = [Dep() for _ in range(4)]
        dgts = [Dep() for _ in range(6)]
        dstg = [Dep(), Dep()]
        dsq = dstg
        dps = [Dep() for _ in range(3)]

        gS = R3[:, 0:8 * TOK].rearrange("p (k t) -> p k t", k=8)
        mg = R3[:].rearrange("p (k t) -> p k t", k=16)
        aT = R2[:].rearrange("p a k t -> p (a k) t")
        w8 = [R4[:, i, :].rearrange("p (k c) -> p k c", k=8) for i in range(4)]
        w16 = [R4[:, 2 * i:2 * i + 2, :].rearrange("p a (k c) -> p (a k) c", k=8) for i in range(2)]

        xv = xT.rearrange("(kt p) t -> p kt t", p=128)
        P.dma("sp", R2[:, 1], ybpT.rearrange("(k p) t -> p k t", p=128), writes=[dR2[1]])
        P.dma("sp", bgl_sb[:], b_glu, writes=[dbgl])
        for i in range(4):
            P.dma("sp", x_sb[:, 4 * i:4 * i + 4, :], xv[:, 4 * i:4 * i + 4, :], writes=[dx])
        P.dma("sp", R2[:, 0], yaT.rearrange("(k p) t -> p k t", p=128), writes=[dR2[0]])
        P.dma("sp", R2[:, 2], ycT.rearrange("(k p) t -> p k t", p=128), writes=[dR2[2]])
        P.dma("sp", g2_sb[:], n2g, writes=[dg2])
        P.dma("sp", gf_sb[:], fing, writes=[dgf])
        P.op("dve", lambda e: e.memset(ones_bf[:], 1.0), writes=[dones])
        P.op("dve", lambda e: e.memset(epsb[:], EPS), writes=[deps])

        jobs = []
        glu_v = w_glu.rearrange("(kt p) c -> p kt c", p=128)
        for cb in range(2):
            jobs.append(("glu", cb, 8, glu_v[:, :, cb * 512:(cb + 1) * 512]))
        for cb in range(4):
            for n in range(3):
                jobs.append(("br", (cb, n), 8, w_br[n].rearrange("(kt p) c -> p kt c", p=128)[:, :, cb * 512:(cb + 1) * 512]))
        wo_v = w_out.rearrange("(kt p) c -> p kt c", p=128)
        for cb in range(4):
            jobs.append(("wo", cb, 16, wo_v[:, :, cb * 512:(cb + 1) * 512]))
        m1_v = w_m1.rearrange("(kt p) c -> p kt c", p=128)
        m2_v = w_m2.rearrange("(kt p) c -> p kt c", p=128)
        for fc in range(4):
            for fb in range(4):
                c0 = fc * 2048 + fb * 512
                jobs.append(("m1", (fc, fb), 16, m1_v[:, :, c0:c0 + 512]))
            for cb in range(4):
                jobs.append(("m2", (fc, cb), 16, m2_v[:, fc * 16:(fc + 1) * 16, cb * 512:(cb + 1) * 512]))
        slot8 = 0
        slots = []
        quarters = []
        for kind, key, nk, src in jobs:
            if nk == 8:
                s = slot8 % 4
                slot8 += 1
                slots.append((w8[s], [dR4[s]]))
                quarters.append([s])
            else:
                if slot8 % 2:
                    slot8 += 1
                s = (slot8 // 2) % 2
                slot8 += 2
                slots.append((w16[s], [dR4[2 * s], dR4[2 * s + 1]]))
                quarters.append([2 * s, 2 * s + 1])
        issued = [0]
        occupant = [None] * 4
        consumed = set()

        def prefetch(upto):
            while issued[0] < min(upto, len(jobs)):
                j = issued[0]
                if any(occupant[q] is not None and occupant[q] not in consumed for q in quarters[j]):
                    return
                for q in quarters[j]:
                    occupant[q] = j
                P.dma("pool", slots[j][0], jobs[j][3], writes=slots[j][1])
                issued[0] += 1

        def done(j):
            consumed.add(j)
            prefetch(j + 4)
        pc = [0]

        def mm_block(j, m, rhs, drhs, nkt):
            pi = pc[0] % 3
            pc[0] += 1
            prefetch(j + 1)
            assert issued[0] > j, "weight job %d not loadable (ring slot busy)" % j
            wv, wd = slots[j]
            for hf in range(2):
                for kt in range(nkt):
                    P.op("pe", lambda e, wv=wv, m=m, hf=hf, kt=kt, pi=pi: e.matmul(
                        ps[pi][:, hf * 512:(hf + 1) * 512], wv[:, kt, m * 128:(m + 1) * 128],
                        rhs[:, kt, hf * 512:(hf + 1) * 512], start=(kt == 0), stop=(kt == nkt - 1)),
                        reads=wd + drhs, writes=[dps[pi]])
            return pi

        jn = [0]
        prefetch(3)
        for k in range(8):
            P.op("act", lambda e, k=k: e.activation(out=gS[:, k, :], in_=R2[:, 1, k, :], func=AF.Gelu_apprx_tanh),
                 reads=[dR2[1]], writes=[dR3])
        sc = [0]
        for cb in range(2):
            j = jn[0]
            jn[0] += 1
            prefetch(j + 3)
            for m in range(4):
                blk = cb * 4 + m
                pi = mm_block(j, m, gS, [dR3], 8)
                si = sc[0] % 2
                sc[0] += 1
                P.op("act", lambda e, pi=pi, si=si, blk=blk: e.activation(
                    out=stg[si][:], in_=ps[pi][:], func=AF.Sigmoid, bias=bgl_sb[:, blk:blk + 1]),
                    reads=[dps[pi], dbgl], writes=[dstg[si]])
                P.op("dve", lambda e, si=si, blk=blk: e.tensor_tensor(
                    out=R2[:, 1, blk, :], in0=gS[:, blk, :], in1=stg[si][:], op=ALU.mult),
                    reads=[dR3, dstg[si]], writes=[dR2[1]])
            done(j)
        if debug:
            P.dma("sp", dbg_yb.rearrange("(k p) t -> p k t", p=128), R2[:, 1], reads=[dR2[1]])
        gc = [0]
        for cb in range(4):
            js = [jn[0], jn[0] + 1, jn[0] + 2]
            jn[0] += 3
            prefetch(js[2] + 2)
            for m in range(4):
                dt_ = cb * 4 + m
                for n in range(3):
                    gi = gc[0] % 6
                    gc[0] += 1
                    r0 = (n * 16 + dt_) * 128
                    P.dma("sp", gts[gi][:], gT[r0:r0 + 128, :], writes=[dgts[gi]])
                    pi = mm_block(js[n], m, R2[:, n], [dR2[n]], 8)
                    if n == 0:
                        P.op("dve", lambda e, pi=pi, gi=gi: e.tensor_tensor(out=acc[:], in0=ps[pi][:], in1=gts[gi][:],
                                                                            op=ALU.mult),
                             reads=[dps[pi], dgts[gi]], writes=[dacc])
                    else:
                        P.op("dve", lambda e, pi=pi, gi=gi: e.tensor_tensor(out=tmp[:], in0=ps[pi][:], in1=gts[gi][:],
                                                                            op=ALU.mult),
                             reads=[dps[pi], dgts[gi]], writes=[dtmp])
                        if n == 1:
                            P.op("dve", lambda e: e.tensor_tensor(out=acc[:], in0=acc[:], in1=tmp[:], op=ALU.add),
                                 reads=[dacc, dtmp], writes=[dacc])
                        else:
                            P.op("dve", lambda e, dt_=dt_: e.tensor_tensor(out=mg[:, dt_, :], in0=acc[:], in1=tmp[:],
                                                                          op=ALU.add),
                                 reads=[dacc, dtmp], writes=[dR3])
            for j_ in js:
                done(j_)
        if debug:
            P.dma("sp", dbg_mg.rearrange("(k p) t -> p k t", p=128), mg, reads=[dR3])
        for cb in range(4):
            j = jn[0]
            jn[0] += 1
            prefetch(j + 2)
            for m in range(4):
                dt_ = cb * 4 + m
                pi = mm_block(j, m, mg, [dR3], 16)
                P.op("dve", lambda e, pi=pi, dt_=dt_: e.tensor_tensor(out=x_sb[:, dt_, :], in0=x_sb[:, dt_, :],
                                                                      in1=ps[pi][:], op=ALU.add),
                     reads=[dps[pi], dx], writes=[dx])
            done(j)
        if debug:
            P.dma("sp", dbg_x1.rearrange("(k p) t -> p k t", p=128), x_sb[:], reads=[dx])
        emit_rmsnorm(P, C, x_sb, dx, g2_sb, dg2, mg, dR3, ones_bf, dones, sq, dsq, ps[0], dps[0], rstd, drstd,
                     epsb, deps, TOK)
        if debug:
            P.dma("sp", dbg_h2.rearrange("(k p) t -> p k t", p=128), mg, reads=[dR3])
        for fc in range(4):
            for fb in range(4):
                j = jn[0]
                jn[0] += 1
                prefetch(j + 2)
                for m in range(4):
                    ft = fb * 4 + m
                    pi = mm_block(j, m, mg, [dR3], 16)
                    P.op("act", lambda e, pi=pi: e.activation(out=tmp[:], in_=ps[pi][:], func=AF.Relu),
                         reads=[dps[pi]], writes=[dtmp])
                    P.op("dve", lambda e, ft=ft: e.tensor_tensor(out=aT[:, ft, :], in0=tmp[:], in1=tmp[:], op=ALU.mult),
                         reads=[dtmp], writes=dR2)
                done(j)
            for cb in range(4):
                j = jn[0]
                jn[0] += 1
                prefetch(j + 2)
                for m in range(4):
                    dt_ = cb * 4 + m
                    pi = mm_block(j, m, aT, dR2, 16)
                    P.op("dve", lambda e, pi=pi, dt_=dt_: e.tensor_tensor(out=x_sb[:, dt_, :], in0=x_sb[:, dt_, :],
                                                                          in1=ps[pi][:], op=ALU.add),
                         reads=[dps[pi], dx], writes=[dx])
                done(j)
        xov = xo.rearrange("(kt p) t -> p kt t", p=128)
        if not last:
            for i in range(4):
                P.dma("sp", xov[:, 4 * i:4 * i + 4, :], x_sb[:, 4 * i:4 * i + 4, :], reads=[dx])
        else:
            for kt in range(16):
                i = kt % 2
                P.op("act", lambda e, kt=kt, i=i: e.activation(out=sq[i][:], in_=x_sb[:, kt, :], func=AF.Square),
                     reads=[dx], writes=[dsq[i]])
                for hf in range(2):
                    P.op("pe", lambda e, kt=kt, i=i, hf=hf: e.matmul(ps[0][:, hf * 512:(hf + 1) * 512], ones_bf[:],
                                                                     sq[i][:, hf * 512:(hf + 1) * 512],
                                                                     start=(kt == 0), stop=(kt == 15)),
                         reads=[dsq[i], dones], writes=[dps[0]])
            P.op("act", lambda e: e.activation(out=rstd[:], in_=ps[0][:], func=AF.Sqrt, bias=epsb[:], scale=1.0 / D),
                 reads=[dps[0], deps], writes=[drstd])
            P.op("dve", lambda e: e.reciprocal(out=rstd[:], in_=rstd[:]), reads=[drstd], writes=[drstd])
            fo = [acc, tmp]
            dfo = [dacc, dtmp]
            for kt in range(16):
                i = kt % 2
                P.op("dve", lambda e, kt=kt, i=i: e.scalar_tensor_tensor(out=fo[i][:], in0=x_sb[:, kt, :],
                                                                       scalar=gf_sb[:, kt:kt + 1], in1=rstd[:],
                                                                       op0=ALU.mult, op1=ALU.mult),
                     reads=[dx, dgf, drstd], writes=[dfo[i]])
                P.dma("sp", xov[:, kt, :], fo[i][:], reads=[dfo[i]])
        P.finish()
    return nc


def host_C(x_tok, l, inp, yaT, ybpT, ycT, gT, last):
    maps = []
    r = lambda v, n: np.ascontiguousarray(v.reshape(n, 128).T)
    com = {"w_glu": np.ascontiguousarray(inp["s5_w_glu"][l]), "b_glu": r(inp["s5_b_glu"][l], 8),
           "w_br": np.ascontiguousarray(inp["w_branch"][l]), "w_out": np.ascontiguousarray(inp["w_out"][l]),
           "n2g": r(inp["norm2_g"][l], 16), "w_m1": np.ascontiguousarray(inp["w_mlp_in"][l]),
           "w_m2": np.ascontiguousarray(inp["w_mlp_out"][l]), "fing": r(inp["final_g"], 16)}
    for c in range(NCORES):
        m = dict(com)
        m["xT"] = np.ascontiguousarray(x_tok[c * TOK:(c + 1) * TOK, :].T)
        m["yaT"], m["ybpT"], m["ycT"], m["gT"] = yaT[c], ybpT[c], ycT[c], gT[c]
        maps.append(m)
    return maps


SEQ = 4096
NPAIR = 2


def emit_B1(nc, C, P, merged=False):
    qT = C.din("qT", [NPAIR, 128, SEQ], BF16)
    kT = C.din("kT", [NPAIR, 128, SEQ], BF16)
    vv = C.din("v", [NPAIR, SEQ, 128], BF16)
    yc = C.dout("yc", [NPAIR, 128, SEQ], BF16)
    NB = 1 if merged else NPAIR
    q_sb = C.sb("q_sb", [128, NB, SEQ], BF16)
    k_sb = C.sb("k_sb", [128, NB, SEQ], BF16)
    v_sb = C.sb("v_sb", [128, NB, 32, 128], BF16)
    o_st = [C.sb("o_st%d" % i, [128, 512], BF16) for i in range(2)]
    ones_f = C.sb("ones_f", [128, 128], F32)
    mstrict = C.sb("mstrict", [128, 128], BF16)
    negtri = C.sb("negtri", [128, 128], BF16)
    negones = C.sb("negones", [128, 128], BF16)
    spsum = [C.sb("spsum%d" % i, [128, 512], BF16) for i in range(2)]
    ebuf = [C.sb("ebuf%d" % i, [128, 512], F32) for i in range(2)]
    spb = [C.sb("spb%d" % i, [128, 512], BF16) for i in range(3)]
    wbuf = [C.sb("wbuf%d" % i, [128, 512], BF16) for i in range(3)]
    psA = [C.ps("psA%d" % i, [128, 512]) for i in range(2)]
    psB = [C.ps("psB%d" % i, [128, 512]) for i in range(2)]
    psO = [C.ps("psO%d" % i, [128, 512]) for i in range(1 if merged else 2)]
    if merged:
        psO = [psO[0], psO[0]]
    dq, dk, dv, dconst = [Dep() for _ in range(4)]
    dspsum = [Dep(), Dep()]
    do_ = [Dep(), Dep()]
    debuf = [Dep(), Dep()]
    dspb = [Dep() for _ in range(3)]
    dwbuf = [Dep() for _ in range(3)]
    dpsA = [Dep(), Dep()]
    dpsB = [Dep(), Dep()]
    dpsO = [Dep(), Dep()]
    if merged:
        dpsO = [dpsO[0], dpsO[0]]

    def load_pair(p):
        pb = p % NB
        P.dma("sp", q_sb[:, pb, :], qT[p], writes=[dq])
        P.dma("sp", k_sb[:, pb, :], kT[p], writes=[dk])
        P.dma("sp", v_sb[:, pb], vv[p].rearrange("(b s) d -> s b d", s=128), writes=[dv])
    for p in range(NB):
        load_pair(p)
    P.op("pool", lambda e: e.memset(ones_f[:], 1.0), writes=[dconst])
    P.op("pool", lambda e: e.affine_select(out=mstrict[:], in_=ones_f[:], pattern=[[1, 128]],
                                           compare_op=ALU.is_gt, fill=0.0, base=0, channel_multiplier=-1),
         reads=[dconst], writes=[dconst])
    P.op("pool", lambda e: e.memset(ones_f[:], -1.0), reads=[dconst], writes=[dconst])
    P.op("pool", lambda e: e.affine_select(out=negtri[:], in_=ones_f[:], pattern=[[-1, 128]],
                                           compare_op=ALU.is_ge, fill=0.0, base=0, channel_multiplier=1),
         reads=[dconst], writes=[dconst])
    P.op("pool", lambda e: e.tensor_copy(out=negones[:], in_=ones_f[:]), reads=[dconst], writes=[dconst])

    steps = []
    for p in range(NPAIR):
        for g in range(8):
            for sb in range(4 * g + 3, -1, -1):
                steps.append((p, g, sb))
    n = len(steps)

    def info(i):
        p, g, sb = steps[i]
        tl = max(0, sb - 4 * g) * 128
        return p, g, sb, tl, (sb >= 4 * g)

    def stage1(i):
        p, g, sb, tl, diag = info(i)
        a, s3 = i % 2, i % 3
        if merged and p > 0 and g == 0 and sb == 3:
            load_pair(p)
        P.op("pe", lambda e: e.matmul(psA[a][:, tl:512], k_sb[:, p % NB, sb * 128:(sb + 1) * 128],
                                      q_sb[:, p % NB, g * 512 + tl:(g + 1) * 512], start=True, stop=True),
             reads=[dq, dk], writes=[dpsA[a]])
        P.op("act", lambda e: e.activation(out=ebuf[a][:, tl:512], in_=psA[a][:, tl:512], func=AF.Exp),
             reads=[dpsA[a]], writes=[debuf[a]])
        P.op("act", lambda e: e.activation(out=spb[s3][:, tl:512], in_=ebuf[a][:, tl:512], func=AF.Ln, bias=1.0),
             reads=[debuf[a]], writes=[dspb[s3]])
        if diag:
            P.op("pool", lambda e: e.tensor_tensor(out=spb[s3][:, tl:tl + 128], in0=spb[s3][:, tl:tl + 128],
                                                   in1=mstrict[:], op=ALU.mult),
                 reads=[dspb[s3], dconst], writes=[dspb[s3]])

    def stage2(i):
        p, g, sb, tl, diag = info(i)
        a, s3 = i % 2, i % 3
        c0, c1 = i % 2, (i + 1) % 2
        if sb == 4 * g + 3:
            P.op("pool", lambda e: e.memset(spsum[c0][:], 0.0), writes=[dspsum[c0]])
            P.op("pool", lambda e: e.memset(spsum[c1][:], 0.0), writes=[dspsum[c1]])
        P.op("pe", lambda e: e.matmul(psB[a][:, tl:512], k_sb[:, p % NB, sb * 128:(sb + 1) * 128],
                                      q_sb[:, p % NB, g * 512 + tl:(g + 1) * 512], start=True, stop=False),
             reads=[dq, dk], writes=[dpsB[a]])
        P.op("pe", lambda e: e.matmul(psB[a][:, tl:512], negtri[:], spb[s3][:, tl:512], start=False, stop=False),
             reads=[dspb[s3], dconst], writes=[dpsB[a]])
        P.op("pe", lambda e: e.matmul(psB[a][:, tl:512], negones[:], spsum[c0][:, tl:512], start=False, stop=True),
             reads=[dspsum[c0], dconst], writes=[dpsB[a]])
        P.op("pool", lambda e: e.tensor_tensor(out=spsum[c1][:, tl:512], in0=spsum[c0][:, tl:512],
                                               in1=spb[s3][:, tl:512], op=ALU.add),
             reads=[dspsum[c0], dspb[s3]], writes=[dspsum[c1]])
        P.op("act", lambda e: e.activation(out=wbuf[s3][:, tl:512], in_=psB[a][:, tl:512], func=AF.Exp),
             reads=[dpsB[a]], writes=[dwbuf[s3]])
        if diag:
            P.op("pool", lambda e: e.tensor_tensor(out=wbuf[s3][:, tl:tl + 128], in0=wbuf[s3][:, tl:tl + 128],
                                                   in1=mstrict[:], op=ALU.mult),
                 reads=[dwbuf[s3], dconst], writes=[dwbuf[s3]])

    def stage3(i):
        p, g, sb, tl, diag = info(i)
        s3 = i % 3
        o = (p * 8 + g) % 2
        for tb in range(tl // 128, 4):
            P.op("pe", lambda e, tb=tb: e.matmul(psO[o][:, tb * 128:(tb + 1) * 128], v_sb[:, p % NB, sb, :],
                                                 wbuf[s3][:, tb * 128:(tb + 1) * 128],
                                                 start=(sb == 4 * g + 3 and tb == 3), stop=(sb == 0),
                                                 skip_group_check=True),
                 reads=[dv, dwbuf[s3]], writes=[dpsO[o]])
        if sb == 0:
            if merged:
                P.op("act", lambda e: e.activation(out=o_st[o][:], in_=psO[o][:], func=AF.Copy), reads=[dpsO[o]],
                     writes=[do_[o]])
            else:
                P.op("dve", lambda e: e.tensor_copy(out=o_st[o][:], in_=psO[o][:]), reads=[dpsO[o]], writes=[do_[o]])
            P.dma("sp", yc[p][:, g * 512:(g + 1) * 512], o_st[o][:], reads=[do_[o]])

    for it in range(n + 2):
        if it < n:
            stage1(it)
        if 0 <= it - 1 < n:
            stage2(it - 1)
        if 0 <= it - 2 < n:
            stage3(it - 2)


def build_B1():
    nc = bass.Bass("TRN2", target_bir_lowering=False)
    with ExitStack() as st:
        st.enter_context(nc.allow_low_precision("bf16 matmul operands, fp32 accumulation"))
        C = Ctx(nc, st)
        P = Prog(nc, st)
        emit_B1(nc, C, P)
        P.finish()
    return nc


I32 = mybir.dt.int32
PI = 3.14159265358979
TWO_PI = 2.0 * PI
NT = 2 * SEQ


def emit_B2(nc, C, P, merged=False, debug=False):
    uT = C.din("uT", [128, NT], BF16)
    lam_re = C.din("lam_re", [128, 4])
    lam_im = C.din("lam_im", [128, 4])
    log_dt = C.din("log_dt", [128, 4])
    b_re = C.din("b_re", [128, 4, 16])
    b_im = C.din("b_im", [128, 4, 16])
    c_reT = C.din("c_reT", [128, 4, 16])
    c_imT = C.din("c_imT", [128, 4, 16])
    dvec = C.din("dvec", [128, 1])
    yp = C.dout("yp", [128, NT], BF16)
    f4 = lambda n: C.sb(n, [128, 4], F32)
    u_sb = C.sb("u_sb", [128, NT], BF16)
    y_sb = C.sb("y_sb", [128, NT], BF16)
    lre, lim, ldt, dtt, rho, th, mag, cth, sth, abre, abim, nre, den, kre, kim, q0, q1, q2 = [
        f4("s4_%d" % i) for i in range(18)]
    m127, a7re, a7im = f4("m127"), f4("a7re"), f4("a7im")
    Adre = C.sb("Adre", [128, 4, 5], F32)
    Adim = C.sb("Adim", [128, 4, 5], F32)
    Adimn = C.sb("Adimn", [128, 4, 5], F32)
    abimn = f4("abimn")
    s127n = f4("s127n")
    bre = C.sb("bre", [128, 4, 16], F32)
    bim = C.sb("bim", [128, 4, 16], F32)
    cre = C.sb("cre", [128, 4, 16], F32)
    cim = C.sb("cim", [128, 4, 16], F32)
    bbre = C.sb("bbre", [128, 4, 16], F32)
    bbim = C.sb("bbim", [128, 4, 16], F32)
    bt0 = C.sb("bt0", [128, 16], F32)
    d_sb = C.sb("d_sb", [128, 1], F32)
    ident = C.sb("ident", [128, 128], F32)
    pad = C.sb("pad", [128, 128], F32)
    BBT = [C.sb("BBT%d" % i, [128, 4, 128], BF16) for i in range(2)]
    Cp = [C.sb("Cp%d" % i, [128, 4, 128], BF16) for i in range(2)]
    jidx = C.sb("jidx", [128, 512], F32)
    ang = C.sb("ang", [128, 512], F32)
    kf = C.sb("kf", [128, 512], F32)
    ki = C.sb("ki", [128, 512], I32)
    cs = [C.sb("cs%d" % i, [128, 4, 512], F32) for i in range(2)]
    m0 = C.sb("m0", [128, SEQ], BF16)
    rfull = C.sb("rfull", [128, SEQ], F32)
    w = [C.sb("w%d" % i, [128, SEQ], F32) for i in range(2)]
    z = [C.sb("z%d" % i, [128, SEQ], F32) for i in range(2)]
    bu = [C.sb("bu%d" % i, [128, 512], F32) for i in range(2)]
    tt = [C.sb("tt%d" % i, [128, 512], F32) for i in range(4)]
    sbf = [C.sb("sbf%d" % i, [128, 512], BF16) for i in range(2)]
    cst = [C.sb("cst%d" % i, [128, 32], F32) for i in range(10)]
    psb = [C.ps("psb%d" % i, [128, 512]) for i in range(2)]
    if merged:
        psy0 = C.ps("psy0", [128, 512])
        psy = [psy0, psy0]
        pst = psy0
    else:
        psy = [C.ps("psy%d" % i, [128, 512]) for i in range(2)]
        pst = C.ps("pst", [128, 128])

    du, dy, dpre, dtab, dm0, drf = [Dep() for _ in range(6)]
    dw = [Dep(), Dep()]
    dz = [Dep(), Dep()]
    dbu = [Dep(), Dep()]
    dtq = [Dep() for _ in range(4)]
    dsbf = [Dep(), Dep()]
    dcst = Dep()
    dpsb = [Dep(), Dep()]
    dpsy = [Dep(), Dep()]
    dpst = Dep()
    if merged:
        dpsy = [dpsy[0], dpsy[0]]
        dpst = dpsy[0]

    def V(fn, r, wr, eng="dve"):
        P.op(eng, fn, reads=r, writes=wr)

    P.dma("sp", u_sb[:, 0:SEQ], uT[:, 0:SEQ], writes=[du])
    P.dma("sp", u_sb[:, SEQ:NT], uT[:, SEQ:NT], writes=[du])
    for t_, src in ((lre, lam_re), (lim, lam_im), (ldt, log_dt), (bre, b_re), (bim, b_im), (cre, c_reT),
                    (cim, c_imT), (d_sb, dvec)):
        P.dma("sp", t_[:], src, writes=[dpre])
    pre = [dpre]

    V(lambda e: e.memset(pad[:], 1.0), [], pre, "pool")
    V(lambda e: e.affine_select(out=ident[:], in_=pad[:], pattern=[[-1, 128]], compare_op=ALU.is_equal,
                                fill=0.0, base=0, channel_multiplier=1), pre, pre, "pool")
    V(lambda e: e.iota(jidx[:], pattern=[[0, 4], [1, 128]], base=0, channel_multiplier=0,
                       allow_small_or_imprecise_dtypes=True), [], [dtab], "pool")
    V(lambda e: e.memset(m0[:], 1.0), [], [dm0], "pool")
    V(lambda e: e.memset(m0[:].rearrange("p (c j) -> p c j", j=128)[:, :, 0:1], 0.0), [dm0], [dm0], "pool")

    def reduce_angle(x, n):
        V(lambda e: e.tensor_scalar(out=kf[:, :n], in0=x, scalar1=1.0 / TWO_PI, scalar2=None, op0=ALU.mult),
          [dtab], [dtab])
        V(lambda e: e.tensor_copy(out=ki[:, :n], in_=kf[:, :n]), [dtab], [dtab])
        V(lambda e: e.tensor_copy(out=kf[:, :n], in_=ki[:, :n]), [dtab], [dtab])
        V(lambda e: e.scalar_tensor_tensor(out=x, in0=kf[:, :n], scalar=-TWO_PI, in1=x, op0=ALU.mult,
                                           op1=ALU.add), [dtab], [dtab])
        V(lambda e: e.tensor_scalar(out=kf[:, :n], in0=x, scalar1=PI, scalar2=-TWO_PI, op0=ALU.is_gt,
                                    op1=ALU.mult), [dtab], [dtab])
        V(lambda e: e.tensor_tensor(out=x, in0=x, in1=kf[:, :n], op=ALU.add), [dtab], [dtab])
        V(lambda e: e.tensor_scalar(out=kf[:, :n], in0=x, scalar1=-PI, scalar2=TWO_PI, op0=ALU.is_lt,
                                    op1=ALU.mult), [dtab], [dtab])
        V(lambda e: e.tensor_tensor(out=x, in0=x, in1=kf[:, :n], op=ALU.add), [dtab], [dtab])

    PT = [dpre, dtab]
    P.op("act", lambda e: e.activation(out=dtt[:], in_=ldt[:], func=AF.Exp), reads=PT, writes=PT)
    V(lambda e: e.tensor_tensor(out=rho[:], in0=lre[:], in1=dtt[:], op=ALU.mult), PT, PT)
    V(lambda e: e.tensor_tensor(out=th[:], in0=lim[:], in1=dtt[:], op=ALU.mult), PT, PT)
    P.op("act", lambda e: e.activation(out=mag[:], in_=rho[:], func=AF.Exp), reads=PT, writes=PT)
    P.op("act", lambda e: e.activation(out=m127[:], in_=rho[:], func=AF.Exp, scale=127.0), reads=PT, writes=PT)
    for gp in range(4):
        for k_, shift in ((1, 0.0), (0, PI / 2)):
            V(lambda e, gp=gp, shift=shift: e.tensor_scalar(out=ang[:], in0=jidx[:], scalar1=th[:, gp:gp + 1],
                                                            scalar2=shift, op0=ALU.mult, op1=ALU.add), PT, PT)
            reduce_angle(ang[:], 512)
            P.op("act", lambda e, gp=gp, k_=k_: e.activation(out=cs[k_][:, gp, :], in_=ang[:], func=AF.Sin),
                 reads=PT, writes=PT)
    V(lambda e: e.tensor_copy(out=cth[:], in_=cs[0][:, :, 1]), PT, PT)
    V(lambda e: e.tensor_copy(out=sth[:], in_=cs[1][:, :, 1]), PT, PT)
    V(lambda e: e.tensor_tensor(out=abre[:], in0=mag[:], in1=cth[:], op=ALU.mult), PT, PT)
    V(lambda e: e.tensor_tensor(out=abim[:], in0=mag[:], in1=sth[:], op=ALU.mult), PT, PT)
    V(lambda e: e.tensor_scalar(out=abimn[:], in0=abim[:], scalar1=-1.0, scalar2=None, op0=ALU.mult), PT, PT)
    V(lambda e: e.tensor_scalar(out=s127n[:], in0=cs[1][:, :, 127], scalar1=-1.0, scalar2=None, op0=ALU.mult), PT, PT)
    V(lambda e: e.tensor_scalar(out=nre[:], in0=abre[:], scalar1=-1.0, scalar2=None, op0=ALU.add), PT, PT)
    V(lambda e: e.tensor_tensor(out=q0[:], in0=lre[:], in1=lre[:], op=ALU.mult), PT, PT)
    V(lambda e: e.tensor_tensor(out=q1[:], in0=lim[:], in1=lim[:], op=ALU.mult), PT, PT)
    V(lambda e: e.tensor_tensor(out=den[:], in0=q0[:], in1=q1[:], op=ALU.add), PT, PT)
    V(lambda e: e.reciprocal(out=den[:], in_=den[:]), PT, PT)
    V(lambda e: e.tensor_tensor(out=q0[:], in0=nre[:], in1=lre[:], op=ALU.mult), PT, PT)
    V(lambda e: e.tensor_tensor(out=q1[:], in0=abim[:], in1=lim[:], op=ALU.mult), PT, PT)
    V(lambda e: e.tensor_tensor(out=q0[:], in0=q0[:], in1=q1[:], op=ALU.add), PT, PT)
    V(lambda e: e.tensor_tensor(out=kre[:], in0=q0[:], in1=den[:], op=ALU.mult), PT, PT)
    V(lambda e: e.tensor_tensor(out=q0[:], in0=abim[:], in1=lre[:], op=ALU.mult), PT, PT)
    V(lambda e: e.tensor_tensor(out=q1[:], in0=nre[:], in1=lim[:], op=ALU.mult), PT, PT)
    V(lambda e: e.tensor_tensor(out=q0[:], in0=q0[:], in1=q1[:], op=ALU.subtract), PT, PT)
    V(lambda e: e.tensor_tensor(out=kim[:], in0=q0[:], in1=den[:], op=ALU.mult), PT, PT)
    for gp in range(4):
        g1 = slice(gp, gp + 1)
        V(lambda e, gp=gp, g1=g1: e.tensor_scalar(out=bt0[:], in0=bim[:, gp, :], scalar1=kim[:, g1], scalar2=None,
                                                  op0=ALU.mult), PT, PT)
        V(lambda e, gp=gp, g1=g1: e.scalar_tensor_tensor(out=bbre[:, gp, :], in0=bre[:, gp, :], scalar=kre[:, g1],
                                                         in1=bt0[:], op0=ALU.mult, op1=ALU.subtract), PT, PT)
        V(lambda e, gp=gp, g1=g1: e.tensor_scalar(out=bt0[:], in0=bre[:, gp, :], scalar1=kim[:, g1], scalar2=None,
                                                  op0=ALU.mult), PT, PT)
        V(lambda e, gp=gp, g1=g1: e.scalar_tensor_tensor(out=bbim[:, gp, :], in0=bim[:, gp, :], scalar=kre[:, g1],
                                                         in1=bt0[:], op0=ALU.mult, op1=ALU.add), PT, PT)
    V(lambda e: e.tensor_tensor(out=a7re[:], in0=m127[:], in1=cs[0][:, :, 127], op=ALU.mult), PT, PT)
    V(lambda e: e.tensor_tensor(out=a7im[:], in0=m127[:], in1=cs[1][:, :, 127], op=ALU.mult), PT, PT)

    def cmul(ore, oim, are, aim, bre_, bim_):
        V(lambda e: e.tensor_tensor(out=q0[:], in0=are, in1=bre_, op=ALU.mult), PT, PT)
        V(lambda e: e.tensor_tensor(out=q1[:], in0=aim, in1=bim_, op=ALU.mult), PT, PT)
        V(lambda e: e.tensor_tensor(out=q2[:], in0=are, in1=bim_, op=ALU.mult), PT, PT)
        V(lambda e: e.tensor_tensor(out=ore, in0=q0[:], in1=q1[:], op=ALU.subtract), PT, PT)
        V(lambda e: e.tensor_tensor(out=q0[:], in0=aim, in1=bre_, op=ALU.mult), PT, PT)
        V(lambda e: e.tensor_tensor(out=oim, in0=q2[:], in1=q0[:], op=ALU.add), PT, PT)
    cmul(Adre[:, :, 0], Adim[:, :, 0], a7re[:], a7im[:], abre[:], abim[:])
    for k in range(1, 5):
        cmul(Adre[:, :, k], Adim[:, :, k], Adre[:, :, k - 1], Adim[:, :, k - 1], Adre[:, :, k - 1], Adim[:, :, k - 1])
    V(lambda e: e.tensor_scalar(out=Adimn[:], in0=Adim[:], scalar1=-1.0, scalar2=None, op0=ALU.mult), PT, PT)
    for gp in range(4):
        for ri, src in ((0, bbre), (1, bbim)):
            V(lambda e: e.memset(pad[:], 0.0), PT, PT)
            for j in range(2):
                c0 = 16 * (2 * gp + j)
                V(lambda e, j=j, c0=c0, src=src, gp=gp: e.tensor_copy(out=pad[64 * j:64 * j + 64, c0:c0 + 16],
                                                                    in_=src[64 * j:64 * j + 64, gp, :]), PT, PT)
            P.op("pe", lambda e: e.transpose(pst[:, 0:128], pad[:], ident[:]), reads=PT, writes=[dpst])
            V(lambda e, ri=ri, gp=gp: e.tensor_copy(out=BBT[ri][:, gp, :], in_=pst[:, 0:128]), [dpst] + PT, PT)
        for ri, src, sgn in ((0, cre, 1.0), (1, cim, -1.0)):
            V(lambda e, ri=ri, gp=gp: e.memset(Cp[ri][:, gp, :], 0.0), PT, PT)
            for j in range(2):
                c0 = 16 * (2 * gp + j)
                V(lambda e, j=j, c0=c0, src=src, gp=gp, ri=ri, sgn=sgn: e.tensor_scalar(
                    out=Cp[ri][64 * j:64 * j + 64, gp, c0:c0 + 16], in0=src[64 * j:64 * j + 64, gp, :],
                    scalar1=sgn, scalar2=None, op0=ALU.mult), PT, PT)

    if debug:
        for nm, t_, shp in (("th", th, [128, 4]), ("mag", mag, [128, 4]), ("kre", kre, [128, 4]), ("kim", kim, [128, 4]),
                            ("cos", cs[0], [128, 4, 512]), ("sin", cs[1], [128, 4, 512]), ("Adre", Adre, [128, 4, 5]),
                            ("Adim", Adim, [128, 4, 5]), ("bbre", bbre, [128, 4, 16]), ("jidx", jidx, [128, 512])):
            P.dma("sp", C.dout("dbg_" + nm, shp), t_[:], reads=PT)
        P.dma("sp", C.dout("dbg_BBT0", [128, 4, 128], BF16), BBT[0][:], reads=PT)
        P.dma("sp", C.dout("dbg_Cp1", [128, 4, 128], BF16), Cp[1][:], reads=PT)
    def mod_tile(gp, tok0, t8):
        sl = slice(t8 * 512, (t8 + 1) * 512)
        for ri in range(2):
            P.op("pe", lambda e, ri=ri: e.matmul(psb[ri][:], BBT[ri][:, gp, :],
                                                 u_sb[:, tok0 + sl.start:tok0 + sl.stop], start=True, stop=True),
                 reads=[du] + PT, writes=[dpsb[ri]])
            P.op("act", lambda e, ri=ri: e.activation(out=bu[ri][:], in_=psb[ri][:], func=AF.Copy),
                 reads=[dpsb[ri]], writes=[dbu[ri]])
        V(lambda e: e.tensor_tensor(out=tt[0][:], in0=bu[0][:], in1=cs[0][:, gp, :], op=ALU.mult),
          [dbu[0]] + PT, [dtq[0]])
        V(lambda e: e.tensor_tensor(out=tt[1][:], in0=bu[1][:], in1=cs[1][:, gp, :], op=ALU.mult),
          [dbu[1]] + PT, [dtq[1]])
        V(lambda e: e.tensor_tensor(out=w[0][:, sl], in0=tt[0][:], in1=tt[1][:], op=ALU.add),
          [dtq[0], dtq[1]], [dw[0]])
        V(lambda e: e.tensor_tensor(out=tt[2][:], in0=bu[1][:], in1=cs[0][:, gp, :], op=ALU.mult),
          [dbu[1]] + PT, [dtq[2]], "pool")
        V(lambda e: e.tensor_tensor(out=tt[3][:], in0=bu[0][:], in1=cs[1][:, gp, :], op=ALU.mult),
          [dbu[0]] + PT, [dtq[3]], "pool")
        V(lambda e: e.tensor_tensor(out=w[1][:, sl], in0=tt[2][:], in1=tt[3][:], op=ALU.subtract),
          [dtq[2], dtq[3]], [dw[1]], "pool")

    def scans(extra):
        for i in range(2):
            V(lambda e, i=i: e.tensor_tensor_scan(out=z[i][:], data0=rfull[:], data1=w[i][:], initial=0.0,
                                                  op0=ALU.mult, op1=ALU.add), [drf, dw[i]] + extra, [dz[i]])

    def hs_step(gp, k, cur):
        d_ = 1 << k
        nxt = 2 - cur
        ar = Adre[:, gp, k:k + 1]
        ai = Adim[:, gp, k:k + 1]
        ain = Adimn[:, gp, k:k + 1]
        sre, sim = cst[cur], cst[cur + 1]
        nre_, nim_ = cst[nxt], cst[nxt + 1]
        CS = [dcst]
        V(lambda e: e.tensor_copy(out=nre_[:, 0:d_], in_=sre[:, 0:d_]), CS, CS)
        V(lambda e: e.tensor_copy(out=nim_[:, 0:d_], in_=sim[:, 0:d_]), CS, CS)
        V(lambda e: e.scalar_tensor_tensor(out=cst[4][:, d_:32], in0=sre[:, 0:32 - d_], scalar=ar, in1=sre[:, d_:32],
                                           op0=ALU.mult, op1=ALU.add), CS + PT, CS)
        V(lambda e: e.scalar_tensor_tensor(out=nre_[:, d_:32], in0=sim[:, 0:32 - d_], scalar=ain,
                                           in1=cst[4][:, d_:32], op0=ALU.mult, op1=ALU.add), CS + PT, CS)
        V(lambda e: e.scalar_tensor_tensor(out=cst[4][:, d_:32], in0=sim[:, 0:32 - d_], scalar=ar, in1=sim[:, d_:32],
                                           op0=ALU.mult, op1=ALU.add), CS + PT, CS)
        V(lambda e: e.scalar_tensor_tensor(out=nim_[:, d_:32], in0=sre[:, 0:32 - d_], scalar=ai,
                                           in1=cst[4][:, d_:32], op0=ALU.mult, op1=ALU.add), CS + PT, CS)
        return nxt

    def carry(gp):
        g1 = slice(gp, gp + 1)
        CS = [dcst]
        zE = [z[i][:].rearrange("p (c j) -> p c j", j=128)[:, :, 127] for i in range(2)]
        c127 = cs[0][:, gp, 127:128]
        s127 = cs[1][:, gp, 127:128]
        V(lambda e: e.tensor_scalar(out=cst[2][:], in0=zE[0], scalar1=c127, scalar2=None, op0=ALU.mult),
          [dz[0]] + PT, CS)
        V(lambda e: e.scalar_tensor_tensor(out=cst[0][:], in0=zE[1], scalar=s127n[:, g1], in1=cst[2][:],
                                           op0=ALU.mult, op1=ALU.add), [dz[1]] + PT + CS, CS)
        V(lambda e: e.tensor_scalar(out=cst[2][:], in0=zE[0], scalar1=s127, scalar2=None, op0=ALU.mult),
          [dz[0]] + PT + CS, CS)
        V(lambda e: e.scalar_tensor_tensor(out=cst[1][:], in0=zE[1], scalar=c127, in1=cst[2][:],
                                           op0=ALU.mult, op1=ALU.add), [dz[1]] + PT + CS, CS)
        cur = 0
        for k in range(5):
            cur = hs_step(gp, k, cur)
        Sre, Sim = cst[cur], cst[cur + 1]
        w0 = [w[i][:].rearrange("p (c j) -> p c j", j=128)[:, 1:32, 0] for i in range(2)]
        V(lambda e: e.scalar_tensor_tensor(out=cst[5][:, 0:31], in0=Sre[:, 0:31], scalar=abre[:, g1],
                                           in1=w0[0], op0=ALU.mult, op1=ALU.add), CS + PT + [dw[0]], CS)
        V(lambda e: e.scalar_tensor_tensor(out=w0[0], in0=Sim[:, 0:31], scalar=abimn[:, g1],
                                           in1=cst[5][:, 0:31], op0=ALU.mult, op1=ALU.add), CS + PT, [dw[0]])
        V(lambda e: e.scalar_tensor_tensor(out=cst[6][:, 0:31], in0=Sim[:, 0:31], scalar=abre[:, g1],
                                           in1=w0[1], op0=ALU.mult, op1=ALU.add), CS + PT + [dw[1]], CS)
        V(lambda e: e.scalar_tensor_tensor(out=w0[1], in0=Sre[:, 0:31], scalar=abim[:, g1],
                                           in1=cst[6][:, 0:31], op0=ALU.mult, op1=ALU.add), CS + PT, [dw[1]])

    def out_tile(gp, tok0, t8):
        sl = slice(t8 * 512, (t8 + 1) * 512)
        V(lambda e: e.tensor_tensor(out=tt[0][:], in0=z[0][:, sl], in1=cs[0][:, gp, :], op=ALU.mult),
          [dz[0]] + PT, [dtq[0]])
        V(lambda e: e.tensor_tensor(out=tt[1][:], in0=z[1][:, sl], in1=cs[1][:, gp, :], op=ALU.mult),
          [dz[1]] + PT, [dtq[1]])
        V(lambda e: e.tensor_tensor(out=sbf[0][:], in0=tt[0][:], in1=tt[1][:], op=ALU.subtract),
          [dtq[0], dtq[1]], [dsbf[0]])
        V(lambda e: e.tensor_tensor(out=tt[2][:], in0=z[0][:, sl], in1=cs[1][:, gp, :], op=ALU.mult),
          [dz[0]] + PT, [dtq[2]], "pool")
        V(lambda e: e.tensor_tensor(out=tt[3][:], in0=z[1][:, sl], in1=cs[0][:, gp, :], op=ALU.mult),
          [dz[1]] + PT, [dtq[3]], "pool")
        V(lambda e: e.tensor_tensor(out=sbf[1][:], in0=tt[2][:], in1=tt[3][:], op=ALU.add),
          [dtq[2], dtq[3]], [dsbf[1]], "pool")
        yi = t8 % 2
        P.op("pe", lambda e: e.matmul(psy[yi][:], Cp[0][:, gp, :], sbf[0][:], start=True, stop=False),
             reads=[dsbf[0]] + PT, writes=[dpsy[yi]])
        P.op("pe", lambda e: e.matmul(psy[yi][:], Cp[1][:, gp, :], sbf[1][:], start=False, stop=True),
             reads=[dsbf[1]] + PT, writes=[dpsy[yi]])
        r0 = 32 * gp
        V(lambda e: e.scalar_tensor_tensor(
            out=y_sb[r0:r0 + 32, tok0 + sl.start:tok0 + sl.stop],
            in0=u_sb[r0:r0 + 32, tok0 + sl.start:tok0 + sl.stop], scalar=d_sb[r0:r0 + 32, 0:1],
            in1=psy[yi][r0:r0 + 32, :], op0=ALU.mult, op1=ALU.add),
          [du, dpsy[yi]] + PT, [dy])

    def set_rfull(gp):
        P.op("act", lambda e: e.activation(out=rfull[:], in_=m0[:], func=AF.Copy, scale=mag[:, gp:gp + 1]),
             reads=[dm0] + PT, writes=[drf])

    for gp in range(4):
        set_rfull(gp)
        for b in range(2):
            for t8 in range(8):
                mod_tile(gp, b * SEQ, t8)
            scans([])
            carry(gp)
            scans([dcst])
            for t8 in range(8):
                out_tile(gp, b * SEQ, t8)
    if debug:
        for nm, t_, shp, dd in (("w0", w[0], [128, SEQ], dw[0]), ("w1", w[1], [128, SEQ], dw[1]),
                                ("z0", z[0], [128, SEQ], dz[0]), ("z1", z[1], [128, SEQ], dz[1]),
                                ("rfull", rfull, [128, SEQ], drf), ("bu0", bu[0], [128, 512], dbu[0]),
                                ("tt0", tt[0], [128, 512], dtq[0])):
            P.dma("sp", C.dout("dbg_" + nm, shp), t_[:], reads=[dd])
        for k in range(7):
            P.dma("sp", C.dout("dbg_cst%d" % k, [128, 32]), cst[k][:], reads=[dcst])
    P.dma("sp", yp[:, 0:SEQ], y_sb[:, 0:SEQ], reads=[dy])
    P.dma("sp", yp[:, SEQ:NT], y_sb[:, SEQ:NT], reads=[dy])


def build_B2(debug=False):
    nc = bass.Bass("TRN2", target_bir_lowering=False)
    with ExitStack() as st:
        st.enter_context(nc.allow_low_precision("bf16 matmul operands, fp32 accumulation"))
        C = Ctx(nc, st)
        P = Prog(nc, st)
        emit_B2(nc, C, P, debug=debug)
        P.finish()
    return nc


def host_B2_params(l, inp, core):
    gs = slice(8 * core, 8 * core + 8)

    def pg(a):
        return np.ascontiguousarray(a.reshape(4, 2, 64).transpose(1, 2, 0).reshape(128, 4))
    lam_re = pg(inp["s5_lambda_re"][l][gs])
    lam_im = pg(inp["s5_lambda_im"][l][gs])
    log_dt = pg(np.repeat(inp["s5_log_dt"][l][gs][:, None], 64, axis=1))

    def pb(a):
        return np.ascontiguousarray(a.reshape(4, 2, 64, 16).transpose(1, 2, 0, 3).reshape(128, 4, 16))
    b_re = pb(inp["s5_b_re"][l][gs])
    b_im = pb(inp["s5_b_im"][l][gs])
    c_reT = pb(np.transpose(inp["s5_c_re"][l][gs], (0, 2, 1)))
    c_imT = pb(np.transpose(inp["s5_c_im"][l][gs], (0, 2, 1)))
    dvec = np.ascontiguousarray(inp["s5_d"][l][128 * core:128 * core + 128].reshape(128, 1))
    return {"lam_re": lam_re, "lam_im": lam_im, "log_dt": log_dt, "b_re": b_re, "b_im": b_im,
            "c_reT": c_reT, "c_imT": c_imT, "dvec": dvec}


_NC_CACHE = {}


def _get(name, fn):
    if name not in _NC_CACHE:
        _NC_CACHE[name] = fn()
    return _NC_CACHE[name]


def _run(nc, maps):
    res = run_bass_kernel_spmd(nc, maps, core_ids=list(range(NCORES)))
    return res.results


def kernel(**inp):
    inp = {k: np.asarray(v) for k, v in inp.items()}
    x_tok = np.ascontiguousarray(inp["x"].reshape(2 * SEQ, D).astype(np.float32))
    depth = inp["w_in"].shape[0]
    for l in range(depth):
        last = (l == depth - 1)
        rA = _run(_get("A", build_A), host_A(x_tok, l, inp))
        yaT = [np.asarray(r["yaT"]) for r in rA]
        gT = [np.asarray(r["gT"]) for r in rA]
        s5T = [np.asarray(r["s5T"]) for r in rA]
        qT = [np.asarray(r["qT"]) for r in rA]
        kT = [np.asarray(r["kT"]) for r in rA]
        vtk = [np.asarray(r["vtk"]) for r in rA]
        mB1, mB2 = [], []
        for c in range(NCORES):
            qs, ks, vs = [], [], []
            for pl in range(NPAIR):
                b, h = divmod(c * NPAIR + pl, 8)
                hs = slice(128 * h, 128 * h + 128)
                qs.append(np.concatenate([qT[4 * b + i][hs, :] for i in range(4)], axis=1))
                ks.append(np.concatenate([kT[4 * b + i][hs, :] for i in range(4)], axis=1))
                vs.append(np.concatenate([vtk[4 * b + i][:, hs] for i in range(4)], axis=0))
            mB1.append({"qT": np.ascontiguousarray(np.stack(qs)), "kT": np.ascontiguousarray(np.stack(ks)),
                        "v": np.ascontiguousarray(np.stack(vs))})
            m2 = host_B2_params(l, inp, c)
            m2["uT"] = np.ascontiguousarray(np.concatenate([s5T[i][128 * c:128 * c + 128, :] for i in range(8)], axis=1))
            mB2.append(m2)
        mB = [dict(mB1[c], **mB2[c]) for c in range(NCORES)]
        rB = _run(_get("B", build_B), mB)
        yc = [np.asarray(r["yc"]) for r in rB]
        yp = [np.asarray(r["yp"]) for r in rB]
        ycT, ybpT = [], []
        for tc in range(NCORES):
            b, i = divmod(tc, 4)
            rows = []
            for h in range(8):
                c, pl = divmod(b * 8 + h, NPAIR)
                rows.append(yc[c][pl][:, i * TOK:(i + 1) * TOK])
            ycT.append(np.ascontiguousarray(np.concatenate(rows, axis=0)))
            ybpT.append(np.ascontiguousarray(np.concatenate([yp[c][:, tc * TOK:(tc + 1) * TOK] for c in range(8)], axis=0)))
        ncC = _get("C%d" % int(last), lambda: build_C(last))
        rC = _run(ncC, host_C(x_tok, l, inp, yaT, ybpT, ycT, gT, last))
        x_tok = np.ascontiguousarray(np.concatenate([np.asarray(r["xo"]).T for r in rC], axis=0).astype(np.float32))
    return x_tok.reshape(2, SEQ, D)


B_SPEED = [1.0, 1.0]


def build_B():
    nc = bass.Bass("TRN2", target_bir_lowering=False)
    with ExitStack() as st:
        st.enter_context(nc.allow_low_precision("bf16 matmul operands, fp32 accumulation"))
        C = Ctx(nc, st)
        P = Prog(nc, st)
        P.begin("b2")
        emit_B2F(nc, C, P, merged=True)
        P.end()
        P.begin("b1")
        emit_B1(nc, C, P, merged=True)
        P.end()
        P.replay(["b2", "b1"], speed=B_SPEED)
        P.finish()
    return nc


TS = 8
NM = SEQ // TS
J2 = 64
NC2 = NM // J2


def emit_B2F(nc, C, P, merged=False):
    uT = C.din("uT", [128, NT], BF16)
    lam_re = C.din("lam_re", [128, 4])
    lam_im = C.din("lam_im", [128, 4])
    log_dt = C.din("log_dt", [128, 4])
    b_re = C.din("b_re", [128, 4, 16])
    b_im = C.din("b_im", [128, 4, 16])
    c_reT = C.din("c_reT", [128, 4, 16])
    c_imT = C.din("c_imT", [128, 4, 16])
    dvec = C.din("dvec", [128, 1])
    yp = C.dout("yp", [128, NT], BF16)
    f4 = lambda n: C.sb(n, [128, 4], F32)
    u_sb = C.sb("u_sb", [128, NT], BF16)
    y_sb = u_sb
    lre, lim, ldt, dtt, rho, th, th8, nre, den, kre, kim, q0, q1, q2 = [f4("s4_%d" % i) for i in range(14)]
    kidx = C.sb("kidx", [128, 16], F32)
    csk = [C.sb("csk%d" % i, [128, 4, 16], F32) for i in range(2)]
    magk = C.sb("magk", [128, 4, 16], F32)
    PW = [C.sb("PW%d" % i, [128, 4, 16], F32) for i in range(2)]
    PWn = [C.sb("PWn%d" % i, [128, 4, 16], F32) for i in range(2)]
    bre = C.sb("bre", [128, 4, 16], F32)
    bim = C.sb("bim", [128, 4, 16], F32)
    cre = C.sb("cre", [128, 4, 16], F32)
    cim = C.sb("cim", [128, 4, 16], F32)
    bbre = C.sb("bbre", [128, 4, 16], F32)
    bbim = C.sb("bbim", [128, 4, 16], F32)
    bt0 = C.sb("bt0", [128, 16], F32)
    d_sb = C.sb("d_sb", [128, 1], F32)
    ident = C.sb("ident", [128, 128], F32)
    R = C.sb("R", [128, 8192], F32)
    pads = [R[:, 4096 * i:4096 * (i + 1)].rearrange("p (t g c) -> p t g c", t=TS, g=4) for i in range(2)]
    Cpf = [C.sb("Cpf%d" % i, [128, 4, 128], F32) for i in range(2)]
    WE = [C.sb("WE%d" % i, [128, TS, 4, 128], BF16) for i in range(2)]
    WC = [C.sb("WC%d" % i, [128, TS, 4, 128], BF16) for i in range(2)]
    Kmat = C.sb("Kmat", [128, TS, 128], BF16)
    jidx = C.sb("jidx", [128, 512], F32)
    ang = R[:, 0:2048]
    kf = R[:, 2048:4096]
    ki = R[:, 4096:6144].bitcast(I32)
    AB = [C.sb("AB%d" % i, [128, TS, 4, 16], F32) for i in range(6)]
    cs2 = [C.sb("cs2_%d" % i, [128, 4, NM], F32) for i in range(2)]
    m02 = C.sb("m02", [128, NM], BF16)
    rf2 = C.sb("rf2", [128, 4, NM], F32)
    Ad = [C.sb("Ad%d" % i, [128, 4, 4], F32) for i in range(3)]
    sq_re, sq_im = f4("sq_re"), f4("sq_im")
    a8imn, s63n = f4("a8imn"), f4("s63n")
    bu = [R[:, NM * i:NM * (i + 1)] for i in range(2)]
    tt = [R[:, NM * (2 + i):NM * (3 + i)] for i in range(4)]
    w2 = [R[:, NM * (6 + i):NM * (7 + i)] for i in range(2)]
    z2 = [R[:, NM * (8 + i):NM * (9 + i)] for i in range(2)]
    cst = [C.sb("cst%d" % i, [128, NC2], F32) for i in range(7)]
    Sprev2 = [C.sb("Sprev%d" % i, [128, 4, 2, NM], BF16) for i in range(2)]
    psb = [C.ps("psb%d" % i, [128, 512]) for i in range(2)]
    psy = C.ps("psy", [128, 512])
    pst = psy

    du, dpre, dtab = [Dep() for _ in range(3)]
    dS2 = [Dep(), Dep()]
    dy = du
    dbu = [Dep(), Dep()]
    dtq = [Dep() for _ in range(4)]
    dw = [Dep(), Dep()]
    dz = [Dep(), Dep()]
    dcst = Dep()
    dpsb = [Dep(), Dep()]
    dpsy = Dep()
    PT = [dpre, dtab]

    ENG2 = "dve" if merged else "pool"

    def V(fn, r, wr, eng="dve"):
        P.op(eng, fn, reads=r, writes=wr)

    def A(fn, r, wr):
        P.op("act", fn, reads=r, writes=wr)

    P.dma("sp", u_sb[:, 0:SEQ], uT[:, 0:SEQ], writes=[du])
    P.dma("sp", u_sb[:, SEQ:NT], uT[:, SEQ:NT], writes=[du])
    for t_, src in ((lre, lam_re), (lim, lam_im), (ldt, log_dt), (bre, b_re), (bim, b_im), (cre, c_reT),
                    (cim, c_imT), (d_sb, dvec)):
        P.dma("sp", t_[:], src, writes=[dpre])
    V(lambda e: e.memset(ang[:, 0:128], 1.0), [], PT, "pool")
    V(lambda e: e.affine_select(out=ident[:], in_=ang[:, 0:128], pattern=[[-1, 128]], compare_op=ALU.is_equal,
                                fill=0.0, base=0, channel_multiplier=1), PT, PT, "pool")
    V(lambda e: e.iota(jidx[:], pattern=[[0, NC2], [1, J2]], base=0, channel_multiplier=0,
                       allow_small_or_imprecise_dtypes=True), [], PT, "pool")
    V(lambda e: e.iota(kidx[:], pattern=[[1, 16]], base=0, channel_multiplier=0,
                       allow_small_or_imprecise_dtypes=True), [], PT, "pool")
    V(lambda e: e.memset(m02[:], 1.0), [], PT, "pool")
    V(lambda e: e.memset(m02[:].rearrange("p (c j) -> p c j", j=J2)[:, :, 0:1], 0.0), PT, PT, "pool")
    for ri in range(2):
        V(lambda e, ri=ri: e.memset(Cpf[ri][:], 0.0), [], PT, "pool")
        V(lambda e, ri=ri: e.memset(WC[ri][:], 0.0), [], PT, "pool")
    for i_ in range(2):
        V(lambda e, i_=i_: e.memset(Sprev2[i_][:, :, :, 0:1], 0.0), [], [dS2[i_]], "pool")

    def reduce_angle(x, n):
        V(lambda e: e.tensor_scalar(out=kf[:, :n], in0=x, scalar1=1.0 / TWO_PI, scalar2=None, op0=ALU.mult), PT, PT)
        V(lambda e: e.tensor_copy(out=ki[:, :n], in_=kf[:, :n]), PT, PT)
        V(lambda e: e.tensor_copy(out=kf[:, :n], in_=ki[:, :n]), PT, PT)
        V(lambda e: e.scalar_tensor_tensor(out=x, in0=kf[:, :n], scalar=-TWO_PI, in1=x, op0=ALU.mult, op1=ALU.add),
          PT, PT)
        V(lambda e: e.tensor_scalar(out=kf[:, :n], in0=x, scalar1=PI, scalar2=-TWO_PI, op0=ALU.is_gt, op1=ALU.mult),
          PT, PT)
        V(lambda e: e.tensor_tensor(out=x, in0=x, in1=kf[:, :n], op=ALU.add), PT, PT)
        V(lambda e: e.tensor_scalar(out=kf[:, :n], in0=x, scalar1=-PI, scalar2=TWO_PI, op0=ALU.is_lt, op1=ALU.mult),
          PT, PT)
        V(lambda e: e.tensor_tensor(out=x, in0=x, in1=kf[:, :n], op=ALU.add), PT, PT)

    def sincos_table(dst, idx_ap, n, thv):
        N4 = 4 * n
        a3 = ang[:, :N4].rearrange("p (g n) -> p g n", g=4)
        idx_b = idx_ap.unsqueeze(1).to_broadcast([128, 4, n])
        th_b = thv[:].unsqueeze(2).to_broadcast([128, 4, n])
        for k_, shift in ((1, 0.0), (0, PI / 2)):
            V(lambda e: e.tensor_tensor(out=a3, in0=idx_b, in1=th_b, op=ALU.mult), PT, PT)
            if shift:
                V(lambda e, shift=shift: e.tensor_scalar(out=ang[:, :N4], in0=ang[:, :N4], scalar1=shift, scalar2=None,
                                                         op0=ALU.add), PT, PT)
            reduce_angle(ang[:, :N4], N4)
            A(lambda e, k_=k_: e.activation(out=dst[k_][:, :, :n], in_=a3, func=AF.Sin), PT, PT)

    A(lambda e: e.activation(out=dtt[:], in_=ldt[:], func=AF.Exp), PT, PT)
    V(lambda e: e.tensor_tensor(out=rho[:], in0=lre[:], in1=dtt[:], op=ALU.mult), PT, PT)
    V(lambda e: e.tensor_tensor(out=th[:], in0=lim[:], in1=dtt[:], op=ALU.mult), PT, PT)
    V(lambda e: e.tensor_scalar(out=th8[:], in0=th[:], scalar1=float(TS), scalar2=None, op0=ALU.mult), PT, PT)
    sincos_table(csk, kidx[:], 16, th)
    for gp in range(4):
        (lambda gp: A(lambda e: e.activation(out=magk[:, gp, :], in_=kidx[:], func=AF.Exp, scale=rho[:, gp:gp + 1]),
                      PT, PT))(gp)
    for ri in range(2):
        V(lambda e, ri=ri: e.tensor_tensor(out=PW[ri][:], in0=magk[:], in1=csk[ri][:], op=ALU.mult), PT, PT)
        V(lambda e, ri=ri: e.tensor_scalar(out=PWn[ri][:], in0=PW[ri][:], scalar1=-1.0, scalar2=None, op0=ALU.mult),
          PT, PT)
    abre, abim = PW[0][:, :, 1], PW[1][:, :, 1]
    V(lambda e: e.tensor_scalar(out=nre[:], in0=abre, scalar1=-1.0, scalar2=None, op0=ALU.add), PT, PT)
    V(lambda e: e.tensor_tensor(out=q0[:], in0=lre[:], in1=lre[:], op=ALU.mult), PT, PT)
    V(lambda e: e.tensor_tensor(out=q1[:], in0=lim[:], in1=lim[:], op=ALU.mult), PT, PT)
    V(lambda e: e.tensor_tensor(out=den[:], in0=q0[:], in1=q1[:], op=ALU.add), PT, PT)
    V(lambda e: e.reciprocal(out=den[:], in_=den[:]), PT, PT)
    V(lambda e: e.tensor_tensor(out=q0[:], in0=nre[:], in1=lre[:], op=ALU.mult), PT, PT)
    V(lambda e: e.tensor_tensor(out=q1[:], in0=abim, in1=lim[:], op=ALU.mult), PT, PT)
    V(lambda e: e.tensor_tensor(out=q0[:], in0=q0[:], in1=q1[:], op=ALU.add), PT, PT)
    V(lambda e: e.tensor_tensor(out=kre[:], in0=q0[:], in1=den[:], op=ALU.mult), PT, PT)
    V(lambda e: e.tensor_tensor(out=q0[:], in0=abim, in1=lre[:], op=ALU.mult), PT, PT)
    V(lambda e: e.tensor_tensor(out=q1[:], in0=nre[:], in1=lim[:], op=ALU.mult), PT, PT)
    V(lambda e: e.tensor_tensor(out=q0[:], in0=q0[:], in1=q1[:], op=ALU.subtract), PT, PT)
    V(lambda e: e.tensor_tensor(out=kim[:], in0=q0[:], in1=den[:], op=ALU.mult), PT, PT)

    def bbar(gp):
        g1 = slice(gp, gp + 1)
        V(lambda e: e.tensor_scalar(out=bt0[:], in0=bim[:, gp, :], scalar1=kim[:, g1], scalar2=None, op0=ALU.mult), PT, PT)
        V(lambda e: e.scalar_tensor_tensor(out=bbre[:, gp, :], in0=bre[:, gp, :], scalar=kre[:, g1], in1=bt0[:],
                                           op0=ALU.mult, op1=ALU.subtract), PT, PT)
        V(lambda e: e.tensor_scalar(out=bt0[:], in0=bre[:, gp, :], scalar1=kim[:, g1], scalar2=None, op0=ALU.mult), PT, PT)
        V(lambda e: e.scalar_tensor_tensor(out=bbim[:, gp, :], in0=bim[:, gp, :], scalar=kre[:, g1], in1=bt0[:],
                                           op0=ALU.mult, op1=ALU.add), PT, PT)
    for gp in range(4):
        bbar(gp)

    sincos_table(cs2, jidx[:], NM, th8)
    for gp in range(4):
        (lambda gp: A(lambda e: e.activation(out=rf2[:, gp, :], in_=m02[:], func=AF.Copy, scale=magk[:, gp, 8:9]),
                      PT, PT))(gp)
    V(lambda e: e.tensor_scalar(out=a8imn[:], in0=PW[1][:, :, 8], scalar1=-1.0, scalar2=None, op0=ALU.mult), PT, PT)
    V(lambda e: e.tensor_scalar(out=s63n[:], in0=cs2[1][:, :, J2 - 1], scalar1=-1.0, scalar2=None, op0=ALU.mult), PT, PT)

    for ri in range(2):
        V(lambda e, ri=ri: e.memset(pads[ri], 0.0), PT, PT, "pool")

    def cpow_scale(o_re, o_im, xr, xi, k0_, neg_im):
        shp = [128, TS, 4, 16]
        pw = [PW[i][:, :, k0_:k0_ + TS].rearrange("p g t -> p t g").unsqueeze(3).to_broadcast(shp) for i in range(2)]
        pn = [PWn[i][:, :, k0_:k0_ + TS].rearrange("p g t -> p t g").unsqueeze(3).to_broadcast(shp) for i in range(2)]
        xr_b = xr.unsqueeze(1).to_broadcast(shp)
        xi_b = xi.unsqueeze(1).to_broadcast(shp)
        t0_, t1_ = AB[4][:], AB[5][:]
        V(lambda e: e.tensor_tensor(out=t0_, in0=xr_b, in1=pw[0], op=ALU.mult), PT, PT)
        V(lambda e: e.tensor_tensor(out=t1_, in0=xi_b, in1=pw[1], op=ALU.mult), PT, PT)
        V(lambda e: e.tensor_tensor(out=o_re, in0=t0_, in1=t1_, op=ALU.subtract), PT, PT)
        q_ = pn if neg_im else pw
        V(lambda e: e.tensor_tensor(out=t0_, in0=xr_b, in1=q_[1], op=ALU.mult), PT, PT)
        V(lambda e: e.tensor_tensor(out=t1_, in0=xi_b, in1=q_[0], op=ALU.mult), PT, PT)
        V(lambda e: e.tensor_tensor(out=o_im, in0=t0_, in1=t1_, op=ALU.add), PT, PT)

    cpow_scale(AB[0][:], AB[1][:], bbre[:], bbim[:], 0, False)
    cpow_scale(AB[2][:], AB[3][:], cre[:], cim[:], 1, True)

    def scatter(gp, j):
        r = slice(64 * j, 64 * j + 64)
        cc = slice(16 * (2 * gp + j), 16 * (2 * gp + j) + 16)
        eng = "dve" if j == 0 else "pool"
        for ri in range(2):
            V(lambda e, ri=ri: e.tensor_copy(out=pads[ri][r, :, gp, cc], in_=AB[ri][r, :, gp, :]), PT, PT, eng)
            V(lambda e, ri=ri: e.tensor_copy(out=WC[ri][r, :, gp, cc], in_=AB[2 + ri][r, :, gp, :]), PT, PT, eng)
        V(lambda e: e.tensor_copy(out=Cpf[0][r, gp, cc], in_=cre[r, gp, :]), PT, PT, eng)
        V(lambda e: e.tensor_scalar(out=Cpf[1][r, gp, cc], in0=cim[r, gp, :], scalar1=-1.0, scalar2=None, op0=ALU.mult),
          PT, PT, eng)
    for gp in range(4):
        for j in range(2):
            scatter(gp, j)

    def build_we(ri, tau):
        for gp in range(4):
            P.op("pe", lambda e, gp=gp: e.transpose(pst[:, gp * 128:(gp + 1) * 128], pads[ri][:, tau, gp, :], ident[:]),
                 reads=PT, writes=[dpsy])
        V(lambda e: e.tensor_copy(out=WE[ri][:, TS - 1 - tau, :, :].rearrange("p g q -> p (g q)"), in_=pst[:]),
          [dpsy], PT)
    for ri in range(2):
        for tau in range(TS):
            build_we(ri, tau)

    def build_k(t0):
        for tau in range(t0, t0 + 4):
            n_ = 0
            for gp in range(4):
                for ri in range(2):
                    P.op("pe", lambda e, tau=tau, gp=gp, ri=ri, n_=n_: e.matmul(
                        pst[:, (tau - t0) * 128:(tau - t0 + 1) * 128], pads[ri][:, tau, gp, :], Cpf[ri][:, gp, :],
                        start=(n_ == 0), stop=(n_ == 7), skip_group_check=True), reads=PT, writes=[dpsy])
                    n_ += 1
        V(lambda e: e.tensor_copy(out=Kmat[:, t0:t0 + 4, :].rearrange("p t c -> p (t c)"), in_=pst[:]), [dpsy], PT)
    build_k(0)
    build_k(4)

    def cmul(ore, oim, are, aim, bre_, bim_):
        V(lambda e: e.tensor_tensor(out=q0[:], in0=are, in1=bre_, op=ALU.mult), PT, PT)
        V(lambda e: e.tensor_tensor(out=q1[:], in0=aim, in1=bim_, op=ALU.mult), PT, PT)
        V(lambda e: e.tensor_tensor(out=q2[:], in0=are, in1=bim_, op=ALU.mult), PT, PT)
        V(lambda e: e.tensor_tensor(out=den[:], in0=aim, in1=bre_, op=ALU.mult), PT, PT)
        V(lambda e: e.tensor_tensor(out=ore, in0=q0[:], in1=q1[:], op=ALU.subtract), PT, PT)
        V(lambda e: e.tensor_tensor(out=oim, in0=q2[:], in1=den[:], op=ALU.add), PT, PT)
    V(lambda e: e.tensor_copy(out=sq_re[:], in_=PW[0][:, :, 8]), PT, PT)
    V(lambda e: e.tensor_copy(out=sq_im[:], in_=PW[1][:, :, 8]), PT, PT)
    for _ in range(6):
        cmul(sq_re[:], sq_im[:], sq_re[:], sq_im[:], sq_re[:], sq_im[:])
    for k in range(3):
        (lambda k: (V(lambda e: e.tensor_copy(out=Ad[0][:, :, k], in_=sq_re[:]), PT, PT),
                    V(lambda e: e.tensor_copy(out=Ad[1][:, :, k], in_=sq_im[:]), PT, PT),
                    V(lambda e: e.tensor_scalar(out=Ad[2][:, :, k], in0=sq_im[:], scalar1=-1.0, scalar2=None,
                                                op0=ALU.mult), PT, PT)))(k)
        if k < 2:
            cmul(sq_re[:], sq_im[:], sq_re[:], sq_im[:], sq_re[:], sq_im[:])

    V(lambda e: e.memset(bt0[:], 0.0), PT, PT)

    ud = C.sb("ud", [128, 2, TS, NM], BF16)
    dud = Dep()
    for b_ in range(2):
        (lambda b_: P.op("act", lambda e: e.activation(
            out=ud[:, b_], in_=u_sb[:, b_ * SEQ:(b_ + 1) * SEQ].rearrange("p (m s) -> p s m", s=TS), func=AF.Copy),
            reads=[du], writes=[dud]))(b_)

    def uview(b, s):
        return ud[:, b, s, :]

    def e_pass(gp, b):
        for ri in range(2):
            for s in range(TS):
                P.op("pe", lambda e, ri=ri, s=s: e.matmul(psb[ri][:], WE[ri][:, s, gp, :], uview(b, s),
                                                          start=(s == 0), stop=(s == TS - 1)),
                     reads=[dud] + PT, writes=[dpsb[ri]])
            V(lambda e, ri=ri: e.tensor_copy(out=bu[ri][:], in_=psb[ri][:]), [dpsb[ri]], [dbu[ri]])
        c_, s_ = cs2[0][:, gp, :], cs2[1][:, gp, :]
        V(lambda e: e.tensor_tensor(out=tt[0][:], in0=bu[0][:], in1=c_, op=ALU.mult), [dbu[0]] + PT, [dtq[0]])
        V(lambda e: e.tensor_tensor(out=tt[1][:], in0=bu[1][:], in1=s_, op=ALU.mult), [dbu[1]] + PT, [dtq[1]])
        V(lambda e: e.tensor_tensor(out=w2[0][:], in0=tt[0][:], in1=tt[1][:], op=ALU.add), [dtq[0], dtq[1]], [dw[0]])
        V(lambda e: e.tensor_tensor(out=tt[2][:], in0=bu[1][:], in1=c_, op=ALU.mult), [dbu[1]] + PT, [dtq[2]], ENG2)
        V(lambda e: e.tensor_tensor(out=tt[3][:], in0=bu[0][:], in1=s_, op=ALU.mult), [dbu[0]] + PT, [dtq[3]], ENG2)
        V(lambda e: e.tensor_tensor(out=w2[1][:], in0=tt[2][:], in1=tt[3][:], op=ALU.subtract), [dtq[2], dtq[3]],
          [dw[1]], ENG2)

    def scans(gp, extra):
        for i in range(2):
            V(lambda e, i=i: e.tensor_tensor_scan(out=z2[i][:], data0=rf2[:, gp, :], data1=w2[i][:], initial=0.0,
                                                  op0=ALU.mult, op1=ALU.add), [dw[i]] + PT + extra, [dz[i]])

    def hs_step(gp, k, cur):
        d_ = 1 << k
        nxt = 2 - cur
        ar, ai, ain = Ad[0][:, gp, k:k + 1], Ad[1][:, gp, k:k + 1], Ad[2][:, gp, k:k + 1]
        sre, sim = cst[cur], cst[cur + 1]
        nre_, nim_ = cst[nxt], cst[nxt + 1]
        CS = [dcst]
        n = NC2
        V(lambda e: e.tensor_copy(out=nre_[:, 0:d_], in_=sre[:, 0:d_]), CS, CS)
        V(lambda e: e.tensor_copy(out=nim_[:, 0:d_], in_=sim[:, 0:d_]), CS, CS)
        V(lambda e: e.scalar_tensor_tensor(out=cst[4][:, d_:n], in0=sre[:, 0:n - d_], scalar=ar, in1=sre[:, d_:n],
                                           op0=ALU.mult, op1=ALU.add), CS + PT, CS)
        V(lambda e: e.scalar_tensor_tensor(out=nre_[:, d_:n], in0=sim[:, 0:n - d_], scalar=ain, in1=cst[4][:, d_:n],
                                           op0=ALU.mult, op1=ALU.add), CS + PT, CS)
        V(lambda e: e.scalar_tensor_tensor(out=cst[4][:, d_:n], in0=sim[:, 0:n - d_], scalar=ar, in1=sim[:, d_:n],
                                           op0=ALU.mult, op1=ALU.add), CS + PT, CS)
        V(lambda e: e.scalar_tensor_tensor(out=nim_[:, d_:n], in0=sre[:, 0:n - d_], scalar=ai, in1=cst[4][:, d_:n],
                                           op0=ALU.mult, op1=ALU.add), CS + PT, CS)
        return nxt

    def carry(gp):
        g1 = slice(gp, gp + 1)
        CS = [dcst]
        zE = [z2[i][:].rearrange("p (c j) -> p c j", j=J2)[:, :, J2 - 1] for i in range(2)]
        c63 = cs2[0][:, gp, J2 - 1:J2]
        s63 = cs2[1][:, gp, J2 - 1:J2]
        V(lambda e: e.tensor_scalar(out=cst[2][:], in0=zE[0], scalar1=c63, scalar2=None, op0=ALU.mult), [dz[0]] + PT, CS)
        V(lambda e: e.scalar_tensor_tensor(out=cst[0][:], in0=zE[1], scalar=s63n[:, g1], in1=cst[2][:], op0=ALU.mult,
                                           op1=ALU.add), [dz[1]] + PT + CS, CS)
        V(lambda e: e.tensor_scalar(out=cst[2][:], in0=zE[0], scalar1=s63, scalar2=None, op0=ALU.mult),
          [dz[0]] + PT + CS, CS)
        V(lambda e: e.scalar_tensor_tensor(out=cst[1][:], in0=zE[1], scalar=c63, in1=cst[2][:], op0=ALU.mult,
                                           op1=ALU.add), [dz[1]] + PT + CS, CS)
        cur = 0
        for k in range(3):
            cur = hs_step(gp, k, cur)
        Sre, Sim = cst[cur], cst[cur + 1]
        n1 = NC2 - 1
        w0 = [w2[i][:].rearrange("p (c j) -> p c j", j=J2)[:, 1:NC2, 0] for i in range(2)]
        a8r, a8i = PW[0][:, gp, 8:9], PW[1][:, gp, 8:9]
        V(lambda e: e.scalar_tensor_tensor(out=cst[5][:, 0:n1], in0=Sre[:, 0:n1], scalar=a8r, in1=w0[0], op0=ALU.mult,
                                           op1=ALU.add), CS + PT + [dw[0]], CS)
        V(lambda e: e.scalar_tensor_tensor(out=w0[0], in0=Sim[:, 0:n1], scalar=a8imn[:, g1], in1=cst[5][:, 0:n1],
                                           op0=ALU.mult, op1=ALU.add), CS + PT, [dw[0]])
        V(lambda e: e.scalar_tensor_tensor(out=cst[6][:, 0:n1], in0=Sim[:, 0:n1], scalar=a8r, in1=w0[1], op0=ALU.mult,
                                           op1=ALU.add), CS + PT + [dw[1]], CS)
        V(lambda e: e.scalar_tensor_tensor(out=w0[1], in0=Sre[:, 0:n1], scalar=a8i, in1=cst[6][:, 0:n1],
                                           op0=ALU.mult, op1=ALU.add), CS + PT, [dw[1]])

    def demod(gp, b):
        n1 = NM - 1
        Sp, dSp = Sprev2[b], dS2[b]
        c_, s_ = cs2[0][:, gp, 0:n1], cs2[1][:, gp, 0:n1]
        V(lambda e: e.tensor_tensor(out=tt[0][:, 0:n1], in0=z2[0][:, 0:n1], in1=c_, op=ALU.mult), [dz[0]] + PT, [dtq[0]])
        V(lambda e: e.tensor_tensor(out=tt[1][:, 0:n1], in0=z2[1][:, 0:n1], in1=s_, op=ALU.mult), [dz[1]] + PT, [dtq[1]])
        V(lambda e: e.tensor_tensor(out=Sp[:, gp, 0, 1:NM], in0=tt[0][:, 0:n1], in1=tt[1][:, 0:n1], op=ALU.subtract),
          [dtq[0], dtq[1]], [dSp])
        V(lambda e: e.tensor_tensor(out=tt[2][:, 0:n1], in0=z2[0][:, 0:n1], in1=s_, op=ALU.mult), [dz[0]] + PT, [dtq[2]],
          ENG2)
        V(lambda e: e.tensor_tensor(out=tt[3][:, 0:n1], in0=z2[1][:, 0:n1], in1=c_, op=ALU.mult), [dz[1]] + PT, [dtq[3]],
          ENG2)
        V(lambda e: e.tensor_tensor(out=Sp[:, gp, 1, 1:NM], in0=tt[2][:, 0:n1], in1=tt[3][:, 0:n1], op=ALU.add),
          [dtq[2], dtq[3]], [dSp], ENG2)

    def y_out(b, s):
        mm = []
        for sp_ in range(s + 1):
            mm.append((Kmat[:, s - sp_, :], uview(b, sp_), [dud] + PT))
        for gp in range(4):
            for ri in range(2):
                mm.append((WC[ri][:, s, gp, :], Sprev2[b][:, gp, ri, :], [dS2[b]] + PT))
        for n_, (l_, r_, dd) in enumerate(mm):
            P.op("pe", lambda e, l_=l_, r_=r_, n_=n_: e.matmul(psy[:], l_, r_, start=(n_ == 0), stop=(n_ == len(mm) - 1)),
                 reads=dd, writes=[dpsy])
        yv = y_sb[:, b * SEQ:(b + 1) * SEQ].rearrange("p (m s) -> p m s", s=TS)[:, :, s]
        V(lambda e: e.scalar_tensor_tensor(out=yv, in0=uview(b, s), scalar=d_sb[:, 0:1], in1=psy[:], op0=ALU.mult,
                                           op1=ALU.add), [dud, dpsy] + PT, [dy])

    def one_pass(gp, b):
        e_pass(gp, b)
        scans(gp, [])
        carry(gp)
        scans(gp, [dcst])
        demod(gp, b)
    for gp in range(4):
        one_pass(gp, 0)
    for gp in range(4):
        one_pass(gp, 1)
        if gp == 1 or not merged:
            if gp == 1 or gp == 3:
                pass
        if gp == 1:
            for s in range(TS):
                y_out(0, s)
    for s in range(TS):
        y_out(1, s)
    P.dma("sp", yp[:, 0:SEQ], y_sb[:, 0:SEQ], reads=[dy])
    P.dma("sp", yp[:, SEQ:NT], y_sb[:, SEQ:NT], reads=[dy])


def build_B2F():
    nc = bass.Bass("TRN2", target_bir_lowering=False)
    with ExitStack() as st:
        st.enter_context(nc.allow_low_precision("bf16 matmul operands, fp32 accumulation"))
        C = Ctx(nc, st)
        P = Prog(nc, st)
        emit_B2F(nc, C, P)
        P.finish()
    return nc
```

```python
import numpy as np
from contextlib import ExitStack
import ml_dtypes
import concourse.bass as bass
import concourse.mybir as mybir
from concourse.bass_utils import run_bass_kernel_spmd

F32 = mybir.dt.float32
BF16 = mybir.dt.bfloat16
AF = mybir.ActivationFunctionType
ALU = mybir.AluOpType
AX = mybir.AxisListType
NPBF = ml_dtypes.bfloat16

NCORES = 8
TOK = 1024
D = 2048
EPS = 1e-6
QSCALE = 128 ** -0.5


SAME_SYNC = {"pe": False, "act": True, "dve": True, "pool": True, "sp": True}


class Dep:
    __slots__ = ("w", "r")

    def __init__(self):
        self.w = None
        self.r = {}


class Prog:
    ENGS = ("pe", "act", "dve", "pool", "sp")

    def __init__(self, nc, stack, n_dma_sems=(("sp", 12), ("pool", 8), ("act", 4))):
        self.nc = nc
        self.stack = stack
        self.q = {e: [] for e in self.ENGS}
        self.sem = {e: stack.enter_context(nc.semaphore("s_" + e)) for e in ("pe", "act", "dve", "pool")}
        self.cnt = {e: 0 for e in self.sem}
        tot = sum(n for _, n in n_dma_sems)
        self.dsem = [stack.enter_context(nc.semaphore("d%d" % i)) for i in range(tot)]
        self.dcnt = [0] * tot
        self.dpool = {}
        b = 0
        for qn, n in n_dma_sems:
            self.dpool[qn] = [list(range(b, b + n)), 0]
            b += n
        self.seen = {e: {} for e in self.ENGS}
        self.same_sync = dict(SAME_SYNC)
        self._rec = None
        self._streams = {}

    def _semobj(self, key):
        return self.sem[key[1]] if key[0] == "e" else self.dsem[key[1]]

    def _collect(self, eng, reads, writes, extra=None):
        need = dict(extra or {})

        def req(k, v):
            if v > need.get(k, 0):
                need[k] = v
        for t in reads:
            if t.w:
                req(*t.w)
        for t in writes:
            if t.w:
                req(*t.w)
            for k, v in t.r.items():
                req(k, v)
        for k, v in need.items():
            if k == ("e", eng) and not self.same_sync[eng]:
                continue
            if self.seen[eng].get(k, 0) < v:
                self.seen[eng][k] = v
                s = self._semobj(k)
                self.q[eng].append(lambda e, s=s, v=v: e.wait_ge(s, v))

    def begin(self, name):
        self._rec = []
        self._streams[name] = self._rec

    def end(self):
        self._rec = None

    def replay(self, names, speed=None):
        recs = [self._streams[n] for n in names]
        pos = [0] * len(recs)
        total = sum(len(r) for r in recs)
        for _ in range(total):
            best, bestv = None, None
            for i, r in enumerate(recs):
                if pos[i] < len(r):
                    v = (pos[i] + 0.5) / (len(r) * (speed[i] if speed else 1.0))
                    if bestv is None or v < bestv:
                        best, bestv = i, v
            kind, args, kw = recs[best][pos[best]]
            pos[best] += 1
            getattr(self, kind)(*args, **kw)

    def op(self, eng, fn, reads=(), writes=()):
        if self._rec is not None:
            self._rec.append(("op", (eng, fn, tuple(reads), tuple(writes)), {}))
            return
        self._collect(eng, reads, writes)
        self.cnt[eng] += 1
        c = self.cnt[eng]
        key = ("e", eng)
        s = self.sem[eng]
        self.q[eng].append(lambda e, fn=fn, s=s: fn(e).then_inc(s, 1))
        for t in reads:
            if t.r.get(key, 0) < c:
                t.r[key] = c
        for t in writes:
            t.w = (key, c)
            t.r = {}

    def dma(self, queue, out, in_, reads=(), writes=(), in_fn=None, out_fn=None, **kw):
        if self._rec is not None:
            self._rec.append(("dma", (queue, out, in_, tuple(reads), tuple(writes), in_fn, out_fn), dict(kw)))
            return
        pl = self.dpool[queue]
        i = pl[0][pl[1] % len(pl[0])]
        pl[1] += 1
        extra = {}
        if self.dcnt[i] > 0:
            extra[("d", i)] = self.dcnt[i]
        self._collect(queue, reads, writes, extra)
        self.dcnt[i] += 16
        v = self.dcnt[i]
        key = ("d", i)
        s = self.dsem[i]
        self.q[queue].append(
            lambda e, s=s, out=out, in_=in_, kw=kw: e.dma_start(
                out=(out_fn() if out_fn else out), in_=(in_fn() if in_fn else in_), **kw).then_inc(s, 16))
        for t in reads:
            if t.r.get(key, 0) < v:
                t.r[key] = v
        for t in writes:
            t.w = (key, v)
            t.r = {}

    def raw(self, eng, fn, reads=(), writes=()):
        self._collect(eng, reads, writes)
        self.q[eng].append(lambda e, fn=fn: fn(e))

    def coll(self, queue, fn, reads=(), writes=()):
        pl = self.dpool[queue]
        i = pl[0][pl[1] % len(pl[0])]
        pl[1] += 1
        extra = {}
        if self.dcnt[i] > 0:
            extra[("d", i)] = self.dcnt[i]
        self._collect(queue, reads, writes, extra)
        self.dcnt[i] += 16
        v = self.dcnt[i]
        key = ("d", i)
        s = self.dsem[i]
        self.q[queue].append(lambda e, s=s, fn=fn: fn(e).then_inc(s, 16))
        for t in reads:
            if t.r.get(key, 0) < v:
                t.r[key] = v
        for t in writes:
            t.w = (key, v)
            t.r = {}

    def finish(self):
        for i, v in enumerate(self.dcnt):
            if v > 0 and self.seen["sp"].get(("d", i), 0) < v:
                s = self.dsem[i]
                self.q["sp"].append(lambda e, s=s, v=v: e.wait_ge(s, v))
        for e_, c in self.cnt.items():
            if c > 0:
                s = self.sem[e_]
                self.q["sp"].append(lambda e, s=s, c=c: e.wait_ge(s, c))
        q = self.q
        with self.nc.Block() as block:
            @block.tensor
            def _(e):
                for f in q["pe"]:
                    f(e)

            @block.scalar
            def _(e):
                for f in q["act"]:
                    f(e)

            @block.vector
            def _(e):
                for f in q["dve"]:
                    f(e)

            @block.gpsimd
            def _(e):
                for f in q["pool"]:
                    f(e)

            @block.sync
            def _(e):
                for f in q["sp"]:
                    f(e)


class Ctx:
    def __init__(self, nc, st):
        self.nc, self.st = nc, st

    def sb(self, name, shape, dt):
        return self.st.enter_context(self.nc.sbuf_tensor(name, shape, dt))

    def ps(self, name, shape, dt=F32):
        return self.st.enter_context(self.nc.psum_tensor(name, shape, dt))

    def din(self, name, shape, dt=F32):
        return self.nc.dram_tensor(name, list(shape), dt, kind="ExternalInput").ap()

    def dout(self, name, shape, dt=F32):
        return self.nc.dram_tensor(name, list(shape), dt, kind="ExternalOutput").ap()


def emit_rmsnorm(P, C, x_sb, dx, g_sb, dg, hT, dh, ones_bf, dones, sq, dsq, ps, dps, rstd, drstd, epsb, deps, ntok):
    dxl = dx if isinstance(dx, list) else [dx] * 4
    dhl = dh if isinstance(dh, list) else [dh] * 16
    for kt in range(16):
        i = kt % 2
        P.op("act", lambda e, kt=kt, i=i: e.activation(out=sq[i][:], in_=x_sb[:, kt, :], func=AF.Square),
             reads=[dxl[kt // 4]], writes=[dsq[i]])
        for hf in range(ntok // 512):
            P.op("pe", lambda e, kt=kt, i=i, hf=hf: e.matmul(ps[:, hf * 512:(hf + 1) * 512], ones_bf[:],
                                                             sq[i][:, hf * 512:(hf + 1) * 512],
                                                             start=(kt == 0), stop=(kt == 15)),
                 reads=[dsq[i], dones], writes=[dps])
    P.op("act", lambda e: e.activation(out=rstd[:], in_=ps[:, :ntok], func=AF.Sqrt, bias=epsb[:], scale=1.0 / D),
         reads=[dps, deps], writes=[drstd])
    P.op("dve", lambda e: e.reciprocal(out=rstd[:], in_=rstd[:]), reads=[drstd], writes=[drstd])
    for kt in range(16):
        P.op("dve", lambda e, kt=kt: e.scalar_tensor_tensor(out=hT[:, kt, :], in0=x_sb[:, kt, :],
                                                           scalar=g_sb[:, kt:kt + 1], in1=rstd[:],
                                                           op0=ALU.mult, op1=ALU.mult),
             reads=[dxl[kt // 4], dg, drstd], writes=[dhl[kt]])


def build_A():
    nc = bass.Bass("TRN2", target_bir_lowering=False)
    with ExitStack() as st:
        st.enter_context(nc.allow_low_precision("bf16 matmul operands, fp32 accumulation"))
        C = Ctx(nc, st)
        xT = C.din("xT", [D, TOK])
        n1g = C.din("n1g", [128, 16])
        w_in = C.din("w_in", [D, 12288])
        bg = C.din("bg", [128, 48])
        gng = C.din("gng", [1, 1024])
        wsT = C.din("wsT", [8, 128, 128])
        bs = C.din("bs", [1, 1024])
        yaT = C.dout("yaT", [1024, TOK], BF16)
        s5T = C.dout("s5T", [1024, TOK], BF16)
        qT = C.dout("qT", [1024, TOK], BF16)
        kT = C.dout("kT", [1024, TOK], BF16)
        vtk = C.dout("vtk", [TOK, 1024], BF16)
        gT = C.dout("gT", [6144, TOK], BF16)

        P = Prog(nc, st)
        x_sb = C.sb("x_sb", [128, 16, TOK], F32)
        hT = C.sb("hT", [128, 16, TOK], BF16)
        g_sb = C.sb("g_sb", [128, 16], F32)
        bg_sb = C.sb("bg_sb", [128, 48], F32)
        gng_b = C.sb("gng_b", [128, 1024], F32)
        bs_b = C.sb("bs_b", [128, 1024], F32)
        ws_f = C.sb("ws_f", [128, 8, 128], F32)
        ws_b = C.sb("ws_b", [128, 8, 128], BF16)
        ones_bf = C.sb("ones_bf", [128, 128], BF16)
        epsb = C.sb("epsb", [128, 1], F32)
        sq = [C.sb("sq%d" % i, [128, TOK], BF16) for i in range(2)]
        rstd = C.sb("rstd", [128, TOK], F32)
        wt = [C.sb("wt%d" % i, [128, 16, 512], BF16) for i in range(2)]
        stg = [C.sb("stg%d" % i, [128, TOK], BF16) for i in range(2)]
        uT = C.sb("uT", [128, 8, TOK], BF16)
        vtok = C.sb("vtok", [128, 8, 1024], BF16)
        vn = C.sb("vn", [128, 1024], BF16)
        junk = sq[0]
        vss = C.sb("vss", [128, 8], F32)
        vrs = C.sb("vrs", [128, 8], F32)
        mixt = rstd
        ya_sb = C.sb("ya_sb", [128, 8, TOK], BF16)
        ps = [C.ps("ps%d" % i, [128, 1024]) for i in range(2)]
        psm = C.ps("psm", [128, 1024])

        dx, dh, dg, dbg, dgng, dbs, dwsf, dwsb, dones, deps = [Dep() for _ in range(10)]
        dsq = [Dep(), Dep()]
        drstd = Dep()
        dwt = [Dep(), Dep()]
        dstg = [Dep(), Dep()]
        dps = [Dep(), Dep()]
        dpsm, duT, dvtok, dvn, dvss, dvrs, dya = [Dep() for _ in range(7)]
        djunk = dsq[0]
        dmixt = drstd

        xv = xT.rearrange("(kt p) t -> p kt t", p=128)
        dx = [Dep() for _ in range(4)]
        dh = [Dep() for _ in range(16)]
        for i in range(4):
            P.dma("sp" if i % 2 == 0 else "act", x_sb[:, 4 * i:4 * i + 4, :], xv[:, 4 * i:4 * i + 4, :], writes=[dx[i]])
        P.dma("sp", g_sb[:], n1g, writes=[dg])
        P.dma("sp", bg_sb[:], bg, writes=[dbg])
        P.dma("sp", gng_b[:], gng.partition_broadcast(128), writes=[dgng])
        P.dma("sp", bs_b[:], bs.partition_broadcast(128), writes=[dbs])
        P.dma("sp", ws_f[:], wsT.rearrange("g s t -> s g t"), writes=[dwsf])
        wv = w_in.rearrange("(kt p) c -> p kt c", p=128)

        def load_w(cb):
            P.dma("pool", wt[cb % 2][:], wv[:, :, cb * 512:(cb + 1) * 512], writes=[dwt[cb % 2]])
        load_w(0)
        P.op("dve", lambda e: e.memset(ones_bf[:], 1.0), writes=[dones])
        P.op("dve", lambda e: e.memset(epsb[:], EPS), writes=[deps])
        P.op("pool", lambda e: e.affine_select(out=ws_f[:], in_=ws_f[:], pattern=[[0, 8], [1, 128]],
                                               compare_op=ALU.is_ge, fill=0.0, base=0, channel_multiplier=-1),
             reads=[dwsf], writes=[dwsf])
        P.op("dve", lambda e: e.tensor_copy(out=ws_b[:], in_=ws_f[:]), reads=[dwsf], writes=[dwsb])

        emit_rmsnorm(P, C, x_sb, dx, g_sb, dg, hT, dh, ones_bf, dones, sq, dsq, ps[0], dps[0], rstd, drstd,
                     epsb, deps, TOK)

        pcount = [0]
        scount = [0]

        def form2(cb, epi, hook=None):
            w = wt[cb % 2]
            for m in range(4):
                pi = pcount[0] % 2
                pcount[0] += 1
                for hf in range(2):
                    for kt in range(16):
                        P.op("pe", lambda e, w=w, m=m, hf=hf, kt=kt, pi=pi: e.matmul(
                            ps[pi][:, hf * 512:(hf + 1) * 512], w[:, kt, m * 128:(m + 1) * 128],
                            hT[:, kt, hf * 512:(hf + 1) * 512], start=(kt == 0), stop=(kt == 15)),
                            reads=[dwt[cb % 2], dh[kt]], writes=[dps[pi]])
                if hook is not None:
                    hook(m)
                epi(cb * 4 + m, pi)

        def epi_store(func, dst, colbase, scale=1.0, bias_col=None):
            def epi(blk, pi):
                si = scount[0] % 2
                scount[0] += 1
                r0 = (blk - colbase) * 128
                if bias_col is None:
                    P.op("act", lambda e: e.activation(out=stg[si][:], in_=ps[pi][:], func=func, scale=scale),
                         reads=[dps[pi]], writes=[dstg[si]])
                else:
                    bc = bias_col(blk)
                    P.op("act", lambda e: e.activation(out=stg[si][:], in_=ps[pi][:], func=func,
                                                       bias=bg_sb[:, bc:bc + 1]),
                         reads=[dps[pi], dbg], writes=[dstg[si]])
                P.dma("sp", dst[r0:r0 + 128, :], stg[si][:], reads=[dstg[si]])
            return epi

        def epi_u(blk, pi):
            P.op("act", lambda e: e.activation(out=uT[:, blk, :], in_=ps[pi][:], func=AF.Gelu_apprx_tanh),
                 reads=[dps[pi]], writes=[duT])

        def form1(cb, epi):
            w = wt[cb % 2]
            for c in range(8):
                pi = pcount[0] % 2
                pcount[0] += 1
                for kt in range(16):
                    P.op("pe", lambda e, w=w, c=c, kt=kt, pi=pi: e.matmul(
                        ps[pi][:, 0:512], hT[:, kt, c * 128:(c + 1) * 128], w[:, kt, :],
                        start=(kt == 0), stop=(kt == 15)),
                        reads=[dwt[cb % 2], dh[kt]], writes=[dps[pi]])
                epi(cb, c, pi)

        def epi_vg(cb, c, pi):
            off = (cb - 2) * 512
            P.op("act", lambda e: e.activation(out=vtok[:, c, off:off + 512], in_=ps[pi][:, 0:512],
                                               func=AF.Gelu_apprx_tanh),
                 reads=[dps[pi]], writes=[dvtok])

        def epi_va(cb, c, pi):
            off = (cb - 10) * 512
            si = scount[0] % 2
            scount[0] += 1
            P.op("act", lambda e: e.activation(out=stg[si][:, 0:512], in_=ps[pi][:, 0:512], func=AF.Copy),
                 reads=[dps[pi]], writes=[dstg[si]])
            P.dma("sp", vtk[c * 128:(c + 1) * 128, off:off + 512], stg[si][:, 0:512], reads=[dstg[si]])

        def gmlp_stats():
            for c in range(8):
                P.op("act", lambda e, c=c: e.activation(out=junk[:], in_=vtok[:, c, :], func=AF.Square,
                                                        accum_out=vss[:, c:c + 1]),
                     reads=[dvtok], writes=[djunk, dvss])
            P.op("act", lambda e: e.activation(out=vrs[:], in_=vss[:], func=AF.Sqrt, bias=epsb[:], scale=1.0 / 1024),
                 reads=[dvss, deps], writes=[dvrs])
            P.op("dve", lambda e: e.reciprocal(out=vrs[:], in_=vrs[:]), reads=[dvrs], writes=[dvrs])

        def gmlp_chunk(c):
            P.op("dve", lambda e: e.scalar_tensor_tensor(out=vn[:], in0=vtok[:, c, :], scalar=vrs[:, c:c + 1],
                                                         in1=gng_b[:], op0=ALU.mult, op1=ALU.mult),
                 reads=[dvtok, dvrs, dgng], writes=[dvn])
            for g in range(8):
                P.op("pe", lambda e, g=g: e.matmul(psm[:, g * 128:(g + 1) * 128], vn[:, g * 128:(g + 1) * 128],
                                                   ws_b[:, g, :], start=True, stop=True),
                     reads=[dvn, dwsb], writes=[dpsm])
            P.op("dve", lambda e: e.tensor_tensor(out=mixt[:], in0=psm[:], in1=bs_b[:], op=ALU.add),
                 reads=[dpsm, dbs], writes=[dmixt])
            P.op("pool", lambda e: e.tensor_tensor(
                out=ya_sb[:, :, c * 128:(c + 1) * 128], in0=mixt[:].rearrange("p (g t) -> p g t", g=8),
                in1=uT[:, :, c * 128:(c + 1) * 128], op=ALU.mult),
                reads=[dmixt, duT], writes=[dya])
            if c == 7:
                P.dma("sp", yaT.rearrange("(g p) t -> p g t", p=128), ya_sb[:], reads=[dya])

        for cb in range(24):
            if cb + 1 < 24:
                load_w(cb + 1)
            if cb < 2:
                form2(cb, epi_u)
            elif cb < 4:
                form1(cb, epi_vg)
                if cb == 3:
                    gmlp_stats()
            elif cb < 6:
                form2(cb, epi_store(AF.Copy, s5T, 16), hook=lambda m, cb=cb: gmlp_chunk((cb - 4) * 4 + m))
            elif cb < 8:
                form2(cb, epi_store(AF.Copy, qT, 24, scale=QSCALE))
            elif cb < 10:
                form2(cb, epi_store(AF.Copy, kT, 32))
            elif cb < 12:
                form1(cb, epi_va)
            else:
                form2(cb, epi_store(AF.Sigmoid, gT, 48, bias_col=lambda blk: blk - 48))
        P.finish()
    return nc


def host_A(x_tok, l, inp):
    maps = []
    n1g = np.ascontiguousarray(inp["norm1_g"][l].reshape(16, 128).T)
    bgv = np.ascontiguousarray(inp["b_gate"][l].reshape(48, 128).T)
    gng = np.ascontiguousarray(inp["gm_norm_g"][l].reshape(1, 1024))
    wsT = np.ascontiguousarray(np.transpose(inp["gm_w_s"][l], (0, 2, 1)))
    bsv = np.ascontiguousarray(inp["gm_b_s"][l].reshape(1, 1024))
    w_in = np.ascontiguousarray(inp["w_in"][l])
    for c in range(NCORES):
        xT = np.ascontiguousarray(x_tok[c * TOK:(c + 1) * TOK, :].T)
        maps.append({"xT": xT, "n1g": n1g, "w_in": w_in, "bg": bgv, "gng": gng, "wsT": wsT, "bs": bsv})
    return maps


def build_C(last, debug=False):
    nc = bass.Bass("TRN2", target_bir_lowering=False)
    with ExitStack() as st:
        st.enter_context(nc.allow_low_precision("bf16 matmul operands, fp32 accumulation"))
        C = Ctx(nc, st)
        xT = C.din("xT", [D, TOK])
        yaT = C.din("yaT", [1024, TOK], BF16)
        ybpT = C.din("ybpT", [1024, TOK], BF16)
        ycT = C.din("ycT", [1024, TOK], BF16)
        gT = C.din("gT", [6144, TOK], BF16)
        w_glu = C.din("w_glu", [1024, 1024])
        b_glu = C.din("b_glu", [128, 8])
        w_br = C.din("w_br", [3, 1024, D])
        w_out = C.din("w_out", [D, D])
        n2g = C.din("n2g", [128, 16])
        w_m1 = C.din("w_m1", [D, 8192])
        w_m2 = C.din("w_m2", [8192, D])
        fing = C.din("fing", [128, 16])
        xo = C.dout("xo", [D, TOK])
        if debug:
            dbg_yb = C.dout("dbg_yb", [1024, TOK], BF16)
            dbg_mg = C.dout("dbg_mg", [D, TOK], BF16)
            dbg_x1 = C.dout("dbg_x1", [D, TOK])
            dbg_h2 = C.dout("dbg_h2", [D, TOK], BF16)

        P = Prog(nc, st)
        x_sb = C.sb("x_sb", [128, 16, TOK], F32)
        R2 = C.sb("R2", [128, 3, 8, TOK], BF16)
        R3 = C.sb("R3", [128, 16 * TOK], BF16)
        R4 = C.sb("R4", [128, 4, 8 * 512], BF16)
        gts = [C.sb("gts%d" % i, [128, TOK], BF16) for i in range(6)]
        acc = C.sb("acc", [128, TOK], F32)
        tmp = C.sb("tmp", [128, TOK], F32)
        stg = [C.sb("stg%d" % i, [128, TOK], BF16) for i in range(2)]
        rstd = C.sb("rstd", [128, TOK], F32)
        sq = stg
        g2_sb = C.sb("g2_sb", [128, 16], F32)
        gf_sb = C.sb("gf_sb", [128, 16], F32)
        bgl_sb = C.sb("bgl_sb", [128, 8], F32)
        ones_bf = C.sb("ones_bf", [128, 128], BF16)
        epsb = C.sb("epsb", [128, 1], F32)
        ps = [C.ps("ps%d" % i, [128, 1024]) for i in range(3)]
        ps_ss = C.ps("ps_ss", [128, 1024])
        dps_ss = Dep()
        dh2 = [Dep() for _ in range(16)]
        dxt = [Dep() for _ in range(16)]

        dx, dR3, dg2, dgf, dbgl, dones, deps, dacc, dtmp, drstd = [Dep() for _ in range(10)]
        dR2 = [Dep(), Dep(), Dep()]
        dR4 = [Dep() for _ in range(4)]
        dgts = [Dep() for _ in range(6)]
        dstg = [Dep(), Dep()]
        dsq = dstg
        dps = [Dep() for _ in range(3)]

        gS = R3[:, 0:8 * TOK].rearrange("p (k t) -> p k t", k=8)
        mg = R3[:].rearrange("p (k t) -> p k t", k=16)
        aT = R2[:].rearrange("p a k t -> p (a k) t")
        w8 = [R4[:, i, :].rearrange("p (k c) -> p k c", k=8) for i in range(4)]
        w16 = [R4[:, 2 * i:2 * i + 2, :].rearrange("p a (k c) -> p (a k) c", k=8) for i in range(2)]

        xv = xT.rearrange("(kt p) t -> p kt t", p=128)
        P.dma("sp", R2[:, 1], ybpT.rearrange("(k p) t -> p k t", p=128), writes=[dR2[1]])
        P.dma("sp", bgl_sb[:], b_glu, writes=[dbgl])
        for i in range(4):
            P.dma("sp", x_sb[:, 4 * i:4 * i + 4, :], xv[:, 4 * i:4 * i + 4, :], writes=[dx])
        P.dma("sp", R2[:, 0], yaT.rearrange("(k p) t -> p k t", p=128), writes=[dR2[0]])
        P.dma("sp", R2[:, 2], ycT.rearrange("(k p) t -> p k t", p=128), writes=[dR2[2]])
        P.dma("sp", g2_sb[:], n2g, writes=[dg2])
        P.dma("sp", gf_sb[:], fing, writes=[dgf])
        P.op("dve", lambda e: e.memset(ones_bf[:], 1.0), writes=[dones])
        P.op("dve", lambda e: e.memset(epsb[:], EPS), writes=[deps])

        jobs = []
        glu_v = w_glu.rearrange("(kt p) c -> p kt c", p=128)
        for cb in range(2):
            jobs.append(("glu", cb, 8, glu_v[:, :, cb * 512:(cb + 1) * 512]))
        for cb in range(4):
            for n in range(3):
                jobs.append(("br", (cb, n), 8, w_br[n].rearrange("(kt p) c -> p kt c", p=128)[:, :, cb * 512:(cb + 1) * 512]))
        wo_v = w_out.rearrange("(kt p) c -> p kt c", p=128)
        for cb in range(4):
            jobs.append(("wo", cb, 16, wo_v[:, :, cb * 512:(cb + 1) * 512]))
        m1_v = w_m1.rearrange("(kt p) c -> p kt c", p=128)
        m2_v = w_m2.rearrange("(kt p) c -> p kt c", p=128)
        for fc in range(4):
            for fb in range(4):
                c0 = fc * 2048 + fb * 512
                jobs.append(("m1", (fc, fb), 16, m1_v[:, :, c0:c0 + 512]))
            for cb in range(4):
                jobs.append(("m2", (fc, cb), 16, m2_v[:, fc * 16:(fc + 1) * 16, cb * 512:(cb + 1) * 512]))
        slot8 = 0
        slots = []
        quarters = []
        for kind, key, nk, src in jobs:
            if nk == 8:
                s = slot8 % 4
                slot8 += 1
                slots.append((w8[s], [dR4[s]]))
                quarters.append([s])
            else:
                if slot8 % 2:
                    slot8 += 1
                s = (slot8 // 2) % 2
                slot8 += 2
                slots.append((w16[s], [dR4[2 * s], dR4[2 * s + 1]]))
                quarters.append([2 * s, 2 * s + 1])
        issued = [0]
        occupant = [None] * 4
        consumed = set()

        def prefetch(upto):
            while issued[0] < min(upto, len(jobs)):
                j = issued[0]
                if any(occupant[q] is not None and occupant[q] not in consumed for q in quarters[j]):
                    return
                for q in quarters[j]:
                    occupant[q] = j
                P.dma("pool", slots[j][0], jobs[j][3], writes=slots[j][1])
                issued[0] += 1

        def done(j):
            consumed.add(j)
            prefetch(j + 4)
        pc = [0]

        def mm_block(j, m, rhs, drhs, nkt, drhs_kt=None):
            pi = pc[0] % 3
            pc[0] += 1
            prefetch(j + 1)
            assert issued[0] > j, "weight job %d not loadable (ring slot busy)" % j
            wv, wd = slots[j]
            for hf in range(2):
                for kt in range(nkt):
                    P.op("pe", lambda e, wv=wv, m=m, hf=hf, kt=kt, pi=pi: e.matmul(
                        ps[pi][:, hf * 512:(hf + 1) * 512], wv[:, kt, m * 128:(m + 1) * 128],
                        rhs[:, kt, hf * 512:(hf + 1) * 512], start=(kt == 0), stop=(kt == nkt - 1)),
                        reads=wd + drhs + ([drhs_kt[kt]] if drhs_kt else []), writes=[dps[pi]])
            return pi

        jn = [0]
        prefetch(3)
        for k in range(8):
            P.op("act", lambda e, k=k: e.activation(out=gS[:, k, :], in_=R2[:, 1, k, :], func=AF.Gelu_apprx_tanh),
                 reads=[dR2[1]], writes=[dR3])
        sc = [0]
        for cb in range(2):
            j = jn[0]
            jn[0] += 1
            prefetch(j + 3)
            for m in range(4):
                blk = cb * 4 + m
                pi = mm_block(j, m, gS, [dR3], 8)
                si = sc[0] % 2
                sc[0] += 1
                P.op("act", lambda e, pi=pi, si=si, blk=blk: e.activation(
                    out=stg[si][:], in_=ps[pi][:], func=AF.Sigmoid, bias=bgl_sb[:, blk:blk + 1]),
                    reads=[dps[pi], dbgl], writes=[dstg[si]])
                P.op("dve", lambda e, si=si, blk=blk: e.tensor_tensor(
                    out=R2[:, 1, blk, :], in0=gS[:, blk, :], in1=stg[si][:], op=ALU.mult),
                    reads=[dR3, dstg[si]], writes=[dR2[1]])
            done(j)
        if debug:
            P.dma("sp", dbg_yb.rearrange("(k p) t -> p k t", p=128), R2[:, 1], reads=[dR2[1]])
        gc = [0]
        for cb in range(4):
            js = [jn[0], jn[0] + 1, jn[0] + 2]
            jn[0] += 3
            prefetch(js[2] + 2)
            for m in range(4):
                dt_ = cb * 4 + m
                for n in range(3):
                    gi = gc[0] % 6
                    gc[0] += 1
                    r0 = (n * 16 + dt_) * 128
                    P.dma("sp", gts[gi][:], gT[r0:r0 + 128, :], writes=[dgts[gi]])
                    pi = mm_block(js[n], m, R2[:, n], [dR2[n]], 8)
                    if n == 0:
                        P.op("dve", lambda e, pi=pi, gi=gi: e.tensor_tensor(out=acc[:], in0=ps[pi][:], in1=gts[gi][:],
                                                                            op=ALU.mult),
                             reads=[dps[pi], dgts[gi]], writes=[dacc])
                    else:
                        P.op("dve", lambda e, pi=pi, gi=gi: e.tensor_tensor(out=tmp[:], in0=ps[pi][:], in1=gts[gi][:],
                                                                            op=ALU.mult),
                             reads=[dps[pi], dgts[gi]], writes=[dtmp])
                        if n == 1:
                            P.op("dve", lambda e: e.tensor_tensor(out=acc[:], in0=acc[:], in1=tmp[:], op=ALU.add),
                                 reads=[dacc, dtmp], writes=[dacc])
                        else:
                            P.op("dve", lambda e, dt_=dt_: e.tensor_tensor(out=mg[:, dt_, :], in0=acc[:], in1=tmp[:],
                                                                          op=ALU.add),
                                 reads=[dacc, dtmp], writes=[dR3])
            for j_ in js:
                done(j_)
        if debug:
            P.dma("sp", dbg_mg.rearrange("(k p) t -> p k t", p=128), mg, reads=[dR3])
        def ss_tile(t_):
            qi = t_ % 2
            P.op("act", lambda e: e.activation(out=sq[qi][:], in_=x_sb[:, t_, :], func=AF.Square),
                 reads=[dxt[t_]], writes=[dsq[qi]])
            for hf in range(2):
                P.op("pe", lambda e, hf=hf: e.matmul(ps_ss[:, hf * 512:(hf + 1) * 512], ones_bf[:],
                                                     sq[qi][:, hf * 512:(hf + 1) * 512],
                                                     start=(t_ == 0), stop=(t_ == 15)),
                     reads=[dsq[qi], dones], writes=[dps_ss])

        for cb in range(4):
            j = jn[0]
            jn[0] += 1
            prefetch(j + 2)
            for m in range(4):
                dt_ = cb * 4 + m
                pi = mm_block(j, m, mg, [dR3], 16)
                P.op("dve", lambda e, pi=pi, dt_=dt_: e.tensor_tensor(out=x_sb[:, dt_, :], in0=x_sb[:, dt_, :],
                                                                      in1=ps[pi][:], op=ALU.add),
                     reads=[dps[pi], dx], writes=[dx, dxt[dt_]])
                if dt_ > 0:
                    ss_tile(dt_ - 1)
            done(j)
        if debug:
            P.dma("sp", dbg_x1.rearrange("(k p) t -> p k t", p=128), x_sb[:], reads=[dx])
        ss_tile(15)
        P.op("act", lambda e: e.activation(out=rstd[:], in_=ps_ss[:], func=AF.Sqrt, bias=epsb[:], scale=1.0 / D),
             reads=[dps_ss, deps], writes=[drstd])
        P.op("dve", lambda e: e.reciprocal(out=rstd[:], in_=rstd[:]), reads=[drstd], writes=[drstd])
        for kt in range(16):
            P.op("dve", lambda e, kt=kt: e.scalar_tensor_tensor(out=mg[:, kt, :], in0=x_sb[:, kt, :],
                                                               scalar=g2_sb[:, kt:kt + 1], in1=rstd[:],
                                                               op0=ALU.mult, op1=ALU.mult),
                 reads=[dx, dg2, drstd], writes=[dh2[kt], dR3])
        if debug:
            P.dma("sp", dbg_h2.rearrange("(k p) t -> p k t", p=128), mg, reads=[dR3])
        for fc in range(4):
            for fb in range(4):
                j = jn[0]
                jn[0] += 1
                prefetch(j + 2)
                for m in range(4):
                    ft = fb * 4 + m
                    pi = mm_block(j, m, mg, [], 16, drhs_kt=dh2)
                    P.op("act", lambda e, pi=pi: e.activation(out=tmp[:], in_=ps[pi][:], func=AF.Relu),
                         reads=[dps[pi]], writes=[dtmp])
                    P.op("dve", lambda e, ft=ft: e.tensor_tensor(out=aT[:, ft, :], in0=tmp[:], in1=tmp[:], op=ALU.mult),
                         reads=[dtmp], writes=dR2)
                done(j)
            for cb in range(4):
                j = jn[0]
                jn[0] += 1
                prefetch(j + 2)
                for m in range(4):
                    dt_ = cb * 4 + m
                    pi = mm_block(j, m, aT, dR2, 16)
                    P.op("dve", lambda e, pi=pi, dt_=dt_: e.tensor_tensor(out=x_sb[:, dt_, :], in0=x_sb[:, dt_, :],
                                                                          in1=ps[pi][:], op=ALU.add),
                         reads=[dps[pi], dx], writes=[dx])
                done(j)
        xov = xo.rearrange("(kt p) t -> p kt t", p=128)
        if not last:
            for i in range(4):
                P.dma("sp", xov[:, 4 * i:4 * i + 4, :], x_sb[:, 4 * i:4 * i + 4, :], reads=[dx])
        else:
            for kt in range(16):
                i = kt % 2
                P.op("act", lambda e, kt=kt, i=i: e.activation(out=sq[i][:], in_=x_sb[:, kt, :], func=AF.Square),
                     reads=[dx], writes=[dsq[i]])
                for hf in range(2):
                    P.op("pe", lambda e, kt=kt, i=i, hf=hf: e.matmul(ps[0][:, hf * 512:(hf + 1) * 512], ones_bf[:],
                                                                     sq[i][:, hf * 512:(hf + 1) * 512],
                                                                     start=(kt == 0), stop=(kt == 15)),
                         reads=[dsq[i], dones], writes=[dps[0]])
            P.op("act", lambda e: e.activation(out=rstd[:], in_=ps[0][:], func=AF.Sqrt, bias=epsb[:], scale=1.0 / D),
                 reads=[dps[0], deps], writes=[drstd])
            P.op("dve", lambda e: e.reciprocal(out=rstd[:], in_=rstd[:]), reads=[drstd], writes=[drstd])
            fo = [acc, tmp]
            dfo = [dacc, dtmp]
            for kt in range(16):
                i = kt % 2
                P.op("dve", lambda e, kt=kt, i=i: e.scalar_tensor_tensor(out=fo[i][:], in0=x_sb[:, kt, :],
                                                                       scalar=gf_sb[:, kt:kt + 1], in1=rstd[:],
                                                                       op0=ALU.mult, op1=ALU.mult),
                     reads=[dx, dgf, drstd], writes=[dfo[i]])
                P.dma("sp", xov[:, kt, :], fo[i][:], reads=[dfo[i]])
        P.finish()
    return nc


def host_C(x_tok, l, inp, yaT, ybpT, ycT, gT, last):
    maps = []
    r = lambda v, n: np.ascontiguousarray(v.reshape(n, 128).T)
    com = {"w_glu": np.ascontiguousarray(inp["s5_w_glu"][l]), "b_glu": r(inp["s5_b_glu"][l], 8),
           "w_br": np.ascontiguousarray(inp["w_branch"][l]), "w_out": np.ascontiguousarray(inp["w_out"][l]),
           "n2g": r(inp["norm2_g"][l], 16), "w_m1": np.ascontiguousarray(inp["w_mlp_in"][l]),
           "w_m2": np.ascontiguousarray(inp["w_mlp_out"][l]), "fing": r(inp["final_g"], 16)}
    for c in range(NCORES):
        m = dict(com)
        m["xT"] = np.ascontiguousarray(x_tok[c * TOK:(c + 1) * TOK, :].T)
        m["yaT"], m["ybpT"], m["ycT"], m["gT"] = yaT[c], ybpT[c], ycT[c], gT[c]
        maps.append(m)
    return maps


SEQ = 4096
NPAIR = 2


def emit_B1(nc, C, P, merged=False):
    qT = C.din("qT", [NPAIR, 128, SEQ], BF16)
    kT = C.din("kT", [NPAIR, 128, SEQ], BF16)
    vv = C.din("v", [NPAIR, SEQ, 128], BF16)
    yc = C.dout("yc", [NPAIR, 128, SEQ], BF16)
    NB = 1 if merged else NPAIR
    q_sb = C.sb("q_sb", [128, NB, SEQ], BF16)
    k_sb = C.sb("k_sb", [128, NB, SEQ], BF16)
    v_sb = C.sb("v_sb", [128, NB, 32, 128], BF16)
    o_st = [C.sb("o_st%d" % i, [128, 512], BF16) for i in range(2)]
    ones_f = C.sb("ones_f", [128, 128], F32)
    mstrict = C.sb("mstrict", [128, 128], BF16)
    negtri = C.sb("negtri", [128, 128], BF16)
    negones = C.sb("negones", [128, 128], BF16)
    spsum = [C.sb("spsum%d" % i, [128, 512], BF16) for i in range(2)]
    ebuf = [C.sb("ebuf%d" % i, [128, 512], F32) for i in range(2)]
    spb = [C.sb("spb%d" % i, [128, 512], BF16) for i in range(3)]
    wbuf = [C.sb("wbuf%d" % i, [128, 512], BF16) for i in range(3)]
    psA = [C.ps("psA%d" % i, [128, 512]) for i in range(2)]
    psB = [C.ps("psB%d" % i, [128, 512]) for i in range(2)]
    psO = [C.ps("psO%d" % i, [128, 512]) for i in range(1 if merged else 2)]
    if merged:
        psO = [psO[0], psO[0]]
    dq, dk, dv, dconst = [Dep() for _ in range(4)]
    dspsum = [Dep(), Dep()]
    do_ = [Dep(), Dep()]
    debuf = [Dep(), Dep()]
    dspb = [Dep() for _ in range(3)]
    dwbuf = [Dep() for _ in range(3)]
    dpsA = [Dep(), Dep()]
    dpsB = [Dep(), Dep()]
    dpsO = [Dep(), Dep()]
    if merged:
        dpsO = [dpsO[0], dpsO[0]]

    def load_pair(p):
        pb = p % NB
        P.dma("sp", q_sb[:, pb, :], qT[p], writes=[dq])
        P.dma("sp", k_sb[:, pb, :], kT[p], writes=[dk])
        P.dma("sp", v_sb[:, pb], vv[p].rearrange("(b s) d -> s b d", s=128), writes=[dv])
    for p in range(NB):
        load_pair(p)
    P.op("pool", lambda e: e.memset(ones_f[:], 1.0), writes=[dconst])
    P.op("pool", lambda e: e.affine_select(out=mstrict[:], in_=ones_f[:], pattern=[[1, 128]],
                                           compare_op=ALU.is_gt, fill=0.0, base=0, channel_multiplier=-1),
         reads=[dconst], writes=[dconst])
    P.op("pool", lambda e: e.memset(ones_f[:], -1.0), reads=[dconst], writes=[dconst])
    P.op("pool", lambda e: e.affine_select(out=negtri[:], in_=ones_f[:], pattern=[[-1, 128]],
                                           compare_op=ALU.is_ge, fill=0.0, base=0, channel_multiplier=1),
         reads=[dconst], writes=[dconst])
    P.op("pool", lambda e: e.tensor_copy(out=negones[:], in_=ones_f[:]), reads=[dconst], writes=[dconst])

    steps = []
    for p in range(NPAIR):
        for g in range(8):
            for sb in range(4 * g + 3, -1, -1):
                steps.append((p, g, sb))
    n = len(steps)

    def info(i):
        p, g, sb = steps[i]
        tl = max(0, sb - 4 * g) * 128
        return p, g, sb, tl, (sb >= 4 * g)

    def stage1(i):
        p, g, sb, tl, diag = info(i)
        a, s3 = i % 2, i % 3
        if merged and p > 0 and g == 0 and sb == 3:
            load_pair(p)
        P.op("pe", lambda e: e.matmul(psA[a][:, tl:512], k_sb[:, p % NB, sb * 128:(sb + 1) * 128],
                                      q_sb[:, p % NB, g * 512 + tl:(g + 1) * 512], start=True, stop=True),
             reads=[dq, dk], writes=[dpsA[a]])
        P.op("act", lambda e: e.activation(out=ebuf[a][:, tl:512], in_=psA[a][:, tl:512], func=AF.Exp),
             reads=[dpsA[a]], writes=[debuf[a]])
        P.op("act", lambda e: e.activation(out=spb[s3][:, tl:512], in_=ebuf[a][:, tl:512], func=AF.Ln, bias=1.0),
             reads=[debuf[a]], writes=[dspb[s3]])
        if diag:
            P.op("pool", lambda e: e.tensor_tensor(out=spb[s3][:, tl:tl + 128], in0=spb[s3][:, tl:tl + 128],
                                                   in1=mstrict[:], op=ALU.mult),
                 reads=[dspb[s3], dconst], writes=[dspb[s3]])

    def stage2(i):
        p, g, sb, tl, diag = info(i)
        a, s3 = i % 2, i % 3
        c0, c1 = i % 2, (i + 1) % 2
        if sb == 4 * g + 3:
            P.op("pool", lambda e: e.memset(spsum[c0][:], 0.0), writes=[dspsum[c0]])
            P.op("pool", lambda e: e.memset(spsum[c1][:], 0.0), writes=[dspsum[c1]])
        P.op("pe", lambda e: e.matmul(psB[a][:, tl:512], k_sb[:, p % NB, sb * 128:(sb + 1) * 128],
                                      q_sb[:, p % NB, g * 512 + tl:(g + 1) * 512], start=True, stop=False),
             reads=[dq, dk], writes=[dpsB[a]])
        P.op("pe", lambda e: e.matmul(psB[a][:, tl:512], negtri[:], spb[s3][:, tl:512], start=False, stop=False),
             reads=[dspb[s3], dconst], writes=[dpsB[a]])
        P.op("pe", lambda e: e.matmul(psB[a][:, tl:512], negones[:], spsum[c0][:, tl:512], start=False, stop=True),
             reads=[dspsum[c0], dconst], writes=[dpsB[a]])
        P.op("pool", lambda e: e.tensor_tensor(out=spsum[c1][:, tl:512], in0=spsum[c0][:, tl:512],
                                               in1=spb[s3][:, tl:512], op=ALU.add),
             reads=[dspsum[c0], dspb[s3]], writes=[dspsum[c1]])
        P.op("act", lambda e: e.activation(out=wbuf[s3][:, tl:512], in_=psB[a][:, tl:512], func=AF.Exp),
             reads=[dpsB[a]], writes=[dwbuf[s3]])
        if diag:
            P.op("pool", lambda e: e.tensor_tensor(out=wbuf[s3][:, tl:tl + 128], in0=wbuf[s3][:, tl:tl + 128],
                                                   in1=mstrict[:], op=ALU.mult),
                 reads=[dwbuf[s3], dconst], writes=[dwbuf[s3]])

    def stage3(i):
        p, g, sb, tl, diag = info(i)
        s3 = i % 3
        o = (p * 8 + g) % 2
        for tb in range(tl // 128, 4):
            P.op("pe", lambda e, tb=tb: e.matmul(psO[o][:, tb * 128:(tb + 1) * 128], v_sb[:, p % NB, sb, :],
                                                 wbuf[s3][:, tb * 128:(tb + 1) * 128],
                                                 start=(sb == 4 * g + 3 and tb == 3), stop=(sb == 0),
                                                 skip_group_check=True),
                 reads=[dv, dwbuf[s3]], writes=[dpsO[o]])
        if sb == 0:
            if merged:
                P.op("act", lambda e: e.activation(out=o_st[o][:], in_=psO[o][:], func=AF.Copy), reads=[dpsO[o]],
                     writes=[do_[o]])
            else:
                P.op("dve", lambda e: e.tensor_copy(out=o_st[o][:], in_=psO[o][:]), reads=[dpsO[o]], writes=[do_[o]])
            P.dma("sp", yc[p][:, g * 512:(g + 1) * 512], o_st[o][:], reads=[do_[o]])

    for it in range(n + 2):
        if it < n:
            stage1(it)
        if 0 <= it - 1 < n:
            stage2(it - 1)
        if 0 <= it - 2 < n:
            stage3(it - 2)


def build_B1():
    nc = bass.Bass("TRN2", target_bir_lowering=False)
    with ExitStack() as st:
        st.enter_context(nc.allow_low_precision("bf16 matmul operands, fp32 accumulation"))
        C = Ctx(nc, st)
        P = Prog(nc, st)
        emit_B1(nc, C, P)
        P.finish()
    return nc


I32 = mybir.dt.int32
PI = 3.14159265358979
TWO_PI = 2.0 * PI
NT = 2 * SEQ


def emit_B2(nc, C, P, merged=False, debug=False):
    uT = C.din("uT", [128, NT], BF16)
    lam_re = C.din("lam_re", [128, 4])
    lam_im = C.din("lam_im", [128, 4])
    log_dt = C.din("log_dt", [128, 4])
    b_re = C.din("b_re", [128, 4, 16])
    b_im = C.din("b_im", [128, 4, 16])
    c_reT = C.din("c_reT", [128, 4, 16])
    c_imT = C.din("c_imT", [128, 4, 16])
    dvec = C.din("dvec", [128, 1])
    yp = C.dout("yp", [128, NT], BF16)
    f4 = lambda n: C.sb(n, [128, 4], F32)
    u_sb = C.sb("u_sb", [128, NT], BF16)
    y_sb = C.sb("y_sb", [128, NT], BF16)
    lre, lim, ldt, dtt, rho, th, mag, cth, sth, abre, abim, nre, den, kre, kim, q0, q1, q2 = [
        f4("s4_%d" % i) for i in range(18)]
    m127, a7re, a7im = f4("m127"), f4("a7re"), f4("a7im")
    Adre = C.sb("Adre", [128, 4, 5], F32)
    Adim = C.sb("Adim", [128, 4, 5], F32)
    Adimn = C.sb("Adimn", [128, 4, 5], F32)
    abimn = f4("abimn")
    s127n = f4("s127n")
    bre = C.sb("bre", [128, 4, 16], F32)
    bim = C.sb("bim", [128, 4, 16], F32)
    cre = C.sb("cre", [128, 4, 16], F32)
    cim = C.sb("cim", [128, 4, 16], F32)
    bbre = C.sb("bbre", [128, 4, 16], F32)
    bbim = C.sb("bbim", [128, 4, 16], F32)
    bt0 = C.sb("bt0", [128, 16], F32)
    d_sb = C.sb("d_sb", [128, 1], F32)
    ident = C.sb("ident", [128, 128], F32)
    pad = C.sb("pad", [128, 128], F32)
    BBT = [C.sb("BBT%d" % i, [128, 4, 128], BF16) for i in range(2)]
    Cp = [C.sb("Cp%d" % i, [128, 4, 128], BF16) for i in range(2)]
    jidx = C.sb("jidx", [128, 512], F32)
    ang = C.sb("ang", [128, 512], F32)
    kf = C.sb("kf", [128, 512], F32)
    ki = C.sb("ki", [128, 512], I32)
    cs = [C.sb("cs%d" % i, [128, 4, 512], F32) for i in range(2)]
    m0 = C.sb("m0", [128, SEQ], BF16)
    rfull = C.sb("rfull", [128, SEQ], F32)
    w = [C.sb("w%d" % i, [128, SEQ], F32) for i in range(2)]
    z = [C.sb("z%d" % i, [128, SEQ], F32) for i in range(2)]
    bu = [C.sb("bu%d" % i, [128, 512], F32) for i in range(2)]
    tt = [C.sb("tt%d" % i, [128, 512], F32) for i in range(4)]
    sbf = [C.sb("sbf%d" % i, [128, 512], BF16) for i in range(2)]
    cst = [C.sb("cst%d" % i, [128, 32], F32) for i in range(10)]
    psb = [C.ps("psb%d" % i, [128, 512]) for i in range(2)]
    if merged:
        psy0 = C.ps("psy0", [128, 512])
        psy = [psy0, psy0]
        pst = psy0
    else:
        psy = [C.ps("psy%d" % i, [128, 512]) for i in range(2)]
        pst = C.ps("pst", [128, 128])

    du, dy, dpre, dtab, dm0, drf = [Dep() for _ in range(6)]
    dw = [Dep(), Dep()]
    dz = [Dep(), Dep()]
    dbu = [Dep(), Dep()]
    dtq = [Dep() for _ in range(4)]
    dsbf = [Dep(), Dep()]
    dcst = Dep()
    dpsb = [Dep(), Dep()]
    dpsy = [Dep(), Dep()]
    dpst = Dep()
    if merged:
        dpsy = [dpsy[0], dpsy[0]]
        dpst = dpsy[0]

    def V(fn, r, wr, eng="dve"):
        P.op(eng, fn, reads=r, writes=wr)

    P.dma("sp", u_sb[:, 0:SEQ], uT[:, 0:SEQ], writes=[du])
    P.dma("sp", u_sb[:, SEQ:NT], uT[:, SEQ:NT], writes=[du])
    for t_, src in ((lre, lam_re), (lim, lam_im), (ldt, log_dt), (bre, b_re), (bim, b_im), (cre, c_reT),
                    (cim, c_imT), (d_sb, dvec)):
        P.dma("sp", t_[:], src, writes=[dpre])
    pre = [dpre]

    V(lambda e: e.memset(pad[:], 1.0), [], pre, "pool")
    V(lambda e: e.affine_select(out=ident[:], in_=pad[:], pattern=[[-1, 128]], compare_op=ALU.is_equal,
                                fill=0.0, base=0, channel_multiplier=1), pre, pre, "pool")
    V(lambda e: e.iota(jidx[:], pattern=[[0, 4], [1, 128]], base=0, channel_multiplier=0,
                       allow_small_or_imprecise_dtypes=True), [], [dtab], "pool")
    V(lambda e: e.memset(m0[:], 1.0), [], [dm0], "pool")
    V(lambda e: e.memset(m0[:].rearrange("p (c j) -> p c j", j=128)[:, :, 0:1], 0.0), [dm0], [dm0], "pool")

    def reduce_angle(x, n):
        V(lambda e: e.tensor_scalar(out=kf[:, :n], in0=x, scalar1=1.0 / TWO_PI, scalar2=None, op0=ALU.mult),
          [dtab], [dtab])
        V(lambda e: e.tensor_copy(out=ki[:, :n], in_=kf[:, :n]), [dtab], [dtab])
        V(lambda e: e.tensor_copy(out=kf[:, :n], in_=ki[:, :n]), [dtab], [dtab])
        V(lambda e: e.scalar_tensor_tensor(out=x, in0=kf[:, :n], scalar=-TWO_PI, in1=x, op0=ALU.mult,
                                           op1=ALU.add), [dtab], [dtab])
        V(lambda e: e.tensor_scalar(out=kf[:, :n], in0=x, scalar1=PI, scalar2=-TWO_PI, op0=ALU.is_gt,
                                    op1=ALU.mult), [dtab], [dtab])
        V(lambda e: e.tensor_tensor(out=x, in0=x, in1=kf[:, :n], op=ALU.add), [dtab], [dtab])
        V(lambda e: e.tensor_scalar(out=kf[:, :n], in0=x, scalar1=-PI, scalar2=TWO_PI, op0=ALU.is_lt,
                                    op1=ALU.mult), [dtab], [dtab])
        V(lambda e: e.tensor_tensor(out=x, in0=x, in1=kf[:, :n], op=ALU.add), [dtab], [dtab])

    PT = [dpre, dtab]
    P.op("act", lambda e: e.activation(out=dtt[:], in_=ldt[:], func=AF.Exp), reads=PT, writes=PT)
    V(lambda e: e.tensor_tensor(out=rho[:], in0=lre[:], in1=dtt[:], op=ALU.mult), PT, PT)
    V(lambda e: e.tensor_tensor(out=th[:], in0=lim[:], in1=dtt[:], op=ALU.mult), PT, PT)
    P.op("act", lambda e: e.activation(out=mag[:], in_=rho[:], func=AF.Exp), reads=PT, writes=PT)
    P.op("act", lambda e: e.activation(out=m127[:], in_=rho[:], func=AF.Exp, scale=127.0), reads=PT, writes=PT)
    for gp in range(4):
        for k_, shift in ((1, 0.0), (0, PI / 2)):
            V(lambda e, gp=gp, shift=shift: e.tensor_scalar(out=ang[:], in0=jidx[:], scalar1=th[:, gp:gp + 1],
                                                            scalar2=shift, op0=ALU.mult, op1=ALU.add), PT, PT)
            reduce_angle(ang[:], 512)
            P.op("act", lambda e, gp=gp, k_=k_: e.activation(out=cs[k_][:, gp, :], in_=ang[:], func=AF.Sin),
                 reads=PT, writes=PT)
    V(lambda e: e.tensor_copy(out=cth[:], in_=cs[0][:, :, 1]), PT, PT)
    V(lambda e: e.tensor_copy(out=sth[:], in_=cs[1][:, :, 1]), PT, PT)
    V(lambda e: e.tensor_tensor(out=abre[:], in0=mag[:], in1=cth[:], op=ALU.mult), PT, PT)
    V(lambda e: e.tensor_tensor(out=abim[:], in0=mag[:], in1=sth[:], op=ALU.mult), PT, PT)
    V(lambda e: e.tensor_scalar(out=abimn[:], in0=abim[:], scalar1=-1.0, scalar2=None, op0=ALU.mult), PT, PT)
    V(lambda e: e.tensor_scalar(out=s127n[:], in0=cs[1][:, :, 127], scalar1=-1.0, scalar2=None, op0=ALU.mult), PT, PT)
    V(lambda e: e.tensor_scalar(out=nre[:], in0=abre[:], scalar1=-1.0, scalar2=None, op0=ALU.add), PT, PT)
    V(lambda e: e.tensor_tensor(out=q0[:], in0=lre[:], in1=lre[:], op=ALU.mult), PT, PT)
    V(lambda e: e.tensor_tensor(out=q1[:], in0=lim[:], in1=lim[:], op=ALU.mult), PT, PT)
    V(lambda e: e.tensor_tensor(out=den[:], in0=q0[:], in1=q1[:], op=ALU.add), PT, PT)
    V(lambda e: e.reciprocal(out=den[:], in_=den[:]), PT, PT)
    V(lambda e: e.tensor_tensor(out=q0[:], in0=nre[:], in1=lre[:], op=ALU.mult), PT, PT)
    V(lambda e: e.tensor_tensor(out=q1[:], in0=abim[:], in1=lim[:], op=ALU.mult), PT, PT)
    V(lambda e: e.tensor_tensor(out=q0[:], in0=q0[:], in1=q1[:], op=ALU.add), PT, PT)
    V(lambda e: e.tensor_tensor(out=kre[:], in0=q0[:], in1=den[:], op=ALU.mult), PT, PT)
    V(lambda e: e.tensor_tensor(out=q0[:], in0=abim[:], in1=lre[:], op=ALU.mult), PT, PT)
    V(lambda e: e.tensor_tensor(out=q1[:], in0=nre[:], in1=lim[:], op=ALU.mult), PT, PT)
    V(lambda e: e.tensor_tensor(out=q0[:], in0=q0[:], in1=q1[:], op=ALU.subtract), PT, PT)
    V(lambda e: e.tensor_tensor(out=kim[:], in0=q0[:], in1=den[:], op=ALU.mult), PT, PT)
    for gp in range(4):
        g1 = slice(gp, gp + 1)
        V(lambda e, gp=gp, g1=g1: e.tensor_scalar(out=bt0[:], in0=bim[:, gp, :], scalar1=kim[:, g1], scalar2=None,
                                                  op0=ALU.mult), PT, PT)
        V(lambda e, gp=gp, g1=g1: e.scalar_tensor_tensor(out=bbre[:, gp, :], in0=bre[:, gp, :], scalar=kre[:, g1],
                                                         in1=bt0[:], op0=ALU.mult, op1=ALU.subtract), PT, PT)
        V(lambda e, gp=gp, g1=g1: e.tensor_scalar(out=bt0[:], in0=bre[:, gp, :], scalar1=kim[:, g1], scalar2=None,
                                                  op0=ALU.mult), PT, PT)
        V(lambda e, gp=gp, g1=g1: e.scalar_tensor_tensor(out=bbim[:, gp, :], in0=bim[:, gp, :], scalar=kre[:, g1],
                                                         in1=bt0[:], op0=ALU.mult, op1=ALU.add), PT, PT)
    V(lambda e: e.tensor_tensor(out=a7re[:], in0=m127[:], in1=cs[0][:, :, 127], op=ALU.mult), PT, PT)
    V(lambda e: e.tensor_tensor(out=a7im[:], in0=m127[:], in1=cs[1][:, :, 127], op=ALU.mult), PT, PT)

    def cmul(ore, oim, are, aim, bre_, bim_):
        V(lambda e: e.tensor_tensor(out=q0[:], in0=are, in1=bre_, op=ALU.mult), PT, PT)
        V(lambda e: e.tensor_tensor(out=q1[:], in0=aim, in1=bim_, op=ALU.mult), PT, PT)
        V(lambda e: e.tensor_tensor(out=q2[:], in0=are, in1=bim_, op=ALU.mult), PT, PT)
        V(lambda e: e.tensor_tensor(out=ore, in0=q0[:], in1=q1[:], op=ALU.subtract), PT, PT)
        V(lambda e: e.tensor_tensor(out=q0[:], in0=aim, in1=bre_, op=ALU.mult), PT, PT)
        V(lambda e: e.tensor_tensor(out=oim, in0=q2[:], in1=q0[:], op=ALU.add), PT, PT)
    cmul(Adre[:, :, 0], Adim[:, :, 0], a7re[:], a7im[:], abre[:], abim[:])
    for k in range(1, 5):
        cmul(Adre[:, :, k], Adim[:, :, k], Adre[:, :, k - 1], Adim[:, :, k - 1], Adre[:, :, k - 1], Adim[:, :, k - 1])
    V(lambda e: e.tensor_scalar(out=Adimn[:], in0=Adim[:], scalar1=-1.0, scalar2=None, op0=ALU.mult), PT, PT)
    for gp in range(4):
        for ri, src in ((0, bbre), (1, bbim)):
            V(lambda e: e.memset(pad[:], 0.0), PT, PT)
            for j in range(2):
                c0 = 16 * (2 * gp + j)
                V(lambda e, j=j, c0=c0, src=src, gp=gp: e.tensor_copy(out=pad[64 * j:64 * j + 64, c0:c0 + 16],
                                                                    in_=src[64 * j:64 * j + 64, gp, :]), PT, PT)
            P.op("pe", lambda e: e.transpose(pst[:, 0:128], pad[:], ident[:]), reads=PT, writes=[dpst])
            V(lambda e, ri=ri, gp=gp: e.tensor_copy(out=BBT[ri][:, gp, :], in_=pst[:, 0:128]), [dpst] + PT, PT)
        for ri, src, sgn in ((0, cre, 1.0), (1, cim, -1.0)):
            V(lambda e, ri=ri, gp=gp: e.memset(Cp[ri][:, gp, :], 0.0), PT, PT)
            for j in range(2):
                c0 = 16 * (2 * gp + j)
                V(lambda e, j=j, c0=c0, src=src, gp=gp, ri=ri, sgn=sgn: e.tensor_scalar(
                    out=Cp[ri][64 * j:64 * j + 64, gp, c0:c0 + 16], in0=src[64 * j:64 * j + 64, gp, :],
                    scalar1=sgn, scalar2=None, op0=ALU.mult), PT, PT)

    if debug:
        for nm, t_, shp in (("th", th, [128, 4]), ("mag", mag, [128, 4]), ("kre", kre, [128, 4]), ("kim", kim, [128, 4]),
                            ("cos", cs[0], [128, 4, 512]), ("sin", cs[1], [128, 4, 512]), ("Adre", Adre, [128, 4, 5]),
                            ("Adim", Adim, [128, 4, 5]), ("bbre", bbre, [128, 4, 16]), ("jidx", jidx, [128, 512])):
            P.dma("sp", C.dout("dbg_" + nm, shp), t_[:], reads=PT)
        P.dma("sp", C.dout("dbg_BBT0", [128, 4, 128], BF16), BBT[0][:], reads=PT)
        P.dma("sp", C.dout("dbg_Cp1", [128, 4, 128], BF16), Cp[1][:], reads=PT)
    def mod_tile(gp, tok0, t8):
        sl = slice(t8 * 512, (t8 + 1) * 512)
        for ri in range(2):
            P.op("pe", lambda e, ri=ri: e.matmul(psb[ri][:], BBT[ri][:, gp, :],
                                                 u_sb[:, tok0 + sl.start:tok0 + sl.stop], start=True, stop=True),
                 reads=[du] + PT, writes=[dpsb[ri]])
            P.op("act", lambda e, ri=ri: e.activation(out=bu[ri][:], in_=psb[ri][:], func=AF.Copy),
                 reads=[dpsb[ri]], writes=[dbu[ri]])
        V(lambda e: e.tensor_tensor(out=tt[0][:], in0=bu[0][:], in1=cs[0][:, gp, :], op=ALU.mult),
          [dbu[0]] + PT, [dtq[0]])
        V(lambda e: e.tensor_tensor(out=tt[1][:], in0=bu[1][:], in1=cs[1][:, gp, :], op=ALU.mult),
          [dbu[1]] + PT, [dtq[1]])
        V(lambda e: e.tensor_tensor(out=w[0][:, sl], in0=tt[0][:], in1=tt[1][:], op=ALU.add),
          [dtq[0], dtq[1]], [dw[0]])
        V(lambda e: e.tensor_tensor(out=tt[2][:], in0=bu[1][:], in1=cs[0][:, gp, :], op=ALU.mult),
          [dbu[1]] + PT, [dtq[2]], "pool")
        V(lambda e: e.tensor_tensor(out=tt[3][:], in0=bu[0][:], in1=cs[1][:, gp, :], op=ALU.mult),
          [dbu[0]] + PT, [dtq[3]], "pool")
        V(lambda e: e.tensor_tensor(out=w[1][:, sl], in0=tt[2][:], in1=tt[3][:], op=ALU.subtract),
          [dtq[2], dtq[3]], [dw[1]], "pool")

    def scans(extra):
        for i in range(2):
            V(lambda e, i=i: e.tensor_tensor_scan(out=z[i][:], data0=rfull[:], data1=w[i][:], initial=0.0,
                                                  op0=ALU.mult, op1=ALU.add), [drf, dw[i]] + extra, [dz[i]])

    def hs_step(gp, k, cur):
        d_ = 1 << k
        nxt = 2 - cur
        ar = Adre[:, gp, k:k + 1]
        ai = Adim[:, gp, k:k + 1]
        ain = Adimn[:, gp, k:k + 1]
        sre, sim = cst[cur], cst[cur + 1]
        nre_, nim_ = cst[nxt], cst[nxt + 1]
        CS = [dcst]
        V(lambda e: e.tensor_copy(out=nre_[:, 0:d_], in_=sre[:, 0:d_]), CS, CS)
        V(lambda e: e.tensor_copy(out=nim_[:, 0:d_], in_=sim[:, 0:d_]), CS, CS)
        V(lambda e: e.scalar_tensor_tensor(out=cst[4][:, d_:32], in0=sre[:, 0:32 - d_], scalar=ar, in1=sre[:, d_:32],
                                           op0=ALU.mult, op1=ALU.add), CS + PT, CS)
        V(lambda e: e.scalar_tensor_tensor(out=nre_[:, d_:32], in0=sim[:, 0:32 - d_], scalar=ain,
                                           in1=cst[4][:, d_:32], op0=ALU.mult, op1=ALU.add), CS + PT, CS)
        V(lambda e: e.scalar_tensor_tensor(out=cst[4][:, d_:32], in0=sim[:, 0:32 - d_], scalar=ar, in1=sim[:, d_:32],
                                           op0=ALU.mult, op1=ALU.add), CS + PT, CS)
        V(lambda e: e.scalar_tensor_tensor(out=nim_[:, d_:32], in0=sre[:, 0:32 - d_], scalar=ai,
                                           in1=cst[4][:, d_:32], op0=ALU.mult, op1=ALU.add), CS + PT, CS)
        return nxt

    def carry(gp):
        g1 = slice(gp, gp + 1)
        CS = [dcst]
        zE = [z[i][:].rearrange("p (c j) -> p c j", j=128)[:, :, 127] for i in range(2)]
        c127 = cs[0][:, gp, 127:128]
        s127 = cs[1][:, gp, 127:128]
        V(lambda e: e.tensor_scalar(out=cst[2][:], in0=zE[0], scalar1=c127, scalar2=None, op0=ALU.mult),
          [dz[0]] + PT, CS)
        V(lambda e: e.scalar_tensor_tensor(out=cst[0][:], in0=zE[1], scalar=s127n[:, g1], in1=cst[2][:],
                                           op0=ALU.mult, op1=ALU.add), [dz[1]] + PT + CS, CS)
        V(lambda e: e.tensor_scalar(out=cst[2][:], in0=zE[0], scalar1=s127, scalar2=None, op0=ALU.mult),
          [dz[0]] + PT + CS, CS)
        V(lambda e: e.scalar_tensor_tensor(out=cst[1][:], in0=zE[1], scalar=c127, in1=cst[2][:],
                                           op0=ALU.mult, op1=ALU.add), [dz[1]] + PT + CS, CS)
        cur = 0
        for k in range(5):
            cur = hs_step(gp, k, cur)
        Sre, Sim = cst[cur], cst[cur + 1]
        w0 = [w[i][:].rearrange("p (c j) -> p c j", j=128)[:, 1:32, 0] for i in range(2)]
        V(lambda e: e.scalar_tensor_tensor(out=cst[5][:, 0:31], in0=Sre[:, 0:31], scalar=abre[:, g1],
                                           in1=w0[0], op0=ALU.mult, op1=ALU.add), CS + PT + [dw[0]], CS)
        V(lambda e: e.scalar_tensor_tensor(out=w0[0], in0=Sim[:, 0:31], scalar=abimn[:, g1],
                                           in1=cst[5][:, 0:31], op0=ALU.mult, op1=ALU.add), CS + PT, [dw[0]])
        V(lambda e: e.scalar_tensor_tensor(out=cst[6][:, 0:31], in0=Sim[:, 0:31], scalar=abre[:, g1],
                                           in1=w0[1], op0=ALU.mult, op1=ALU.add), CS + PT + [dw[1]], CS)
        V(lambda e: e.scalar_tensor_tensor(out=w0[1], in0=Sre[:, 0:31], scalar=abim[:, g1],
                                           in1=cst[6][:, 0:31], op0=ALU.mult, op1=ALU.add), CS + PT, [dw[1]])

    def out_tile(gp, tok0, t8):
        sl = slice(t8 * 512, (t8 + 1) * 512)
        V(lambda e: e.tensor_tensor(out=tt[0][:], in0=z[0][:, sl], in1=cs[0][:, gp, :], op=ALU.mult),
          [dz[0]] + PT, [dtq[0]])
        V(lambda e: e.tensor_tensor(out=tt[1][:], in0=z[1][:, sl], in1=cs[1][:, gp, :], op=ALU.mult),
          [dz[1]] + PT, [dtq[1]])
        V(lambda e: e.tensor_tensor(out=sbf[0][:], in0=tt[0][:], in1=tt[1][:], op=ALU.subtract),
          [dtq[0], dtq[1]], [dsbf[0]])
        V(lambda e: e.tensor_tensor(out=tt[2][:], in0=z[0][:, sl], in1=cs[1][:, gp, :], op=ALU.mult),
          [dz[0]] + PT, [dtq[2]], "pool")
        V(lambda e: e.tensor_tensor(out=tt[3][:], in0=z[1][:, sl], in1=cs[0][:, gp, :], op=ALU.mult),
          [dz[1]] + PT, [dtq[3]], "pool")
        V(lambda e: e.tensor_tensor(out=sbf[1][:], in0=tt[2][:], in1=tt[3][:], op=ALU.add),
          [dtq[2], dtq[3]], [dsbf[1]], "pool")
        yi = t8 % 2
        P.op("pe", lambda e: e.matmul(psy[yi][:], Cp[0][:, gp, :], sbf[0][:], start=True, stop=False),
             reads=[dsbf[0]] + PT, writes=[dpsy[yi]])
        P.op("pe", lambda e: e.matmul(psy[yi][:], Cp[1][:, gp, :], sbf[1][:], start=False, stop=True),
             reads=[dsbf[1]] + PT, writes=[dpsy[yi]])
        r0 = 32 * gp
        V(lambda e: e.scalar_tensor_tensor(
            out=y_sb[r0:r0 + 32, tok0 + sl.start:tok0 + sl.stop],
            in0=u_sb[r0:r0 + 32, tok0 + sl.start:tok0 + sl.stop], scalar=d_sb[r0:r0 + 32, 0:1],
            in1=psy[yi][r0:r0 + 32, :], op0=ALU.mult, op1=ALU.add),
          [du, dpsy[yi]] + PT, [dy])

    def set_rfull(gp):
        P.op("act", lambda e: e.activation(out=rfull[:], in_=m0[:], func=AF.Copy, scale=mag[:, gp:gp + 1]),
             reads=[dm0] + PT, writes=[drf])

    for gp in range(4):
        set_rfull(gp)
        for b in range(2):
            for t8 in range(8):
                mod_tile(gp, b * SEQ, t8)
            scans([])
            carry(gp)
            scans([dcst])
            for t8 in range(8):
                out_tile(gp, b * SEQ, t8)
    if debug:
        for nm, t_, shp, dd in (("w0", w[0], [128, SEQ], dw[0]), ("w1", w[1], [128, SEQ], dw[1]),
                                ("z0", z[0], [128, SEQ], dz[0]), ("z1", z[1], [128, SEQ], dz[1]),
                                ("rfull", rfull, [128, SEQ], drf), ("bu0", bu[0], [128, 512], dbu[0]),
                                ("tt0", tt[0], [128, 512], dtq[0])):
            P.dma("sp", C.dout("dbg_" + nm, shp), t_[:], reads=[dd])
        for k in range(7):
            P.dma("sp", C.dout("dbg_cst%d" % k, [128, 32]), cst[k][:], reads=[dcst])
    P.dma("sp", yp[:, 0:SEQ], y_sb[:, 0:SEQ], reads=[dy])
    P.dma("sp", yp[:, SEQ:NT], y_sb[:, SEQ:NT], reads=[dy])


def build_B2(debug=False):
    nc = bass.Bass("TRN2", target_bir_lowering=False)
    with ExitStack() as st:
        st.enter_context(nc.allow_low_precision("bf16 matmul operands, fp32 accumulation"))
        C = Ctx(nc, st)
        P = Prog(nc, st)
        emit_B2(nc, C, P, debug=debug)
        P.finish()
    return nc


def host_B2_params(l, inp, core):
    gs = slice(8 * core, 8 * core + 8)

    def pg(a):
        return np.ascontiguousarray(a.reshape(4, 2, 64).transpose(1, 2, 0).reshape(128, 4))
    lam_re = pg(inp["s5_lambda_re"][l][gs])
    lam_im = pg(inp["s5_lambda_im"][l][gs])
    log_dt = pg(np.repeat(inp["s5_log_dt"][l][gs][:, None], 64, axis=1))

    def pb(a):
        return np.ascontiguousarray(a.reshape(4, 2, 64, 16).transpose(1, 2, 0, 3).reshape(128, 4, 16))
    b_re = pb(inp["s5_b_re"][l][gs])
    b_im = pb(inp["s5_b_im"][l][gs])
    c_reT = pb(np.transpose(inp["s5_c_re"][l][gs], (0, 2, 1)))
    c_imT = pb(np.transpose(inp["s5_c_im"][l][gs], (0, 2, 1)))
    dvec = np.ascontiguousarray(inp["s5_d"][l][128 * core:128 * core + 128].reshape(128, 1))
    return {"lam_re": lam_re, "lam_im": lam_im, "log_dt": log_dt, "b_re": b_re, "b_im": b_im,
            "c_reT": c_reT, "c_imT": c_imT, "dvec": dvec}


_NC_CACHE = {}


def _get(name, fn):
    if name not in _NC_CACHE:
        _NC_CACHE[name] = fn()
    return _NC_CACHE[name]


def _run(nc, maps):
    res = run_bass_kernel_spmd(nc, maps, core_ids=list(range(NCORES)))
    return res.results


def kernel(**inp):
    inp = {k: np.asarray(v) for k, v in inp.items()}
    x_tok = np.ascontiguousarray(inp["x"].reshape(2 * SEQ, D).astype(np.float32))
    depth = inp["w_in"].shape[0]
    for l in range(depth):
        last = (l == depth - 1)
        rA = _run(_get("A", build_A), host_A(x_tok, l, inp))
        yaT = [np.asarray(r["yaT"]) for r in rA]
        gT = [np.asarray(r["gT"]) for r in rA]
        s5T = [np.asarray(r["s5T"]) for r in rA]
        qT = [np.asarray(r["qT"]) for r in rA]
        kT = [np.asarray(r["kT"]) for r in rA]
        vtk = [np.asarray(r["vtk"]) for r in rA]
        mB1, mB2 = [], []
        for c in range(NCORES):
            qs, ks, vs = [], [], []
            for pl in range(NPAIR):
                b, h = divmod(c * NPAIR + pl, 8)
                hs = slice(128 * h, 128 * h + 128)
                qs.append(np.concatenate([qT[4 * b + i][hs, :] for i in range(4)], axis=1))
                ks.append(np.concatenate([kT[4 * b + i][hs, :] for i in range(4)], axis=1))
                vs.append(np.concatenate([vtk[4 * b + i][:, hs] for i in range(4)], axis=0))
            mB1.append({"qT": np.ascontiguousarray(np.stack(qs)), "kT": np.ascontiguousarray(np.stack(ks)),
                        "v": np.ascontiguousarray(np.stack(vs))})
            m2 = host_B2_params(l, inp, c)
            m2["uT"] = np.ascontiguousarray(np.concatenate([s5T[i][128 * c:128 * c + 128, :] for i in range(8)], axis=1))
            mB2.append(m2)
        mB = [dict(mB1[c], **mB2[c]) for c in range(NCORES)]
        rB = _run(_get("B", build_B), mB)
        yc = [np.asarray(r["yc"]) for r in rB]
        yp = [np.asarray(r["yp"]) for r in rB]
        ycT, ybpT = [], []
        for tc in range(NCORES):
            b, i = divmod(tc, 4)
            rows = []
            for h in range(8):
                c, pl = divmod(b * 8 + h, NPAIR)
                rows.append(yc[c][pl][:, i * TOK:(i + 1) * TOK])
            ycT.append(np.ascontiguousarray(np.concatenate(rows, axis=0)))
            ybpT.append(np.ascontiguousarray(np.concatenate([yp[c][:, tc * TOK:(tc + 1) * TOK] for c in range(8)], axis=0)))
        ncC = _get("C%d" % int(last), lambda: build_C(last))
        rC = _run(ncC, host_C(x_tok, l, inp, yaT, ybpT, ycT, gT, last))
        x_tok = np.ascontiguousarray(np.concatenate([np.asarray(r["xo"]).T for r in rC], axis=0).astype(np.float32))
    return x_tok.reshape(2, SEQ, D)


B_SPEED = [1.0, 1.0]


def build_B():
    nc = bass.Bass("TRN2", target_bir_lowering=False)
    with ExitStack() as st:
        st.enter_context(nc.allow_low_precision("bf16 matmul operands, fp32 accumulation"))
        C = Ctx(nc, st)
        P = Prog(nc, st)
        P.begin("b2")
        emit_B2F(nc, C, P, merged=True)
        P.end()
        P.begin("b1")
        emit_B1(nc, C, P, merged=True)
        P.end()
        P.replay(["b2", "b1"], speed=B_SPEED)
        P.finish()
    return nc


TS = 8
NM = SEQ // TS
J2 = 64
NC2 = NM // J2


def emit_B2F(nc, C, P, merged=False):
    uT = C.din("uT", [128, NT], BF16)
    lam_re = C.din("lam_re", [128, 4])
    lam_im = C.din("lam_im", [128, 4])
    log_dt = C.din("log_dt", [128, 4])
    b_re = C.din("b_re", [128, 4, 16])
    b_im = C.din("b_im", [128, 4, 16])
    c_reT = C.din("c_reT", [128, 4, 16])
    c_imT = C.din("c_imT", [128, 4, 16])
    dvec = C.din("dvec", [128, 1])
    yp = C.dout("yp", [128, NT], BF16)
    f4 = lambda n: C.sb(n, [128, 4], F32)
    u_sb = C.sb("u_sb", [128, NT], BF16)
    y_sb = u_sb
    lre, lim, ldt, dtt, rho, th, th8, nre, den, kre, kim, q0, q1, q2 = [f4("s4_%d" % i) for i in range(14)]
    kidx = C.sb("kidx", [128, 16], F32)
    csk = [C.sb("csk%d" % i, [128, 4, 16], F32) for i in range(2)]
    magk = C.sb("magk", [128, 4, 16], F32)
    PW = [C.sb("PW%d" % i, [128, 4, 16], F32) for i in range(2)]
    PWn = [C.sb("PWn%d" % i, [128, 4, 16], F32) for i in range(2)]
    bre = C.sb("bre", [128, 4, 16], F32)
    bim = C.sb("bim", [128, 4, 16], F32)
    cre = C.sb("cre", [128, 4, 16], F32)
    cim = C.sb("cim", [128, 4, 16], F32)
    bbre = C.sb("bbre", [128, 4, 16], F32)
    bbim = C.sb("bbim", [128, 4, 16], F32)
    bt0 = C.sb("bt0", [128, 16], F32)
    d_sb = C.sb("d_sb", [128, 1], F32)
    ident = C.sb("ident", [128, 128], F32)
    R = C.sb("R", [128, 8192], F32)
    pads = [R[:, 4096 * i:4096 * (i + 1)].rearrange("p (t g c) -> p t g c", t=TS, g=4) for i in range(2)]
    Cpf = [C.sb("Cpf%d" % i, [128, 4, 128], F32) for i in range(2)]
    WE = [C.sb("WE%d" % i, [128, TS, 4, 128], BF16) for i in range(2)]
    WC = [C.sb("WC%d" % i, [128, TS, 4, 128], BF16) for i in range(2)]
    Kmat = C.sb("Kmat", [128, TS, 128], BF16)
    jidx = C.sb("jidx", [128, 512], F32)
    ang = R[:, 0:2048]
    kf = R[:, 2048:4096]
    ki = R[:, 4096:6144].bitcast(I32)
    AB = [C.sb("AB%d" % i, [128, TS, 4, 16], F32) for i in range(6)]
    cs2 = [C.sb("cs2_%d" % i, [128, 4, NM], F32) for i in range(2)]
    m02 = C.sb("m02", [128, NM], BF16)
    rf2 = C.sb("rf2", [128, 4, NM], F32)
    Ad = [C.sb("Ad%d" % i, [128, 4, 4], F32) for i in range(3)]
    sq_re, sq_im = f4("sq_re"), f4("sq_im")
    a8imn, s63n = f4("a8imn"), f4("s63n")
    bu = [R[:, NM * i:NM * (i + 1)] for i in range(2)]
    tt = [R[:, NM * (2 + i):NM * (3 + i)] for i in range(4)]
    w2 = [R[:, NM * (6 + i):NM * (7 + i)] for i in range(2)]
    z2 = [R[:, NM * (8 + i):NM * (9 + i)] for i in range(2)]
    cst = [C.sb("cst%d" % i, [128, NC2], F32) for i in range(7)]
    Sprev2 = [C.sb("Sprev%d" % i, [128, 4, 2, NM], BF16) for i in range(2)]
    psb = [C.ps("psb%d" % i, [128, 512]) for i in range(2)]
    psy = C.ps("psy", [128, 512])
    pst = psy

    du, dpre, dtab = [Dep() for _ in range(3)]
    dS2 = [Dep(), Dep()]
    dy = du
    dbu = [Dep(), Dep()]
    dtq = [Dep() for _ in range(4)]
    dw = [Dep(), Dep()]
    dz = [Dep(), Dep()]
    dcst = Dep()
    dpsb = [Dep(), Dep()]
    dpsy = Dep()
    PT = [dpre, dtab]

    ENG2 = "dve" if merged else "pool"

    def V(fn, r, wr, eng="dve"):
        P.op(eng, fn, reads=r, writes=wr)

    def A(fn, r, wr):
        P.op("act", fn, reads=r, writes=wr)

    P.dma("sp", u_sb[:, 0:SEQ], uT[:, 0:SEQ], writes=[du])
    P.dma("sp", u_sb[:, SEQ:NT], uT[:, SEQ:NT], writes=[du])
    for t_, src in ((lre, lam_re), (lim, lam_im), (ldt, log_dt), (bre, b_re), (bim, b_im), (cre, c_reT),
                    (cim, c_imT), (d_sb, dvec)):
        P.dma("sp", t_[:], src, writes=[dpre])
    V(lambda e: e.memset(ang[:, 0:128], 1.0), [], PT, "pool")
    V(lambda e: e.affine_select(out=ident[:], in_=ang[:, 0:128], pattern=[[-1, 128]], compare_op=ALU.is_equal,
                                fill=0.0, base=0, channel_multiplier=1), PT, PT, "pool")
    V(lambda e: e.iota(jidx[:], pattern=[[0, NC2], [1, J2]], base=0, channel_multiplier=0,
                       allow_small_or_imprecise_dtypes=True), [], PT, "pool")
    V(lambda e: e.iota(kidx[:], pattern=[[1, 16]], base=0, channel_multiplier=0,
                       allow_small_or_imprecise_dtypes=True), [], PT, "pool")
    V(lambda e: e.memset(m02[:], 1.0), [], PT, "pool")
    V(lambda e: e.memset(m02[:].rearrange("p (c j) -> p c j", j=J2)[:, :, 0:1], 0.0), PT, PT, "pool")
    for ri in range(2):
        V(lambda e, ri=ri: e.memset(Cpf[ri][:], 0.0), [], PT, "pool")
        V(lambda e, ri=ri: e.memset(WC[ri][:], 0.0), [], PT, "pool")
    for i_ in range(2):
        V(lambda e, i_=i_: e.memset(Sprev2[i_][:, :, :, 0:1], 0.0), [], [dS2[i_]], "pool")

    def reduce_angle(x, n):
        V(lambda e: e.tensor_scalar(out=kf[:, :n], in0=x, scalar1=1.0 / TWO_PI, scalar2=None, op0=ALU.mult), PT, PT)
        V(lambda e: e.tensor_copy(out=ki[:, :n], in_=kf[:, :n]), PT, PT)
        V(lambda e: e.tensor_copy(out=kf[:, :n], in_=ki[:, :n]), PT, PT)
        V(lambda e: e.scalar_tensor_tensor(out=x, in0=kf[:, :n], scalar=-TWO_PI, in1=x, op0=ALU.mult, op1=ALU.add),
          PT, PT)
        V(lambda e: e.tensor_scalar(out=kf[:, :n], in0=x, scalar1=PI, scalar2=-TWO_PI, op0=ALU.is_gt, op1=ALU.mult),
          PT, PT)
        V(lambda e: e.tensor_tensor(out=x, in0=x, in1=kf[:, :n], op=ALU.add), PT, PT)
        V(lambda e: e.tensor_scalar(out=kf[:, :n], in0=x, scalar1=-PI, scalar2=TWO_PI, op0=ALU.is_lt, op1=ALU.mult),
          PT, PT)
        V(lambda e: e.tensor_tensor(out=x, in0=x, in1=kf[:, :n], op=ALU.add), PT, PT)

    def sincos_table(dst, idx_ap, n, thv):
        N4 = 4 * n
        a3 = ang[:, :N4].rearrange("p (g n) -> p g n", g=4)
        idx_b = idx_ap.unsqueeze(1).to_broadcast([128, 4, n])
        th_b = thv[:].unsqueeze(2).to_broadcast([128, 4, n])
        for k_, shift in ((1, 0.0), (0, PI / 2)):
            V(lambda e: e.tensor_tensor(out=a3, in0=idx_b, in1=th_b, op=ALU.mult), PT, PT)
            if shift:
                V(lambda e, shift=shift: e.tensor_scalar(out=ang[:, :N4], in0=ang[:, :N4], scalar1=shift, scalar2=None,
                                                         op0=ALU.add), PT, PT)
            reduce_angle(ang[:, :N4], N4)
            A(lambda e, k_=k_: e.activation(out=dst[k_][:, :, :n], in_=a3, func=AF.Sin), PT, PT)

    A(lambda e: e.activation(out=dtt[:], in_=ldt[:], func=AF.Exp), PT, PT)
    V(lambda e: e.tensor_tensor(out=rho[:], in0=lre[:], in1=dtt[:], op=ALU.mult), PT, PT)
    V(lambda e: e.tensor_tensor(out=th[:], in0=lim[:], in1=dtt[:], op=ALU.mult), PT, PT)
    V(lambda e: e.tensor_scalar(out=th8[:], in0=th[:], scalar1=float(TS), scalar2=None, op0=ALU.mult), PT, PT)
    sincos_table(csk, kidx[:], 16, th)
    for gp in range(4):
        (lambda gp: A(lambda e: e.activation(out=magk[:, gp, :], in_=kidx[:], func=AF.Exp, scale=rho[:, gp:gp + 1]),
                      PT, PT))(gp)
    for ri in range(2):
        V(lambda e, ri=ri: e.tensor_tensor(out=PW[ri][:], in0=magk[:], in1=csk[ri][:], op=ALU.mult), PT, PT)
        V(lambda e, ri=ri: e.tensor_scalar(out=PWn[ri][:], in0=PW[ri][:], scalar1=-1.0, scalar2=None, op0=ALU.mult),
          PT, PT)
    abre, abim = PW[0][:, :, 1], PW[1][:, :, 1]
    V(lambda e: e.tensor_scalar(out=nre[:], in0=abre, scalar1=-1.0, scalar2=None, op0=ALU.add), PT, PT)
    V(lambda e: e.tensor_tensor(out=q0[:], in0=lre[:], in1=lre[:], op=ALU.mult), PT, PT)
    V(lambda e: e.tensor_tensor(out=q1[:], in0=lim[:], in1=lim[:], op=ALU.mult), PT, PT)
    V(lambda e: e.tensor_tensor(out=den[:], in0=q0[:], in1=q1[:], op=ALU.add), PT, PT)
    V(lambda e: e.reciprocal(out=den[:], in_=den[:]), PT, PT)
    V(lambda e: e.tensor_tensor(out=q0[:], in0=nre[:], in1=lre[:], op=ALU.mult), PT, PT)
    V(lambda e: e.tensor_tensor(out=q1[:], in0=abim, in1=lim[:], op=ALU.mult), PT, PT)
    V(lambda e: e.tensor_tensor(out=q0[:], in0=q0[:], in1=q1[:], op=ALU.add), PT, PT)
    V(lambda e: e.tensor_tensor(out=kre[:], in0=q0[:], in1=den[:], op=ALU.mult), PT, PT)
    V(lambda e: e.tensor_tensor(out=q0[:], in0=abim, in1=lre[:], op=ALU.mult), PT, PT)
    V(lambda e: e.tensor_tensor(out=q1[:], in0=nre[:], in1=lim[:], op=ALU.mult), PT, PT)
    V(lambda e: e.tensor_tensor(out=q0[:], in0=q0[:], in1=q1[:], op=ALU.subtract), PT, PT)
    V(lambda e: e.tensor_tensor(out=kim[:], in0=q0[:], in1=den[:], op=ALU.mult), PT, PT)

    def bbar(gp):
        g1 = slice(gp, gp + 1)
        V(lambda e: e.tensor_scalar(out=bt0[:], in0=bim[:, gp, :], scalar1=kim[:, g1], scalar2=None, op0=ALU.mult), PT, PT)
        V(lambda e: e.scalar_tensor_tensor(out=bbre[:, gp, :], in0=bre[:, gp, :], scalar=kre[:, g1], in1=bt0[:],
                                           op0=ALU.mult, op1=ALU.subtract), PT, PT)
        V(lambda e: e.tensor_scalar(out=bt0[:], in0=bre[:, gp, :], scalar1=kim[:, g1], scalar2=None, op0=ALU.mult), PT, PT)
        V(lambda e: e.scalar_tensor_tensor(out=bbim[:, gp, :], in0=bim[:, gp, :], scalar=kre[:, g1], in1=bt0[:],
                                           op0=ALU.mult, op1=ALU.add), PT, PT)
    for gp in range(4):
        bbar(gp)

    sincos_table(cs2, jidx[:], NM, th8)
    for gp in range(4):
        (lambda gp: A(lambda e: e.activation(out=rf2[:, gp, :], in_=m02[:], func=AF.Copy, scale=magk[:, gp, 8:9]),
                      PT, PT))(gp)
    V(lambda e: e.tensor_scalar(out=a8imn[:], in0=PW[1][:, :, 8], scalar1=-1.0, scalar2=None, op0=ALU.mult), PT, PT)
    V(lambda e: e.tensor_scalar(out=s63n[:], in0=cs2[1][:, :, J2 - 1], scalar1=-1.0, scalar2=None, op0=ALU.mult), PT, PT)

    for ri in range(2):
        V(lambda e, ri=ri: e.memset(pads[ri], 0.0), PT, PT, "pool")

    def cpow_scale(o_re, o_im, xr, xi, k0_, neg_im):
        shp = [128, TS, 4, 16]
        pw = [PW[i][:, :, k0_:k0_ + TS].rearrange("p g t -> p t g").unsqueeze(3).to_broadcast(shp) for i in range(2)]
        pn = [PWn[i][:, :, k0_:k0_ + TS].rearrange("p g t -> p t g").unsqueeze(3).to_broadcast(shp) for i in range(2)]
        xr_b = xr.unsqueeze(1).to_broadcast(shp)
        xi_b = xi.unsqueeze(1).to_broadcast(shp)
        t0_, t1_ = AB[4][:], AB[5][:]
        V(lambda e: e.tensor_tensor(out=t0_, in0=xr_b, in1=pw[0], op=ALU.mult), PT, PT)
        V(lambda e: e.tensor_tensor(out=t1_, in0=xi_b, in1=pw[1], op=ALU.mult), PT, PT)
        V(lambda e: e.tensor_tensor(out=o_re, in0=t0_, in1=t1_, op=ALU.subtract), PT, PT)
        q_ = pn if neg_im else pw
        V(lambda e: e.tensor_tensor(out=t0_, in0=xr_b, in1=q_[1], op=ALU.mult), PT, PT)
        V(lambda e: e.tensor_tensor(out=t1_, in0=xi_b, in1=q_[0], op=ALU.mult), PT, PT)
        V(lambda e: e.tensor_tensor(out=o_im, in0=t0_, in1=t1_, op=ALU.add), PT, PT)

    cpow_scale(AB[0][:], AB[1][:], bbre[:], bbim[:], 0, False)
    cpow_scale(AB[2][:], AB[3][:], cre[:], cim[:], 1, True)

    def scatter(gp, j):
        r = slice(64 * j, 64 * j + 64)
        cc = slice(16 * (2 * gp + j), 16 * (2 * gp + j) + 16)
        eng = "dve" if j == 0 else "pool"
        for ri in range(2):
            V(lambda e, ri=ri: e.tensor_copy(out=pads[ri][r, :, gp, cc], in_=AB[ri][r, :, gp, :]), PT, PT, eng)
            V(lambda e, ri=ri: e.tensor_copy(out=WC[ri][r, :, gp, cc], in_=AB[2 + ri][r, :, gp, :]), PT, PT, eng)
        V(lambda e: e.tensor_copy(out=Cpf[0][r, gp, cc], in_=cre[r, gp, :]), PT, PT, eng)
        V(lambda e: e.tensor_scalar(out=Cpf[1][r, gp, cc], in0=cim[r, gp, :], scalar1=-1.0, scalar2=None, op0=ALU.mult),
          PT, PT, eng)
    for gp in range(4):
        for j in range(2):
            scatter(gp, j)

    def build_we(ri, tau):
        for gp in range(4):
            P.op("pe", lambda e, gp=gp: e.transpose(pst[:, gp * 128:(gp + 1) * 128], pads[ri][:, tau, gp, :], ident[:]),
                 reads=PT, writes=[dpsy])
        V(lambda e: e.tensor_copy(out=WE[ri][:, TS - 1 - tau, :, :].rearrange("p g q -> p (g q)"), in_=pst[:]),
          [dpsy], PT)
    for ri in range(2):
        for tau in range(TS):
            build_we(ri, tau)

    def build_k(t0):
        for tau in range(t0, t0 + 4):
            n_ = 0
            for gp in range(4):
                for ri in range(2):
                    P.op("pe", lambda e, tau=tau, gp=gp, ri=ri, n_=n_: e.matmul(
                        pst[:, (tau - t0) * 128:(tau - t0 + 1) * 128], pads[ri][:, tau, gp, :], Cpf[ri][:, gp, :],
                        start=(n_ == 0), stop=(n_ == 7), skip_group_check=True), reads=PT, writes=[dpsy])
                    n_ += 1
        V(lambda e: e.tensor_copy(out=Kmat[:, t0:t0 + 4, :].rearrange("p t c -> p (t c)"), in_=pst[:]), [dpsy], PT)
    build_k(0)
    build_k(4)

    def cmul(ore, oim, are, aim, bre_, bim_):
        V(lambda e: e.tensor_tensor(out=q0[:], in0=are, in1=bre_, op=ALU.mult), PT, PT)
        V(lambda e: e.tensor_tensor(out=q1[:], in0=aim, in1=bim_, op=ALU.mult), PT, PT)
        V(lambda e: e.tensor_tensor(out=q2[:], in0=are, in1=bim_, op=ALU.mult), PT, PT)
        V(lambda e: e.tensor_tensor(out=den[:], in0=aim, in1=bre_, op=ALU.mult), PT, PT)
        V(lambda e: e.tensor_tensor(out=ore, in0=q0[:], in1=q1[:], op=ALU.subtract), PT, PT)
        V(lambda e: e.tensor_tensor(out=oim, in0=q2[:], in1=den[:], op=ALU.add), PT, PT)
    V(lambda e: e.tensor_copy(out=sq_re[:], in_=PW[0][:, :, 8]), PT, PT)
    V(lambda e: e.tensor_copy(out=sq_im[:], in_=PW[1][:, :, 8]), PT, PT)
    for _ in range(6):
        cmul(sq_re[:], sq_im[:], sq_re[:], sq_im[:], sq_re[:], sq_im[:])
    for k in range(3):
        (lambda k: (V(lambda e: e.tensor_copy(out=Ad[0][:, :, k], in_=sq_re[:]), PT, PT),
                    V(lambda e: e.tensor_copy(out=Ad[1][:, :, k], in_=sq_im[:]), PT, PT),
                    V(lambda e: e.tensor_scalar(out=Ad[2][:, :, k], in0=sq_im[:], scalar1=-1.0, scalar2=None,
                                                op0=ALU.mult), PT, PT)))(k)
        if k < 2:
            cmul(sq_re[:], sq_im[:], sq_re[:], sq_im[:], sq_re[:], sq_im[:])

    V(lambda e: e.memset(bt0[:], 0.0), PT, PT)

    ud = C.sb("ud", [128, 2, TS, NM], BF16)
    dud = Dep()
    for b_ in range(2):
        (lambda b_: P.op("act", lambda e: e.activation(
            out=ud[:, b_], in_=u_sb[:, b_ * SEQ:(b_ + 1) * SEQ].rearrange("p (m s) -> p s m", s=TS), func=AF.Copy),
            reads=[du], writes=[dud]))(b_)

    def uview(b, s):
        return ud[:, b, s, :]

    def e_pass(gp, b):
        for ri in range(2):
            for s in range(TS):
                P.op("pe", lambda e, ri=ri, s=s: e.matmul(psb[ri][:], WE[ri][:, s, gp, :], uview(b, s),
                                                          start=(s == 0), stop=(s == TS - 1)),
                     reads=[dud] + PT, writes=[dpsb[ri]])
            V(lambda e, ri=ri: e.tensor_copy(out=bu[ri][:], in_=psb[ri][:]), [dpsb[ri]], [dbu[ri]])
        c_, s_ = cs2[0][:, gp, :], cs2[1][:, gp, :]
        V(lambda e: e.tensor_tensor(out=tt[0][:], in0=bu[0][:], in1=c_, op=ALU.mult), [dbu[0]] + PT, [dtq[0]])
        V(lambda e: e.tensor_tensor(out=tt[1][:], in0=bu[1][:], in1=s_, op=ALU.mult), [dbu[1]] + PT, [dtq[1]])
        V(lambda e: e.tensor_tensor(out=w2[0][:], in0=tt[0][:], in1=tt[1][:], op=ALU.add), [dtq[0], dtq[1]], [dw[0]])
        V(lambda e: e.tensor_tensor(out=tt[2][:], in0=bu[1][:], in1=c_, op=ALU.mult), [dbu[1]] + PT, [dtq[2]], ENG2)
        V(lambda e: e.tensor_tensor(out=tt[3][:], in0=bu[0][:], in1=s_, op=ALU.mult), [dbu[0]] + PT, [dtq[3]], ENG2)
        V(lambda e: e.tensor_tensor(out=w2[1][:], in0=tt[2][:], in1=tt[3][:], op=ALU.subtract), [dtq[2], dtq[3]],
          [dw[1]], ENG2)

    def scans(gp, extra):
        for i in range(2):
            V(lambda e, i=i: e.tensor_tensor_scan(out=z2[i][:], data0=rf2[:, gp, :], data1=w2[i][:], initial=0.0,
                                                  op0=ALU.mult, op1=ALU.add), [dw[i]] + PT + extra, [dz[i]])

    def hs_step(gp, k, cur):
        d_ = 1 << k
        nxt = 2 - cur
        ar, ai, ain = Ad[0][:, gp, k:k + 1], Ad[1][:, gp, k:k + 1], Ad[2][:, gp, k:k + 1]
        sre, sim = cst[cur], cst[cur + 1]
        nre_, nim_ = cst[nxt], cst[nxt + 1]
        CS = [dcst]
        n = NC2
        V(lambda e: e.tensor_copy(out=nre_[:, 0:d_], in_=sre[:, 0:d_]), CS, CS)
        V(lambda e: e.tensor_copy(out=nim_[:, 0:d_], in_=sim[:, 0:d_]), CS, CS)
        V(lambda e: e.scalar_tensor_tensor(out=cst[4][:, d_:n], in0=sre[:, 0:n - d_], scalar=ar, in1=sre[:, d_:n],
                                           op0=ALU.mult, op1=ALU.add), CS + PT, CS)
        V(lambda e: e.scalar_tensor_tensor(out=nre_[:, d_:n], in0=sim[:, 0:n - d_], scalar=ain, in1=cst[4][:, d_:n],
                                           op0=ALU.mult, op1=ALU.add), CS + PT, CS)
        V(lambda e: e.scalar_tensor_tensor(out=cst[4][:, d_:n], in0=sim[:, 0:n - d_], scalar=ar, in1=sim[:, d_:n],
                                           op0=ALU.mult, op1=ALU.add), CS + PT, CS)
        V(lambda e: e.scalar_tensor_tensor(out=nim_[:, d_:n], in0=sre[:, 0:n - d_], scalar=ai, in1=cst[4][:, d_:n],
                                           op0=ALU.mult, op1=ALU.add), CS + PT, CS)
        return nxt

    def carry(gp):
        g1 = slice(gp, gp + 1)
        CS = [dcst]
        zE = [z2[i][:].rearrange("p (c j) -> p c j", j=J2)[:, :, J2 - 1] for i in range(2)]
        c63 = cs2[0][:, gp, J2 - 1:J2]
        s63 = cs2[1][:, gp, J2 - 1:J2]
        V(lambda e: e.tensor_scalar(out=cst[2][:], in0=zE[0], scalar1=c63, scalar2=None, op0=ALU.mult), [dz[0]] + PT, CS)
        V(lambda e: e.scalar_tensor_tensor(out=cst[0][:], in0=zE[1], scalar=s63n[:, g1], in1=cst[2][:], op0=ALU.mult,
                                           op1=ALU.add), [dz[1]] + PT + CS, CS)
        V(lambda e: e.tensor_scalar(out=cst[2][:], in0=zE[0], scalar1=s63, scalar2=None, op0=ALU.mult),
          [dz[0]] + PT + CS, CS)
        V(lambda e: e.scalar_tensor_tensor(out=cst[1][:], in0=zE[1], scalar=c63, in1=cst[2][:], op0=ALU.mult,
                                           op1=ALU.add), [dz[1]] + PT + CS, CS)
        cur = 0
        for k in range(3):
            cur = hs_step(gp, k, cur)
        Sre, Sim = cst[cur], cst[cur + 1]
        n1 = NC2 - 1
        w0 = [w2[i][:].rearrange("p (c j) -> p c j", j=J2)[:, 1:NC2, 0] for i in range(2)]
        a8r, a8i = PW[0][:, gp, 8:9], PW[1][:, gp, 8:9]
        V(lambda e: e.scalar_tensor_tensor(out=cst[5][:, 0:n1], in0=Sre[:, 0:n1], scalar=a8r, in1=w0[0], op0=ALU.mult,
                                           op1=ALU.add), CS + PT + [dw[0]], CS)
        V(lambda e: e.scalar_tensor_tensor(out=w0[0], in0=Sim[:, 0:n1], scalar=a8imn[:, g1], in1=cst[5][:, 0:n1],
                                           op0=ALU.mult, op1=ALU.add), CS + PT, [dw[0]])
        V(lambda e: e.scalar_tensor_tensor(out=cst[6][:, 0:n1], in0=Sim[:, 0:n1], scalar=a8r, in1=w0[1], op0=ALU.mult,
                                           op1=ALU.add), CS + PT + [dw[1]], CS)
        V(lambda e: e.scalar_tensor_tensor(out=w0[1], in0=Sre[:, 0:n1], scalar=a8i, in1=cst[6][:, 0:n1],
                                           op0=ALU.mult, op1=ALU.add), CS + PT, [dw[1]])

    def demod(gp, b):
        n1 = NM - 1
        Sp, dSp = Sprev2[b], dS2[b]
        c_, s_ = cs2[0][:, gp, 0:n1], cs2[1][:, gp, 0:n1]
        V(lambda e: e.tensor_tensor(out=tt[0][:, 0:n1], in0=z2[0][:, 0:n1], in1=c_, op=ALU.mult), [dz[0]] + PT, [dtq[0]])
        V(lambda e: e.tensor_tensor(out=tt[1][:, 0:n1], in0=z2[1][:, 0:n1], in1=s_, op=ALU.mult), [dz[1]] + PT, [dtq[1]])
        V(lambda e: e.tensor_tensor(out=Sp[:, gp, 0, 1:NM], in0=tt[0][:, 0:n1], in1=tt[1][:, 0:n1], op=ALU.subtract),
          [dtq[0], dtq[1]], [dSp])
        V(lambda e: e.tensor_tensor(out=tt[2][:, 0:n1], in0=z2[0][:, 0:n1], in1=s_, op=ALU.mult), [dz[0]] + PT, [dtq[2]],
          ENG2)
        V(lambda e: e.tensor_tensor(out=tt[3][:, 0:n1], in0=z2[1][:, 0:n1], in1=c_, op=ALU.mult), [dz[1]] + PT, [dtq[3]],
          ENG2)
        V(lambda e: e.tensor_tensor(out=Sp[:, gp, 1, 1:NM], in0=tt[2][:, 0:n1], in1=tt[3][:, 0:n1], op=ALU.add),
          [dtq[2], dtq[3]], [dSp], ENG2)

    def y_out(b, s):
        mm = []
        for sp_ in range(s + 1):
            mm.append((Kmat[:, s - sp_, :], uview(b, sp_), [dud] + PT))
        for gp in range(4):
            for ri in range(2):
                mm.append((WC[ri][:, s, gp, :], Sprev2[b][:, gp, ri, :], [dS2[b]] + PT))
        for n_, (l_, r_, dd) in enumerate(mm):
            P.op("pe", lambda e, l_=l_, r_=r_, n_=n_: e.matmul(psy[:], l_, r_, start=(n_ == 0), stop=(n_ == len(mm) - 1)),
                 reads=dd, writes=[dpsy])
        yv = y_sb[:, b * SEQ:(b + 1) * SEQ].rearrange("p (m s) -> p m s", s=TS)[:, :, s]
        V(lambda e: e.scalar_tensor_tensor(out=yv, in0=uview(b, s), scalar=d_sb[:, 0:1], in1=psy[:], op0=ALU.mult,
                                           op1=ALU.add), [dud, dpsy] + PT, [dy])

    def one_pass(gp, b):
        e_pass(gp, b)
        scans(gp, [])
        carry(gp)
        scans(gp, [dcst])
        demod(gp, b)
    for gp in range(4):
        one_pass(gp, 0)
    for gp in range(4):
        one_pass(gp, 1)
        if gp == 1 or not merged:
            if gp == 1 or gp == 3:
                pass
        if gp == 1:
            for s in range(TS):
                y_out(0, s)
    for s in range(TS):
        y_out(1, s)
    P.dma("sp", yp[:, 0:SEQ], y_sb[:, 0:SEQ], reads=[dy])
    P.dma("sp", yp[:, SEQ:NT], y_sb[:, SEQ:NT], reads=[dy])


def build_B2F():
    nc = bass.Bass("TRN2", target_bir_lowering=False)
    with ExitStack() as st:
        st.enter_context(nc.allow_low_precision("bf16 matmul operands, fp32 accumulation"))
        C = Ctx(nc, st)
        P = Prog(nc, st)
        emit_B2F(nc, C, P)
        P.finish()
    return nc
```
